# Optimizing a Trainium2 kernel written in Bass

```python
import math
import jax
import jax.numpy as jnp
from jax import lax
import numpy as np

D_MODEL = 1024
BATCH = 8
SEQ = 4096
DEPTH = 2
DEC_BATCH = 32
DEC_SEQ = 32
PAST_LEN = 2048

CHUNK = 64
N_EVEN = (DEPTH + 1) // 2
N_ODD = DEPTH // 2
RMS_EPS = 1e-6
ROPE_THETA = 500000.0
NEG_INF = -1e30

SWA_HEADS = 8
SWA_KV_HEADS = 2
SWA_GROUP = SWA_HEADS // SWA_KV_HEADS
HEAD_DIM = 64
ROT_DIM = HEAD_DIM // 4
WINDOW = 128
WINDOW_CHUNKS = WINDOW // CHUNK
BAND = (WINDOW_CHUNKS + 1) * CHUNK
SWA_Q = SWA_HEADS * HEAD_DIM
SWA_KV = SWA_KV_HEADS * HEAD_DIM
SWA_SCALE = HEAD_DIM ** -0.5

S5_WIDTH = 512
S5_GROUP = 16
S5_GROUPS = S5_WIDTH // S5_GROUP
S5_STATE = 64

POOL_WIDTH = 512
POOL_WINDOWS = (2, 4, 8, 16)
POOL_GROUP = POOL_WIDTH // len(POOL_WINDOWS)
POOL_MAX = 16
POOL_BUF = POOL_MAX - 1

MLA_HEADS = 8
Q_LORA = 512
KV_LORA = 256
NOPE_DIM = 64
ROPE_DIM = 32
V_DIM = 64
Q_BLOCK = 128
MLA_SCALE = (NOPE_DIM + ROPE_DIM) ** -0.5

EVEN_IN = SWA_Q + 2 * SWA_KV + S5_WIDTH
EVEN_MIX = SWA_Q + S5_WIDTH
ODD_IN = Q_LORA + KV_LORA + ROPE_DIM + POOL_WIDTH
ODD_MIX = POOL_WIDTH + MLA_HEADS * V_DIM

PEER_HEADS = 8
N_KEYS = 128
N_EXPERTS = N_KEYS * N_KEYS
D_KEY = 128
D_HALF = D_KEY // 2
PEER_TOPK = 16
PEER_BLOCK = 256

kernel_name = 'hybrid_streaming_encoder_step'


def rmsnorm(x, g):
    xf = x.astype(jnp.float32)
    y = xf * lax.rsqrt(jnp.mean(xf * xf, axis=-1, keepdims=True) + RMS_EPS)
    return (y * g.astype(jnp.float32)).astype(x.dtype)


def rope(x, pos):
    r = x.shape[-1]
    inv = ROPE_THETA ** (-jnp.arange(0, r, 2, dtype=jnp.float32) / r)
    ang = pos.astype(jnp.float32)[:, None] * inv[None, :]
    cos = jnp.cos(ang)[:, None, :]
    sin = jnp.sin(ang)[:, None, :]
    xf = x.astype(jnp.float32)
    x1, x2 = xf[..., : r // 2], xf[..., r // 2:]
    return jnp.concatenate([x1 * cos - x2 * sin, x2 * cos + x1 * sin], axis=-1).astype(x.dtype)


def partial_rope(x, pos):
    return jnp.concatenate([rope(x[..., :ROT_DIM], pos), x[..., ROT_DIM:]], axis=-1)


def sink_softmax(s, sink):
    sk = sink.astype(jnp.float32).reshape(SWA_KV_HEADS, SWA_GROUP)[:, :, None, None]
    m = jnp.maximum(jnp.max(s, axis=-1, keepdims=True), sk)
    p = jnp.exp(s - m)
    return p / (jnp.sum(p, axis=-1, keepdims=True) + jnp.exp(sk - m))


def swa_attend_prompt(q, k, v, sink):
    b, s = q.shape[:2]
    nc = s // CHUNK
    qc = q.reshape(b, nc, CHUNK, SWA_KV_HEADS, SWA_GROUP, HEAD_DIM)
    pad = WINDOW_CHUNKS * CHUNK

    def band(t):
        tp = jnp.pad(t, ((0, 0), (pad, 0), (0, 0), (0, 0)))
        tp = tp.reshape(b, nc + WINDOW_CHUNKS, CHUNK, SWA_KV_HEADS, HEAD_DIM)
        return jnp.concatenate([tp[:, i:i + nc] for i in range(WINDOW_CHUNKS + 1)], axis=2)

    kb, vb = band(k), band(v)
    sc = jnp.einsum('bcqhgd,bckhd->bchgqk', qc, kb, preferred_element_type=jnp.float32) * SWA_SCALE
    key_pos = (jnp.arange(nc)[:, None] - WINDOW_CHUNKS) * CHUNK + jnp.arange(BAND)[None, :]
    sc = jnp.where((key_pos >= 0)[None, :, None, None, None, :], sc, NEG_INF)
    p = sink_softmax(sc, sink).astype(v.dtype)
    o = jnp.einsum('bchgqk,bckhd->bcqhgd', p, vb)
    return o.reshape(b, s, SWA_Q)


def swa_attend_sample(q, k_all, v_all, sink):
    b, t = q.shape[:2]
    qg = q.reshape(b, t, SWA_KV_HEADS, SWA_GROUP, HEAD_DIM)
    sc = jnp.einsum('bqhgd,bkhd->bhgqk', qg, k_all, preferred_element_type=jnp.float32) * SWA_SCALE
    p = sink_softmax(sc, sink).astype(v_all.dtype)
    o = jnp.einsum('bhgqk,bkhd->bqhgd', p, v_all)
    return o.reshape(b, t, SWA_Q)


def s5_scan(u, h0_re, h0_im, lam_re, lam_im, log_dt, b_re, b_im, c_re, c_im, d_skip):
    bsz, t = u.shape[:2]
    uf = u.astype(jnp.float32).reshape(bsz, t, S5_GROUPS, S5_GROUP)
    lr = jnp.minimum(lam_re.astype(jnp.float32), -1e-4)
    li = lam_im.astype(jnp.float32)
    dt = jnp.exp(log_dt.astype(jnp.float32))[:, None]
    mag = jnp.exp(lr * dt)
    ang = li * dt
    ab_re, ab_im = mag * jnp.cos(ang), mag * jnp.sin(ang)
    den = lr * lr + li * li
    nr, ni = ab_re - 1.0, ab_im
    f_re = (nr * lr + ni * li) / den
    f_im = (ni * lr - nr * li) / den
    br, bi = b_re.astype(jnp.float32), b_im.astype(jnp.float32)
    bb_re = f_re[..., None] * br - f_im[..., None] * bi
    bb_im = f_re[..., None] * bi + f_im[..., None] * br
    bu_re = jnp.einsum('btgc,gnc->btgn', uf, bb_re)
    bu_im = jnp.einsum('btgc,gnc->btgn', uf, bb_im)
    h0r, h0i = h0_re.astype(jnp.float32), h0_im.astype(jnp.float32)
    bu_re = bu_re.at[:, 0].add(ab_re * h0r - ab_im * h0i)
    bu_im = bu_im.at[:, 0].add(ab_re * h0i + ab_im * h0r)
    a_re = jnp.broadcast_to(ab_re, (1, t, S5_GROUPS, S5_STATE))
    a_im = jnp.broadcast_to(ab_im, (1, t, S5_GROUPS, S5_STATE))

    def combine(e1, e2):
        a1r, a1i, b1r, b1i = e1
        a2r, a2i, b2r, b2i = e2
        return (a2r * a1r - a2i * a1i, a2r * a1i + a2i * a1r,
                a2r * b1r - a2i * b1i + b2r, a2r * b1i + a2i * b1r + b2i)

    _, _, hr, hi = lax.associative_scan(combine, (a_re, a_im, bu_re, bu_im), axis=1)
    y = (jnp.einsum('btgn,gcn->btgc', hr, c_re.astype(jnp.float32))
         - jnp.einsum('btgn,gcn->btgc', hi, c_im.astype(jnp.float32))
         + d_skip.astype(jnp.float32).reshape(S5_GROUPS, S5_GROUP) * uf)
    return y.reshape(bsz, t, S5_WIDTH).astype(u.dtype), hr[:, -1], hi[:, -1]


def pool_mix(u, prev, pos, pool_w, pool_scale):
    t = u.shape[1]
    ext = jnp.concatenate([prev.astype(u.dtype), u], axis=1)
    extf = ext.astype(jnp.float32)
    cs = jnp.pad(jnp.cumsum(extf, axis=1), ((0, 0), (1, 0), (0, 0)))
    uf = u.astype(jnp.float32)
    outs = []
    for gi, w in enumerate(POOL_WINDOWS):
        sl = slice(gi * POOL_GROUP, (gi + 1) * POOL_GROUP)
        tot = cs[:, POOL_MAX:POOL_MAX + t, sl] - cs[:, POOL_MAX - w:POOL_MAX - w + t, sl]
        cnt = jnp.minimum(pos + 1, w).astype(jnp.float32)[None, :, None]
        outs.append(tot / cnt - uf[..., sl])
    m = jnp.stack(outs, axis=2)
    y = jnp.einsum('btgc,gcd->btgd', m, pool_w.astype(jnp.float32)).reshape(u.shape[0], t, POOL_WIDTH)
    y = y * pool_scale.astype(jnp.float32)
    return y.astype(u.dtype), ext[:, -POOL_BUF:]


def mla_block(qa, qp, c, kp, mask=None):
    s = (jnp.einsum('bthc,bsc->bhts', qa, c, preferred_element_type=jnp.float32)
         + jnp.einsum('bthr,bsr->bhts', qp, kp, preferred_element_type=jnp.float32)) * MLA_SCALE
    if mask is not None:
        s = jnp.where(mask[None, None], s, NEG_INF)
    p = jax.nn.softmax(s, axis=-1).astype(c.dtype)
    return jnp.einsum('bhts,bsc->bthc', p, c)


def mla_prompt(qa, qp, c, kp):
    b, s = qa.shape[:2]
    nb = s // Q_BLOCK
    kchunk = jnp.arange(s) // CHUNK
    qa_b = qa.reshape(b, nb, Q_BLOCK, MLA_HEADS, KV_LORA).swapaxes(0, 1)
    qp_b = qp.reshape(b, nb, Q_BLOCK, MLA_HEADS, ROPE_DIM).swapaxes(0, 1)

    def one(args):
        qa_i, qp_i, i = args
        qchunk = (i * Q_BLOCK + jnp.arange(Q_BLOCK)) // CHUNK
        mask = kchunk[None, :] <= qchunk[:, None]
        return mla_block(qa_i, qp_i, c, kp, mask)

    o = lax.map(one, (qa_b, qp_b, jnp.arange(nb)))
    return o.swapaxes(0, 1).reshape(b, s, MLA_HEADS, KV_LORA)


def even_mixer(hn, pos, k_prev, v_prev, hre_prev, him_prev, w_in, w_out, sink, lam_re, lam_im,
               log_dt, b_re, b_im, c_re, c_im, d_skip, w_glu, b_glu):
    b, t = hn.shape[:2]
    proj = hn @ w_in
    q = proj[..., :SWA_Q].reshape(b, t, SWA_HEADS, HEAD_DIM)
    k = proj[..., SWA_Q:SWA_Q + SWA_KV].reshape(b, t, SWA_KV_HEADS, HEAD_DIM)
    v = proj[..., SWA_Q + SWA_KV:SWA_Q + 2 * SWA_KV].reshape(b, t, SWA_KV_HEADS, HEAD_DIM)
    u = proj[..., SWA_Q + 2 * SWA_KV:]
    q = partial_rope(q, pos)
    k = partial_rope(k, pos)
    if k_prev is None:
        att = swa_attend_prompt(q, k, v, sink)
        k_all, v_all = k, v
        hre_prev = jnp.zeros((b, S5_GROUPS, S5_STATE), jnp.float32)
        him_prev = jnp.zeros((b, S5_GROUPS, S5_STATE), jnp.float32)
    else:
        k_all = jnp.concatenate([k_prev.astype(k.dtype), k], axis=1)
        v_all = jnp.concatenate([v_prev.astype(v.dtype), v], axis=1)
        att = swa_attend_sample(q, k_all, v_all, sink)
    y, hre, him = s5_scan(u, hre_prev, him_prev, lam_re, lam_im, log_dt, b_re, b_im, c_re, c_im, d_skip)
    z = jax.nn.gelu(y)
    s5o = z * jax.nn.sigmoid(z @ w_glu + b_glu)
    out = jnp.concatenate([att, s5o.astype(att.dtype)], axis=-1) @ w_out
    return out, k_all[:, -WINDOW:], v_all[:, -WINDOW:], hre, him


def odd_mixer(hn, pos, pool_prev, ckv_prev, kpe_prev, w_in, w_out, pool_w, pool_scale,
              q_norm, kv_norm, w_uq, w_uk, w_uv):
    b, t = hn.shape[:2]
    proj = hn @ w_in
    o0 = Q_LORA
    o1 = o0 + KV_LORA
    o2 = o1 + ROPE_DIM
    cq, ckv, kpe, u = proj[..., :o0], proj[..., o0:o1], proj[..., o1:o2], proj[..., o2:]
    if pool_prev is None:
        pool_prev = jnp.zeros((b, POOL_BUF, POOL_WIDTH), u.dtype)
    pool_out, pool_new = pool_mix(u, pool_prev, pos, pool_w, pool_scale)
    q = (rmsnorm(cq, q_norm) @ w_uq).reshape(b, t, MLA_HEADS, NOPE_DIM + ROPE_DIM)
    q_nope = q[..., :NOPE_DIM]
    q_pe = rope(q[..., NOPE_DIM:], pos)
    c = rmsnorm(ckv, kv_norm)
    kp = rope(kpe[:, :, None, :], pos)[:, :, 0]
    qa = jnp.einsum('bthn,chn->bthc', q_nope, w_uk)
    if ckv_prev is None:
        o_lat = mla_prompt(qa, q_pe, c, kp)
    else:
        c_all = jnp.concatenate([ckv_prev.astype(c.dtype), c], axis=1)
        kp_all = jnp.concatenate([kpe_prev.astype(kp.dtype), kp], axis=1)
        o_lat = mla_block(qa, q_pe, c_all, kp_all)
    mla_out = jnp.einsum('bthc,chv->bthv', o_lat, w_uv).reshape(b, t, MLA_HEADS * V_DIM)
    out = jnp.concatenate([pool_out, mla_out.astype(pool_out.dtype)], axis=-1) @ w_out
    return out, pool_new, c, kp


def peer(x, w_q, sub_keys, u_tab, v_tab):
    shp = x.shape
    xt = x.reshape(-1, D_MODEL)
    n = xt.shape[0]
    blk = min(PEER_BLOCK, n)
    n_pad = -(-n // blk) * blk
    xt = jnp.pad(xt, ((0, n_pad - n), (0, 0)))

    def one(xb):
        q = (xb @ w_q).reshape(blk, PEER_HEADS, 2, D_HALF)
        s = jnp.einsum('nhpd,hpkd->nhpk', q, sub_keys, preferred_element_type=jnp.float32)
        sv, si = lax.top_k(s, PEER_TOPK)
        cand = (sv[:, :, 0, :, None] + sv[:, :, 1, None, :]).reshape(blk, PEER_HEADS, PEER_TOPK * PEER_TOPK)
        cv, ci = lax.top_k(cand, PEER_TOPK)
        i1 = jnp.take_along_axis(si[:, :, 0], ci // PEER_TOPK, axis=-1)
        i2 = jnp.take_along_axis(si[:, :, 1], ci % PEER_TOPK, axis=-1)
        eid = i1 * N_KEYS + i2
        g = jax.nn.softmax(cv, axis=-1)
        act = jax.nn.gelu(jnp.einsum('nd,nhkd->nhk', xb, u_tab[eid], preferred_element_type=jnp.float32),
                          approximate=False)
        return jnp.einsum('nhk,nhkd->nd', (g * act).astype(xb.dtype), v_tab[eid])

    out = lax.map(one, xt.reshape(n_pad // blk, blk, D_MODEL)).reshape(n_pad, D_MODEL)[:n]
    return out.reshape(shp).astype(x.dtype)


def setup_inputs(seed: int = 0) -> dict:
    key = jax.random.key(seed)
    ks = iter(jax.random.split(key, 64))

    def nrm(shape, scale):
        return jax.random.normal(next(ks), shape, jnp.float32) * scale

    E, O, L = N_EVEN, N_ODD, DEPTH
    return {
        'x_prompt': nrm((BATCH, SEQ, D_MODEL), 1.0),
        'x_sample': nrm((DEC_BATCH, DEC_SEQ, D_MODEL), 1.0),
        'cache_swa_k': nrm((E, DEC_BATCH, WINDOW, SWA_KV_HEADS, HEAD_DIM), 1.0),
        'cache_swa_v': nrm((E, DEC_BATCH, WINDOW, SWA_KV_HEADS, HEAD_DIM), 1.0),
        'state_ssm_re': nrm((E, DEC_BATCH, S5_GROUPS, S5_STATE), 0.3),
        'state_ssm_im': nrm((E, DEC_BATCH, S5_GROUPS, S5_STATE), 0.3),
        'state_pool': nrm((O, DEC_BATCH, POOL_BUF, POOL_WIDTH), 1.0),
        'cache_mla_ckv': nrm((O, DEC_BATCH, PAST_LEN, KV_LORA), 1.0),
        'cache_mla_kpe': nrm((O, DEC_BATCH, PAST_LEN, ROPE_DIM), 1.0),
        'norm_mix': 1.0 + nrm((L, D_MODEL), 0.02),
        'norm_ffn': 1.0 + nrm((L, D_MODEL), 0.02),
        'norm_final': 1.0 + nrm((D_MODEL,), 0.02),
        'w_in_even': nrm((E, D_MODEL, EVEN_IN), D_MODEL ** -0.5),
        'w_out_even': nrm((E, EVEN_MIX, D_MODEL), EVEN_MIX ** -0.5),
        'swa_sink': nrm((E, SWA_HEADS), 1.0),
        's5_lam_re': -0.5 + nrm((E, S5_GROUPS, S5_STATE), 0.01),
        's5_lam_im': jnp.pi * jnp.arange(S5_STATE, dtype=jnp.float32) + nrm((E, S5_GROUPS, S5_STATE), 0.01),
        's5_log_dt': jax.random.uniform(next(ks), (E, S5_GROUPS), jnp.float32, math.log(1e-3), math.log(1e-1)),
        's5_b_re': nrm((E, S5_GROUPS, S5_STATE, S5_GROUP), (2 * S5_GROUP) ** -0.5),
        's5_b_im': nrm((E, S5_GROUPS, S5_STATE, S5_GROUP), (2 * S5_GROUP) ** -0.5),
        's5_c_re': nrm((E, S5_GROUPS, S5_GROUP, S5_STATE), S5_STATE ** -0.5),
        's5_c_im': nrm((E, S5_GROUPS, S5_GROUP, S5_STATE), S5_STATE ** -0.5),
        's5_d': nrm((E, S5_WIDTH), 1.0),
        's5_w_glu': nrm((E, S5_WIDTH, S5_WIDTH), S5_WIDTH ** -0.5),
        's5_b_glu': nrm((E, S5_WIDTH), 0.02),
        'w_in_odd': nrm((O, D_MODEL, ODD_IN), D_MODEL ** -0.5),
        'w_out_odd': nrm((O, ODD_MIX, D_MODEL), ODD_MIX ** -0.5),
        'pool_w': nrm((O, len(POOL_WINDOWS), POOL_GROUP, POOL_GROUP), POOL_GROUP ** -0.5),
        'pool_scale': 1.0 + nrm((O, POOL_WIDTH), 0.1),
        'mla_q_norm': 1.0 + nrm((O, Q_LORA), 0.02),
        'mla_kv_norm': 1.0 + nrm((O, KV_LORA), 0.02),
        'mla_w_uq': nrm((O, Q_LORA, MLA_HEADS * (NOPE_DIM + ROPE_DIM)), Q_LORA ** -0.5),
        'mla_w_uk': nrm((O, KV_LORA, MLA_HEADS, NOPE_DIM), KV_LORA ** -0.5),
        'mla_w_uv': nrm((O, KV_LORA, MLA_HEADS, V_DIM), KV_LORA ** -0.5),
        'peer_w_q': nrm((L, D_MODEL, PEER_HEADS * D_KEY), D_MODEL ** -0.5),
        'peer_keys': nrm((L, PEER_HEADS, 2, N_KEYS, D_HALF), D_HALF ** -0.5),
        'peer_u': nrm((L, N_EXPERTS, D_MODEL), D_MODEL ** -0.5),
        'peer_v': nrm((L, N_EXPERTS, D_MODEL), (PEER_HEADS * PEER_TOPK) ** -0.5),
    }


def reference(x_prompt, x_sample, cache_swa_k, cache_swa_v, state_ssm_re, state_ssm_im, state_pool,
              cache_mla_ckv, cache_mla_kpe, norm_mix, norm_ffn, norm_final, w_in_even, w_out_even,
              swa_sink, s5_lam_re, s5_lam_im, s5_log_dt, s5_b_re, s5_b_im, s5_c_re, s5_c_im, s5_d,
              s5_w_glu, s5_b_glu, w_in_odd, w_out_odd, pool_w, pool_scale, mla_q_norm, mla_kv_norm,
              mla_w_uq, mla_w_uk, mla_w_uv, peer_w_q, peer_keys, peer_u, peer_v):
    pos_p = jnp.arange(x_prompt.shape[1])
    pos_s = PAST_LEN + jnp.arange(x_sample.shape[1])
    xp, xs = x_prompt, x_sample
    kp_l, vp_l, rp_l, ip_l, poolp_l, cp_l, ep_l = [], [], [], [], [], [], []
    ks_l, vs_l, rs_l, is_l, pools_l, cs_l, es_l = [], [], [], [], [], [], []
    for layer in range(DEPTH):
        i = layer // 2
        if layer % 2 == 0:
            ew = (w_in_even[i], w_out_even[i], swa_sink[i], s5_lam_re[i], s5_lam_im[i], s5_log_dt[i],
                  s5_b_re[i], s5_b_im[i], s5_c_re[i], s5_c_im[i], s5_d[i], s5_w_glu[i], s5_b_glu[i])
            mp, k1, v1, r1, i1 = even_mixer(rmsnorm(xp, norm_mix[layer]), pos_p, None, None, None, None, *ew)
            ms, k2, v2, r2, i2 = even_mixer(rmsnorm(xs, norm_mix[layer]), pos_s, cache_swa_k[i], cache_swa_v[i],
                                            state_ssm_re[i], state_ssm_im[i], *ew)
            kp_l.append(k1); vp_l.append(v1); rp_l.append(r1); ip_l.append(i1)
            ks_l.append(k2); vs_l.append(v2); rs_l.append(r2); is_l.append(i2)
        else:
            ow = (w_in_odd[i], w_out_odd[i], pool_w[i], pool_scale[i], mla_q_norm[i], mla_kv_norm[i],
                  mla_w_uq[i], mla_w_uk[i], mla_w_uv[i])
            mp, p1, c1, e1 = odd_mixer(rmsnorm(xp, norm_mix[layer]), pos_p, None, None, None, *ow)
            ms, p2, c2, e2 = odd_mixer(rmsnorm(xs, norm_mix[layer]), pos_s, state_pool[i], cache_mla_ckv[i],
                                       cache_mla_kpe[i], *ow)
            poolp_l.append(p1); cp_l.append(c1); ep_l.append(e1)
            pools_l.append(p2); cs_l.append(c2); es_l.append(e2)
        xp = xp + mp
        xs = xs + ms
        xp = xp + peer(rmsnorm(xp, norm_ffn[layer]), peer_w_q[layer], peer_keys[layer], peer_u[layer], peer_v[layer])
        xs = xs + peer(rmsnorm(xs, norm_ffn[layer]), peer_w_q[layer], peer_keys[layer], peer_u[layer], peer_v[layer])
    y_prompt = rmsnorm(xp, norm_final)
    y_sample = rmsnorm(xs, norm_final)
    return (y_prompt, y_sample,
            jnp.stack(kp_l), jnp.stack(vp_l), jnp.stack(rp_l), jnp.stack(ip_l),
            jnp.stack(poolp_l), jnp.stack(cp_l), jnp.stack(ep_l),
            jnp.stack(ks_l), jnp.stack(vs_l), jnp.stack(rs_l), jnp.stack(is_l),
            jnp.stack(pools_l), jnp.stack(cs_l), jnp.stack(es_l))
```

```python
import numpy as np
from contextlib import ExitStack, contextmanager
import concourse.bass as bass
import concourse.mybir as mybir
from concourse.bass_utils import run_bass_kernel_spmd

F32 = mybir.dt.float32
BF16 = mybir.dt.bfloat16
I32 = mybir.dt.int32
U32 = mybir.dt.uint32
AF = mybir.ActivationFunctionType
ALU = mybir.AluOpType
AX = mybir.AxisListType

DEBUG_ISA = False
DEBUG_SEM = None
ENGS = ("pe", "act", "dve", "pool", "sp")
CENG = ("pe", "act", "dve", "pool")

D_MODEL = 1024
RMS_EPS = 1e-6
N_EXPERTS = 16384


class KB:
    def __init__(self):
        self.nc = bass.Bass("TRN2", target_bir_lowering=False)
        self.es = ExitStack()
        self.q = {e: [] for e in ENGS}
        self.csem = {e: self.es.enter_context(self.nc.semaphore("c_" + e)) for e in CENG}
        self.ccnt = {e: 0 for e in CENG}
        self.dsem = {}
        self.dcnt = {}
        self.waited = {e: {} for e in ENGS}
        self.lastw = {}
        self.readers = {}
        self.consts = {}
        self.dram_in = {}
        self.dram_out = {}
        self.stack = [self.es]
        self.uid = 0
        self.bufsem = {}
        self.dfree = {"sw": [], "hw": []}
        self.psum_ids = set()

    @staticmethod
    def _where():
        import sys
        f = sys._getframe(2)
        out = []
        while f is not None and len(out) < 4:
            if f.f_code.co_name not in ("op", "dma", "mm", "tr", "actv", "tt", "ts", "stt", "cp", "memset", "load"):
                out.append(f.f_lineno)
            f = f.f_back
        return out

    def sb(self, name, shape, dtype):
        self.uid += 1
        return self.stack[-1].enter_context(self.nc.sbuf_tensor(f"{name}_{self.uid}", list(shape), dtype))

    def ps(self, name, shape, dtype):
        self.uid += 1
        t = self.stack[-1].enter_context(self.nc.psum_tensor(f"{name}_{self.uid}", list(shape), dtype))
        self.psum_ids.add(id(t))
        return t

    def inp(self, name, shape, dtype):
        t = self.nc.dram_tensor(name, list(shape), dtype, kind="ExternalInput")
        self.dram_in[name] = t
        return t.ap()

    def outp(self, name, shape, dtype):
        t = self.nc.dram_tensor(name, list(shape), dtype, kind="ExternalOutput")
        self.dram_out[name] = t
        return t.ap()

    def scratch(self, name, shape, dtype):
        t = self.nc.dram_tensor(name, list(shape), dtype, kind="Internal")
        return t.ap()

    def const(self, name, arr):
        arr = np.ascontiguousarray(arr)
        dt = {np.dtype(np.float32): F32, np.dtype(np.int32): I32}[arr.dtype]
        self.consts[name] = arr
        return self.inp(name, arr.shape, dt)

    @contextmanager
    def phase(self):
        st = ExitStack()
        self.stack.append(st)
        yield
        self.stack.pop()
        self.barrier()
        st.close()

    def _sem(self, key):
        return self.csem[key] if key in self.csem else self.dsem[key]

    @staticmethod
    def _k(b):
        return b if isinstance(b, (str, tuple)) else id(b)

    def _toks(self, bs):
        return [self._k(b) for b in bs if not (isinstance(b, tuple) and b and b[0] == "dram")]

    def _dsem_for(self, buf, queue):
        qt = "sw" if queue == "pool" else "hw"
        k = (buf, qt)
        if k not in self.bufsem:
            if self.dfree[qt]:
                sk = self.dfree[qt].pop()
            else:
                sk = "d%s%d" % (qt, len(self.dsem))
                self.dsem[sk] = self.es.enter_context(self.nc.semaphore(sk))
                self.dcnt[sk] = 0
            self.bufsem[k] = sk
        return self.bufsem[k]

    def _deps(self, eng, reads, writes, own=None):
        waits = {}

        def need(k, v, waw=False):
            if (waw and k == own) or (eng == "pe" and k == "pe"):
                return
            if k in self.dcnt:
                v = self.dcnt[k]
            if self.waited[eng].get(k, 0) >= v:
                return
            if waits.get(k, 0) < v:
                waits[k] = v

        for b in reads:
            for k, v in self.lastw.get(b, {}).items():
                need(k, v)
            if b in self.psum_ids:
                for k, v in self.readers.get(b, {}).items():
                    if k != eng:
                        need(k, v)
        for b in writes:
            for k, v in self.lastw.get(b, {}).items():
                need(k, v, True)
            for k, v in self.readers.get(b, {}).items():
                need(k, v)
        for k, v in waits.items():
            self.waited[eng][k] = v
        return list(waits.items())

    def _commit(self, tok, reads, writes):
        for b in reads:
            d = self.readers.setdefault(b, {})
            if d.get(tok[0], 0) < tok[1]:
                d[tok[0]] = tok[1]
        for b in writes:
            self.lastw[b] = {tok[0]: tok[1]}
            self.readers[b] = {}

    def op(self, eng, fn, reads=(), writes=()):
        reads, writes = self._toks(reads), self._toks(writes)
        waits = self._deps(eng, reads, writes)
        self.ccnt[eng] += 1
        tok = (eng, self.ccnt[eng])
        self._commit(tok, reads, writes)
        self.q[eng].append((waits, fn, (eng, 1), self._where()))

    def dma(self, queue, fn, sem, reads=(), writes=()):
        reads, writes = self._toks(reads), self._toks(writes)
        buf = writes[0] if writes else reads[0]
        sk = self._dsem_for(buf, queue)
        waits = self._deps(queue, reads, writes, own=sk)
        self.dcnt[sk] += 16
        tok = (sk, self.dcnt[sk])
        self._commit(tok, reads, writes)
        self.q[queue].append((waits, fn, (sk, 16), self._where()))

    def barrier(self):
        allv = dict(self.ccnt)
        allv.update(self.dcnt)
        for e in ENGS:
            waits = []
            for k, v in allv.items():
                if v > 0 and self.waited[e].get(k, 0) < v:
                    waits.append((k, v))
                    self.waited[e][k] = v
            if waits:
                self.q[e].append((waits, None, None, 0))
        self.lastw = {}
        self.readers = {}
        for (bk, qt), sk in self.bufsem.items():
            self.dfree[qt].append(sk)
        self.bufsem = {}

    def emit(self):
        self.barrier()
        nc = self.nc
        with nc.Block() as block:
            def mk(name):
                def body(eng):
                    for waits, fn, inc, _w in self.q[name]:
                        for k, v in waits:
                            if DEBUG_SEM and k == DEBUG_SEM:
                                print("SEMDBG-WAIT:", name, k, v)
                            eng.wait_ge(self._sem(k), v)
                        if fn is not None:
                            ins = fn(eng)
                            if DEBUG_ISA and isinstance(ins.ins, mybir.InstISA):
                                print("InstISA:", name, type(ins.ins).__name__, str(ins)[:300])
                            if DEBUG_SEM and inc[0] == DEBUG_SEM:
                                print("SEMDBG:", name, [w for w in waits], str(ins)[:260])
                            ins.then_inc(self._sem(inc[0]), inc[1])
                return body
            block.tensor(mk("pe"))
            block.scalar(mk("act"))
            block.vector(mk("dve"))
            block.gpsimd(mk("pool"))
            block.sync(mk("sp"))
        self.es.close()
        return nc

    def mm(self, out, lhsT, rhs, start, stop, reads, writes):
        self.op("pe", lambda e: e.matmul(out, lhsT, rhs, start=start, stop=stop), reads, writes)

    def tr(self, out, in_, ident, reads, writes):
        self.op("pe", lambda e: e.transpose(out, in_, ident), reads, writes)

    def actv(self, out, in_, func, reads, writes, bias=None, scale=None, accum_out=None, eng="act"):
        kw = {}
        if bias is not None:
            kw["bias"] = bias
        if scale is not None:
            kw["scale"] = scale
        if accum_out is not None:
            kw["accum_out"] = accum_out
        self.op(eng, lambda e: e.activation(out, in_, func, **kw), reads, writes)

    def tt(self, out, in0, in1, op, reads, writes, eng="dve"):
        self.op(eng, lambda e: e.tensor_tensor(out, in0, in1, op), reads, writes)

    def ts(self, out, in0, s1, s2, op0, op1, reads, writes, eng="dve"):
        if op1 is None:
            self.op(eng, lambda e: e.tensor_scalar(out, in0, s1, None, op0), reads, writes)
        else:
            self.op(eng, lambda e: e.tensor_scalar(out, in0, s1, s2, op0, op1), reads, writes)

    def stt(self, out, in0, scalar, in1, op0, op1, reads, writes):
        self.op("dve", lambda e: e.scalar_tensor_tensor(out, in0, scalar, in1, op0, op1), reads, writes)

    def cp(self, out, in_, reads, writes, eng="dve"):
        if eng == "act":
            self.op("act", lambda e: e.activation(out, in_, AF.Copy), reads, writes)
        else:
            self.op(eng, lambda e: e.tensor_copy(out, in_), reads, writes)

    def memset(self, ap, val, writes, eng="dve"):
        self.op(eng, lambda e: e.memset(ap, val), (), writes)

    def load(self, out, in_, sem, reads, writes, queue="sp", **kw):
        self.dma(queue, lambda e: e.dma_start(out=out, in_=in_, **kw), sem, reads, writes)


def bcast_rows(ap_row, p=128):
    return bass.AP(ap_row.tensor, ap_row.offset, [[0, p]] + [list(x) for x in ap_row.ap[-1:]])


def phase_convert(K, pairs):
    with K.phase():
        NSL = 4
        ub = [K.sb(f"cvu{i}", [128, 4, 1024], BF16) for i in range(NSL)]
        vb = [K.sb(f"cvv{i}", [128, 4, 1024], BF16) for i in range(NSL)]
        i = 0
        for (u_d, v_d, tab) in pairs:
            for ch in range(N_EXPERTS // 512):
                s_ = i % NSL
                i += 1
                rows = slice(ch * 512, (ch + 1) * 512)
                K.dma("pool", lambda e, s_=s_, rows=rows, u_d=u_d: e.dma_start(
                    out=ub[s_][:], in_=u_d[rows, :].rearrange("(p j) d -> p j d", j=4), max_dma_last_dim=4096), None, (), [ub[s_]])
                K.dma("pool", lambda e, s_=s_, rows=rows, v_d=v_d: e.dma_start(
                    out=vb[s_][:], in_=v_d[rows, :].rearrange("(p j) d -> p j d", j=4), max_dma_last_dim=4096), None, (), [vb[s_]])
                K.load(tab[rows, 0:1024].rearrange("(p j) d -> p j d", j=4), ub[s_][:], None, [ub[s_]], ())
                K.load(tab[rows, 1024:2048].rearrange("(p j) d -> p j d", j=4), vb[s_][:], None, [vb[s_]], ())


def phase_peer(K, x_in, x_out, ntiles, wq_d, keys_d, tab_d, gffn_d, ident_bf_d, ident_f_d, iota16_d,
               final=None):
    nc = K.nc
    with K.phase():
        ident_bf = K.sb("identbf", [128, 128], BF16)
        ident_f = K.sb("identf", [128, 128], F32)
        iota16 = K.sb("iota16", [128, 16], F32)
        gffn = K.sb("gffn", [128, 1024], F32)
        wq = K.sb("wq", [128, 8, 1024], BF16)
        keysBD = K.sb("keysBD", [128, 8, 256], BF16)
        K.load(ident_f[:], ident_f_d, "c0", (), [ident_f])
        K.load(ident_bf[:], ident_f_d, "c0", (), [ident_bf], queue="pool")
        K.load(iota16[:], iota16_d, "c0", (), [iota16])
        K.load(gffn[:], bcast_rows(gffn_d), "c0", (), [gffn])
        for c in range(8):
            K.load(wq[:, c, :], wq_d[c * 128:(c + 1) * 128, :], "c1", (), [wq], queue="pool")
        if final is not None:
            gfin = K.sb("gfin", [128, 1024], F32)
            K.load(gfin[:], bcast_rows(final[0]), "c0", (), [gfin])

        psB = K.ps("psB", [128, 2048], F32)
        psO = K.ps("psO", [128, 1024], F32)
        psA = K.ps("psA", [128, 512], F32)
        psA_bf = psA[:].bitcast(BF16)

        def two(name, shape, dt):
            return [K.sb(name + str(i), shape, dt) for i in range(2)]
        xt = two("xt", [128, 1024], F32)
        hn = two("hn", [128, 1024], BF16)
        hnT = two("hnT", [128, 8, 128], BF16)
        qT = two("qT", [128, 8, 128], BF16)
        eid = two("eid", [128, 128], I32)
        gate = two("gate", [128, 128], F32)
        act = two("actv", [128, 128], F32)
        wgt = two("wgt", [128, 128], F32)
        wg2 = two("wg2", [128, 128], F32)
        ss = two("ss", [128, 4], F32)
        junk = K.sb("junk", [128, 1024], BF16)
        junk2 = K.sb("junk2", [128, 1024], BF16)
        s2 = K.sb("s2", [128, 128], F32)
        sv = K.sb("sv", [128, 16, 16], F32)
        si = K.sb("si", [128, 16, 16], U32)
        sif = K.sb("sif", [128, 16, 16], BF16)
        cand = K.sb("cand", [128, 8, 256], F32)
        cand2 = K.sb("cand2", [128, 256], F32)
        cv = K.sb("cv", [128, 8, 16], F32)
        ci = K.sb("ci", [128, 8, 16], U32)
        ca = K.sb("ca", [128, 8, 16], U32)
        cb = K.sb("cb", [128, 8, 16], U32)
        oh = K.sb("oh", [128, 8, 16, 16], BF16)
        oh2 = K.sb("oh2", [128, 8, 16, 16], BF16)
        i1f = K.sb("i1f", [128, 8, 16], F32)
        i2f = K.sb("i2f", [128, 8, 16], F32)
        ce = K.sb("ce", [128, 8, 16], F32)
        csum = K.sb("csum", [128, 8], F32)
        NS = 8
        UV = [K.sb(f"UV{i}", [128, 2048], BF16) for i in range(NS)]
        NDG = 4
        dg = [K.sb(f"dg{i}", [128, 128], BF16) for i in range(NDG)]
        xo = two("xo", [128, 1024], F32)
        if final is not None:
            yo = [K.sb("yo", [128, 1024], F32)] * 2

        def prep(i):
            p = i % 2
            X, HN, HT, QT = xt[p], hn[p], hnT[p], qT[p]
            K.load(X[:], x_in[i * 128:(i + 1) * 128, :], f"xl{p}", (), [X])
            K.actv(junk2[:], X[:], AF.Square, [X], [ss[p], junk2], accum_out=ss[p][:, 0:1])
            K.actv(ss[p][:, 1:2], ss[p][:, 0:1], AF.Sqrt, [ss[p], eps_t], [ss[p]], scale=1.0 / D_MODEL, bias=eps_t[:, 0:1])
            K.op("dve", lambda e: e.reciprocal(ss[p][:, 2:3], ss[p][:, 1:2]), [ss[p]], [ss[p]])
            K.stt(HN[:], X[:], ss[p][:, 2:3], gffn[:], ALU.mult, ALU.mult, [X, ss[p], gffn], [HN])
            yield
            for c in range(8):
                K.tr(psA_bf[:, c * 128:(c + 1) * 128], HN[:, c * 128:(c + 1) * 128], ident_bf[:],
                     [HN, ident_bf], [psA])
            K.cp(HT[:].rearrange("p c t -> p (c t)"), psA_bf[:, :], [psA], [HT], eng="act")
            for co in range(8):
                for kc in range(8):
                    K.mm(psB[:, co * 128:(co + 1) * 128], wq[:, kc, co * 128:(co + 1) * 128], HT[:, kc, :],
                         kc == 0, kc == 7, [wq, HT], [psB])
            K.cp(QT[:].rearrange("p c t -> p (c t)"), psB[:, 0:1024], [psB], [QT], eng="act")
            for c in range(8):
                K.mm(psB[:, c * 256:(c + 1) * 256], QT[:, c, :], keysBD[:, c, :], True, True, [QT, keysBD], [psB])
            yield
            for gq in range(4):
                for g_ in range(4):
                    g = gq * 4 + g_
                    S = psB[:, g * 128:(g + 1) * 128]
                    K.op("dve", lambda e, S=S, g=g: e.max(sv[:, g, 0:8], S), [psB], [sv])
                    K.op("dve", lambda e, S=S, g=g: e.match_replace(s2[:], sv[:, g, 0:8], S, -1e30), [psB, sv], [s2])
                    K.op("dve", lambda e, g=g: e.max(sv[:, g, 8:16], s2[:]), [s2], [sv])
                    K.op("dve", lambda e, S=S, g=g: e.max_index(si[:, g, 0:8], sv[:, g, 0:8], S), [psB, sv], [si])
                    K.op("dve", lambda e, S=S, g=g: e.max_index(si[:, g, 8:16], sv[:, g, 8:16], S), [psB, sv], [si])
                yield
            sv4 = sv[:].rearrange("n (h p) k -> n h p k", p=2)
            K.tt(cand[:].rearrange("n h (a b) -> n h a b", b=16),
                 sv4[:, :, 0, :].unsqueeze(3).to_broadcast([128, 8, 16, 16]),
                 sv4[:, :, 1, :].unsqueeze(2).to_broadcast([128, 8, 16, 16]), ALU.add, [sv], [cand])
            K.cp(sif[:], si[:], [si], [sif])
            for h in range(8):
                C = cand[:, h, :]
                K.op("dve", lambda e, C=C, h=h: e.max(cv[:, h, 0:8], C), [cand], [cv])
                K.op("dve", lambda e, C=C, h=h: e.match_replace(cand2[:], cv[:, h, 0:8], C, -1e30), [cand, cv], [cand2])
                K.op("dve", lambda e, h=h: e.max(cv[:, h, 8:16], cand2[:]), [cand2], [cv])
                K.op("dve", lambda e, C=C, h=h: e.max_index(ci[:, h, 0:8], cv[:, h, 0:8], C), [cand, cv], [ci])
                K.op("dve", lambda e, C=C, h=h: e.max_index(ci[:, h, 8:16], cv[:, h, 8:16], C), [cand, cv], [ci])
                if h % 4 == 3:
                    yield
            K.ts(ca[:], ci[:], bitc[:, 0:1], None, ALU.logical_shift_right, None, [ci, bitc], [ca])
            K.ts(cb[:], ci[:], bitc[:, 1:2], None, ALU.bitwise_and, None, [ci, bitc], [cb])
            sif4 = sif[:].rearrange("n (h p) k -> n h p k", p=2)
            io = iota16[:].unsqueeze(1).unsqueeze(1).to_broadcast([128, 8, 16, 16])
            for (cx, pp, dst) in ((ca, 0, i1f), (cb, 1, i2f)):
                K.tt(oh[:], cx[:].unsqueeze(3).to_broadcast([128, 8, 16, 16]), io, ALU.is_equal, [cx, iota16], [oh])
                K.tt(oh2[:], oh[:], sif4[:, :, pp, :].unsqueeze(2).to_broadcast([128, 8, 16, 16]), ALU.mult,
                     [oh, sif], [oh2])
                K.op("dve", lambda e, dst=dst: e.tensor_reduce(dst[:], oh2[:], AX.X, ALU.add), [oh2], [dst])
            K.stt(eid[p][:].rearrange("n (h k) -> n h k", k=16), i1f[:], 128.0, i2f[:], ALU.mult, ALU.add,
                  [i1f, i2f], [eid[p]])
            yield
            K.tt(ce[:], cv[:], cv[:, :, 0:1].to_broadcast([128, 8, 16]), ALU.subtract, [cv], [ce])
            K.actv(ce[:], ce[:], AF.Exp, [ce], [ce])
            K.op("dve", lambda e: e.tensor_reduce(csum[:], ce[:], AX.X, ALU.add), [ce], [csum])
            K.op("dve", lambda e: e.reciprocal(csum[:], csum[:]), [csum], [csum])
            K.tt(gate[p][:].rearrange("n (h k) -> n h k", k=16), ce[:],
                 csum[:].unsqueeze(2).to_broadcast([128, 8, 16]), ALU.mult, [ce, csum], [gate[p]])
            yield

        eps_t = K.sb("eps_t", [128, 1], F32)
        K.memset(eps_t[:], RMS_EPS, [eps_t])
        bitc = K.sb("bitc", [128, 2], U32)
        K.memset(bitc[:, 0:1], 4, [bitc])
        K.memset(bitc[:, 1:2], 15, [bitc])

        slot_u = [0]
        slot_of = {}
        dgc = [0]

        def gatherU(p, r):
            s = slot_u[0] % NS
            slot_u[0] += 1
            slot_of[(p, r)] = s
            K.dma("pool", lambda e, s=s, r=r: e.indirect_dma_start(
                out=UV[s][:, :], out_offset=None, in_=tab_d,
                in_offset=bass.IndirectOffsetOnAxis(ap=eid[p][:, r:r + 1], axis=0)),
                None, [eid[p]], [UV[s]])
            K.op("dve", lambda e, s=s, r=r: e.scalar_tensor_tensor(
                junk[:], UV[s][:, 0:1024], 1.0, hn[p][:], ALU.mult, ALU.mult, accum_out=act[p][:, r:r + 1]),
                [UV[s], hn[p]], [("act", p, r), junk])

        def gatherV(p, r):
            s = slot_of[(p, r)]
            d = dgc[0] % NDG
            dgc[0] += 1
            K.actv(wg2[p][:, r:r + 1], wgt[p][:, r:r + 1], AF.Copy, [("wgt", p, r), gate[p]], [("wg2", p, r)],
                   scale=gate[p][:, r:r + 1])
            K.actv(dg[d][:], ident_bf[:], AF.Copy, [ident_bf, ("wg2", p, r)], [dg[d]], scale=wg2[p][:, r:r + 1])
            for hf in range(2):
                K.mm(psO[:, hf * 512:(hf + 1) * 512], dg[d][:], UV[s][:, 1024 + hf * 512:1024 + (hf + 1) * 512],
                     r == 0, r == 127, [dg[d], UV[s]], [psO])

        def run(gen):
            for _ in gen:
                pass

        ksc = K.phase()
        ksc.__enter__()
        knat = K.sb("knat", [128, 16, 64], F32)
        K.load(knat[:], keys_d.rearrange("h p k d -> k (h p) d"), "c0", (), [knat])
        K.memset(keysBD[:], 0.0, [keysBD])
        for c in range(8):
            K.tr(psA[:, 0:128], knat[:, 2 * c:2 * c + 2, :].rearrange("k a d -> k (a d)"), ident_f[:],
                 [knat, ident_f], [psA])
            K.cp(keysBD[0:64, c, 0:128], psA[0:64, 0:128], [psA], [keysBD], eng="act")
            K.cp(keysBD[64:128, c, 128:256], psA[64:128, 0:128], [psA], [keysBD], eng="act")

        ksc.__exit__(None, None, None)
        run(prep(0))
        for i in range(ntiles):
            p = i % 2
            nxt = prep(i + 1) if i + 1 < ntiles else iter(())
            for r in range(128):
                gatherU(p, r)
                K.actv(wgt[p][:, r:r + 1], act[p][:, r:r + 1], AF.Gelu, [("act", p, r)], [("wgt", p, r)])
                if r >= 1:
                    gatherV(p, r - 1)
                if r % 16 == 15:
                    next(nxt, None)
            gatherV(p, 127)
            run(nxt)
            K.tt(xo[p][:], psO[:], xt[p][:], ALU.add, [psO, xt[p]], [xo[p]])
            if final is None:
                K.load(x_out[i * 128:(i + 1) * 128, :], xo[p][:], f"xs{p}", [xo[p]], [("dram", "x_out")])
            if final is not None:
                K.actv(junk2[:], xo[p][:], AF.Square, [xo[p]], [ss[p], junk2], accum_out=ss[p][:, 3:4])
                K.actv(ss[p][:, 3:4], ss[p][:, 3:4], AF.Sqrt, [ss[p], eps_t], [ss[p]], scale=1.0 / D_MODEL, bias=eps_t[:, 0:1])
                K.op("dve", lambda e, p=p: e.reciprocal(ss[p][:, 3:4], ss[p][:, 3:4]), [ss[p]], [ss[p]])
                K.stt(yo[p][:], xo[p][:], ss[p][:, 3:4], gfin[:], ALU.mult, ALU.mult, [xo[p], ss[p], gfin], [yo[p]])
                for (row0, nrows, oap) in final[1]:
                    lo, hi = max(row0, i * 128), min(row0 + nrows, (i + 1) * 128)
                    if lo < hi:
                        K.load(oap[lo - row0:hi - row0, :], yo[p][lo - i * 128:hi - i * 128, :], f"ys{p}",
                               [yo[p]], [("dram", "y")])


class Rot:
    def __init__(self, items):
        self.items = list(items)
        self.i = 0

    def next(self):
        t = self.items[self.i % len(self.items)]
        self.i += 1
        return t


CHUNK = 64
SWA_SCALE = 64 ** -0.5
MLA_SCALE = 96 ** -0.5
ROPE_THETA = 500000.0
PAST_LEN = 2048
DEC_SEQ = 32
NSB = 4
MASKV = -30000.0


def rmsnorm_tile(K, X, HN, gb, ssb, eps_t, junk, xtok, hntok, gtok, jtok, width=D_MODEL):
    K.actv(junk, X, AF.Square, [xtok], [ssb, jtok], accum_out=ssb[:, 0:1])
    K.actv(ssb[:, 1:2], ssb[:, 0:1], AF.Sqrt, [ssb, eps_t], [ssb], scale=1.0 / width, bias=eps_t[:, 0:1])
    K.op("dve", lambda e: e.reciprocal(ssb[:, 2:3], ssb[:, 1:2]), [ssb], [ssb])
    K.stt(HN, X, ssb[:, 2:3], gb, ALU.mult, ALU.mult, [xtok, ssb, gtok], [hntok])


def phase_swa(K, cfg, x_all, w_in_d, w_inP_d, gmix_d, sink_d, ck_d, cv_d, ropeC_d, ropeS_d, maskA_d, maskB_d, maskS_d,
              ident_f_d, attT_d, uT_d, outs):
    SEQ = cfg["SEQ"]
    NTOK = SEQ + 128
    NTP = SEQ // 128
    with K.phase():
        ident_f = K.sb("identf", [128, 128], F32)
        ident_bf = K.sb("identbf", [128, 128], BF16)
        K.load(ident_f[:], ident_f_d, "c0", (), [ident_f])
        K.load(ident_bf[:], ident_f_d, "c1", (), [ident_bf], queue="pool")
        gmix = K.sb("gmix", [128, 1024], F32)
        K.load(gmix[:], bcast_rows(gmix_d), "c0", (), [gmix])
        w_in = K.sb("w_in", [128, 8, 1280], BF16)
        w_inP = K.sb("w_inP", [128, 8, 640], BF16)
        for c in range(8):
            K.load(w_in[:, c, 0:640], w_in_d[c * 128:(c + 1) * 128, 0:640], "c1", (), [w_in], queue="pool")
            K.load(w_in[:, c, 640:1280], w_in_d[c * 128:(c + 1) * 128, 640:1280], "c1", (), [w_in], queue="pool")
            K.load(w_inP[:, c, :], w_inP_d[c * 128:(c + 1) * 128, :], "c1", (), [w_inP], queue="pool")
        maskA = K.sb("maskA", [128, 512], BF16)
        maskB = K.sb("maskB", [128, 512], BF16)
        K.load(maskA[:], maskA_d, "c1", (), [maskA], queue="pool")
        K.load(maskB[:], maskB_d, "c1", (), [maskB], queue="pool")
        sinkexp = K.sb("sinkexp", [128, 8], F32)
        K.load(sinkexp[:], bcast_rows(sink_d), "c0", (), [sinkexp])
        K.actv(sinkexp[:], sinkexp[:], AF.Exp, [sinkexp], [sinkexp])
        eps_t = K.sb("eps_t", [128, 1], F32)
        K.memset(eps_t[:], RMS_EPS, [eps_t])

        kT_all = K.sb("kT_all", [64, 2, NTOK], BF16)
        v_all = K.sb("v_all", [128, NTP + 1, 2, 65], BF16)
        K.memset(v_all[:, :, :, 64:65], 1.0, [v_all])
        junk = K.sb("junk", [128, 1024], BF16)
        xt = [K.sb(f"xt{i}", [128, 1024], F32) for i in range(2)]
        ssb = [K.sb(f"ssb{i}", [128, 4], F32) for i in range(2)]
        hn = [K.sb(f"hn{i}", [128, 1024], BF16) for i in range(2)]
        hnT = K.sb("hnT", [128, 8, 512], BF16)
        ropeC = [K.sb(f"ropeC{i}", [64, 512], F32) for i in range(2)]
        ropeS = [K.sb(f"ropeS{i}", [64, 512], F32) for i in range(2)]
        qT = K.sb("qT", [64, 8, 512], BF16)
        t1 = [K.sb(f"t1_{i}", [64, 512], F32) for i in range(2)]
        t2 = [K.sb(f"t2_{i}", [64, 512], F32) for i in range(2)]
        kf32 = K.sb("kf32", [64, 2, 128], F32)
        ktok_p = K.sb("ktok", [128, 128], F32)
        vtok_p = K.sb("vtok", [128, 128], F32)
        ktok_s = K.sb("ktok_s", [128, 128], F32)
        vtok_s = K.sb("vtok_s", [128, 128], F32)
        uT = [K.sb(f"uT{i}", [128, 4, 512], BF16) for i in range(2)]
        attT = [K.sb(f"attT{i}", [128, 4, 512], BF16) for i in range(2)]
        PT = [K.sb(f"PT{i}", [128, 2, 2, 512], BF16) for i in range(2)]
        att_tok = [K.sb(f"att_tok{i}", [128, 512], BF16) for i in range(2)]
        den = K.sb("den", [128, 8], F32)
        kc32 = [K.sb(f"kc32_{i}", [128, 128], F32) for i in range(2)]
        vc32 = [K.sb(f"vc32_{i}", [128, 128], F32) for i in range(2)]
        kcb = [K.sb(f"kcb{i}", [128, 128], BF16) for i in range(2)]
        kcT = K.sb("kcT", [64, NSB, 2, 128], BF16)
        vcb = K.sb("vcb", [128, NSB, 2, 65], BF16)
        K.memset(vcb[:, :, :, 64:65], 1.0, [vcb])
        PTc = K.sb("PTc", [128, NSB, 2, 512], BF16)
        PTn = K.sb("PTn", [128, 2, 512], BF16)
        maskS = K.sb("maskS", [128, NSB + 1, 512], BF16)
        K.load(maskS[:], maskS_d, "c1", (), [maskS], queue="pool")

        psT = K.ps("psT", [128, 512], F32)
        psT_bf = psT[:].bitcast(BF16)
        gen = Rot([K.ps(f"gen{i}", [128, 512], F32) for i in range(4)])
        psOX = K.ps("psOX", [128, 512], F32)
        psOY = K.ps("psOY", [128, 512], F32)
        psAT = K.ps("psAT", [128, 512], F32)
        psAT_bf = psAT[:].bitcast(BF16)

        STOP = cfg.get("stop", 99)
        SSTOP = cfg.get("sstop", 99)
        if STOP <= 0:
            return
        nsup = SEQ // 512
        sups = [(s * 512, 512, False) for s in range(nsup)] + [(SEQ, 128, True)]
        xi = 0
        for si_, (tok0, W, is_s) in enumerate(sups):
            sp = si_ % 2
            ntl = W // 128
            if is_s and STOP <= 5:
                return
            K.load(ropeC[sp][:, 0:W], ropeC_d[:, tok0:tok0 + W], f"rope{sp}", (), [ropeC[sp]])
            K.load(ropeS[sp][:, 0:W], ropeS_d[:, tok0:tok0 + W], f"rope{sp}", (), [ropeS[sp]])
            for tl in range(ntl):
                p = xi % 2
                xi += 1
                K.load(xt[p][:], x_all[tok0 + tl * 128: tok0 + (tl + 1) * 128, :], f"xl{p}", (), [xt[p]])
                rmsnorm_tile(K, xt[p][:], hn[p][:], gmix[:], ssb[p], eps_t, junk[:], xt[p], hn[p], gmix, junk)
                for c in range(8):
                    K.tr(psT_bf[:, c * 128:(c + 1) * 128], hn[p][:, c * 128:(c + 1) * 128], ident_bf[:],
                         [hn[p], ident_bf], [psT])
                K.cp(hnT[:, :, tl * 128:(tl + 1) * 128], psT_bf[:, :].rearrange("p (c t) -> p c t", c=8), [psT], [hnT],
                     eng="act")
            if STOP <= 1 or (is_s and SSTOP <= 1):
                return
            for hh in range(10):
                col0 = hh * 64
                pq, pp = gen.next(), gen.next()
                for kc in range(8):
                    K.mm(pq[0:64, 0:W], w_in[:, kc, col0:col0 + 64], hnT[:, kc, 0:W], kc == 0, kc == 7, [w_in, hnT], [pq])
                for kc in range(8):
                    K.mm(pp[0:64, 0:W], w_inP[:, kc, col0:col0 + 64], hnT[:, kc, 0:W], kc == 0, kc == 7, [w_inP, hnT], [pp])
                a, b = t1[hh % 2], t2[hh % 2]
                K.tt(a[:, 0:W], pq[0:64, 0:W], ropeC[sp][:, 0:W], ALU.mult, [pq, ropeC[sp]], [a])
                K.tt(b[:, 0:W], pp[0:64, 0:W], ropeS[sp][:, 0:W], ALU.mult, [pp, ropeS[sp]], [b])
                if hh < 8:
                    K.tt(qT[:, hh, 0:W], a[:, 0:W], b[:, 0:W], ALU.add, [a, b], [qT])
                else:
                    g = hh - 8
                    K.tt(kT_all[:, g, tok0:tok0 + W], a[:, 0:W], b[:, 0:W], ALU.add, [a, b], [kT_all])
                    if is_s or tok0 + W == SEQ:
                        K.tt(kf32[:, g, :], a[:, W - 128:W], b[:, W - 128:W], ALU.add, [a, b], [kf32])
            if STOP <= 2 or (is_s and SSTOP <= 2):
                return
            for c in range(4):
                pu = gen.next()
                for kc in range(8):
                    K.mm(pu[:, 0:W], w_in[:, kc, 768 + c * 128:768 + (c + 1) * 128], hnT[:, kc, 0:W], kc == 0, kc == 7,
                         [w_in, hnT], [pu])
                K.cp(uT[sp][:, c, 0:W], pu[:, 0:W], [pu], [uT[sp]], eng="act")
            for c in range(4):
                K.load(uT_d[c * 128:(c + 1) * 128, tok0:tok0 + W], uT[sp][:, c, 0:W], f"ust{sp}", [uT[sp]], [("dram", "uT")])
            if STOP <= 3 or (is_s and SSTOP <= 3):
                return
            last_tile = is_s or (tok0 + W == SEQ)
            ktok, vtok = (ktok_s, vtok_s) if is_s else (ktok_p, vtok_p)
            for tl in range(ntl):
                j = tok0 // 128 + tl
                pv = gen.next()
                for kc in range(8):
                    K.mm(pv[:, 0:128], hnT[:, kc, tl * 128:(tl + 1) * 128], w_in[:, kc, 640:768], kc == 0, kc == 7,
                         [w_in, hnT], [pv])
                K.cp(v_all[:, j, :, 0:64], pv[:, 0:128].rearrange("p (g d) -> p g d", g=2), [pv], [v_all], eng="act")
                if last_tile and tl == ntl - 1:
                    K.cp(vtok[:], pv[:, 0:128], [pv], [vtok])
            if last_tile and not (is_s and cfg.get("nok")):
                pk = gen.next()
                for g in range(2):
                    K.tr(pk[:, g * 64:(g + 1) * 64], kf32[:, g, :], ident_f[0:64, 0:64], [kf32, ident_f], [pk])
                K.cp(ktok[:], pk[:, 0:128], [pk], [ktok])
                if not is_s:
                    K.load(outs["k_p"], ktok[:], "ost", [ktok], [("dram", "o")])
                    K.load(outs["v_p"], vtok[:], "ost", [vtok], [("dram", "o")])
                elif not cfg.get("nostore"):
                    for b in range(NSB):
                        K.load(outs["k_s"][b, 96:128, :], ktok[b * 32:(b + 1) * 32, :], "ost", [ktok], [("dram", "o")])
                        K.load(outs["v_s"][b, 96:128, :], vtok[b * 32:(b + 1) * 32, :], "ost", [vtok], [("dram", "o")])
            if STOP <= 4 or (is_s and SSTOP <= 4):
                return
            if not is_s:
                for tl in range(ntl):
                    j = tok0 // 128 + tl
                    ap_ = j % 2
                    jjs = [jj for jj in (j - 1, j) if jj >= 0]
                    for g in range(2):
                        for jj in jjs:
                            pS = gen.next()
                            K.mm(pS[:, :], kT_all[:, g, jj * 128:(jj + 1) * 128], qT[:, 4 * g:4 * g + 4, tl * 128:(tl + 1) * 128],
                                 True, False, [kT_all, qT], [pS])
                            K.mm(pS[:, :], ident_bf[:], (maskA if jj == j - 1 else maskB)[:], False, True,
                                 [ident_bf, maskA, maskB], [pS])
                            K.actv(PT[ap_][:, jj - j + 1, g, :], pS[:, :], AF.Exp, [pS], [PT[ap_]], scale=SWA_SCALE)
                    for h in range(8):
                        g = h // 4
                        po = psOX if h < 4 else psOY
                        for n_, jj in enumerate(jjs):
                            K.mm(po[:, (h % 4) * 65:(h % 4) * 65 + 65],
                                 PT[ap_][:, jj - j + 1, g, (h % 4) * 128:(h % 4) * 128 + 128], v_all[:, jj, g, :],
                                 n_ == 0, n_ == len(jjs) - 1, [PT[ap_], v_all], [po])
                    swa_finish(K, psOX, psOY, den, sinkexp, att_tok[ap_], psAT, psAT_bf, ident_bf, attT[sp], tl, 128)
            else:
                if STOP <= 6:
                    return
                for b in range(NSB):
                    bp = b % 2
                    K.load(kc32[bp][:], ck_d[b], f"kcl{bp}", (), [kc32[bp]])
                    K.load(vc32[bp][:], cv_d[b], f"kcl{bp}", (), [vc32[bp]])
                    K.load(outs["k_s"][b, 0:96, :], kc32[bp][32:128, :], "ost", [kc32[bp]], [("dram", "o")])
                    K.load(outs["v_s"][b, 0:96, :], vc32[bp][32:128, :], "ost", [vc32[bp]], [("dram", "o")])
                    K.cp(kcb[bp][:], kc32[bp][:], [kc32[bp]], [kcb[bp]])
                    K.cp(vcb[:, b, :, 0:64], vc32[bp][:].rearrange("p (g d) -> p g d", g=2), [vc32[bp]], [vcb])
                    for g in range(2):
                        K.tr(psT_bf[0:64, g * 128:(g + 1) * 128], kcb[bp][:, g * 64:(g + 1) * 64], ident_bf[:],
                             [kcb[bp], ident_bf], [psT])
                    K.cp(kcT[:, b, :, :].rearrange("p g t -> p (g t)"), psT_bf[0:64, 0:256], [psT], [kcT], eng="act")
                    for g in range(2):
                        pS = gen.next()
                        K.mm(pS[:, :], kcT[:, b, g, :], qT[:, 4 * g:4 * g + 4, 0:128], True, False, [kcT, qT], [pS])
                        K.mm(pS[:, :], ident_bf[:], maskS[:, b, :], False, True, [ident_bf, maskS], [pS])
                        K.actv(PTc[:, b, g, :], pS[:, :], AF.Exp, [pS], [PTc], scale=SWA_SCALE)
                if STOP <= 7:
                    return
                for g in range(2):
                    pS = gen.next()
                    K.mm(pS[:, :], kT_all[:, g, SEQ:SEQ + 128], qT[:, 4 * g:4 * g + 4, 0:128], True, False, [kT_all, qT], [pS])
                    K.mm(pS[:, :], ident_bf[:], maskS[:, 4, :], False, True, [ident_bf, maskS], [pS])
                    K.actv(PTn[:, g, :], pS[:, :], AF.Exp, [pS], [PTn], scale=SWA_SCALE)
                if STOP <= 8:
                    return
                for h in range(8):
                    g = h // 4
                    po = psOX if h < 4 else psOY
                    oo = po[:, (h % 4) * 65:(h % 4) * 65 + 65]
                    hs = slice((h % 4) * 128, (h % 4) * 128 + 128)
                    for b in range(NSB):
                        K.mm(oo, PTc[:, b, g, hs], vcb[:, b, g, :], b == 0, False, [PTc, vcb], [po])
                    K.mm(oo, PTn[:, g, hs], v_all[:, NTP, g, :], False, True, [PTn, v_all], [po])
                swa_finish(K, psOX, psOY, den, sinkexp, att_tok[0], psAT, psAT_bf, ident_bf, attT[sp], 0, 128)
            for c in range(4):
                K.load(attT_d[c * 128:(c + 1) * 128, tok0:tok0 + W], attT[sp][:, c, 0:W], f"ast{sp}", [attT[sp]],
                       [("dram", "attT")])


def swa_finish(K, psOX, psOY, den, sinkexp, att_tok, psAT, psAT_bf, ident_bf, attT, tl, W):
    for half, po in enumerate((psOX, psOY)):
        o3 = po[:, 0:260].rearrange("p (h e) -> p h e", e=65)
        K.tt(den[:, half * 4:half * 4 + 4], o3[:, :, 64], sinkexp[:, half * 4:half * 4 + 4], ALU.add, [po, sinkexp], [den])
    K.op("dve", lambda e: e.reciprocal(den[:], den[:]), [den], [den])
    for half, po in enumerate((psOX, psOY)):
        o3 = po[:, 0:260].rearrange("p (h e) -> p h e", e=65)
        K.tt(att_tok[:, half * 256:(half + 1) * 256].rearrange("p (h d) -> p h d", d=64), o3[:, :, 0:64],
             den[:, half * 4:half * 4 + 4].unsqueeze(2).to_broadcast([128, 4, 64]), ALU.mult, [po, den], [att_tok])
    for c in range(4):
        K.tr(psAT_bf[:, c * 128:(c + 1) * 128], att_tok[:, c * 128:(c + 1) * 128], ident_bf[:], [att_tok, ident_bf], [psAT])
    K.cp(attT[:, :, tl * 128:(tl + 1) * 128], psAT_bf[:, 0:512].rearrange("p (c t) -> p c t", c=4), [psAT], [attT], eng="act")


def rope_tables(pos, rot, head_dim, nrep=1):
    half = rot // 2
    inv = ROPE_THETA ** (-np.arange(0, rot, 2, dtype=np.float32) / rot)
    ang = pos.astype(np.float32)[None, :] * inv.astype(np.float32)[:, None]
    cos, sin = np.cos(ang).astype(np.float32), np.sin(ang).astype(np.float32)
    C = np.ones((head_dim, len(pos)), np.float32)
    S = np.zeros((head_dim, len(pos)), np.float32)
    C[0:half] = cos
    C[half:rot] = cos
    S[0:half] = -sin
    S[half:rot] = sin
    return C, S


def perm_rope_cols(w, head_dim, rot):
    half = rot // 2
    n = w.shape[1]
    idx = np.arange(n)
    d = idx % head_dim
    src = np.where(d < half, idx + half, np.where(d < rot, idx - half, idx))
    return np.ascontiguousarray(w[:, src])


def token_positions(SEQ):
    return np.concatenate([np.arange(SEQ), np.tile(PAST_LEN + np.arange(DEC_SEQ), NSB)])


def swa_consts(SEQ):
    C, S = rope_tables(token_positions(SEQ), 16, 64)
    k = np.arange(128)[:, None]
    q = (np.arange(512) % 128)[None, :]
    maskA = np.where((k < 64) & (q >= 64), MASKV, 0.0).astype(np.float32)
    maskB = np.where((k >= 64) & (q < 64), MASKV, 0.0).astype(np.float32)
    qb = ((np.arange(512) % 128) // 32)[None, :]
    maskS = np.zeros((128, NSB + 1, 512), np.float32)
    for b in range(NSB):
        maskS[:, b, :] = np.where(qb == b, 0.0, MASKV)
    maskS[:, NSB, :] = np.where(qb == (np.arange(128) // 32)[:, None], 0.0, MASKV)
    return {"ropeC": C, "ropeS": S, "maskA": maskA, "maskB": maskB, "maskS": maskS, "ident": np.eye(128, dtype=np.float32)}


TWO_PI = float(2.0 * np.pi)
PI = float(np.pi)


def sincos(K, ang, sin_o, cos_o, tmp_i, tmp_a, tmp_b, toks_in, tok_sin, tok_cos, tok_tmp):
    ki, ka, kb = tok_tmp
    a, b = tmp_a, tmp_b
    K.ts(b, ang, 1.0 / TWO_PI, None, ALU.mult, None, toks_in, [kb])
    K.cp(tmp_i, b, [kb], [ki])
    K.cp(b, tmp_i, [ki], [kb])
    K.stt(a, b, -TWO_PI, ang, ALU.mult, ALU.add, [kb] + list(toks_in), [ka])
    for _ in range(2):
        K.ts(b, a, PI, -TWO_PI, ALU.is_gt, ALU.mult, [ka], [kb])
        K.tt(a, a, b, ALU.add, [ka, kb], [ka])
        K.ts(b, a, -PI, TWO_PI, ALU.is_lt, ALU.mult, [ka], [kb])
        K.tt(a, a, b, ALU.add, [ka, kb], [ka])
    K.actv(sin_o, a, AF.Sin, [ka], [tok_sin])
    K.ts(a, a, PI / 2, None, ALU.add, None, [ka], [ka])
    K.ts(b, a, PI, -TWO_PI, ALU.is_gt, ALU.mult, [ka], [kb])
    K.tt(a, a, b, ALU.add, [ka, kb], [ka])
    K.actv(cos_o, a, AF.Sin, [ka], [tok_cos])


def phase_s5(K, cfg, x_all, uT_d, attT_d, x1_d, P, ident_f_d, iotaL_d, outs):
    SEQ = cfg["SEQ"]
    LC = 256
    with K.phase():
        ident_f = K.sb("identf", [128, 128], F32)
        K.load(ident_f[:], ident_f_d, None, (), [ident_f])
        iotaL = K.sb("iotaL", [128, LC], F32)
        K.load(iotaL[:], iotaL_d, None, (), [iotaL])
        w_glu = K.sb("w_glu", [128, 4, 512], BF16)
        w_out = K.sb("w_out", [128, 8, 1024], BF16)
        for c in range(4):
            K.load(w_glu[:, c, :], P["w_glu"][c * 128:(c + 1) * 128, :], None, (), [w_glu], queue="pool")
        for c in range(8):
            K.load(w_out[:, c, :], P["w_out"][c * 128:(c + 1) * 128, :], None, (), [w_out], queue="pool")
        psS = K.ps("psS", [128, 512], F32)
        gp = Rot([K.ps(f"pb{i}", [128, 512], F32) for i in range(2)])
        pyr = Rot([K.ps(f"py{i}", [128, 512], F32) for i in range(2)])
        psG = K.ps("psG", [128, 512], F32)
        psO = K.ps("psO", [128, 1024], F32)

        def load_T(name, src_16x128):
            raw = K.sb(name + "_raw", [16, 128], F32)
            dst = K.sb(name, [128, 16], F32)
            K.load(raw[:], src_16x128, None, (), [raw])
            K.tr(psS[:, 0:16], raw[:], ident_f[0:16, 0:16], [raw, ident_f], [psS])
            K.cp(dst[:], psS[:, 0:16], [psS], [dst])
            return dst
        lre = load_T("lre", P["lam_re"])
        lim = load_T("lim", P["lam_im"])
        ldr = K.sb("ldr", [16, 2], F32)
        K.load(ldr[:], P["log_dt"], None, (), [ldr])
        ldx = K.sb("ldx", [16, 2, 64], F32)
        K.cp(ldx[:], ldr[:].unsqueeze(2).to_broadcast([16, 2, 64]), [ldr], [ldx])
        dtt = K.sb("dtt", [128, 16], F32)
        K.tr(psS[:, 0:16], ldx[:].rearrange("t a n -> t (a n)"), ident_f[0:16, 0:16], [ldx, ident_f], [psS])
        K.actv(dtt[:], psS[:, 0:16], AF.Exp, [psS], [dtt])

        def sm(name):
            return K.sb(name, [128, 16], F32)
        lr, th, mag, sn, cs, abr, abi = sm("lr"), sm("th"), sm("mag"), sm("sn"), sm("cs"), sm("abr"), sm("abi")
        ta, tb, nr, dn, fre, fim = sm("ta"), sm("tb"), sm("nr"), sm("dn"), sm("fre"), sm("fim")
        ti = K.sb("ti", [128, 16], I32)
        K.ts(lr[:], lre[:], -1e-4, None, ALU.min, None, [lre], [lr])
        K.tt(th[:], lim[:], dtt[:], ALU.mult, [lim, dtt], [th])
        K.tt(mag[:], lr[:], dtt[:], ALU.mult, [lr, dtt], [mag])
        K.actv(mag[:], mag[:], AF.Exp, [mag], [mag])
        sincos(K, th[:], sn[:], cs[:], ti[:], ta[:], tb[:], [th], sn, cs, (ti, ta, tb))
        K.tt(abr[:], mag[:], cs[:], ALU.mult, [mag, cs], [abr])
        K.tt(abi[:], mag[:], sn[:], ALU.mult, [mag, sn], [abi])
        K.ts(nr[:], abr[:], -1.0, None, ALU.add, None, [abr], [nr])
        K.tt(dn[:], lr[:], lr[:], ALU.mult, [lr], [dn])
        K.tt(ta[:], lim[:], lim[:], ALU.mult, [lim], [ta])
        K.tt(dn[:], dn[:], ta[:], ALU.add, [dn, ta], [dn])
        K.op("dve", lambda e: e.reciprocal(dn[:], dn[:]), [dn], [dn])
        K.tt(ta[:], nr[:], lr[:], ALU.mult, [nr, lr], [ta])
        K.tt(tb[:], abi[:], lim[:], ALU.mult, [abi, lim], [tb])
        K.tt(fre[:], ta[:], tb[:], ALU.add, [ta, tb], [fre])
        K.tt(fre[:], fre[:], dn[:], ALU.mult, [fre, dn], [fre])
        K.tt(ta[:], abi[:], lr[:], ALU.mult, [abi, lr], [ta])
        K.tt(tb[:], nr[:], lim[:], ALU.mult, [nr, lim], [tb])
        K.tt(fim[:], ta[:], tb[:], ALU.subtract, [ta, tb], [fim])
        K.tt(fim[:], fim[:], dn[:], ALU.mult, [fim, dn], [fim])

        cosT = K.sb("cosT", [128, 16, LC], F32)
        sinT = K.sb("sinT", [128, 16, LC], F32)
        BTp = [K.sb(f"BTp{j}", [128, 16, 128], BF16) for j in range(2)]
        CTp = [K.sb(f"CTp{j}", [128, 16, 128], F32) for j in range(2)]
        dcol = K.sb("dcol", [128, 4], F32)
        bgcol = K.sb("bgcol", [128, 4], F32)
        setup_scope = K.phase()
        setup_scope.__enter__()
        angT = K.sb("angT", [128, 16, LC], F32)
        tmpT = K.sb("tmpT", [128, 16, LC], F32)
        tmpI = K.sb("tmpI", [128, 16, LC], I32)
        K.tt(angT[:], th[:].unsqueeze(2).to_broadcast([128, 16, LC]), iotaL[:].unsqueeze(1).to_broadcast([128, 16, LC]),
             ALU.mult, [th, iotaL], [angT])
        sincos_big(K, angT, sinT, cosT, tmpI, tmpT)

        br = K.sb("br", [128, 16, 16], F32)
        bi = K.sb("bi", [128, 16, 16], F32)
        K.load(br[:], P["b_re"].rearrange("(t p) c -> p t c", p=128), None, (), [br])
        K.load(bi[:], P["b_im"].rearrange("(t p) c -> p t c", p=128), None, (), [bi])
        bbr = K.sb("bbr", [128, 16, 16], F32)
        bbi = K.sb("bbi", [128, 16, 16], F32)
        tq = K.sb("tq", [128, 16, 16], F32)
        fre_b = fre[:].unsqueeze(2).to_broadcast([128, 16, 16])
        fim_b = fim[:].unsqueeze(2).to_broadcast([128, 16, 16])
        K.tt(bbr[:], br[:], fre_b, ALU.mult, [br, fre], [bbr])
        K.tt(tq[:], bi[:], fim_b, ALU.mult, [bi, fim], [tq])
        K.tt(bbr[:], bbr[:], tq[:], ALU.subtract, [bbr, tq], [bbr])
        K.tt(bbi[:], bi[:], fre_b, ALU.mult, [bi, fre], [bbi])
        K.tt(tq[:], br[:], fim_b, ALU.mult, [br, fim], [tq])
        K.tt(bbi[:], bbi[:], tq[:], ALU.add, [bbi, tq], [bbi])
        bpad = K.sb("bpad", [128, 16, 128], F32)
        for j, bb in enumerate((bbr, bbi)):
            K.memset(bpad[:], 0.0, [bpad])
            bp4 = bpad[:].rearrange("p (r i) f -> p r i f", i=4)
            bb4 = bb[:].rearrange("p (r i) c -> p r i c", i=4)
            for i in range(4):
                K.cp(bp4[0:64, :, i, 32 * i:32 * i + 16], bb4[0:64, :, i, :], [bb], [bpad])
                K.cp(bp4[64:128, :, i, 32 * i + 16:32 * i + 32], bb4[64:128, :, i, :], [bb], [bpad])
            for t in range(16):
                K.tr(psS[:, 0:128], bpad[:, t, :], ident_f[:], [bpad, ident_f], [psS])
                K.cp(BTp[j][:, t, :], psS[:, 0:128], [psS], [BTp[j]], eng="act")
        cin = K.sb("cin", [128, 128], F32)
        for j, src in enumerate((P["c_re"], P["c_im"])):
            K.memset(CTp[j][:], 0.0, [CTp[j]])
            for rt in range(4):
                K.memset(cin[:], 0.0, [cin])
                for gl in range(8):
                    K.load(cin[gl * 16:(gl + 1) * 16, (gl % 2) * 64:(gl % 2) * 64 + 64],
                           src[rt * 128 + gl * 16: rt * 128 + (gl + 1) * 16, :], None, (), [cin])
                K.tr(psS[:, 0:128], cin[:], ident_f[:], [cin, ident_f], [psS])
                for i in range(4):
                    if j == 0:
                        K.cp(CTp[j][:, 4 * rt + i, 32 * i:32 * i + 32], psS[:, 32 * i:32 * i + 32], [psS], [CTp[j]], eng="act")
                    else:
                        K.ts(CTp[j][:, 4 * rt + i, 32 * i:32 * i + 32], psS[:, 32 * i:32 * i + 32], -1.0, None, ALU.mult, None,
                             [psS], [CTp[j]])
        dsk = K.sb("dsk_raw", [4, 128], F32)
        K.load(dsk[:], P["dsk"], None, (), [dsk])
        K.tr(psS[:, 0:4], dsk[:], ident_f[0:4, 0:4], [dsk, ident_f], [psS])
        K.cp(dcol[:], psS[:, 0:4], [psS], [dcol])
        bgr = K.sb("bg_raw", [4, 128], F32)
        K.load(bgr[:], P["b_glu"], None, (), [bgr])
        K.tr(psS[:, 0:4], bgr[:], ident_f[0:4, 0:4], [bgr, ident_f], [psS])
        K.cp(bgcol[:], psS[:, 0:4], [psS], [bgcol])

        setup_scope.__exit__(None, None, None)
        hpr = K.sb("hpr", [128, 16], F32)
        hpi = K.sb("hpi", [128, 16], F32)
        uTb = [K.sb(f"uTb{i}", [128, 4, LC], BF16) for i in range(2)]
        mixT = [K.sb(f"mixT{i}", [128, 8, LC], BF16) for i in range(2)]
        mixS = K.sb("mixS", [128, 8, 128], BF16)
        W = {n: [K.sb(f"{n}{i}", [128, LC], F32) for i in range(2)] for n in
             ("t1", "t2", "t3", "t4", "p1", "p2", "p3", "p4", "wri", "wii", "wr", "wi", "hr", "hi")}
        yT = K.sb("yT", [128, LC], F32)
        sq = K.sb("sq", [128, LC], F32)
        z2 = [K.sb(f"z2_{i}", [128, 4, LC], BF16) for i in range(2)]
        gt = K.sb("gt", [128, LC], F32)
        xt = [K.sb(f"xt{i}", [128, 1024], F32) for i in range(2)]
        xo = [K.sb(f"xo{i}", [128, 1024], F32) for i in range(2)]
        hout = K.sb("hout", [16, 128], F32)
        h0raw = K.sb("h0raw", [16, 128], F32)

        def set_state(src_re, src_im):
            if src_re is None:
                K.memset(hpr[:], 0.0, [hpr])
                K.memset(hpi[:], 0.0, [hpi])
                return
            for src, dst in ((src_re, hpr), (src_im, hpi)):
                K.load(h0raw[:], src, None, (), [h0raw])
                K.tr(psS[:, 0:16], h0raw[:], ident_f[0:16, 0:16], [h0raw, ident_f], [psS])
                K.cp(dst[:], psS[:, 0:16], [psS], [dst])

        def put_state(dst_re, dst_im):
            for dst, src in ((dst_re, hpr), (dst_im, hpi)):
                K.tr(psS[0:16, 0:128], src[:], ident_f[:], [src, ident_f], [psS])
                K.cp(hout[:], psS[0:16, 0:128], [psS], [hout])
                K.load(dst, hout[:], None, [hout], ())

        cnt = [0]

        def s5_chunk(tok0, L, mix, mcol0):
            ci = cnt[0]
            cnt[0] += 1
            ub, zz = uTb[ci % 2], z2[ci % 2]
            for c in range(4):
                K.load(ub[:, c, 0:L], uT_d[c * 128:(c + 1) * 128, tok0:tok0 + L], None, (), [ub])
                K.load(mix[:, c, mcol0:mcol0 + L], attT_d[c * 128:(c + 1) * 128, tok0:tok0 + L], None, (), [mix])
            for rt in range(4):
                py = pyr.next()
                for i in range(4):
                    t = 4 * rt + i
                    w = {n: W[n][t % 2] for n in W}
                    pb = gp.next()
                    K.mm(pb[:, 0:L], BTp[0][:, t, :], ub[:, rt, 0:L], True, True, [BTp[0], ub], [pb])
                    K.mm(pb[:, 256:256 + L], BTp[1][:, t, :], ub[:, rt, 0:L], True, True, [BTp[1], ub], [pb])
                    cT, sT = cosT[:, t, 0:L], sinT[:, t, 0:L]
                    bre, bim = pb[:, 0:L], pb[:, 256:256 + L]
                    K.tt(w["t1"][:, 0:L], bre, cT, ALU.mult, [pb, cosT], [w["t1"]])
                    K.tt(w["t2"][:, 0:L], bim, sT, ALU.mult, [pb, sinT], [w["t2"]])
                    K.tt(w["t3"][:, 0:L], bim, cT, ALU.mult, [pb, cosT], [w["t3"]])
                    K.tt(w["t4"][:, 0:L], bre, sT, ALU.mult, [pb, sinT], [w["t4"]])
                    K.tt(w["wri"][:, 0:L], w["t1"][:, 0:L], w["t2"][:, 0:L], ALU.add, [w["t1"], w["t2"]], [w["wri"]])
                    K.tt(w["wii"][:, 0:L], w["t3"][:, 0:L], w["t4"][:, 0:L], ALU.subtract, [w["t3"], w["t4"]], [w["wii"]])
                    rb = mag[:, t:t + 1].to_broadcast([128, L])
                    K.op("dve", lambda e, w=w, rb=rb, t=t: e.tensor_tensor_scan(
                        w["wr"][:, 0:L], rb, w["wri"][:, 0:L], hpr[:, t:t + 1], ALU.mult, ALU.add),
                        [mag, w["wri"], hpr], [w["wr"]])
                    K.op("dve", lambda e, w=w, rb=rb, t=t: e.tensor_tensor_scan(
                        w["wi"][:, 0:L], rb, w["wii"][:, 0:L], hpi[:, t:t + 1], ALU.mult, ALU.add),
                        [mag, w["wii"], hpi], [w["wi"]])
                    PE_ = cfg.get("s5_post_eng", "pool")
                    K.tt(w["p1"][:, 0:L], w["wr"][:, 0:L], cT, ALU.mult, [w["wr"], cosT], [w["p1"]], eng=PE_)
                    K.tt(w["p2"][:, 0:L], w["wi"][:, 0:L], sT, ALU.mult, [w["wi"], sinT], [w["p2"]], eng=PE_)
                    K.tt(w["p3"][:, 0:L], w["wi"][:, 0:L], cT, ALU.mult, [w["wi"], cosT], [w["p3"]], eng=PE_)
                    K.tt(w["p4"][:, 0:L], w["wr"][:, 0:L], sT, ALU.mult, [w["wr"], sinT], [w["p4"]], eng=PE_)
                    K.tt(w["hr"][:, 0:L], w["p1"][:, 0:L], w["p2"][:, 0:L], ALU.subtract, [w["p1"], w["p2"]], [w["hr"]], eng=PE_)
                    K.tt(w["hi"][:, 0:L], w["p3"][:, 0:L], w["p4"][:, 0:L], ALU.add, [w["p3"], w["p4"]], [w["hi"]], eng=PE_)
                    K.cp(hpr[:, t:t + 1], w["hr"][:, L - 1:L], [w["hr"]], [hpr], eng="act")
                    K.cp(hpi[:, t:t + 1], w["hi"][:, L - 1:L], [w["hi"]], [hpi], eng="act")
                    K.mm(py[:, 0:L], CTp[0][:, t, :], w["hr"][:, 0:L], i == 0, False, [CTp[0], w["hr"]], [py])
                    K.mm(py[:, 0:L], CTp[1][:, t, :], w["hi"][:, 0:L], False, i == 3, [CTp[1], w["hi"]], [py])
                K.stt(yT[:, 0:L], ub[:, rt, 0:L], dcol[:, rt:rt + 1], py[:, 0:L], ALU.mult, ALU.add, [ub, dcol, py], [yT])
                K.actv(sq[:, 0:L], yT[:, 0:L], AF.Square, [yT], [sq])
                K.ts(sq[:, 0:L], sq[:, 0:L], 0.044715, 1.0, ALU.mult, ALU.add, [sq], [sq])
                K.tt(sq[:, 0:L], sq[:, 0:L], yT[:, 0:L], ALU.mult, [sq, yT], [sq])
                K.actv(sq[:, 0:L], sq[:, 0:L], AF.Tanh, [sq], [sq], scale=float(np.sqrt(2.0 / np.pi)))
                K.stt(zz[:, rt, 0:L], sq[:, 0:L], 1.0, yT[:, 0:L], ALU.add, ALU.mult, [sq, yT], [zz])
            for fo in range(4):
                for kc in range(4):
                    K.mm(psG[:, 0:L], w_glu[:, kc, fo * 128:(fo + 1) * 128], zz[:, kc, 0:L], kc == 0, kc == 3, [w_glu, zz], [psG])
                K.actv(gt[:, 0:L], psG[:, 0:L], AF.Sigmoid, [psG, bgcol], [gt], scale=0.5, bias=bgcol[:, fo:fo + 1])
                K.stt(mix[:, 4 + fo, mcol0:mcol0 + L], zz[:, fo, 0:L], 0.5, gt[:, 0:L], ALU.mult, ALU.mult, [zz, gt], [mix])

        xc = [0]

        def out_proj(mix, col0, tok0):
            p = xc[0] % 2
            xc[0] += 1
            K.load(xt[p][:], x_all[tok0:tok0 + 128, :], None, (), [xt[p]])
            for hf in range(2):
                for kc in range(8):
                    K.mm(psO[:, hf * 512:(hf + 1) * 512], mix[:, kc, col0:col0 + 128], w_out[:, kc, hf * 512:(hf + 1) * 512],
                         kc == 0, kc == 7, [mix, w_out], [psO])
            K.tt(xo[p][:], psO[:], xt[p][:], ALU.add, [psO, xt[p]], [xo[p]])
            K.load(x1_d[tok0:tok0 + 128, :], xo[p][:], None, [xo[p]], ())

        set_state(None, None)
        for ci in range(SEQ // LC):
            mx = mixT[ci % 2]
            s5_chunk(ci * LC, LC, mx, 0)
            for tl in range(LC // 128):
                out_proj(mx, tl * 128, ci * LC + tl * 128)
        put_state(outs["hp_re"], outs["hp_im"])
        for b in range(NSB):
            set_state(P["h0_re"][b], P["h0_im"][b])
            s5_chunk(SEQ + b * 32, 32, mixS, b * 32)
            put_state(outs["hs_re"][b], outs["hs_im"][b])
        out_proj(mixS, 0, SEQ)


def sincos_big(K, angT, sinT, cosT, tmpI, tmpT):
    a = angT[:]
    b = tmpT[:]
    K.ts(b, a, 1.0 / TWO_PI, None, ALU.mult, None, [angT], [tmpT])
    K.cp(tmpI[:], b, [tmpT], [tmpI])
    K.cp(b, tmpI[:], [tmpI], [tmpT])
    K.stt(a, b, -TWO_PI, a, ALU.mult, ALU.add, [tmpT, angT], [angT])
    for _ in range(2):
        K.ts(b, a, PI, -TWO_PI, ALU.is_gt, ALU.mult, [angT], [tmpT])
        K.tt(a, a, b, ALU.add, [angT, tmpT], [angT])
        K.ts(b, a, -PI, TWO_PI, ALU.is_lt, ALU.mult, [angT], [tmpT])
        K.tt(a, a, b, ALU.add, [angT, tmpT], [angT])
    K.actv(sinT[:], a, AF.Sin, [angT], [sinT])
    K.ts(a, a, PI / 2, None, ALU.add, None, [angT], [angT])
    K.ts(b, a, PI, -TWO_PI, ALU.is_gt, ALU.mult, [angT], [tmpT])
    K.tt(a, a, b, ALU.add, [angT, tmpT], [angT])
    K.actv(cosT[:], a, AF.Sin, [angT], [cosT])


def s5_consts():
    return {"ident": np.eye(128, dtype=np.float32), "iotaL": np.tile(np.arange(1, 257, dtype=np.float32), (128, 1))}


def phase_odd(K, cfg, x_in, x_out, Wd, Cd, caches, outs):
    SEQ = cfg["SEQ"]
    NTOK = SEQ + 128
    NTP = SEQ // 128
    with K.phase():
        ident_f = K.sb("identf", [128, 128], F32)
        ident_bf = K.sb("identbf", [128, 128], BF16)
        ones_bf = K.sb("onesbf", [128, 128], BF16)
        K.load(ident_f[:], Cd["ident"], None, (), [ident_f])
        K.load(ident_bf[:], Cd["ident"], None, (), [ident_bf], queue="pool")
        K.memset(ones_bf[:], 1.0, [ones_bf])
        eps_t = K.sb("eps_t", [128, 1], F32)
        K.memset(eps_t[:], RMS_EPS, [eps_t])
        gmix = K.sb("gmix", [128, 1024], F32)
        K.load(gmix[:], bcast_rows(Wd["gmix"]), None, (), [gmix])
        qn = K.sb("qn", [128, 512], F32)
        K.load(qn[:], bcast_rows(Wd["qn"]), None, (), [qn])
        kvn = K.sb("kvn", [128, 256], F32)
        K.load(kvn[:], bcast_rows(Wd["kvn"]), None, (), [kvn])
        maskcol = K.sb("maskcol", [128, 4], F32)
        K.load(maskcol[:], Cd["maskcol"], None, (), [maskcol])
        mask4 = K.sb("mask4", [128, 4, 512], BF16)
        K.load(mask4[:], Cd["mask4"], None, (), [mask4], queue="pool")
        maskK = K.sb("maskK", [128, 4, 256], BF16)
        K.load(maskK[:], Cd["maskK"], None, (), [maskK], queue="pool")

        def wload(name, src, kch, ncol):
            t = K.sb(name, [128, kch, ncol], BF16)
            for c in range(kch):
                for c0 in range(0, ncol, 1024):
                    c1 = min(ncol, c0 + 1024)
                    K.load(t[:, c, c0:c1], src[c * 128:(c + 1) * 128, c0:c1], None, (), [t], queue="pool")
            return t
        w_in = wload("w_in", Wd["w_in"], 8, 1312)
        w_out = wload("w_out", Wd["w_out"], 8, 1024)
        w_uqN = wload("w_uqN", Wd["w_uqN"], 4, 512)
        w_uqR = wload("w_uqR", Wd["w_uqR"], 4, 256)
        w_uqRP = wload("w_uqRP", Wd["w_uqRP"], 4, 256)
        w_uv = wload("w_uv", Wd["w_uv"], 2, 512)
        pool_w = K.sb("pool_w", [128, 4, 128], BF16)
        for g in range(4):
            K.load(pool_w[:, g, :], Wd["pool_w"][g], None, (), [pool_w], queue="pool")
        psr = K.sb("psr", [4, 128], F32)
        K.load(psr[:], Wd["pscale"], None, (), [psr])
        pscol = K.sb("pscol", [128, 4], F32)

        gen = Rot([K.ps(f"gen{i}", [128, 512], F32) for i in range(3)])
        psSc = Rot([K.ps(f"psSc{i}", [128, 512], F32) for i in range(2)])
        psOa = K.ps("psOa", [128, 512], F32)
        psOb = K.ps("psOb", [128, 512], F32)
        psDn = K.ps("psDn", [128, 512], F32)

        g0 = gen.next()
        K.tr(g0[:, 0:4], psr[:], ident_f[0:4, 0:4], [psr, ident_f], [g0])
        K.cp(pscol[:], g0[:, 0:4], [g0], [pscol])
        wuk_raw = K.sb("wuk_raw", [128, 2, 512], F32)
        K.load(wuk_raw[:], Wd["w_uk"].rearrange("(ct p) f -> p ct f", p=128), None, (), [wuk_raw])
        w_ukT = K.sb("w_ukT", [128, 4, 256], BF16)
        for hp in range(4):
            for ct in range(2):
                g1 = gen.next()
                K.tr(g1[:, 0:128], wuk_raw[:, ct, hp * 128:(hp + 1) * 128], ident_f[:], [wuk_raw, ident_f], [g1])
                K.cp(w_ukT[:, hp, ct * 128:(ct + 1) * 128], g1[:, 0:128], [g1], [w_ukT], eng="act")

        cT_all = K.sb("cT_all", [128, 2, NTOK], BF16)
        c_tok_all = K.sb("c_tok_all", [128, NTP + 1, 256], BF16)
        kpT4_all = K.sb("kpT4_all", [128, NTOK], BF16)

        junk = K.sb("junk", [128, 1024], BF16)
        xt = [K.sb(f"xt{i}", [128, 1024], F32) for i in range(2)]
        xo = [K.sb(f"xo{i}", [128, 1024], F32) for i in range(2)]
        ssb = [K.sb(f"ssb{i}", [128, 4], F32) for i in range(2)]
        sscq = [K.sb(f"sscq{i}", [128, 4], F32) for i in range(2)]
        sscc = [K.sb(f"sscc{i}", [128, 4], F32) for i in range(2)]
        hn = [K.sb(f"hn{i}", [128, 1024], BF16) for i in range(2)]
        cqn = [K.sb(f"cqn{i}", [128, 512], BF16) for i in range(2)]
        cf = [K.sb(f"cf{i}", [128, 256], F32) for i in range(2)]
        kt1 = [K.sb(f"kt1_{i}", [128, 32], F32) for i in range(2)]
        kt2 = [K.sb(f"kt2_{i}", [128, 32], F32) for i in range(2)]
        kpf = [K.sb(f"kpf{i}", [128, 32], F32) for i in range(2)]
        kp4 = [K.sb(f"kp4{i}", [128, 4, 32], BF16) for i in range(2)]
        ropeK = [K.sb(f"ropeK{i}", [128, 64], F32) for i in range(2)]

        def make_bufs(W):
            B = {}
            B["hnT"] = K.sb("hnT", [128, 8, W], BF16)
            B["cqnT"] = K.sb("cqnT", [128, 4, W], BF16)
            B["qnT"] = K.sb("qnT", [128, 4, W], BF16)
            B["qpeT"] = K.sb("qpeT", [128, 2, W], BF16)
            B["rC"] = K.sb("rC", [128, W], F32)
            B["rS"] = K.sb("rS", [128, W], F32)
            B["r1"] = K.sb("r1", [128, W], F32)
            B["r2"] = K.sb("r2", [128, W], F32)
            B["extA"] = K.sb("extA", [128, 15 + W], F32)
            B["extB"] = K.sb("extB", [128, 15 + W], F32)
            B["extC"] = K.sb("extC", [128, 15 + W], F32)
            for nm in ("extA", "extB", "extC"):
                K.memset(B[nm][:], 0.0, [B[nm]])
            B["rc"] = K.sb("rc", [128, W], F32)
            B["mT"] = K.sb("mT", [128, W], BF16)
            B["mixT"] = K.sb("mixT", [128, 8, W], BF16)
            B["hist"] = K.sb("hist", [128, 4, 15], F32)
            B["utail"] = K.sb("utail", [15, 512], F32)
            B["hraw"] = K.sb("hraw", [15, 512], F32)
            B["uS"] = K.sb("uS", [128, W], F32)
            return B

        def norm_small(X, OUT, gb, width, sc, xtok, otok, gtok):
            K.actv(junk[:, 0:width], X, AF.Square, [xtok], [sc, junk], accum_out=sc[:, 0:1])
            K.actv(sc[:, 1:2], sc[:, 0:1], AF.Sqrt, [sc, eps_t], [sc], scale=1.0 / width, bias=eps_t[:, 0:1])
            K.op("dve", lambda e: e.reciprocal(sc[:, 2:3], sc[:, 1:2]), [sc], [sc])
            K.stt(OUT, X, sc[:, 2:3], gb, ALU.mult, ALU.mult, [xtok, sc, gtok], [otok])

        cnt = {"x": 0, "t": 0}

        def front(B, tok0, W, hist_src):
            ntl = W // 128
            hnT, cqnT = B["hnT"], B["cqnT"]
            for tl in range(ntl):
                p = cnt["t"] % 2
                cnt["t"] += 1
                X = xt[cnt["x"] % 2]
                cnt["x"] += 1
                j = (tok0 + tl * 128) // 128
                K.load(X[:], x_in[tok0 + tl * 128: tok0 + (tl + 1) * 128, :], None, (), [X])
                K.load(ropeK[p][:], Cd["ropeK"][tok0 + tl * 128: tok0 + (tl + 1) * 128, :], None, (), [ropeK[p]])
                rmsnorm_tile(K, X[:], hn[p][:], gmix[:], ssb[p], eps_t, junk[:], X, hn[p], gmix, junk)
                gT = gen.next()
                gT_bf = gT[:].bitcast(BF16)
                for c in range(8):
                    K.tr(gT_bf[:, c * 128:(c + 1) * 128], hn[p][:, c * 128:(c + 1) * 128], ident_bf[:], [hn[p], ident_bf], [gT])
                K.cp(hnT[:, :, tl * 128:(tl + 1) * 128], gT_bf[:, :].rearrange("p (c t) -> p c t", c=8), [gT], [hnT], eng="act")
                pq, pc = gen.next(), gen.next()
                for kc in range(8):
                    K.mm(pq[:, 0:512], hnT[:, kc, tl * 128:(tl + 1) * 128], w_in[:, kc, 0:512], kc == 0, kc == 7, [hnT, w_in], [pq])
                for kc in range(8):
                    K.mm(pc[:, 0:288], hnT[:, kc, tl * 128:(tl + 1) * 128], w_in[:, kc, 512:800], kc == 0, kc == 7, [hnT, w_in], [pc])
                norm_small(pq[:, 0:512], cqn[p][:], qn[:], 512, sscq[p], pq, cqn[p], qn)
                norm_small(pc[:, 0:256], cf[p][:], kvn[:], 256, sscc[p], pc, cf[p], kvn)
                K.tt(kt1[p][:], pc[:, 256:288], ropeK[p][:, 0:32], ALU.mult, [pc, ropeK[p]], [kt1[p]])
                K.tt(kt2[p][:, 0:16], pc[:, 272:288], ropeK[p][:, 32:48], ALU.mult, [pc, ropeK[p]], [kt2[p]])
                K.tt(kt2[p][:, 16:32], pc[:, 256:272], ropeK[p][:, 48:64], ALU.mult, [pc, ropeK[p]], [kt2[p]])
                K.tt(kpf[p][:], kt1[p][:], kt2[p][:], ALU.add, [kt1[p], kt2[p]], [kpf[p]])
                K.cp(kp4[p][:], kpf[p][:].unsqueeze(1).to_broadcast([128, 4, 32]), [kpf[p]], [kp4[p]])
                K.cp(c_tok_all[:, j, :], cf[p][:], [cf[p]], [c_tok_all])
                if tok0 < SEQ:
                    K.load(outs["ckv_p"][tok0 + tl * 128: tok0 + (tl + 1) * 128, :], cf[p][:], None, [cf[p]], ())
                    K.load(outs["kpe_p"][tok0 + tl * 128: tok0 + (tl + 1) * 128, :], kpf[p][:], None, [kpf[p]], ())
                else:
                    for b in range(NSB):
                        K.load(outs["ckv_s"][b], cf[p][b * 32:(b + 1) * 32, :], None, [cf[p]], ())
                        K.load(outs["kpe_s"][b], kpf[p][b * 32:(b + 1) * 32, :], None, [kpf[p]], ())
                gT2 = gen.next()
                gT2_bf = gT2[:].bitcast(BF16)
                for c in range(4):
                    K.tr(gT2_bf[:, c * 128:(c + 1) * 128], cqn[p][:, c * 128:(c + 1) * 128], ident_bf[:], [cqn[p], ident_bf], [gT2])
                K.cp(cqnT[:, :, tl * 128:(tl + 1) * 128], gT2_bf[:, 0:512].rearrange("p (c t) -> p c t", c=4), [gT2], [cqnT], eng="act")
                gT3 = gen.next()
                gT3_bf = gT3[:].bitcast(BF16)
                for c in range(2):
                    K.tr(gT3_bf[:, c * 128:(c + 1) * 128], c_tok_all[:, j, c * 128:(c + 1) * 128], ident_bf[:],
                         [c_tok_all, ident_bf], [gT3])
                K.tr(gT3_bf[:, 256:384], kp4[p][:].rearrange("p a r -> p (a r)"), ident_bf[:], [kp4[p], ident_bf], [gT3])
                K.cp(cT_all[:, :, j * 128:(j + 1) * 128], gT3_bf[:, 0:256].rearrange("p (c t) -> p c t", c=2), [gT3], [cT_all], eng="act")
                K.cp(kpT4_all[:, j * 128:(j + 1) * 128], gT3_bf[:, 256:384], [gT3], [kpT4_all], eng="act")
            K.load(B["rC"][:, 0:W], Cd["ropeQC"][:, tok0:tok0 + W], None, (), [B["rC"]])
            K.load(B["rS"][:, 0:W], Cd["ropeQS"][:, tok0:tok0 + W], None, (), [B["rS"]])
            for c in range(4):
                pn = gen.next()
                for kc in range(4):
                    K.mm(pn[:, 0:W], w_uqN[:, kc, c * 128:(c + 1) * 128], cqnT[:, kc, 0:W], kc == 0, kc == 3, [w_uqN, cqnT], [pn])
                K.cp(B["qnT"][:, c, 0:W], pn[:, 0:W], [pn], [B["qnT"]], eng="act")
            for c in range(2):
                pr, pp = gen.next(), gen.next()
                for kc in range(4):
                    K.mm(pr[:, 0:W], w_uqR[:, kc, c * 128:(c + 1) * 128], cqnT[:, kc, 0:W], kc == 0, kc == 3, [w_uqR, cqnT], [pr])
                for kc in range(4):
                    K.mm(pp[:, 0:W], w_uqRP[:, kc, c * 128:(c + 1) * 128], cqnT[:, kc, 0:W], kc == 0, kc == 3, [w_uqRP, cqnT], [pp])
                K.tt(B["r1"][:, 0:W], pr[:, 0:W], B["rC"][:, 0:W], ALU.mult, [pr, B["rC"]], [B["r1"]])
                K.tt(B["r2"][:, 0:W], pp[:, 0:W], B["rS"][:, 0:W], ALU.mult, [pp, B["rS"]], [B["r2"]])
                K.tt(B["qpeT"][:, c, 0:W], B["r1"][:, 0:W], B["r2"][:, 0:W], ALU.add, [B["r1"], B["r2"]], [B["qpeT"]])
            segs = [(0, W)] if not isinstance(hist_src, list) else [(b * 32, 32) for b in range(len(hist_src))]
            if isinstance(hist_src, list):
                hraw = B["hraw"]
            for g in range(4):
                pu = gen.next()
                for kc in range(8):
                    K.mm(pu[:, 0:W], w_in[:, kc, 800 + g * 128:800 + (g + 1) * 128], hnT[:, kc, 0:W], kc == 0, kc == 7, [w_in, hnT], [pu])
                K.cp(B["uS"][:, 0:W], pu[:, 0:W], [pu], [B["uS"]], eng="act")
                for si_, (s0, L) in enumerate(segs):
                    A, Bb, Cc = B["extA"], B["extB"], B["extC"]
                    if hist_src == "zero":
                        K.memset(A[:, 0:15], 0.0, [A])
                    elif isinstance(hist_src, list):
                        K.load(hraw[:, g * 128:(g + 1) * 128], hist_src[si_][:, g * 128:(g + 1) * 128], None, (), [hraw])
                        ph = gen.next()
                        K.tr(ph[:, 0:15], hraw[:, g * 128:(g + 1) * 128], ident_f[0:15, 0:15], [hraw, ident_f], [ph])
                        K.cp(A[:, 0:15], ph[:, 0:15], [ph], [A])
                    else:
                        K.cp(A[:, 0:15], B["hist"][:, g, :], [B["hist"]], [A])
                    K.cp(A[:, 15:15 + L], B["uS"][:, s0:s0 + L], [B["uS"]], [A])
                    if hist_src is None or hist_src == "zero":
                        K.cp(B["hist"][:, g, :], A[:, L:L + 15], [A], [B["hist"]])
                    src, dst = A, Bb
                    n = 15 + L
                    for st in range(g + 1):
                        sh = 1 << st
                        K.tt(dst[:, sh:n], src[:, sh:n], src[:, 0:n - sh], ALU.add, [src], [dst])
                        src, dst = dst, (Cc if dst is Bb else Bb)
                    K.load(B["rc"][:, 0:L], bcast_rows(Cd["rcnt"][g:g + 1, tok0 + s0: tok0 + s0 + L]), None, (), [B["rc"]])
                    K.tt(Cc[:, 15:n] if src is not Cc else Bb[:, 15:n], src[:, 15:n], B["rc"][:, 0:L], ALU.mult, [src, B["rc"]],
                         [Cc if src is not Cc else Bb])
                    tot = Cc if src is not Cc else Bb
                    K.tt(B["mT"][:, s0:s0 + L], tot[:, 15:n], A[:, 15:n], ALU.subtract, [tot, A], [B["mT"]])
                    last = (tok0 + W == SEQ) or isinstance(hist_src, list)
                    if last:
                        pt = gen.next()
                        K.tr(pt[0:15, 0:128], A[:, L:L + 15], ident_f[:], [A, ident_f], [pt])
                        K.cp(B["utail"][:, g * 128:(g + 1) * 128], pt[0:15, 0:128], [pt], [B["utail"]])
                        if g == 3 or isinstance(hist_src, list):
                            dsto = outs["pool_p"] if not isinstance(hist_src, list) else outs["pool_s"][si_]
                            K.load(dsto[:, g * 128:(g + 1) * 128], B["utail"][:, g * 128:(g + 1) * 128], None, [B["utail"]], ())
                            if not isinstance(hist_src, list):
                                for g2 in range(3):
                                    K.load(dsto[:, g2 * 128:(g2 + 1) * 128], B["utail"][:, g2 * 128:(g2 + 1) * 128], None,
                                           [B["utail"]], ())
                po = gen.next()
                K.mm(po[:, 0:W], pool_w[:, g, :], B["mT"][:, 0:W], True, True, [pool_w, B["mT"]], [po])
                K.actv(B["mixT"][:, g, 0:W], po[:, 0:W], AF.Copy, [po, pscol], [B["mixT"]], scale=pscol[:, g:g + 1])

        def qhead(B, h, W, qa, qpad):
            r0 = (h % 2) * 64
            for cc in range(2):
                pa = gen.next()
                K.mm(pa[:, 0:W], w_ukT[r0:r0 + 64, h // 2, cc * 128:(cc + 1) * 128], B["qnT"][r0:r0 + 64, h // 2, 0:W], True, True,
                     [w_ukT, B["qnT"]], [pa])
                K.cp(qa[:, cc, 0:W], pa[:, 0:W], [pa], [qa], eng="act")
            K.ts(qpad[:, 0:W], B["qpeT"][:, h // 4, 0:W], maskcol[:, h % 4:h % 4 + 1], None, ALU.mult, None,
                 [B["qpeT"], maskcol], [qpad])

        def out_proj(B, W, tok0):
            for tl in range(W // 128):
                p = cnt["t"] % 2
                cnt["t"] += 1
                X = xt[cnt["x"] % 2]
                cnt["x"] += 1
                K.load(X[:], x_in[tok0 + tl * 128: tok0 + (tl + 1) * 128, :], None, (), [X])
                pso = [gen.next(), gen.next()]
                for hf in range(2):
                    for kc in range(8):
                        K.mm(pso[hf][:, :], B["mixT"][:, kc, tl * 128:(tl + 1) * 128], w_out[:, kc, hf * 512:(hf + 1) * 512],
                             kc == 0, kc == 7, [B["mixT"], w_out], [pso[hf]])
                    K.tt(xo[p][:, hf * 512:(hf + 1) * 512], pso[hf][:, :], X[:, hf * 512:(hf + 1) * 512], ALU.add, [pso[hf], X], [xo[p]])
                K.load(x_out[tok0 + tl * 128: tok0 + (tl + 1) * 128, :], xo[p][:], None, [xo[p]], ())

        pscope = K.phase()
        pscope.__enter__()
        B = make_bufs(512)
        qa = [K.sb(f"qa{i}", [128, 2, 512], BF16) for i in range(2)]
        qpad = [K.sb(f"qpad{i}", [128, 512], BF16) for i in range(2)]
        PT = [K.sb(f"PT{i}", [128, 512], BF16) for i in range(3)]
        rden = K.sb("rden", [128, 512], F32)
        olat = [K.sb(f"olat{i}", [128, 2, 512], BF16) for i in range(2)]
        pti = 0
        for s in range(SEQ // 512):
            tok0 = s * 512
            front(B, tok0, 512, "zero" if s == 0 else None)
            nkt = (tok0 + 512) // 128
            for h in range(8):
                qh, qp = qa[h % 2], qpad[h % 2]
                qhead(B, h, 512, qh, qp)
                for kt in range(nkt):
                    pS = psSc.next()
                    diag = kt - tok0 // 128
                    K.mm(pS[:, :], cT_all[:, 0, kt * 128:(kt + 1) * 128], qh[:, 0, :], True, False, [cT_all, qh], [pS])
                    K.mm(pS[:, :], cT_all[:, 1, kt * 128:(kt + 1) * 128], qh[:, 1, :], False, False, [cT_all, qh], [pS])
                    K.mm(pS[:, :], kpT4_all[:, kt * 128:(kt + 1) * 128], qp[:, :], False, diag < 0, [kpT4_all, qp], [pS])
                    if diag >= 0:
                        K.mm(pS[:, :], ident_bf[:], mask4[:, diag, :], False, True, [ident_bf, mask4], [pS])
                    P_ = PT[pti % 3]
                    pti += 1
                    K.actv(P_[:], pS[:, :], AF.Exp, [pS], [P_], scale=MLA_SCALE)
                    K.mm(psOa[:, :], c_tok_all[:, kt, 0:128], P_[:], kt == 0, kt == nkt - 1, [c_tok_all, P_], [psOa])
                    K.mm(psOb[:, :], c_tok_all[:, kt, 128:256], P_[:], kt == 0, kt == nkt - 1, [c_tok_all, P_], [psOb])
                    K.mm(psDn[:, :], ones_bf[:], P_[:], kt == 0, kt == nkt - 1, [ones_bf, P_], [psDn])
                ol = olat[h % 2]
                K.op("dve", lambda e: e.reciprocal(rden[:], psDn[:, :]), [psDn], [rden])
                K.tt(ol[:, 0, :], psOa[:, :], rden[:], ALU.mult, [psOa, rden], [ol])
                K.tt(ol[:, 1, :], psOb[:, :], rden[:], ALU.mult, [psOb, rden], [ol])
                if h % 2 == 0:
                    pm = gen.next()
                for cc in range(2):
                    K.mm(pm[(h % 2) * 64:(h % 2) * 64 + 64, :], w_uv[:, cc, h * 64:(h + 1) * 64], ol[:, cc, :], cc == 0, cc == 1,
                         [w_uv, ol], [pm])
                if h % 2 == 1:
                    K.cp(B["mixT"][:, 4 + h // 2, :], pm[:, :], [pm], [B["mixT"]], eng="act")
            out_proj(B, 512, tok0)
        pscope.__exit__(None, None, None)

        B = make_bufs(128)
        qaS = K.sb("qaS", [128, 8, 2, 128], BF16)
        qpadS = K.sb("qpadS", [128, 8, 128], BF16)
        front(B, SEQ, 128, [caches["pool"][b] for b in range(NSB)])
        for h in range(8):
            r0 = (h % 2) * 64
            for cc in range(2):
                pa = gen.next()
                K.mm(pa[:, 0:128], w_ukT[r0:r0 + 64, h // 2, cc * 128:(cc + 1) * 128], B["qnT"][r0:r0 + 64, h // 2, 0:128], True, True,
                     [w_ukT, B["qnT"]], [pa])
                K.cp(qaS[:, h, cc, :], pa[:, 0:128], [pa], [qaS], eng="act")
            K.ts(qpadS[:, h, :], B["qpeT"][:, h // 4, 0:128], maskcol[:, h % 4:h % 4 + 1], None, ALU.mult, None,
                 [B["qpeT"], maskcol], [qpadS])
        cc_tok = [K.sb(f"cc_tok{i}", [128, 16, 256], BF16) for i in range(2)]
        ccT = [K.sb(f"ccT{i}", [128, 2, 2048], BF16) for i in range(2)]
        ckp = [K.sb(f"ckp{i}", [128, 16, 32], BF16) for i in range(2)]
        ckp4 = [K.sb(f"ckp4{i}", [128, 16, 4, 32], BF16) for i in range(1)] * 2
        ckpT = [K.sb(f"ckpT{i}", [128, 2048], BF16) for i in range(1)] * 2
        PTs = [K.sb(f"PTs{i}", [128, 256], BF16) for i in range(3)]
        rdenS = K.sb("rdenS", [128, 256], F32)
        olS = K.sb("olS", [128, 2, 256], BF16)
        pti = 0
        for b in range(NSB):
            bp = b % 2
            for half in range(2):
                K.load(cc_tok[bp][:, half * 8:(half + 1) * 8, :],
                       caches["ckv"][b, half * 1024:(half + 1) * 1024, :].rearrange("(kt p) c -> p kt c", p=128), None, (),
                       [cc_tok[bp]], queue="pool")
            K.load(ckp[bp][:], caches["kpe"][b].rearrange("(kt p) r -> p kt r", p=128), None, (), [ckp[bp]], queue="pool")
            K.cp(ckp4[bp][:], ckp[bp][:].unsqueeze(2).to_broadcast([128, 16, 4, 32]), [ckp[bp]], [ckp4[bp]])
            for kt in range(16):
                gt_ = gen.next()
                gt_bf = gt_[:].bitcast(BF16)
                for c in range(2):
                    K.tr(gt_bf[:, c * 128:(c + 1) * 128], cc_tok[bp][:, kt, c * 128:(c + 1) * 128], ident_bf[:], [cc_tok[bp], ident_bf], [gt_])
                K.tr(gt_bf[:, 256:384], ckp4[bp][:, kt, :, :].rearrange("p a r -> p (a r)"), ident_bf[:], [ckp4[bp], ident_bf], [gt_])
                K.cp(ccT[bp][:, :, kt * 128:(kt + 1) * 128], gt_bf[:, 0:256].rearrange("p (c t) -> p c t", c=2), [gt_], [ccT[bp]], eng="act")
                K.cp(ckpT[bp][:, kt * 128:(kt + 1) * 128], gt_bf[:, 256:384], [gt_], [ckpT[bp]], eng="act")
            qs = slice(b * 32, (b + 1) * 32)
            for kt in range(17):
                pS = psSc.next()
                if kt < 16:
                    l0, l1, l2 = ccT[bp][:, 0, kt * 128:(kt + 1) * 128], ccT[bp][:, 1, kt * 128:(kt + 1) * 128], ckpT[bp][:, kt * 128:(kt + 1) * 128]
                    ltoks = [ccT[bp], ckpT[bp]]
                    vtok, vt = cc_tok[bp], cc_tok[bp][:, kt, :]
                else:
                    l0, l1, l2 = cT_all[:, 0, SEQ:SEQ + 128], cT_all[:, 1, SEQ:SEQ + 128], kpT4_all[:, SEQ:SEQ + 128]
                    ltoks = [cT_all, kpT4_all]
                    vtok, vt = c_tok_all, c_tok_all[:, NTP, :]
                K.mm(pS[:, 0:256], l0, qaS[:, :, 0, qs], True, False, ltoks + [qaS], [pS])
                K.mm(pS[:, 0:256], l1, qaS[:, :, 1, qs], False, False, ltoks + [qaS], [pS])
                K.mm(pS[:, 0:256], l2, qpadS[:, :, qs], False, kt < 16, ltoks + [qpadS], [pS])
                if kt == 16:
                    K.mm(pS[:, 0:256], ident_bf[:], maskK[:, b, :], False, True, [ident_bf, maskK], [pS])
                P_ = PTs[pti % 3]
                pti += 1
                K.actv(P_[:], pS[:, 0:256], AF.Exp, [pS], [P_], scale=MLA_SCALE)
                K.mm(psOa[:, 0:256], vt[:, 0:128], P_[:], kt == 0, kt == 16, [vtok, P_], [psOa])
                K.mm(psOb[:, 0:256], vt[:, 128:256], P_[:], kt == 0, kt == 16, [vtok, P_], [psOb])
                K.mm(psDn[:, 0:256], ones_bf[:], P_[:], kt == 0, kt == 16, [ones_bf, P_], [psDn])
            K.op("dve", lambda e: e.reciprocal(rdenS[:], psDn[:, 0:256]), [psDn], [rdenS])
            K.tt(olS[:, 0, :], psOa[:, 0:256], rdenS[:], ALU.mult, [psOa, rdenS], [olS])
            K.tt(olS[:, 1, :], psOb[:, 0:256], rdenS[:], ALU.mult, [psOb, rdenS], [olS])
            for hp in range(4):
                pm = gen.next()
                for hh in range(2):
                    h = 2 * hp + hh
                    for cc in range(2):
                        K.mm(pm[hh * 64:hh * 64 + 64, 0:32], w_uv[:, cc, h * 64:(h + 1) * 64], olS[:, cc, h * 32:(h + 1) * 32],
                             cc == 0, cc == 1, [w_uv, olS], [pm])
                K.cp(B["mixT"][:, 4 + hp, qs], pm[:, 0:32], [pm], [B["mixT"]], eng="act")
        out_proj(B, 128, SEQ)


def odd_consts(SEQ):
    pos = token_positions(SEQ)
    NTOK = SEQ + 128
    C, S = rope_tables(pos, 32, 32)
    ropeQC, ropeQS = np.tile(C, (4, 1)), np.tile(S, (4, 1))
    inv = ROPE_THETA ** (-np.arange(0, 32, 2, dtype=np.float32) / 32)
    ang = pos.astype(np.float32)[:, None] * inv.astype(np.float32)[None, :]
    cos, sin = np.cos(ang).astype(np.float32), np.sin(ang).astype(np.float32)
    ropeK = np.concatenate([cos, cos, -sin, sin], axis=1).astype(np.float32)
    rcnt = np.stack([1.0 / np.minimum(pos + 1, w).astype(np.float32) for w in (2, 4, 8, 16)]).astype(np.float32)
    maskcol = (np.arange(128)[:, None] // 32 == np.arange(4)[None, :]).astype(np.float32)
    k = np.arange(128)[:, None]
    q = np.arange(512)[None, :]
    mask4 = np.zeros((128, 4, 512), np.float32)
    for d in range(4):
        kchunk = 2 * d + (k >= 64)
        qchunk = 2 * (q // 128) + ((q % 128) >= 64)
        mask4[:, d, :] = np.where(kchunk <= qchunk, 0.0, MASKV)
    maskK = np.zeros((128, NSB, 256), np.float32)
    for b in range(NSB):
        maskK[:, b, :] = np.where((np.arange(128) // 32 == b)[:, None], 0.0, MASKV)
    return {"ident": np.eye(128, dtype=np.float32), "ropeQC": ropeQC, "ropeQS": ropeQS, "ropeK": ropeK, "rcnt": rcnt,
            "maskcol": maskcol, "mask4": mask4, "maskK": maskK}


def odd_weights(w_in, w_out, pool_w, pool_scale, q_norm, kv_norm, w_uq, w_uk, w_uv, gmix):
    uq = w_uq.reshape(512, 8, 96)
    w_uqN = np.ascontiguousarray(uq[:, :, :64].reshape(512, 512))
    w_uqR = np.ascontiguousarray(uq[:, :, 64:].reshape(512, 256))
    return {"w_in": w_in, "gmix": gmix.reshape(1, 1024), "w_out": w_out, "pool_w": pool_w, "pscale": pool_scale.reshape(4, 128),
            "qn": q_norm.reshape(1, 512), "kvn": kv_norm.reshape(1, 256), "w_uqN": w_uqN, "w_uqR": w_uqR,
            "w_uqRP": perm_rope_cols(w_uqR, 32, 32), "w_uk": np.ascontiguousarray(w_uk.reshape(256, 512)),
            "w_uv": np.ascontiguousarray(w_uv.reshape(256, 512))}


SEQ_FULL = 4096
N_CORES = 8


def build(SEQ, upto=5):
    cfg = {"SEQ": SEQ}
    NTOK = SEQ + 128
    K = KB()
    consts = {}
    for pre, d in (("s_", swa_consts(SEQ)), ("f_", s5_consts()), ("o_", odd_consts(SEQ))):
        for k, v in d.items():
            consts[pre + k] = v
    consts["iota16"] = np.tile(np.arange(16, dtype=np.float32), (128, 1))
    C = {k: K.const(k, v) for k, v in consts.items()}
    I = lambda n, shp: K.inp(n, shp, F32)
    O = lambda n, shp: K.outp(n, shp, F32)
    x_all = I("x_all", [NTOK, 1024])
    w_in0, w_in0P, gmix0, sink = I("w_in0", [1024, 1280]), I("w_in0P", [1024, 640]), I("gmix0", [1, 1024]), I("sink", [1, 8])
    ck, cv = I("ck", [NSB, 128, 128]), I("cv", [NSB, 128, 128])
    attT = K.outp("attT_scr", [512, NTOK], BF16)
    uT = K.outp("uT_scr", [512, NTOK], BF16)
    y = O("y", [NTOK, 1024])
    x1 = x2 = x3 = x4 = y
    outs0 = {"k_p": O("k_p", [128, 128]), "v_p": O("v_p", [128, 128]), "k_s": O("k_s", [NSB, 128, 128]), "v_s": O("v_s", [NSB, 128, 128])}
    phase_swa(K, cfg, x_all, w_in0, w_in0P, gmix0, sink, ck, cv, C["s_ropeC"], C["s_ropeS"], C["s_maskA"], C["s_maskB"], C["s_maskS"],
              C["s_ident"], attT, uT, outs0)
    P = {"lam_re": I("lam_re", [16, 128]), "lam_im": I("lam_im", [16, 128]), "log_dt": I("log_dt", [16, 2]),
         "b_re": I("b_re", [2048, 16]), "b_im": I("b_im", [2048, 16]), "c_re": I("c_re", [512, 64]), "c_im": I("c_im", [512, 64]),
         "dsk": I("dsk", [4, 128]), "w_glu": I("w_glu", [512, 512]), "b_glu": I("b_glu", [4, 128]), "w_out": I("w_out0", [1024, 1024]),
         "h0_re": I("h0_re", [NSB, 16, 128]), "h0_im": I("h0_im", [NSB, 16, 128])}
    outs1 = {"hp_re": O("hp_re", [16, 128]), "hp_im": O("hp_im", [16, 128]), "hs_re": O("hs_re", [NSB, 16, 128]), "hs_im": O("hs_im", [NSB, 16, 128])}
    if upto >= 2:
        phase_s5(K, cfg, x_all, uT, attT, x1, P, C["f_ident"], C["f_iotaL"], outs1)
    peer_in = []
    for l in range(2):
        peer_in.append({"wq": I(f"pwq{l}", [1024, 1024]), "keys": I(f"pkeys{l}", [8, 2, 128, 64]), "u": I(f"pu{l}", [N_EXPERTS, 1024]),
                        "v": I(f"pv{l}", [N_EXPERTS, 1024]), "g": I(f"gffn{l}", [1, 1024])})
    gfin = I("gfin", [1, 1024])
    pi = peer_in[0]
    tabs = [K.scratch(f"peer_tab{l}", [N_EXPERTS, 2048], BF16) for l in range(2)]
    if upto >= 3:
      phase_convert(K, [(peer_in[l]["u"], peer_in[l]["v"], tabs[l]) for l in range(2)])
      phase_peer(K, x1, x2, NTOK // 128, pi["wq"], pi["keys"], tabs[0], pi["g"], C["s_ident"], C["s_ident"], C["iota16"])
    wshapes = {"w_in": [1024, 1312], "gmix": [1, 1024], "w_out": [1024, 1024], "pool_w": [4, 128, 128], "pscale": [4, 128], "qn": [1, 512],
               "kvn": [1, 256], "w_uqN": [512, 512], "w_uqR": [512, 256], "w_uqRP": [512, 256], "w_uk": [256, 512], "w_uv": [256, 512]}
    Wd = {k: I("W1_" + k, shp) for k, shp in wshapes.items()}
    Cd = {k[2:]: v for k, v in C.items() if k.startswith("o_")}
    caches = {"pool": I("c_pool", [NSB, 15, 512]), "ckv": I("c_ckv", [NSB, PAST_LEN, 256]), "kpe": I("c_kpe", [NSB, PAST_LEN, 32])}
    outs3 = {"pool_p": O("pool_p", [15, 512]), "ckv_p": O("ckv_p", [SEQ, 256]), "kpe_p": O("kpe_p", [SEQ, 32]),
             "pool_s": O("pool_s", [NSB, 15, 512]), "ckv_s": O("ckv_s", [NSB, 32, 256]), "kpe_s": O("kpe_s", [NSB, 32, 32])}
    if upto >= 4:
        phase_odd(K, cfg, x2, x3, Wd, Cd, caches, outs3)
    pi = peer_in[1]
    if upto >= 5:
      phase_peer(K, x3, x4, NTOK // 128, pi["wq"], pi["keys"], tabs[1], pi["g"], C["s_ident"], C["s_ident"], C["iota16"],
               final=(gfin, [(0, NTOK, y)]))
    nc = K.emit()
    return nc, K


def make_in_maps(inp, SEQ, n_cores, K):
    f = lambda a: np.ascontiguousarray(np.asarray(a, dtype=np.float32))
    g0 = lambda k: f(inp[k])[0]
    w_in0 = g0("w_in_even")
    shared = {
        "w_in0": w_in0, "w_in0P": perm_rope_cols(w_in0[:, :640], 64, 16), "gmix0": f(inp["norm_mix"])[0:1], "sink": g0("swa_sink").reshape(1, 8),
        "lam_re": g0("s5_lam_re").reshape(16, 128), "lam_im": g0("s5_lam_im").reshape(16, 128), "log_dt": g0("s5_log_dt").reshape(16, 2),
        "b_re": g0("s5_b_re").reshape(2048, 16), "b_im": g0("s5_b_im").reshape(2048, 16),
        "c_re": g0("s5_c_re").reshape(512, 64), "c_im": g0("s5_c_im").reshape(512, 64), "dsk": g0("s5_d").reshape(4, 128),
        "w_glu": g0("s5_w_glu"), "b_glu": g0("s5_b_glu").reshape(4, 128), "w_out0": g0("w_out_even"), "gfin": f(inp["norm_final"]).reshape(1, 1024),
    }
    for l in range(2):
        shared[f"pwq{l}"] = f(inp["peer_w_q"])[l]
        shared[f"pkeys{l}"] = f(inp["peer_keys"])[l]
        shared[f"pu{l}"] = f(inp["peer_u"])[l]
        shared[f"pv{l}"] = f(inp["peer_v"])[l]
        shared[f"gffn{l}"] = f(inp["norm_ffn"])[l:l + 1]
    W1 = odd_weights(g0("w_in_odd"), g0("w_out_odd"), g0("pool_w"), g0("pool_scale"), g0("mla_q_norm"), g0("mla_kv_norm"), g0("mla_w_uq"),
                     g0("mla_w_uk"), g0("mla_w_uv"), f(inp["norm_mix"])[1])
    for k, v in W1.items():
        shared["W1_" + k] = np.ascontiguousarray(v)
    shared.update(K.consts)
    xp, xs = f(inp["x_prompt"]), f(inp["x_sample"])
    maps = []
    for c in range(n_cores):
        sb = slice(NSB * c, NSB * (c + 1))
        m = dict(shared)
        m["x_all"] = np.ascontiguousarray(np.concatenate([xp[c, :SEQ], xs[sb].reshape(NSB * DEC_SEQ, 1024)], 0))
        m["ck"] = np.ascontiguousarray(g0("cache_swa_k")[sb].reshape(NSB, 128, 128))
        m["cv"] = np.ascontiguousarray(g0("cache_swa_v")[sb].reshape(NSB, 128, 128))
        m["h0_re"] = np.ascontiguousarray(g0("state_ssm_re")[sb].reshape(NSB, 16, 128))
        m["h0_im"] = np.ascontiguousarray(g0("state_ssm_im")[sb].reshape(NSB, 16, 128))
        m["c_pool"] = np.ascontiguousarray(g0("state_pool")[sb])
        m["c_ckv"] = np.ascontiguousarray(g0("cache_mla_ckv")[sb])
        m["c_kpe"] = np.ascontiguousarray(g0("cache_mla_kpe")[sb])
        maps.append(m)
    return maps


def assemble(results, SEQ, n_cores):
    R = [{k: np.asarray(v) for k, v in r.items()} for r in results]
    st = lambda fn: np.stack([fn(r) for r in R])
    cat = lambda fn: np.concatenate([fn(r) for r in R], 0)
    y_p = st(lambda r: r["y"][:SEQ])
    y_s = cat(lambda r: r["y"][SEQ:].reshape(NSB, DEC_SEQ, 1024))
    out = (y_p, y_s,
           st(lambda r: r["k_p"].reshape(128, 2, 64))[None], st(lambda r: r["v_p"].reshape(128, 2, 64))[None],
           st(lambda r: r["hp_re"].reshape(32, 64))[None], st(lambda r: r["hp_im"].reshape(32, 64))[None],
           st(lambda r: r["pool_p"])[None], st(lambda r: r["ckv_p"])[None], st(lambda r: r["kpe_p"])[None],
           cat(lambda r: r["k_s"].reshape(NSB, 128, 2, 64))[None], cat(lambda r: r["v_s"].reshape(NSB, 128, 2, 64))[None],
           cat(lambda r: r["hs_re"].reshape(NSB, 32, 64))[None], cat(lambda r: r["hs_im"].reshape(NSB, 32, 64))[None],
           cat(lambda r: r["pool_s"])[None], cat(lambda r: r["ckv_s"])[None], cat(lambda r: r["kpe_s"])[None])
    return tuple(np.ascontiguousarray(o.astype(np.float32)) for o in out)


def kernel(_upto=5, **inputs):
    SEQ = int(np.asarray(inputs["x_prompt"]).shape[1])
    n_cores = int(np.asarray(inputs["x_prompt"]).shape[0])
    nc, K = build(SEQ, _upto)
    print("n sems", len(K.dsem) + 4, {e: len(K.q[e]) for e in ENGS}, flush=True)
    maps = make_in_maps(inputs, SEQ, n_cores, K)
    res = run_bass_kernel_spmd(nc, maps, core_ids=list(range(n_cores)))
    return assemble(res.results, SEQ, n_cores)
```

```python
import numpy as np
from contextlib import ExitStack, contextmanager
import concourse.bass as bass
import concourse.mybir as mybir
from concourse.bass_utils import run_bass_kernel_spmd

F32 = mybir.dt.float32
BF16 = mybir.dt.bfloat16
I32 = mybir.dt.int32
U32 = mybir.dt.uint32
AF = mybir.ActivationFunctionType
ALU = mybir.AluOpType
AX = mybir.AxisListType

DEBUG_ISA = False
DEBUG_SEM = None
ENGS = ("pe", "act", "dve", "pool", "sp")
CENG = ("pe", "act", "dve", "pool")

D_MODEL = 1024
RMS_EPS = 1e-6
N_EXPERTS = 16384


class KB:
    def __init__(self):
        self.nc = bass.Bass("TRN2", target_bir_lowering=False)
        self.es = ExitStack()
        self.q = {e: [] for e in ENGS}
        self.csem = {e: self.es.enter_context(self.nc.semaphore("c_" + e)) for e in CENG}
        self.ccnt = {e: 0 for e in CENG}
        self.dsem = {}
        self.dcnt = {}
        self.waited = {e: {} for e in ENGS}
        self.lastw = {}
        self.readers = {}
        self.consts = {}
        self.dram_in = {}
        self.dram_out = {}
        self.stack = [self.es]
        self.uid = 0
        self.bufsem = {}
        self.dfree = {"sw": [], "hw": []}
        self.psum_ids = set()

    @staticmethod
    def _where():
        import sys
        f = sys._getframe(2)
        out = []
        while f is not None and len(out) < 4:
            if f.f_code.co_name not in ("op", "dma", "mm", "tr", "actv", "tt", "ts", "stt", "cp", "memset", "load"):
                out.append(f.f_lineno)
            f = f.f_back
        return out

    def sb(self, name, shape, dtype):
        self.uid += 1
        return self.stack[-1].enter_context(self.nc.sbuf_tensor(f"{name}_{self.uid}", list(shape), dtype))

    def ps(self, name, shape, dtype):
        self.uid += 1
        t = self.stack[-1].enter_context(self.nc.psum_tensor(f"{name}_{self.uid}", list(shape), dtype))
        self.psum_ids.add(id(t))
        return t

    def inp(self, name, shape, dtype):
        t = self.nc.dram_tensor(name, list(shape), dtype, kind="ExternalInput")
        self.dram_in[name] = t
        return t.ap()

    def outp(self, name, shape, dtype):
        t = self.nc.dram_tensor(name, list(shape), dtype, kind="ExternalOutput")
        self.dram_out[name] = t
        return t.ap()

    def scratch(self, name, shape, dtype):
        t = self.nc.dram_tensor(name, list(shape), dtype, kind="Internal")
        return t.ap()

    def const(self, name, arr):
        arr = np.ascontiguousarray(arr)
        dt = {np.dtype(np.float32): F32, np.dtype(np.int32): I32}[arr.dtype]
        self.consts[name] = arr
        return self.inp(name, arr.shape, dt)

    @contextmanager
    def phase(self):
        st = ExitStack()
        self.stack.append(st)
        yield
        self.stack.pop()
        self.barrier()
        st.close()

    def _sem(self, key):
        return self.csem[key] if key in self.csem else self.dsem[key]

    @staticmethod
    def _k(b):
        return b if isinstance(b, (str, tuple)) else id(b)

    def _toks(self, bs):
        return [self._k(b) for b in bs if not (isinstance(b, tuple) and b and b[0] == "dram")]

    def _dsem_for(self, buf, queue):
        qt = "sw" if queue == "pool" else "hw"
        k = (buf, qt)
        if k not in self.bufsem:
            if self.dfree[qt]:
                sk = self.dfree[qt].pop()
            else:
                sk = "d%s%d" % (qt, len(self.dsem))
                self.dsem[sk] = self.es.enter_context(self.nc.semaphore(sk))
                self.dcnt[sk] = 0
            self.bufsem[k] = sk
        return self.bufsem[k]

    def _deps(self, eng, reads, writes, own=None):
        waits = {}

        def need(k, v, waw=False):
            if (waw and k == own) or (eng == "pe" and k == "pe"):
                return
            if k in self.dcnt:
                v = self.dcnt[k]
            if self.waited[eng].get(k, 0) >= v:
                return
            if waits.get(k, 0) < v:
                waits[k] = v

        for b in reads:
            for k, v in self.lastw.get(b, {}).items():
                need(k, v)
            if b in self.psum_ids:
                for k, v in self.readers.get(b, {}).items():
                    if k != eng:
                        need(k, v)
        for b in writes:
            for k, v in self.lastw.get(b, {}).items():
                need(k, v, True)
            for k, v in self.readers.get(b, {}).items():
                need(k, v)
        for k, v in waits.items():
            self.waited[eng][k] = v
        return list(waits.items())

    def _commit(self, tok, reads, writes):
        for b in reads:
            d = self.readers.setdefault(b, {})
            if d.get(tok[0], 0) < tok[1]:
                d[tok[0]] = tok[1]
        for b in writes:
            self.lastw[b] = {tok[0]: tok[1]}
            self.readers[b] = {}

    def op(self, eng, fn, reads=(), writes=()):
        reads, writes = self._toks(reads), self._toks(writes)
        waits = self._deps(eng, reads, writes)
        self.ccnt[eng] += 1
        tok = (eng, self.ccnt[eng])
        self._commit(tok, reads, writes)
        self.q[eng].append((waits, fn, (eng, 1), self._where()))

    def dma(self, queue, fn, sem, reads=(), writes=()):
        reads, writes = self._toks(reads), self._toks(writes)
        buf = writes[0] if writes else reads[0]
        sk = self._dsem_for(buf, queue)
        waits = self._deps(queue, reads, writes, own=sk)
        self.dcnt[sk] += 16
        tok = (sk, self.dcnt[sk])
        self._commit(tok, reads, writes)
        self.q[queue].append((waits, fn, (sk, 16), self._where()))

    def barrier(self):
        allv = dict(self.ccnt)
        allv.update(self.dcnt)
        for e in ENGS:
            waits = []
            for k, v in allv.items():
                if v > 0 and self.waited[e].get(k, 0) < v:
                    waits.append((k, v))
                    self.waited[e][k] = v
            if waits:
                self.q[e].append((waits, None, None, 0))
        self.lastw = {}
        self.readers = {}
        for (bk, qt), sk in self.bufsem.items():
            self.dfree[qt].append(sk)
        self.bufsem = {}

    def emit(self):
        self.barrier()
        nc = self.nc
        with nc.Block() as block:
            def mk(name):
                def body(eng):
                    for waits, fn, inc, _w in self.q[name]:
                        for k, v in waits:
                            if DEBUG_SEM and k == DEBUG_SEM:
                                print("SEMDBG-WAIT:", name, k, v)
                            eng.wait_ge(self._sem(k), v)
                        if fn is not None:
                            ins = fn(eng)
                            if DEBUG_ISA and isinstance(ins.ins, mybir.InstISA):
                                print("InstISA:", name, type(ins.ins).__name__, str(ins)[:300])
                            if DEBUG_SEM and inc[0] == DEBUG_SEM:
                                print("SEMDBG:", name, [w for w in waits], str(ins)[:260])
                            ins.then_inc(self._sem(inc[0]), inc[1])
                return body
            block.tensor(mk("pe"))
            block.scalar(mk("act"))
            block.vector(mk("dve"))
            block.gpsimd(mk("pool"))
            block.sync(mk("sp"))
        self.es.close()
        return nc

    def mm(self, out, lhsT, rhs, start, stop, reads, writes):
        self.op("pe", lambda e: e.matmul(out, lhsT, rhs, start=start, stop=stop), reads, writes)

    def tr(self, out, in_, ident, reads, writes):
        self.op("pe", lambda e: e.transpose(out, in_, ident), reads, writes)

    def actv(self, out, in_, func, reads, writes, bias=None, scale=None, accum_out=None, eng="act"):
        kw = {}
        if bias is not None:
            kw["bias"] = bias
        if scale is not None:
            kw["scale"] = scale
        if accum_out is not None:
            kw["accum_out"] = accum_out
        self.op(eng, lambda e: e.activation(out, in_, func, **kw), reads, writes)

    def tt(self, out, in0, in1, op, reads, writes, eng="dve"):
        self.op(eng, lambda e: e.tensor_tensor(out, in0, in1, op), reads, writes)

    def ts(self, out, in0, s1, s2, op0, op1, reads, writes, eng="dve"):
        if op1 is None:
            self.op(eng, lambda e: e.tensor_scalar(out, in0, s1, None, op0), reads, writes)
        else:
            self.op(eng, lambda e: e.tensor_scalar(out, in0, s1, s2, op0, op1), reads, writes)

    def stt(self, out, in0, scalar, in1, op0, op1, reads, writes):
        self.op("dve", lambda e: e.scalar_tensor_tensor(out, in0, scalar, in1, op0, op1), reads, writes)

    def cp(self, out, in_, reads, writes, eng="dve"):
        if eng == "act":
            self.op("act", lambda e: e.activation(out, in_, AF.Copy), reads, writes)
        else:
            self.op(eng, lambda e: e.tensor_copy(out, in_), reads, writes)

    def memset(self, ap, val, writes, eng="dve"):
        self.op(eng, lambda e: e.memset(ap, val), (), writes)

    def load(self, out, in_, sem, reads, writes, queue="sp", **kw):
        self.dma(queue, lambda e: e.dma_start(out=out, in_=in_, **kw), sem, reads, writes)


def bcast_rows(ap_row, p=128):
    return bass.AP(ap_row.tensor, ap_row.offset, [[0, p]] + [list(x) for x in ap_row.ap[-1:]])


def phase_convert(K, pairs):
    with K.phase():
        NSL = 4
        ub = [K.sb(f"cvu{i}", [128, 4, 1024], BF16) for i in range(NSL)]
        vb = [K.sb(f"cvv{i}", [128, 4, 1024], BF16) for i in range(NSL)]
        i = 0
        for (u_d, v_d, tab) in pairs:
            for ch in range(N_EXPERTS // 512):
                s_ = i % NSL
                i += 1
                rows = slice(ch * 512, (ch + 1) * 512)
                K.dma("pool", lambda e, s_=s_, rows=rows, u_d=u_d: e.dma_start(
                    out=ub[s_][:], in_=u_d[rows, :].rearrange("(p j) d -> p j d", j=4), max_dma_last_dim=4096), None, (), [ub[s_]])
                K.dma("pool", lambda e, s_=s_, rows=rows, v_d=v_d: e.dma_start(
                    out=vb[s_][:], in_=v_d[rows, :].rearrange("(p j) d -> p j d", j=4), max_dma_last_dim=4096), None, (), [vb[s_]])
                K.load(tab[rows, 0:1024].rearrange("(p j) d -> p j d", j=4), ub[s_][:], None, [ub[s_]], ())
                K.load(tab[rows, 1024:2048].rearrange("(p j) d -> p j d", j=4), vb[s_][:], None, [vb[s_]], ())


def phase_peer(K, x_in, x_out, ntiles, wq_d, keys_d, tab_d, gffn_d, ident_bf_d, ident_f_d, iota16_d,
               final=None):
    nc = K.nc
    with K.phase():
        ident_bf = K.sb("identbf", [128, 128], BF16)
        ident_f = K.sb("identf", [128, 128], F32)
        iota16 = K.sb("iota16", [128, 16], F32)
        gffn = K.sb("gffn", [128, 1024], F32)
        wq = K.sb("wq", [128, 8, 1024], BF16)
        keysBD = K.sb("keysBD", [128, 8, 256], BF16)
        K.load(ident_f[:], ident_f_d, "c0", (), [ident_f])
        K.load(ident_bf[:], ident_f_d, "c0", (), [ident_bf], queue="pool")
        K.load(iota16[:], iota16_d, "c0", (), [iota16])
        K.load(gffn[:], bcast_rows(gffn_d), "c0", (), [gffn])
        for c in range(8):
            K.load(wq[:, c, :], wq_d[c * 128:(c + 1) * 128, :], "c1", (), [wq], queue="pool")
        if final is not None:
            gfin = K.sb("gfin", [128, 1024], F32)
            K.load(gfin[:], bcast_rows(final[0]), "c0", (), [gfin])

        psB = K.ps("psB", [128, 2048], F32)
        psO = K.ps("psO", [128, 1024], F32)
        psA = K.ps("psA", [128, 512], F32)
        psA_bf = psA[:].bitcast(BF16)

        def two(name, shape, dt):
            return [K.sb(name + str(i), shape, dt) for i in range(2)]
        xt = two("xt", [128, 1024], F32)
        hn = two("hn", [128, 1024], BF16)
        hnT = two("hnT", [128, 8, 128], BF16)
        qT = two("qT", [128, 8, 128], BF16)
        eid = two("eid", [128, 128], I32)
        gate = two("gate", [128, 128], F32)
        act = two("actv", [128, 128], F32)
        wgt = two("wgt", [128, 128], F32)
        wg2 = two("wg2", [128, 128], F32)
        ss = two("ss", [128, 4], F32)
        junk = K.sb("junk", [128, 1024], BF16)
        junk2 = K.sb("junk2", [128, 1024], BF16)
        s2 = K.sb("s2", [128, 128], F32)
        sv = K.sb("sv", [128, 16, 16], F32)
        si = K.sb("si", [128, 16, 16], U32)
        sif = K.sb("sif", [128, 16, 16], BF16)
        cand = K.sb("cand", [128, 8, 256], F32)
        cand2 = K.sb("cand2", [128, 256], F32)
        cv = K.sb("cv", [128, 8, 16], F32)
        ci = K.sb("ci", [128, 8, 16], U32)
        ca = K.sb("ca", [128, 8, 16], U32)
        cb = K.sb("cb", [128, 8, 16], U32)
        oh = K.sb("oh", [128, 8, 16, 16], BF16)
        oh2 = K.sb("oh2", [128, 8, 16, 16], BF16)
        i1f = K.sb("i1f", [128, 8, 16], F32)
        i2f = K.sb("i2f", [128, 8, 16], F32)
        ce = K.sb("ce", [128, 8, 16], F32)
        csum = K.sb("csum", [128, 8], F32)
        NS = 8
        UV = [K.sb(f"UV{i}", [128, 2048], BF16) for i in range(NS)]
        NDG = 4
        dg = [K.sb(f"dg{i}", [128, 128], BF16) for i in range(NDG)]
        xo = two("xo", [128, 1024], F32)
        if final is not None:
            yo = [K.sb("yo", [128, 1024], F32)] * 2

        def prep(i):
            p = i % 2
            X, HN, HT, QT = xt[p], hn[p], hnT[p], qT[p]
            K.load(X[:], x_in[i * 128:(i + 1) * 128, :], f"xl{p}", (), [X])
            K.actv(junk2[:], X[:], AF.Square, [X], [ss[p], junk2], accum_out=ss[p][:, 0:1])
            K.actv(ss[p][:, 1:2], ss[p][:, 0:1], AF.Sqrt, [ss[p], eps_t], [ss[p]], scale=1.0 / D_MODEL, bias=eps_t[:, 0:1])
            K.op("dve", lambda e: e.reciprocal(ss[p][:, 2:3], ss[p][:, 1:2]), [ss[p]], [ss[p]])
            K.stt(HN[:], X[:], ss[p][:, 2:3], gffn[:], ALU.mult, ALU.mult, [X, ss[p], gffn], [HN])
            yield
            for c in range(8):
                K.tr(psA_bf[:, c * 128:(c + 1) * 128], HN[:, c * 128:(c + 1) * 128], ident_bf[:],
                     [HN, ident_bf], [psA])
            K.cp(HT[:].rearrange("p c t -> p (c t)"), psA_bf[:, :], [psA], [HT], eng="act")
            for co in range(8):
                for kc in range(8):
                    K.mm(psB[:, co * 128:(co + 1) * 128], wq[:, kc, co * 128:(co + 1) * 128], HT[:, kc, :],
                         kc == 0, kc == 7, [wq, HT], [psB])
            K.cp(QT[:].rearrange("p c t -> p (c t)"), psB[:, 0:1024], [psB], [QT], eng="act")
            for c in range(8):
                K.mm(psB[:, c * 256:(c + 1) * 256], QT[:, c, :], keysBD[:, c, :], True, True, [QT, keysBD], [psB])
            yield
            for gq in range(4):
                for g_ in range(4):
                    g = gq * 4 + g_
                    S = psB[:, g * 128:(g + 1) * 128]
                    K.op("dve", lambda e, S=S, g=g: e.max(sv[:, g, 0:8], S), [psB], [sv])
                    K.op("dve", lambda e, S=S, g=g: e.match_replace(s2[:], sv[:, g, 0:8], S, -1e30), [psB, sv], [s2])
                    K.op("dve", lambda e, g=g: e.max(sv[:, g, 8:16], s2[:]), [s2], [sv])
                    K.op("dve", lambda e, S=S, g=g: e.max_index(si[:, g, 0:8], sv[:, g, 0:8], S), [psB, sv], [si])
                    K.op("dve", lambda e, S=S, g=g: e.max_index(si[:, g, 8:16], sv[:, g, 8:16], S), [psB, sv], [si])
                yield
            sv4 = sv[:].rearrange("n (h p) k -> n h p k", p=2)
            K.tt(cand[:].rearrange("n h (a b) -> n h a b", b=16),
                 sv4[:, :, 0, :].unsqueeze(3).to_broadcast([128, 8, 16, 16]),
                 sv4[:, :, 1, :].unsqueeze(2).to_broadcast([128, 8, 16, 16]), ALU.add, [sv], [cand])
            K.cp(sif[:], si[:], [si], [sif])
            for h in range(8):
                C = cand[:, h, :]
                K.op("dve", lambda e, C=C, h=h: e.max(cv[:, h, 0:8], C), [cand], [cv])
                K.op("dve", lambda e, C=C, h=h: e.match_replace(cand2[:], cv[:, h, 0:8], C, -1e30), [cand, cv], [cand2])
                K.op("dve", lambda e, h=h: e.max(cv[:, h, 8:16], cand2[:]), [cand2], [cv])
                K.op("dve", lambda e, C=C, h=h: e.max_index(ci[:, h, 0:8], cv[:, h, 0:8], C), [cand, cv], [ci])
                K.op("dve", lambda e, C=C, h=h: e.max_index(ci[:, h, 8:16], cv[:, h, 8:16], C), [cand, cv], [ci])
                if h % 4 == 3:
                    yield
            K.ts(ca[:], ci[:], bitc[:, 0:1], None, ALU.logical_shift_right, None, [ci, bitc], [ca])
            K.ts(cb[:], ci[:], bitc[:, 1:2], None, ALU.bitwise_and, None, [ci, bitc], [cb])
            sif4 = sif[:].rearrange("n (h p) k -> n h p k", p=2)
            io = iota16[:].unsqueeze(1).unsqueeze(1).to_broadcast([128, 8, 16, 16])
            for (cx, pp, dst) in ((ca, 0, i1f), (cb, 1, i2f)):
                K.tt(oh[:], cx[:].unsqueeze(3).to_broadcast([128, 8, 16, 16]), io, ALU.is_equal, [cx, iota16], [oh])
                K.tt(oh2[:], oh[:], sif4[:, :, pp, :].unsqueeze(2).to_broadcast([128, 8, 16, 16]), ALU.mult,
                     [oh, sif], [oh2])
                K.op("dve", lambda e, dst=dst: e.tensor_reduce(dst[:], oh2[:], AX.X, ALU.add), [oh2], [dst])
            K.stt(eid[p][:].rearrange("n (h k) -> n h k", k=16), i1f[:], 128.0, i2f[:], ALU.mult, ALU.add,
                  [i1f, i2f], [eid[p]])
            yield
            K.tt(ce[:], cv[:], cv[:, :, 0:1].to_broadcast([128, 8, 16]), ALU.subtract, [cv], [ce])
            K.actv(ce[:], ce[:], AF.Exp, [ce], [ce])
            K.op("dve", lambda e: e.tensor_reduce(csum[:], ce[:], AX.X, ALU.add), [ce], [csum])
            K.op("dve", lambda e: e.reciprocal(csum[:], csum[:]), [csum], [csum])
            K.tt(gate[p][:].rearrange("n (h k) -> n h k", k=16), ce[:],
                 csum[:].unsqueeze(2).to_broadcast([128, 8, 16]), ALU.mult, [ce, csum], [gate[p]])
            yield

        eps_t = K.sb("eps_t", [128, 1], F32)
        K.memset(eps_t[:], RMS_EPS, [eps_t])
        bitc = K.sb("bitc", [128, 2], U32)
        K.memset(bitc[:, 0:1], 4, [bitc])
        K.memset(bitc[:, 1:2], 15, [bitc])

        slot_u = [0]
        slot_of = {}
        dgc = [0]

        def gatherU(p, r):
            s = slot_u[0] % NS
            slot_u[0] += 1
            slot_of[(p, r)] = s
            K.dma("pool", lambda e, s=s, r=r: e.indirect_dma_start(
                out=UV[s][:, :], out_offset=None, in_=tab_d,
                in_offset=bass.IndirectOffsetOnAxis(ap=eid[p][:, r:r + 1], axis=0)),
                None, [eid[p]], [UV[s]])
            K.op("dve", lambda e, s=s, r=r: e.scalar_tensor_tensor(
                junk[:], UV[s][:, 0:1024], 1.0, hn[p][:], ALU.mult, ALU.mult, accum_out=act[p][:, r:r + 1]),
                [UV[s], hn[p]], [("act", p, r), junk])

        def gatherV(p, r):
            s = slot_of[(p, r)]
            d = dgc[0] % NDG
            dgc[0] += 1
            K.actv(wg2[p][:, r:r + 1], wgt[p][:, r:r + 1], AF.Copy, [("wgt", p, r), gate[p]], [("wg2", p, r)],
                   scale=gate[p][:, r:r + 1])
            K.actv(dg[d][:], ident_bf[:], AF.Copy, [ident_bf, ("wg2", p, r)], [dg[d]], scale=wg2[p][:, r:r + 1])
            for hf in range(2):
                K.mm(psO[:, hf * 512:(hf + 1) * 512], dg[d][:], UV[s][:, 1024 + hf * 512:1024 + (hf + 1) * 512],
                     r == 0, r == 127, [dg[d], UV[s]], [psO])

        def run(gen):
            for _ in gen:
                pass

        ksc = K.phase()
        ksc.__enter__()
        knat = K.sb("knat", [128, 16, 64], F32)
        K.load(knat[:], keys_d.rearrange("h p k d -> k (h p) d"), "c0", (), [knat])
        K.memset(keysBD[:], 0.0, [keysBD])
        for c in range(8):
            K.tr(psA[:, 0:128], knat[:, 2 * c:2 * c + 2, :].rearrange("k a d -> k (a d)"), ident_f[:],
                 [knat, ident_f], [psA])
            K.cp(keysBD[0:64, c, 0:128], psA[0:64, 0:128], [psA], [keysBD], eng="act")
            K.cp(keysBD[64:128, c, 128:256], psA[64:128, 0:128], [psA], [keysBD], eng="act")

        ksc.__exit__(None, None, None)
        run(prep(0))
        for i in range(ntiles):
            p = i % 2
            nxt = prep(i + 1) if i + 1 < ntiles else iter(())
            for r in range(128):
                gatherU(p, r)
                K.actv(wgt[p][:, r:r + 1], act[p][:, r:r + 1], AF.Gelu, [("act", p, r)], [("wgt", p, r)])
                if r >= 1:
                    gatherV(p, r - 1)
                if r % 16 == 15:
                    next(nxt, None)
            gatherV(p, 127)
            run(nxt)
            K.tt(xo[p][:], psO[:], xt[p][:], ALU.add, [psO, xt[p]], [xo[p]])
            if final is None:
                K.load(x_out[i * 128:(i + 1) * 128, :], xo[p][:], f"xs{p}", [xo[p]], [("dram", "x_out")])
            if final is not None:
                K.actv(junk2[:], xo[p][:], AF.Square, [xo[p]], [ss[p], junk2], accum_out=ss[p][:, 3:4])
                K.actv(ss[p][:, 3:4], ss[p][:, 3:4], AF.Sqrt, [ss[p], eps_t], [ss[p]], scale=1.0 / D_MODEL, bias=eps_t[:, 0:1])
                K.op("dve", lambda e, p=p: e.reciprocal(ss[p][:, 3:4], ss[p][:, 3:4]), [ss[p]], [ss[p]])
                K.stt(yo[p][:], xo[p][:], ss[p][:, 3:4], gfin[:], ALU.mult, ALU.mult, [xo[p], ss[p], gfin], [yo[p]])
                for (row0, nrows, oap) in final[1]:
                    lo, hi = max(row0, i * 128), min(row0 + nrows, (i + 1) * 128)
                    if lo < hi:
                        K.load(oap[lo - row0:hi - row0, :], yo[p][lo - i * 128:hi - i * 128, :], f"ys{p}",
                               [yo[p]], [("dram", "y")])


class Rot:
    def __init__(self, items):
        self.items = list(items)
        self.i = 0

    def next(self):
        t = self.items[self.i % len(self.items)]
        self.i += 1
        return t


CHUNK = 64
SWA_SCALE = 64 ** -0.5
MLA_SCALE = 96 ** -0.5
ROPE_THETA = 500000.0
PAST_LEN = 2048
DEC_SEQ = 32
NSB = 4
MASKV = -30000.0


def rmsnorm_tile(K, X, HN, gb, ssb, eps_t, junk, xtok, hntok, gtok, jtok, width=D_MODEL):
    K.actv(junk, X, AF.Square, [xtok], [ssb, jtok], accum_out=ssb[:, 0:1])
    K.actv(ssb[:, 1:2], ssb[:, 0:1], AF.Sqrt, [ssb, eps_t], [ssb], scale=1.0 / width, bias=eps_t[:, 0:1])
    K.op("dve", lambda e: e.reciprocal(ssb[:, 2:3], ssb[:, 1:2]), [ssb], [ssb])
    K.stt(HN, X, ssb[:, 2:3], gb, ALU.mult, ALU.mult, [xtok, ssb, gtok], [hntok])


def phase_swa(K, cfg, x_all, w_in_d, w_inP_d, gmix_d, sink_d, ck_d, cv_d, ropeC_d, ropeS_d, maskA_d, maskB_d, maskS_d,
              ident_f_d, attT_d, uT_d, outs):
    SEQ = cfg["SEQ"]
    NTOK = SEQ + 128
    NTP = SEQ // 128
    with K.phase():
        ident_f = K.sb("identf", [128, 128], F32)
        ident_bf = K.sb("identbf", [128, 128], BF16)
        K.load(ident_f[:], ident_f_d, "c0", (), [ident_f])
        K.load(ident_bf[:], ident_f_d, "c1", (), [ident_bf], queue="pool")
        gmix = K.sb("gmix", [128, 1024], F32)
        K.load(gmix[:], bcast_rows(gmix_d), "c0", (), [gmix])
        w_in = K.sb("w_in", [128, 8, 1280], BF16)
        w_inP = K.sb("w_inP", [128, 8, 640], BF16)
        for c in range(8):
            K.load(w_in[:, c, 0:640], w_in_d[c * 128:(c + 1) * 128, 0:640], "c1", (), [w_in], queue="pool")
            K.load(w_in[:, c, 640:1280], w_in_d[c * 128:(c + 1) * 128, 640:1280], "c1", (), [w_in], queue="pool")
            K.load(w_inP[:, c, :], w_inP_d[c * 128:(c + 1) * 128, :], "c1", (), [w_inP], queue="pool")
        maskA = K.sb("maskA", [128, 512], BF16)
        maskB = K.sb("maskB", [128, 512], BF16)
        K.load(maskA[:], maskA_d, "c1", (), [maskA], queue="pool")
        K.load(maskB[:], maskB_d, "c1", (), [maskB], queue="pool")
        sinkexp = K.sb("sinkexp", [128, 8], F32)
        K.load(sinkexp[:], bcast_rows(sink_d), "c0", (), [sinkexp])
        K.actv(sinkexp[:], sinkexp[:], AF.Exp, [sinkexp], [sinkexp])
        eps_t = K.sb("eps_t", [128, 1], F32)
        K.memset(eps_t[:], RMS_EPS, [eps_t])

        kT_all = K.sb("kT_all", [64, 2, NTOK], BF16)
        v_all = K.sb("v_all", [128, NTP + 1, 2, 65], BF16)
        K.memset(v_all[:, :, :, 64:65], 1.0, [v_all])
        junk = K.sb("junk", [128, 1024], BF16)
        xt = [K.sb(f"xt{i}", [128, 1024], F32) for i in range(2)]
        ssb = [K.sb(f"ssb{i}", [128, 4], F32) for i in range(2)]
        hn = [K.sb(f"hn{i}", [128, 1024], BF16) for i in range(2)]
        hnT = K.sb("hnT", [128, 8, 512], BF16)
        ropeC = [K.sb(f"ropeC{i}", [64, 512], F32) for i in range(2)]
        ropeS = [K.sb(f"ropeS{i}", [64, 512], F32) for i in range(2)]
        qT = K.sb("qT", [64, 8, 512], BF16)
        t1 = [K.sb(f"t1_{i}", [64, 512], F32) for i in range(2)]
        t2 = [K.sb(f"t2_{i}", [64, 512], F32) for i in range(2)]
        kf32 = K.sb("kf32", [64, 2, 128], F32)
        ktok_p = K.sb("ktok", [128, 128], F32)
        vtok_p = K.sb("vtok", [128, 128], F32)
        ktok_s = K.sb("ktok_s", [128, 128], F32)
        vtok_s = K.sb("vtok_s", [128, 128], F32)
        uT = [K.sb(f"uT{i}", [128, 4, 512], BF16) for i in range(2)]
        attT = [K.sb(f"attT{i}", [128, 4, 512], BF16) for i in range(2)]
        PT = [K.sb(f"PT{i}", [128, 2, 2, 512], BF16) for i in range(2)]
        att_tok = [K.sb(f"att_tok{i}", [128, 512], BF16) for i in range(2)]
        den = K.sb("den", [128, 8], F32)
        kc32 = [K.sb(f"kc32_{i}", [128, 128], F32) for i in range(2)]
        vc32 = [K.sb(f"vc32_{i}", [128, 128], F32) for i in range(2)]
        kcb = [K.sb(f"kcb{i}", [128, 128], BF16) for i in range(2)]
        kcT = K.sb("kcT", [64, NSB, 2, 128], BF16)
        vcb = K.sb("vcb", [128, NSB, 2, 65], BF16)
        K.memset(vcb[:, :, :, 64:65], 1.0, [vcb])
        PTc = K.sb("PTc", [128, NSB, 2, 512], BF16)
        PTn = K.sb("PTn", [128, 2, 512], BF16)
        maskS = K.sb("maskS", [128, NSB + 1, 512], BF16)
        K.load(maskS[:], maskS_d, "c1", (), [maskS], queue="pool")

        psT = K.ps("psT", [128, 512], F32)
        psT_bf = psT[:].bitcast(BF16)
        gen = Rot([K.ps(f"gen{i}", [128, 512], F32) for i in range(4)])
        psOX = K.ps("psOX", [128, 512], F32)
        psOY = K.ps("psOY", [128, 512], F32)
        psAT = K.ps("psAT", [128, 512], F32)
        psAT_bf = psAT[:].bitcast(BF16)

        STOP = cfg.get("stop", 99)
        SSTOP = cfg.get("sstop", 99)
        if STOP <= 0:
            return
        nsup = SEQ // 512
        sups = [(s * 512, 512, False) for s in range(nsup)] + [(SEQ, 128, True)]
        xi = 0
        for si_, (tok0, W, is_s) in enumerate(sups):
            sp = si_ % 2
            ntl = W // 128
            if is_s and STOP <= 5:
                return
            K.load(ropeC[sp][:, 0:W], ropeC_d[:, tok0:tok0 + W], f"rope{sp}", (), [ropeC[sp]])
            K.load(ropeS[sp][:, 0:W], ropeS_d[:, tok0:tok0 + W], f"rope{sp}", (), [ropeS[sp]])
            for tl in range(ntl):
                p = xi % 2
                xi += 1
                K.load(xt[p][:], x_all[tok0 + tl * 128: tok0 + (tl + 1) * 128, :], f"xl{p}", (), [xt[p]])
                rmsnorm_tile(K, xt[p][:], hn[p][:], gmix[:], ssb[p], eps_t, junk[:], xt[p], hn[p], gmix, junk)
                for c in range(8):
                    K.tr(psT_bf[:, c * 128:(c + 1) * 128], hn[p][:, c * 128:(c + 1) * 128], ident_bf[:],
                         [hn[p], ident_bf], [psT])
                K.cp(hnT[:, :, tl * 128:(tl + 1) * 128], psT_bf[:, :].rearrange("p (c t) -> p c t", c=8), [psT], [hnT],
                     eng="act")
            if STOP <= 1 or (is_s and SSTOP <= 1):
                return
            for hh in range(10):
                col0 = hh * 64
                pq, pp = gen.next(), gen.next()
                for kc in range(8):
                    K.mm(pq[0:64, 0:W], w_in[:, kc, col0:col0 + 64], hnT[:, kc, 0:W], kc == 0, kc == 7, [w_in, hnT], [pq])
                for kc in range(8):
                    K.mm(pp[0:64, 0:W], w_inP[:, kc, col0:col0 + 64], hnT[:, kc, 0:W], kc == 0, kc == 7, [w_inP, hnT], [pp])
                a, b = t1[hh % 2], t2[hh % 2]
                K.tt(a[:, 0:W], pq[0:64, 0:W], ropeC[sp][:, 0:W], ALU.mult, [pq, ropeC[sp]], [a])
                K.tt(b[:, 0:W], pp[0:64, 0:W], ropeS[sp][:, 0:W], ALU.mult, [pp, ropeS[sp]], [b])
                if hh < 8:
                    K.tt(qT[:, hh, 0:W], a[:, 0:W], b[:, 0:W], ALU.add, [a, b], [qT])
                else:
                    g = hh - 8
                    K.tt(kT_all[:, g, tok0:tok0 + W], a[:, 0:W], b[:, 0:W], ALU.add, [a, b], [kT_all])
                    if is_s or tok0 + W == SEQ:
                        K.tt(kf32[:, g, :], a[:, W - 128:W], b[:, W - 128:W], ALU.add, [a, b], [kf32])
            if STOP <= 2 or (is_s and SSTOP <= 2):
                return
            for c in range(4):
                pu = gen.next()
                for kc in range(8):
                    K.mm(pu[:, 0:W], w_in[:, kc, 768 + c * 128:768 + (c + 1) * 128], hnT[:, kc, 0:W], kc == 0, kc == 7,
                         [w_in, hnT], [pu])
                K.cp(uT[sp][:, c, 0:W], pu[:, 0:W], [pu], [uT[sp]], eng="act")
            for c in range(4):
                K.load(uT_d[c * 128:(c + 1) * 128, tok0:tok0 + W], uT[sp][:, c, 0:W], f"ust{sp}", [uT[sp]], [("dram", "uT")])
            if STOP <= 3 or (is_s and SSTOP <= 3):
                return
            last_tile = is_s or (tok0 + W == SEQ)
            ktok, vtok = (ktok_s, vtok_s) if is_s else (ktok_p, vtok_p)
            for tl in range(ntl):
                j = tok0 // 128 + tl
                pv = gen.next()
                for kc in range(8):
                    K.mm(pv[:, 0:128], hnT[:, kc, tl * 128:(tl + 1) * 128], w_in[:, kc, 640:768], kc == 0, kc == 7,
                         [w_in, hnT], [pv])
                K.cp(v_all[:, j, :, 0:64], pv[:, 0:128].rearrange("p (g d) -> p g d", g=2), [pv], [v_all], eng="act")
                if last_tile and tl == ntl - 1:
                    K.cp(vtok[:], pv[:, 0:128], [pv], [vtok])
            if last_tile and not (is_s and cfg.get("nok")):
                pk = gen.next()
                for g in range(2):
                    K.tr(pk[:, g * 64:(g + 1) * 64], kf32[:, g, :], ident_f[0:64, 0:64], [kf32, ident_f], [pk])
                K.cp(ktok[:], pk[:, 0:128], [pk], [ktok])
                if not is_s:
                    K.load(outs["k_p"], ktok[:], "ost", [ktok], [("dram", "o")])
                    K.load(outs["v_p"], vtok[:], "ost", [vtok], [("dram", "o")])
                elif not cfg.get("nostore"):
                    for b in range(NSB):
                        K.load(outs["k_s"][b, 96:128, :], ktok[b * 32:(b + 1) * 32, :], "ost", [ktok], [("dram", "o")])
                        K.load(outs["v_s"][b, 96:128, :], vtok[b * 32:(b + 1) * 32, :], "ost", [vtok], [("dram", "o")])
            if STOP <= 4 or (is_s and SSTOP <= 4):
                return
            if not is_s:
                for tl in range(ntl):
                    j = tok0 // 128 + tl
                    ap_ = j % 2
                    jjs = [jj for jj in (j - 1, j) if jj >= 0]
                    for g in range(2):
                        for jj in jjs:
                            pS = gen.next()
                            K.mm(pS[:, :], kT_all[:, g, jj * 128:(jj + 1) * 128], qT[:, 4 * g:4 * g + 4, tl * 128:(tl + 1) * 128],
                                 True, False, [kT_all, qT], [pS])
                            K.mm(pS[:, :], ident_bf[:], (maskA if jj == j - 1 else maskB)[:], False, True,
                                 [ident_bf, maskA, maskB], [pS])
                            K.actv(PT[ap_][:, jj - j + 1, g, :], pS[:, :], AF.Exp, [pS], [PT[ap_]], scale=SWA_SCALE)
                    for h in range(8):
                        g = h // 4
                        po = psOX if h < 4 else psOY
                        for n_, jj in enumerate(jjs):
                            K.mm(po[:, (h % 4) * 65:(h % 4) * 65 + 65],
                                 PT[ap_][:, jj - j + 1, g, (h % 4) * 128:(h % 4) * 128 + 128], v_all[:, jj, g, :],
                                 n_ == 0, n_ == len(jjs) - 1, [PT[ap_], v_all], [po])
                    swa_finish(K, psOX, psOY, den, sinkexp, att_tok[ap_], psAT, psAT_bf, ident_bf, attT[sp], tl, 128)
            else:
                if STOP <= 6:
                    return
                for b in range(NSB):
                    bp = b % 2
                    K.load(kc32[bp][:], ck_d[b], f"kcl{bp}", (), [kc32[bp]])
                    K.load(vc32[bp][:], cv_d[b], f"kcl{bp}", (), [vc32[bp]])
                    K.load(outs["k_s"][b, 0:96, :], kc32[bp][32:128, :], "ost", [kc32[bp]], [("dram", "o")])
                    K.load(outs["v_s"][b, 0:96, :], vc32[bp][32:128, :], "ost", [vc32[bp]], [("dram", "o")])
                    K.cp(kcb[bp][:], kc32[bp][:], [kc32[bp]], [kcb[bp]])
                    K.cp(vcb[:, b, :, 0:64], vc32[bp][:].rearrange("p (g d) -> p g d", g=2), [vc32[bp]], [vcb])
                    for g in range(2):
                        K.tr(psT_bf[0:64, g * 128:(g + 1) * 128], kcb[bp][:, g * 64:(g + 1) * 64], ident_bf[:],
                             [kcb[bp], ident_bf], [psT])
                    K.cp(kcT[:, b, :, :].rearrange("p g t -> p (g t)"), psT_bf[0:64, 0:256], [psT], [kcT], eng="act")
                    for g in range(2):
                        pS = gen.next()
                        K.mm(pS[:, :], kcT[:, b, g, :], qT[:, 4 * g:4 * g + 4, 0:128], True, False, [kcT, qT], [pS])
                        K.mm(pS[:, :], ident_bf[:], maskS[:, b, :], False, True, [ident_bf, maskS], [pS])
                        K.actv(PTc[:, b, g, :], pS[:, :], AF.Exp, [pS], [PTc], scale=SWA_SCALE)
                if STOP <= 7:
                    return
                for g in range(2):
                    pS = gen.next()
                    K.mm(pS[:, :], kT_all[:, g, SEQ:SEQ + 128], qT[:, 4 * g:4 * g + 4, 0:128], True, False, [kT_all, qT], [pS])
                    K.mm(pS[:, :], ident_bf[:], maskS[:, 4, :], False, True, [ident_bf, maskS], [pS])
                    K.actv(PTn[:, g, :], pS[:, :], AF.Exp, [pS], [PTn], scale=SWA_SCALE)
                if STOP <= 8:
                    return
                for h in range(8):
                    g = h // 4
                    po = psOX if h < 4 else psOY
                    oo = po[:, (h % 4) * 65:(h % 4) * 65 + 65]
                    hs = slice((h % 4) * 128, (h % 4) * 128 + 128)
                    for b in range(NSB):
                        K.mm(oo, PTc[:, b, g, hs], vcb[:, b, g, :], b == 0, False, [PTc, vcb], [po])
                    K.mm(oo, PTn[:, g, hs], v_all[:, NTP, g, :], False, True, [PTn, v_all], [po])
                swa_finish(K, psOX, psOY, den, sinkexp, att_tok[0], psAT, psAT_bf, ident_bf, attT[sp], 0, 128)
            for c in range(4):
                K.load(attT_d[c * 128:(c + 1) * 128, tok0:tok0 + W], attT[sp][:, c, 0:W], f"ast{sp}", [attT[sp]],
                       [("dram", "attT")])


def swa_finish(K, psOX, psOY, den, sinkexp, att_tok, psAT, psAT_bf, ident_bf, attT, tl, W):
    for half, po in enumerate((psOX, psOY)):
        o3 = po[:, 0:260].rearrange("p (h e) -> p h e", e=65)
        K.tt(den[:, half * 4:half * 4 + 4], o3[:, :, 64], sinkexp[:, half * 4:half * 4 + 4], ALU.add, [po, sinkexp], [den])
    K.op("dve", lambda e: e.reciprocal(den[:], den[:]), [den], [den])
    for half, po in enumerate((psOX, psOY)):
        o3 = po[:, 0:260].rearrange("p (h e) -> p h e", e=65)
        K.tt(att_tok[:, half * 256:(half + 1) * 256].rearrange("p (h d) -> p h d", d=64), o3[:, :, 0:64],
             den[:, half * 4:half * 4 + 4].unsqueeze(2).to_broadcast([128, 4, 64]), ALU.mult, [po, den], [att_tok])
    for c in range(4):
        K.tr(psAT_bf[:, c * 128:(c + 1) * 128], att_tok[:, c * 128:(c + 1) * 128], ident_bf[:], [att_tok, ident_bf], [psAT])
    K.cp(attT[:, :, tl * 128:(tl + 1) * 128], psAT_bf[:, 0:512].rearrange("p (c t) -> p c t", c=4), [psAT], [attT], eng="act")


def rope_tables(pos, rot, head_dim, nrep=1):
    half = rot // 2
    inv = ROPE_THETA ** (-np.arange(0, rot, 2, dtype=np.float32) / rot)
    ang = pos.astype(np.float32)[None, :] * inv.astype(np.float32)[:, None]
    cos, sin = np.cos(ang).astype(np.float32), np.sin(ang).astype(np.float32)
    C = np.ones((head_dim, len(pos)), np.float32)
    S = np.zeros((head_dim, len(pos)), np.float32)
    C[0:half] = cos
    C[half:rot] = cos
    S[0:half] = -sin
    S[half:rot] = sin
    return C, S


def perm_rope_cols(w, head_dim, rot):
    half = rot // 2
    n = w.shape[1]
    idx = np.arange(n)
    d = idx % head_dim
    src = np.where(d < half, idx + half, np.where(d < rot, idx - half, idx))
    return np.ascontiguousarray(w[:, src])


def token_positions(SEQ):
    return np.concatenate([np.arange(SEQ), np.tile(PAST_LEN + np.arange(DEC_SEQ), NSB)])


def swa_consts(SEQ):
    C, S = rope_tables(token_positions(SEQ), 16, 64)
    k = np.arange(128)[:, None]
    q = (np.arange(512) % 128)[None, :]
    maskA = np.where((k < 64) & (q >= 64), MASKV, 0.0).astype(np.float32)
    maskB = np.where((k >= 64) & (q < 64), MASKV, 0.0).astype(np.float32)
    qb = ((np.arange(512) % 128) // 32)[None, :]
    maskS = np.zeros((128, NSB + 1, 512), np.float32)
    for b in range(NSB):
        maskS[:, b, :] = np.where(qb == b, 0.0, MASKV)
    maskS[:, NSB, :] = np.where(qb == (np.arange(128) // 32)[:, None], 0.0, MASKV)
    return {"ropeC": C, "ropeS": S, "maskA": maskA, "maskB": maskB, "maskS": maskS, "ident": np.eye(128, dtype=np.float32)}


TWO_PI = float(2.0 * np.pi)
PI = float(np.pi)


def sincos(K, ang, sin_o, cos_o, tmp_i, tmp_a, tmp_b, toks_in, tok_sin, tok_cos, tok_tmp):
    ki, ka, kb = tok_tmp
    a, b = tmp_a, tmp_b
    K.ts(b, ang, 1.0 / TWO_PI, None, ALU.mult, None, toks_in, [kb])
    K.cp(tmp_i, b, [kb], [ki])
    K.cp(b, tmp_i, [ki], [kb])
    K.stt(a, b, -TWO_PI, ang, ALU.mult, ALU.add, [kb] + list(toks_in), [ka])
    for _ in range(2):
        K.ts(b, a, PI, -TWO_PI, ALU.is_gt, ALU.mult, [ka], [kb])
        K.tt(a, a, b, ALU.add, [ka, kb], [ka])
        K.ts(b, a, -PI, TWO_PI, ALU.is_lt, ALU.mult, [ka], [kb])
        K.tt(a, a, b, ALU.add, [ka, kb], [ka])
    K.actv(sin_o, a, AF.Sin, [ka], [tok_sin])
    K.ts(a, a, PI / 2, None, ALU.add, None, [ka], [ka])
    K.ts(b, a, PI, -TWO_PI, ALU.is_gt, ALU.mult, [ka], [kb])
    K.tt(a, a, b, ALU.add, [ka, kb], [ka])
    K.actv(cos_o, a, AF.Sin, [ka], [tok_cos])


def phase_s5(K, cfg, x_all, uT_d, attT_d, x1_d, P, ident_f_d, iotaL_d, outs):
    SEQ = cfg["SEQ"]
    LC = 256
    with K.phase():
        ident_f = K.sb("identf", [128, 128], F32)
        K.load(ident_f[:], ident_f_d, None, (), [ident_f])
        iotaL = K.sb("iotaL", [128, LC], F32)
        K.load(iotaL[:], iotaL_d, None, (), [iotaL])
        w_glu = K.sb("w_glu", [128, 4, 512], BF16)
        w_out = K.sb("w_out", [128, 8, 1024], BF16)
        for c in range(4):
            K.load(w_glu[:, c, :], P["w_glu"][c * 128:(c + 1) * 128, :], None, (), [w_glu], queue="pool")
        for c in range(8):
            K.load(w_out[:, c, :], P["w_out"][c * 128:(c + 1) * 128, :], None, (), [w_out], queue="pool")
        psS = K.ps("psS", [128, 512], F32)
        gp = Rot([K.ps(f"pb{i}", [128, 512], F32) for i in range(2)])
        pyr = Rot([K.ps(f"py{i}", [128, 512], F32) for i in range(2)])
        psG = K.ps("psG", [128, 512], F32)
        psO = K.ps("psO", [128, 1024], F32)

        def load_T(name, src_16x128):
            raw = K.sb(name + "_raw", [16, 128], F32)
            dst = K.sb(name, [128, 16], F32)
            K.load(raw[:], src_16x128, None, (), [raw])
            K.tr(psS[:, 0:16], raw[:], ident_f[0:16, 0:16], [raw, ident_f], [psS])
            K.cp(dst[:], psS[:, 0:16], [psS], [dst])
            return dst
        lre = load_T("lre", P["lam_re"])
        lim = load_T("lim", P["lam_im"])
        ldr = K.sb("ldr", [16, 2], F32)
        K.load(ldr[:], P["log_dt"], None, (), [ldr])
        ldx = K.sb("ldx", [16, 2, 64], F32)
        K.cp(ldx[:], ldr[:].unsqueeze(2).to_broadcast([16, 2, 64]), [ldr], [ldx])
        dtt = K.sb("dtt", [128, 16], F32)
        K.tr(psS[:, 0:16], ldx[:].rearrange("t a n -> t (a n)"), ident_f[0:16, 0:16], [ldx, ident_f], [psS])
        K.actv(dtt[:], psS[:, 0:16], AF.Exp, [psS], [dtt])

        def sm(name):
            return K.sb(name, [128, 16], F32)
        lr, th, mag, sn, cs, abr, abi = sm("lr"), sm("th"), sm("mag"), sm("sn"), sm("cs"), sm("abr"), sm("abi")
        ta, tb, nr, dn, fre, fim = sm("ta"), sm("tb"), sm("nr"), sm("dn"), sm("fre"), sm("fim")
        ti = K.sb("ti", [128, 16], I32)
        K.ts(lr[:], lre[:], -1e-4, None, ALU.min, None, [lre], [lr])
        K.tt(th[:], lim[:], dtt[:], ALU.mult, [lim, dtt], [th])
        K.tt(mag[:], lr[:], dtt[:], ALU.mult, [lr, dtt], [mag])
        K.actv(mag[:], mag[:], AF.Exp, [mag], [mag])
        sincos(K, th[:], sn[:], cs[:], ti[:], ta[:], tb[:], [th], sn, cs, (ti, ta, tb))
        K.tt(abr[:], mag[:], cs[:], ALU.mult, [mag, cs], [abr])
        K.tt(abi[:], mag[:], sn[:], ALU.mult, [mag, sn], [abi])
        K.ts(nr[:], abr[:], -1.0, None, ALU.add, None, [abr], [nr])
        K.tt(dn[:], lr[:], lr[:], ALU.mult, [lr], [dn])
        K.tt(ta[:], lim[:], lim[:], ALU.mult, [lim], [ta])
        K.tt(dn[:], dn[:], ta[:], ALU.add, [dn, ta], [dn])
        K.op("dve", lambda e: e.reciprocal(dn[:], dn[:]), [dn], [dn])
        K.tt(ta[:], nr[:], lr[:], ALU.mult, [nr, lr], [ta])
        K.tt(tb[:], abi[:], lim[:], ALU.mult, [abi, lim], [tb])
        K.tt(fre[:], ta[:], tb[:], ALU.add, [ta, tb], [fre])
        K.tt(fre[:], fre[:], dn[:], ALU.mult, [fre, dn], [fre])
        K.tt(ta[:], abi[:], lr[:], ALU.mult, [abi, lr], [ta])
        K.tt(tb[:], nr[:], lim[:], ALU.mult, [nr, lim], [tb])
        K.tt(fim[:], ta[:], tb[:], ALU.subtract, [ta, tb], [fim])
        K.tt(fim[:], fim[:], dn[:], ALU.mult, [fim, dn], [fim])

        cosT = K.sb("cosT", [128, 16, LC], F32)
        sinT = K.sb("sinT", [128, 16, LC], F32)
        BTp = [K.sb(f"BTp{j}", [128, 16, 128], BF16) for j in range(2)]
        CTp = [K.sb(f"CTp{j}", [128, 16, 128], F32) for j in range(2)]
        dcol = K.sb("dcol", [128, 4], F32)
        bgcol = K.sb("bgcol", [128, 4], F32)
        setup_scope = K.phase()
        setup_scope.__enter__()
        angT = K.sb("angT", [128, 16, LC], F32)
        tmpT = K.sb("tmpT", [128, 16, LC], F32)
        tmpI = K.sb("tmpI", [128, 16, LC], I32)
        K.tt(angT[:], th[:].unsqueeze(2).to_broadcast([128, 16, LC]), iotaL[:].unsqueeze(1).to_broadcast([128, 16, LC]),
             ALU.mult, [th, iotaL], [angT])
        sincos_big(K, angT, sinT, cosT, tmpI, tmpT)

        br = K.sb("br", [128, 16, 16], F32)
        bi = K.sb("bi", [128, 16, 16], F32)
        K.load(br[:], P["b_re"].rearrange("(t p) c -> p t c", p=128), None, (), [br])
        K.load(bi[:], P["b_im"].rearrange("(t p) c -> p t c", p=128), None, (), [bi])
        bbr = K.sb("bbr", [128, 16, 16], F32)
        bbi = K.sb("bbi", [128, 16, 16], F32)
        tq = K.sb("tq", [128, 16, 16], F32)
        fre_b = fre[:].unsqueeze(2).to_broadcast([128, 16, 16])
        fim_b = fim[:].unsqueeze(2).to_broadcast([128, 16, 16])
        K.tt(bbr[:], br[:], fre_b, ALU.mult, [br, fre], [bbr])
        K.tt(tq[:], bi[:], fim_b, ALU.mult, [bi, fim], [tq])
        K.tt(bbr[:], bbr[:], tq[:], ALU.subtract, [bbr, tq], [bbr])
        K.tt(bbi[:], bi[:], fre_b, ALU.mult, [bi, fre], [bbi])
        K.tt(tq[:], br[:], fim_b, ALU.mult, [br, fim], [tq])
        K.tt(bbi[:], bbi[:], tq[:], ALU.add, [bbi, tq], [bbi])
        bpad = K.sb("bpad", [128, 16, 128], F32)
        for j, bb in enumerate((bbr, bbi)):
            K.memset(bpad[:], 0.0, [bpad])
            bp4 = bpad[:].rearrange("p (r i) f -> p r i f", i=4)
            bb4 = bb[:].rearrange("p (r i) c -> p r i c", i=4)
            for i in range(4):
                K.cp(bp4[0:64, :, i, 32 * i:32 * i + 16], bb4[0:64, :, i, :], [bb], [bpad])
                K.cp(bp4[64:128, :, i, 32 * i + 16:32 * i + 32], bb4[64:128, :, i, :], [bb], [bpad])
            for t in range(16):
                K.tr(psS[:, 0:128], bpad[:, t, :], ident_f[:], [bpad, ident_f], [psS])
                K.cp(BTp[j][:, t, :], psS[:, 0:128], [psS], [BTp[j]], eng="act")
        cin = K.sb("cin", [128, 128], F32)
        for j, src in enumerate((P["c_re"], P["c_im"])):
            K.memset(CTp[j][:], 0.0, [CTp[j]])
            for rt in range(4):
                K.memset(cin[:], 0.0, [cin])
                for gl in range(8):
                    K.load(cin[gl * 16:(gl + 1) * 16, (gl % 2) * 64:(gl % 2) * 64 + 64],
                           src[rt * 128 + gl * 16: rt * 128 + (gl + 1) * 16, :], None, (), [cin])
                K.tr(psS[:, 0:128], cin[:], ident_f[:], [cin, ident_f], [psS])
                for i in range(4):
                    if j == 0:
                        K.cp(CTp[j][:, 4 * rt + i, 32 * i:32 * i + 32], psS[:, 32 * i:32 * i + 32], [psS], [CTp[j]], eng="act")
                    else:
                        K.ts(CTp[j][:, 4 * rt + i, 32 * i:32 * i + 32], psS[:, 32 * i:32 * i + 32], -1.0, None, ALU.mult, None,
                             [psS], [CTp[j]])
        dsk = K.sb("dsk_raw", [4, 128], F32)
        K.load(dsk[:], P["dsk"], None, (), [dsk])
        K.tr(psS[:, 0:4], dsk[:], ident_f[0:4, 0:4], [dsk, ident_f], [psS])
        K.cp(dcol[:], psS[:, 0:4], [psS], [dcol])
        bgr = K.sb("bg_raw", [4, 128], F32)
        K.load(bgr[:], P["b_glu"], None, (), [bgr])
        K.tr(psS[:, 0:4], bgr[:], ident_f[0:4, 0:4], [bgr, ident_f], [psS])
        K.cp(bgcol[:], psS[:, 0:4], [psS], [bgcol])

        setup_scope.__exit__(None, None, None)
        hpr = K.sb("hpr", [128, 16], F32)
        hpi = K.sb("hpi", [128, 16], F32)
        uTb = [K.sb(f"uTb{i}", [128, 4, LC], BF16) for i in range(2)]
        mixT = [K.sb(f"mixT{i}", [128, 8, LC], BF16) for i in range(2)]
        mixS = K.sb("mixS", [128, 8, 128], BF16)
        W = {n: [K.sb(f"{n}{i}", [128, LC], F32) for i in range(2)] for n in
             ("t1", "t2", "t3", "t4", "p1", "p2", "p3", "p4", "wri", "wii", "wr", "wi", "hr", "hi")}
        yT = K.sb("yT", [128, LC], F32)
        sq = K.sb("sq", [128, LC], F32)
        z2 = [K.sb(f"z2_{i}", [128, 4, LC], BF16) for i in range(2)]
        gt = K.sb("gt", [128, LC], F32)
        xt = [K.sb(f"xt{i}", [128, 1024], F32) for i in range(2)]
        xo = [K.sb(f"xo{i}", [128, 1024], F32) for i in range(2)]
        hout = K.sb("hout", [16, 128], F32)
        h0raw = K.sb("h0raw", [16, 128], F32)

        def set_state(src_re, src_im):
            if src_re is None:
                K.memset(hpr[:], 0.0, [hpr])
                K.memset(hpi[:], 0.0, [hpi])
                return
            for src, dst in ((src_re, hpr), (src_im, hpi)):
                K.load(h0raw[:], src, None, (), [h0raw])
                K.tr(psS[:, 0:16], h0raw[:], ident_f[0:16, 0:16], [h0raw, ident_f], [psS])
                K.cp(dst[:], psS[:, 0:16], [psS], [dst])

        def put_state(dst_re, dst_im):
            for dst, src in ((dst_re, hpr), (dst_im, hpi)):
                K.tr(psS[0:16, 0:128], src[:], ident_f[:], [src, ident_f], [psS])
                K.cp(hout[:], psS[0:16, 0:128], [psS], [hout])
                K.load(dst, hout[:], None, [hout], ())

        cnt = [0]

        def s5_chunk(tok0, L, mix, mcol0):
            ci = cnt[0]
            cnt[0] += 1
            ub, zz = uTb[ci % 2], z2[ci % 2]
            for c in range(4):
                K.load(ub[:, c, 0:L], uT_d[c * 128:(c + 1) * 128, tok0:tok0 + L], None, (), [ub])
                K.load(mix[:, c, mcol0:mcol0 + L], attT_d[c * 128:(c + 1) * 128, tok0:tok0 + L], None, (), [mix])
            for rt in range(4):
                py = pyr.next()
                for i in range(4):
                    t = 4 * rt + i
                    w = {n: W[n][t % 2] for n in W}
                    pb = gp.next()
                    K.mm(pb[:, 0:L], BTp[0][:, t, :], ub[:, rt, 0:L], True, True, [BTp[0], ub], [pb])
                    K.mm(pb[:, 256:256 + L], BTp[1][:, t, :], ub[:, rt, 0:L], True, True, [BTp[1], ub], [pb])
                    cT, sT = cosT[:, t, 0:L], sinT[:, t, 0:L]
                    bre, bim = pb[:, 0:L], pb[:, 256:256 + L]
                    K.tt(w["t1"][:, 0:L], bre, cT, ALU.mult, [pb, cosT], [w["t1"]])
                    K.tt(w["t2"][:, 0:L], bim, sT, ALU.mult, [pb, sinT], [w["t2"]])
                    K.tt(w["t3"][:, 0:L], bim, cT, ALU.mult, [pb, cosT], [w["t3"]])
                    K.tt(w["t4"][:, 0:L], bre, sT, ALU.mult, [pb, sinT], [w["t4"]])
                    K.tt(w["wri"][:, 0:L], w["t1"][:, 0:L], w["t2"][:, 0:L], ALU.add, [w["t1"], w["t2"]], [w["wri"]])
                    K.tt(w["wii"][:, 0:L], w["t3"][:, 0:L], w["t4"][:, 0:L], ALU.subtract, [w["t3"], w["t4"]], [w["wii"]])
                    rb = mag[:, t:t + 1].to_broadcast([128, L])
                    K.op("dve", lambda e, w=w, rb=rb, t=t: e.tensor_tensor_scan(
                        w["wr"][:, 0:L], rb, w["wri"][:, 0:L], hpr[:, t:t + 1], ALU.mult, ALU.add),
                        [mag, w["wri"], hpr], [w["wr"]])
                    K.op("dve", lambda e, w=w, rb=rb, t=t: e.tensor_tensor_scan(
                        w["wi"][:, 0:L], rb, w["wii"][:, 0:L], hpi[:, t:t + 1], ALU.mult, ALU.add),
                        [mag, w["wii"], hpi], [w["wi"]])
                    PE_ = cfg.get("s5_post_eng", "dve")
                    K.tt(w["p1"][:, 0:L], w["wr"][:, 0:L], cT, ALU.mult, [w["wr"], cosT], [w["p1"]], eng=PE_)
                    K.tt(w["p2"][:, 0:L], w["wi"][:, 0:L], sT, ALU.mult, [w["wi"], sinT], [w["p2"]], eng=PE_)
                    K.tt(w["p3"][:, 0:L], w["wi"][:, 0:L], cT, ALU.mult, [w["wi"], cosT], [w["p3"]], eng=PE_)
                    K.tt(w["p4"][:, 0:L], w["wr"][:, 0:L], sT, ALU.mult, [w["wr"], sinT], [w["p4"]], eng=PE_)
                    K.tt(w["hr"][:, 0:L], w["p1"][:, 0:L], w["p2"][:, 0:L], ALU.subtract, [w["p1"], w["p2"]], [w["hr"]], eng=PE_)
                    K.tt(w["hi"][:, 0:L], w["p3"][:, 0:L], w["p4"][:, 0:L], ALU.add, [w["p3"], w["p4"]], [w["hi"]], eng=PE_)
                    K.cp(hpr[:, t:t + 1], w["hr"][:, L - 1:L], [w["hr"]], [hpr], eng="act")
                    K.cp(hpi[:, t:t + 1], w["hi"][:, L - 1:L], [w["hi"]], [hpi], eng="act")
                    K.mm(py[:, 0:L], CTp[0][:, t, :], w["hr"][:, 0:L], i == 0, False, [CTp[0], w["hr"]], [py])
                    K.mm(py[:, 0:L], CTp[1][:, t, :], w["hi"][:, 0:L], False, i == 3, [CTp[1], w["hi"]], [py])
                K.stt(yT[:, 0:L], ub[:, rt, 0:L], dcol[:, rt:rt + 1], py[:, 0:L], ALU.mult, ALU.add, [ub, dcol, py], [yT])
                K.actv(sq[:, 0:L], yT[:, 0:L], AF.Square, [yT], [sq])
                K.ts(sq[:, 0:L], sq[:, 0:L], 0.044715, 1.0, ALU.mult, ALU.add, [sq], [sq])
                K.tt(sq[:, 0:L], sq[:, 0:L], yT[:, 0:L], ALU.mult, [sq, yT], [sq])
                K.actv(sq[:, 0:L], sq[:, 0:L], AF.Tanh, [sq], [sq], scale=float(np.sqrt(2.0 / np.pi)))
                K.stt(zz[:, rt, 0:L], sq[:, 0:L], 1.0, yT[:, 0:L], ALU.add, ALU.mult, [sq, yT], [zz])
            for fo in range(4):
                for kc in range(4):
                    K.mm(psG[:, 0:L], w_glu[:, kc, fo * 128:(fo + 1) * 128], zz[:, kc, 0:L], kc == 0, kc == 3, [w_glu, zz], [psG])
                K.actv(gt[:, 0:L], psG[:, 0:L], AF.Sigmoid, [psG, bgcol], [gt], scale=0.5, bias=bgcol[:, fo:fo + 1])
                K.stt(mix[:, 4 + fo, mcol0:mcol0 + L], zz[:, fo, 0:L], 0.5, gt[:, 0:L], ALU.mult, ALU.mult, [zz, gt], [mix])

        xc = [0]

        def out_proj(mix, col0, tok0):
            p = xc[0] % 2
            xc[0] += 1
            K.load(xt[p][:], x_all[tok0:tok0 + 128, :], None, (), [xt[p]])
            for hf in range(2):
                for kc in range(8):
                    K.mm(psO[:, hf * 512:(hf + 1) * 512], mix[:, kc, col0:col0 + 128], w_out[:, kc, hf * 512:(hf + 1) * 512],
                         kc == 0, kc == 7, [mix, w_out], [psO])
            K.tt(xo[p][:], psO[:], xt[p][:], ALU.add, [psO, xt[p]], [xo[p]])
            K.load(x1_d[tok0:tok0 + 128, :], xo[p][:], None, [xo[p]], ())

        set_state(None, None)
        for ci in range(SEQ // LC):
            mx = mixT[ci % 2]
            s5_chunk(ci * LC, LC, mx, 0)
            for tl in range(LC // 128):
                out_proj(mx, tl * 128, ci * LC + tl * 128)
        put_state(outs["hp_re"], outs["hp_im"])
        for b in range(NSB):
            set_state(P["h0_re"][b], P["h0_im"][b])
            s5_chunk(SEQ + b * 32, 32, mixS, b * 32)
            put_state(outs["hs_re"][b], outs["hs_im"][b])
        out_proj(mixS, 0, SEQ)


def sincos_big(K, angT, sinT, cosT, tmpI, tmpT):
    a = angT[:]
    b = tmpT[:]
    K.ts(b, a, 1.0 / TWO_PI, None, ALU.mult, None, [angT], [tmpT])
    K.cp(tmpI[:], b, [tmpT], [tmpI])
    K.cp(b, tmpI[:], [tmpI], [tmpT])
    K.stt(a, b, -TWO_PI, a, ALU.mult, ALU.add, [tmpT, angT], [angT])
    for _ in range(2):
        K.ts(b, a, PI, -TWO_PI, ALU.is_gt, ALU.mult, [angT], [tmpT])
        K.tt(a, a, b, ALU.add, [angT, tmpT], [angT])
        K.ts(b, a, -PI, TWO_PI, ALU.is_lt, ALU.mult, [angT], [tmpT])
        K.tt(a, a, b, ALU.add, [angT, tmpT], [angT])
    K.actv(sinT[:], a, AF.Sin, [angT], [sinT])
    K.ts(a, a, PI / 2, None, ALU.add, None, [angT], [angT])
    K.ts(b, a, PI, -TWO_PI, ALU.is_gt, ALU.mult, [angT], [tmpT])
    K.tt(a, a, b, ALU.add, [angT, tmpT], [angT])
    K.actv(cosT[:], a, AF.Sin, [angT], [cosT])


def s5_consts():
    return {"ident": np.eye(128, dtype=np.float32), "iotaL": np.tile(np.arange(1, 257, dtype=np.float32), (128, 1))}


def phase_odd(K, cfg, x_in, x_out, Wd, Cd, caches, outs):
    SEQ = cfg["SEQ"]
    NTOK = SEQ + 128
    NTP = SEQ // 128
    with K.phase():
        ident_f = K.sb("identf", [128, 128], F32)
        ident_bf = K.sb("identbf", [128, 128], BF16)
        ones_bf = K.sb("onesbf", [128, 128], BF16)
        K.load(ident_f[:], Cd["ident"], None, (), [ident_f])
        K.load(ident_bf[:], Cd["ident"], None, (), [ident_bf], queue="pool")
        K.memset(ones_bf[:], 1.0, [ones_bf])
        eps_t = K.sb("eps_t", [128, 1], F32)
        K.memset(eps_t[:], RMS_EPS, [eps_t])
        gmix = K.sb("gmix", [128, 1024], F32)
        K.load(gmix[:], bcast_rows(Wd["gmix"]), None, (), [gmix])
        qn = K.sb("qn", [128, 512], F32)
        K.load(qn[:], bcast_rows(Wd["qn"]), None, (), [qn])
        kvn = K.sb("kvn", [128, 256], F32)
        K.load(kvn[:], bcast_rows(Wd["kvn"]), None, (), [kvn])
        maskcol = K.sb("maskcol", [128, 4], F32)
        K.load(maskcol[:], Cd["maskcol"], None, (), [maskcol])
        mask4 = K.sb("mask4", [128, 4, 512], BF16)
        K.load(mask4[:], Cd["mask4"], None, (), [mask4], queue="pool")
        maskK = K.sb("maskK", [128, 4, 256], BF16)
        K.load(maskK[:], Cd["maskK"], None, (), [maskK], queue="pool")

        def wload(name, src, kch, ncol):
            t = K.sb(name, [128, kch, ncol], BF16)
            for c in range(kch):
                for c0 in range(0, ncol, 1024):
                    c1 = min(ncol, c0 + 1024)
                    K.load(t[:, c, c0:c1], src[c * 128:(c + 1) * 128, c0:c1], None, (), [t], queue="pool")
            return t
        w_in = wload("w_in", Wd["w_in"], 8, 1312)
        w_out = wload("w_out", Wd["w_out"], 8, 1024)
        w_uqN = wload("w_uqN", Wd["w_uqN"], 4, 512)
        w_uqR = wload("w_uqR", Wd["w_uqR"], 4, 256)
        w_uqRP = wload("w_uqRP", Wd["w_uqRP"], 4, 256)
        w_uv = wload("w_uv", Wd["w_uv"], 2, 512)
        pool_w = K.sb("pool_w", [128, 4, 128], BF16)
        for g in range(4):
            K.load(pool_w[:, g, :], Wd["pool_w"][g], None, (), [pool_w], queue="pool")
        psr = K.sb("psr", [4, 128], F32)
        K.load(psr[:], Wd["pscale"], None, (), [psr])
        pscol = K.sb("pscol", [128, 4], F32)

        gen = Rot([K.ps(f"gen{i}", [128, 512], F32) for i in range(3)])
        psSc = Rot([K.ps(f"psSc{i}", [128, 512], F32) for i in range(2)])
        psOa = K.ps("psOa", [128, 512], F32)
        psOb = K.ps("psOb", [128, 512], F32)
        psDn = K.ps("psDn", [128, 512], F32)

        g0 = gen.next()
        K.tr(g0[:, 0:4], psr[:], ident_f[0:4, 0:4], [psr, ident_f], [g0])
        K.cp(pscol[:], g0[:, 0:4], [g0], [pscol])
        wuk_raw = K.sb("wuk_raw", [128, 2, 512], F32)
        K.load(wuk_raw[:], Wd["w_uk"].rearrange("(ct p) f -> p ct f", p=128), None, (), [wuk_raw])
        w_ukT = K.sb("w_ukT", [128, 4, 256], BF16)
        for hp in range(4):
            for ct in range(2):
                g1 = gen.next()
                K.tr(g1[:, 0:128], wuk_raw[:, ct, hp * 128:(hp + 1) * 128], ident_f[:], [wuk_raw, ident_f], [g1])
                K.cp(w_ukT[:, hp, ct * 128:(ct + 1) * 128], g1[:, 0:128], [g1], [w_ukT], eng="act")

        cT_all = K.sb("cT_all", [128, 2, NTOK], BF16)
        c_tok_all = K.sb("c_tok_all", [128, NTP + 1, 256], BF16)
        kpT4_all = K.sb("kpT4_all", [128, NTOK], BF16)

        junk = K.sb("junk", [128, 1024], BF16)
        xt = [K.sb(f"xt{i}", [128, 1024], F32) for i in range(2)]
        xo = [K.sb(f"xo{i}", [128, 1024], F32) for i in range(2)]
        ssb = [K.sb(f"ssb{i}", [128, 4], F32) for i in range(2)]
        sscq = [K.sb(f"sscq{i}", [128, 4], F32) for i in range(2)]
        sscc = [K.sb(f"sscc{i}", [128, 4], F32) for i in range(2)]
        hn = [K.sb(f"hn{i}", [128, 1024], BF16) for i in range(2)]
        cqn = [K.sb(f"cqn{i}", [128, 512], BF16) for i in range(2)]
        cf = [K.sb(f"cf{i}", [128, 256], F32) for i in range(2)]
        kt1 = [K.sb(f"kt1_{i}", [128, 32], F32) for i in range(2)]
        kt2 = [K.sb(f"kt2_{i}", [128, 32], F32) for i in range(2)]
        kpf = [K.sb(f"kpf{i}", [128, 32], F32) for i in range(2)]
        kp4 = [K.sb(f"kp4{i}", [128, 4, 32], BF16) for i in range(2)]
        ropeK = [K.sb(f"ropeK{i}", [128, 64], F32) for i in range(2)]

        def make_bufs(W):
            B = {}
            B["hnT"] = K.sb("hnT", [128, 8, W], BF16)
            B["cqnT"] = K.sb("cqnT", [128, 4, W], BF16)
            B["qnT"] = K.sb("qnT", [128, 4, W], BF16)
            B["qpeT"] = K.sb("qpeT", [128, 2, W], BF16)
            B["rC"] = K.sb("rC", [128, W], F32)
            B["rS"] = K.sb("rS", [128, W], F32)
            B["r1"] = K.sb("r1", [128, W], F32)
            B["r2"] = K.sb("r2", [128, W], F32)
            B["extA"] = K.sb("extA", [128, 15 + W], F32)
            B["extB"] = K.sb("extB", [128, 15 + W], F32)
            B["extC"] = K.sb("extC", [128, 15 + W], F32)
            for nm in ("extA", "extB", "extC"):
                K.memset(B[nm][:], 0.0, [B[nm]])
            B["rc"] = K.sb("rc", [128, W], F32)
            B["mT"] = K.sb("mT", [128, W], BF16)
            B["mixT"] = K.sb("mixT", [128, 8, W], BF16)
            B["hist"] = K.sb("hist", [128, 4, 15], F32)
            B["utail"] = K.sb("utail", [15, 512], F32)
            B["hraw"] = K.sb("hraw", [15, 512], F32)
            B["uS"] = K.sb("uS", [128, W], F32)
            return B

        def norm_small(X, OUT, gb, width, sc, xtok, otok, gtok):
            K.actv(junk[:, 0:width], X, AF.Square, [xtok], [sc, junk], accum_out=sc[:, 0:1])
            K.actv(sc[:, 1:2], sc[:, 0:1], AF.Sqrt, [sc, eps_t], [sc], scale=1.0 / width, bias=eps_t[:, 0:1])
            K.op("dve", lambda e: e.reciprocal(sc[:, 2:3], sc[:, 1:2]), [sc], [sc])
            K.stt(OUT, X, sc[:, 2:3], gb, ALU.mult, ALU.mult, [xtok, sc, gtok], [otok])

        cnt = {"x": 0, "t": 0}

        def front(B, tok0, W, hist_src):
            ntl = W // 128
            hnT, cqnT = B["hnT"], B["cqnT"]
            for tl in range(ntl):
                p = cnt["t"] % 2
                cnt["t"] += 1
                X = xt[cnt["x"] % 2]
                cnt["x"] += 1
                j = (tok0 + tl * 128) // 128
                K.load(X[:], x_in[tok0 + tl * 128: tok0 + (tl + 1) * 128, :], None, (), [X])
                K.load(ropeK[p][:], Cd["ropeK"][tok0 + tl * 128: tok0 + (tl + 1) * 128, :], None, (), [ropeK[p]])
                rmsnorm_tile(K, X[:], hn[p][:], gmix[:], ssb[p], eps_t, junk[:], X, hn[p], gmix, junk)
                gT = gen.next()
                gT_bf = gT[:].bitcast(BF16)
                for c in range(8):
                    K.tr(gT_bf[:, c * 128:(c + 1) * 128], hn[p][:, c * 128:(c + 1) * 128], ident_bf[:], [hn[p], ident_bf], [gT])
                K.cp(hnT[:, :, tl * 128:(tl + 1) * 128], gT_bf[:, :].rearrange("p (c t) -> p c t", c=8), [gT], [hnT], eng="act")
                pq, pc = gen.next(), gen.next()
                for kc in range(8):
                    K.mm(pq[:, 0:512], hnT[:, kc, tl * 128:(tl + 1) * 128], w_in[:, kc, 0:512], kc == 0, kc == 7, [hnT, w_in], [pq])
                for kc in range(8):
                    K.mm(pc[:, 0:288], hnT[:, kc, tl * 128:(tl + 1) * 128], w_in[:, kc, 512:800], kc == 0, kc == 7, [hnT, w_in], [pc])
                norm_small(pq[:, 0:512], cqn[p][:], qn[:], 512, sscq[p], pq, cqn[p], qn)
                norm_small(pc[:, 0:256], cf[p][:], kvn[:], 256, sscc[p], pc, cf[p], kvn)
                K.tt(kt1[p][:], pc[:, 256:288], ropeK[p][:, 0:32], ALU.mult, [pc, ropeK[p]], [kt1[p]])
                K.tt(kt2[p][:, 0:16], pc[:, 272:288], ropeK[p][:, 32:48], ALU.mult, [pc, ropeK[p]], [kt2[p]])
                K.tt(kt2[p][:, 16:32], pc[:, 256:272], ropeK[p][:, 48:64], ALU.mult, [pc, ropeK[p]], [kt2[p]])
                K.tt(kpf[p][:], kt1[p][:], kt2[p][:], ALU.add, [kt1[p], kt2[p]], [kpf[p]])
                K.cp(kp4[p][:], kpf[p][:].unsqueeze(1).to_broadcast([128, 4, 32]), [kpf[p]], [kp4[p]])
                K.cp(c_tok_all[:, j, :], cf[p][:], [cf[p]], [c_tok_all])
                if tok0 < SEQ:
                    K.load(outs["ckv_p"][tok0 + tl * 128: tok0 + (tl + 1) * 128, :], cf[p][:], None, [cf[p]], ())
                    K.load(outs["kpe_p"][tok0 + tl * 128: tok0 + (tl + 1) * 128, :], kpf[p][:], None, [kpf[p]], ())
                else:
                    for b in range(NSB):
                        K.load(outs["ckv_s"][b], cf[p][b * 32:(b + 1) * 32, :], None, [cf[p]], ())
                        K.load(outs["kpe_s"][b], kpf[p][b * 32:(b + 1) * 32, :], None, [kpf[p]], ())
                gT2 = gen.next()
                gT2_bf = gT2[:].bitcast(BF16)
                for c in range(4):
                    K.tr(gT2_bf[:, c * 128:(c + 1) * 128], cqn[p][:, c * 128:(c + 1) * 128], ident_bf[:], [cqn[p], ident_bf], [gT2])
                K.cp(cqnT[:, :, tl * 128:(tl + 1) * 128], gT2_bf[:, 0:512].rearrange("p (c t) -> p c t", c=4), [gT2], [cqnT], eng="act")
                gT3 = gen.next()
                gT3_bf = gT3[:].bitcast(BF16)
                for c in range(2):
                    K.tr(gT3_bf[:, c * 128:(c + 1) * 128], c_tok_all[:, j, c * 128:(c + 1) * 128], ident_bf[:],
                         [c_tok_all, ident_bf], [gT3])
                K.tr(gT3_bf[:, 256:384], kp4[p][:].rearrange("p a r -> p (a r)"), ident_bf[:], [kp4[p], ident_bf], [gT3])
                K.cp(cT_all[:, :, j * 128:(j + 1) * 128], gT3_bf[:, 0:256].rearrange("p (c t) -> p c t", c=2), [gT3], [cT_all], eng="act")
                K.cp(kpT4_all[:, j * 128:(j + 1) * 128], gT3_bf[:, 256:384], [gT3], [kpT4_all], eng="act")
            K.load(B["rC"][:, 0:W], Cd["ropeQC"][:, tok0:tok0 + W], None, (), [B["rC"]])
            K.load(B["rS"][:, 0:W], Cd["ropeQS"][:, tok0:tok0 + W], None, (), [B["rS"]])
            for c in range(4):
                pn = gen.next()
                for kc in range(4):
                    K.mm(pn[:, 0:W], w_uqN[:, kc, c * 128:(c + 1) * 128], cqnT[:, kc, 0:W], kc == 0, kc == 3, [w_uqN, cqnT], [pn])
                K.cp(B["qnT"][:, c, 0:W], pn[:, 0:W], [pn], [B["qnT"]], eng="act")
            for c in range(2):
                pr, pp = gen.next(), gen.next()
                for kc in range(4):
                    K.mm(pr[:, 0:W], w_uqR[:, kc, c * 128:(c + 1) * 128], cqnT[:, kc, 0:W], kc == 0, kc == 3, [w_uqR, cqnT], [pr])
                for kc in range(4):
                    K.mm(pp[:, 0:W], w_uqRP[:, kc, c * 128:(c + 1) * 128], cqnT[:, kc, 0:W], kc == 0, kc == 3, [w_uqRP, cqnT], [pp])
                K.tt(B["r1"][:, 0:W], pr[:, 0:W], B["rC"][:, 0:W], ALU.mult, [pr, B["rC"]], [B["r1"]])
                K.tt(B["r2"][:, 0:W], pp[:, 0:W], B["rS"][:, 0:W], ALU.mult, [pp, B["rS"]], [B["r2"]])
                K.tt(B["qpeT"][:, c, 0:W], B["r1"][:, 0:W], B["r2"][:, 0:W], ALU.add, [B["r1"], B["r2"]], [B["qpeT"]])
            segs = [(0, W)] if not isinstance(hist_src, list) else [(b * 32, 32) for b in range(len(hist_src))]
            if isinstance(hist_src, list):
                hraw = B["hraw"]
            for g in range(4):
                pu = gen.next()
                for kc in range(8):
                    K.mm(pu[:, 0:W], w_in[:, kc, 800 + g * 128:800 + (g + 1) * 128], hnT[:, kc, 0:W], kc == 0, kc == 7, [w_in, hnT], [pu])
                K.cp(B["uS"][:, 0:W], pu[:, 0:W], [pu], [B["uS"]], eng="act")
                for si_, (s0, L) in enumerate(segs):
                    A, Bb, Cc = B["extA"], B["extB"], B["extC"]
                    if hist_src == "zero":
                        K.memset(A[:, 0:15], 0.0, [A])
                    elif isinstance(hist_src, list):
                        K.load(hraw[:, g * 128:(g + 1) * 128], hist_src[si_][:, g * 128:(g + 1) * 128], None, (), [hraw])
                        ph = gen.next()
                        K.tr(ph[:, 0:15], hraw[:, g * 128:(g + 1) * 128], ident_f[0:15, 0:15], [hraw, ident_f], [ph])
                        K.cp(A[:, 0:15], ph[:, 0:15], [ph], [A])
                    else:
                        K.cp(A[:, 0:15], B["hist"][:, g, :], [B["hist"]], [A])
                    K.cp(A[:, 15:15 + L], B["uS"][:, s0:s0 + L], [B["uS"]], [A])
                    if hist_src is None or hist_src == "zero":
                        K.cp(B["hist"][:, g, :], A[:, L:L + 15], [A], [B["hist"]])
                    src, dst = A, Bb
                    n = 15 + L
                    for st in range(g + 1):
                        sh = 1 << st
                        K.tt(dst[:, sh:n], src[:, sh:n], src[:, 0:n - sh], ALU.add, [src], [dst])
                        src, dst = dst, (Cc if dst is Bb else Bb)
                    K.load(B["rc"][:, 0:L], bcast_rows(Cd["rcnt"][g:g + 1, tok0 + s0: tok0 + s0 + L]), None, (), [B["rc"]])
                    K.tt(Cc[:, 15:n] if src is not Cc else Bb[:, 15:n], src[:, 15:n], B["rc"][:, 0:L], ALU.mult, [src, B["rc"]],
                         [Cc if src is not Cc else Bb])
                    tot = Cc if src is not Cc else Bb
                    K.tt(B["mT"][:, s0:s0 + L], tot[:, 15:n], A[:, 15:n], ALU.subtract, [tot, A], [B["mT"]])
                    last = (tok0 + W == SEQ) or isinstance(hist_src, list)
                    if last:
                        pt = gen.next()
                        K.tr(pt[0:15, 0:128], A[:, L:L + 15], ident_f[:], [A, ident_f], [pt])
                        K.cp(B["utail"][:, g * 128:(g + 1) * 128], pt[0:15, 0:128], [pt], [B["utail"]])
                        if g == 3 or isinstance(hist_src, list):
                            dsto = outs["pool_p"] if not isinstance(hist_src, list) else outs["pool_s"][si_]
                            K.load(dsto[:, g * 128:(g + 1) * 128], B["utail"][:, g * 128:(g + 1) * 128], None, [B["utail"]], ())
                            if not isinstance(hist_src, list):
                                for g2 in range(3):
                                    K.load(dsto[:, g2 * 128:(g2 + 1) * 128], B["utail"][:, g2 * 128:(g2 + 1) * 128], None,
                                           [B["utail"]], ())
                po = gen.next()
                K.mm(po[:, 0:W], pool_w[:, g, :], B["mT"][:, 0:W], True, True, [pool_w, B["mT"]], [po])
                K.actv(B["mixT"][:, g, 0:W], po[:, 0:W], AF.Copy, [po, pscol], [B["mixT"]], scale=pscol[:, g:g + 1])

        def qhead(B, h, W, qa, qpad):
            r0 = (h % 2) * 64
            for cc in range(2):
                pa = gen.next()
                K.mm(pa[:, 0:W], w_ukT[r0:r0 + 64, h // 2, cc * 128:(cc + 1) * 128], B["qnT"][r0:r0 + 64, h // 2, 0:W], True, True,
                     [w_ukT, B["qnT"]], [pa])
                K.cp(qa[:, cc, 0:W], pa[:, 0:W], [pa], [qa], eng="act")
            K.ts(qpad[:, 0:W], B["qpeT"][:, h // 4, 0:W], maskcol[:, h % 4:h % 4 + 1], None, ALU.mult, None,
                 [B["qpeT"], maskcol], [qpad])

        def out_proj(B, W, tok0):
            for tl in range(W // 128):
                p = cnt["t"] % 2
                cnt["t"] += 1
                X = xt[cnt["x"] % 2]
                cnt["x"] += 1
                K.load(X[:], x_in[tok0 + tl * 128: tok0 + (tl + 1) * 128, :], None, (), [X])
                pso = [gen.next(), gen.next()]
                for hf in range(2):
                    for kc in range(8):
                        K.mm(pso[hf][:, :], B["mixT"][:, kc, tl * 128:(tl + 1) * 128], w_out[:, kc, hf * 512:(hf + 1) * 512],
                             kc == 0, kc == 7, [B["mixT"], w_out], [pso[hf]])
                    K.tt(xo[p][:, hf * 512:(hf + 1) * 512], pso[hf][:, :], X[:, hf * 512:(hf + 1) * 512], ALU.add, [pso[hf], X], [xo[p]])
                K.load(x_out[tok0 + tl * 128: tok0 + (tl + 1) * 128, :], xo[p][:], None, [xo[p]], ())

        pscope = K.phase()
        pscope.__enter__()
        B = make_bufs(512)
        qa = [K.sb(f"qa{i}", [128, 2, 512], BF16) for i in range(2)]
        qpad = [K.sb(f"qpad{i}", [128, 512], BF16) for i in range(2)]
        PT = [K.sb(f"PT{i}", [128, 512], BF16) for i in range(3)]
        rden = K.sb("rden", [128, 512], F32)
        olat = [K.sb(f"olat{i}", [128, 2, 512], BF16) for i in range(2)]
        pti = 0
        for s in range(SEQ // 512):
            tok0 = s * 512
            front(B, tok0, 512, "zero" if s == 0 else None)
            nkt = (tok0 + 512) // 128
            for h in range(8):
                qh, qp = qa[h % 2], qpad[h % 2]
                qhead(B, h, 512, qh, qp)
                for kt in range(nkt):
                    pS = psSc.next()
                    diag = kt - tok0 // 128
                    K.mm(pS[:, :], cT_all[:, 0, kt * 128:(kt + 1) * 128], qh[:, 0, :], True, False, [cT_all, qh], [pS])
                    K.mm(pS[:, :], cT_all[:, 1, kt * 128:(kt + 1) * 128], qh[:, 1, :], False, False, [cT_all, qh], [pS])
                    K.mm(pS[:, :], kpT4_all[:, kt * 128:(kt + 1) * 128], qp[:, :], False, diag < 0, [kpT4_all, qp], [pS])
                    if diag >= 0:
                        K.mm(pS[:, :], ident_bf[:], mask4[:, diag, :], False, True, [ident_bf, mask4], [pS])
                    P_ = PT[pti % 3]
                    pti += 1
                    K.actv(P_[:], pS[:, :], AF.Exp, [pS], [P_], scale=MLA_SCALE)
                    K.mm(psOa[:, :], c_tok_all[:, kt, 0:128], P_[:], kt == 0, kt == nkt - 1, [c_tok_all, P_], [psOa])
                    K.mm(psOb[:, :], c_tok_all[:, kt, 128:256], P_[:], kt == 0, kt == nkt - 1, [c_tok_all, P_], [psOb])
                    K.mm(psDn[:, :], ones_bf[:], P_[:], kt == 0, kt == nkt - 1, [ones_bf, P_], [psDn])
                ol = olat[h % 2]
                K.op("dve", lambda e: e.reciprocal(rden[:], psDn[:, :]), [psDn], [rden])
                K.tt(ol[:, 0, :], psOa[:, :], rden[:], ALU.mult, [psOa, rden], [ol])
                K.tt(ol[:, 1, :], psOb[:, :], rden[:], ALU.mult, [psOb, rden], [ol])
                if h % 2 == 0:
                    pm = gen.next()
                for cc in range(2):
                    K.mm(pm[(h % 2) * 64:(h % 2) * 64 + 64, :], w_uv[:, cc, h * 64:(h + 1) * 64], ol[:, cc, :], cc == 0, cc == 1,
                         [w_uv, ol], [pm])
                if h % 2 == 1:
                    K.cp(B["mixT"][:, 4 + h // 2, :], pm[:, :], [pm], [B["mixT"]], eng="act")
            out_proj(B, 512, tok0)
        pscope.__exit__(None, None, None)

        B = make_bufs(128)
        qaS = K.sb("qaS", [128, 8, 2, 128], BF16)
        qpadS = K.sb("qpadS", [128, 8, 128], BF16)
        front(B, SEQ, 128, [caches["pool"][b] for b in range(NSB)])
        for h in range(8):
            r0 = (h % 2) * 64
            for cc in range(2):
                pa = gen.next()
                K.mm(pa[:, 0:128], w_ukT[r0:r0 + 64, h // 2, cc * 128:(cc + 1) * 128], B["qnT"][r0:r0 + 64, h // 2, 0:128], True, True,
                     [w_ukT, B["qnT"]], [pa])
                K.cp(qaS[:, h, cc, :], pa[:, 0:128], [pa], [qaS], eng="act")
            K.ts(qpadS[:, h, :], B["qpeT"][:, h // 4, 0:128], maskcol[:, h % 4:h % 4 + 1], None, ALU.mult, None,
                 [B["qpeT"], maskcol], [qpadS])
        cc_tok = [K.sb(f"cc_tok{i}", [128, 16, 256], BF16) for i in range(2)]
        ccT = [K.sb(f"ccT{i}", [128, 2, 2048], BF16) for i in range(2)]
        ckp = [K.sb(f"ckp{i}", [128, 16, 32], BF16) for i in range(2)]
        ckp4 = [K.sb(f"ckp4{i}", [128, 16, 4, 32], BF16) for i in range(1)] * 2
        ckpT = [K.sb(f"ckpT{i}", [128, 2048], BF16) for i in range(1)] * 2
        PTs = [K.sb(f"PTs{i}", [128, 256], BF16) for i in range(3)]
        rdenS = K.sb("rdenS", [128, 256], F32)
        olS = K.sb("olS", [128, 2, 256], BF16)
        pti = 0
        for b in range(NSB):
            bp = b % 2
            for half in range(2):
                K.load(cc_tok[bp][:, half * 8:(half + 1) * 8, :],
                       caches["ckv"][b, half * 1024:(half + 1) * 1024, :].rearrange("(kt p) c -> p kt c", p=128), None, (),
                       [cc_tok[bp]], queue="pool")
            K.load(ckp[bp][:], caches["kpe"][b].rearrange("(kt p) r -> p kt r", p=128), None, (), [ckp[bp]], queue="pool")
            K.cp(ckp4[bp][:], ckp[bp][:].unsqueeze(2).to_broadcast([128, 16, 4, 32]), [ckp[bp]], [ckp4[bp]])
            for kt in range(16):
                gt_ = gen.next()
                gt_bf = gt_[:].bitcast(BF16)
                for c in range(2):
                    K.tr(gt_bf[:, c * 128:(c + 1) * 128], cc_tok[bp][:, kt, c * 128:(c + 1) * 128], ident_bf[:], [cc_tok[bp], ident_bf], [gt_])
                K.tr(gt_bf[:, 256:384], ckp4[bp][:, kt, :, :].rearrange("p a r -> p (a r)"), ident_bf[:], [ckp4[bp], ident_bf], [gt_])
                K.cp(ccT[bp][:, :, kt * 128:(kt + 1) * 128], gt_bf[:, 0:256].rearrange("p (c t) -> p c t", c=2), [gt_], [ccT[bp]], eng="act")
                K.cp(ckpT[bp][:, kt * 128:(kt + 1) * 128], gt_bf[:, 256:384], [gt_], [ckpT[bp]], eng="act")
            qs = slice(b * 32, (b + 1) * 32)
            for kt in range(17):
                pS = psSc.next()
                if kt < 16:
                    l0, l1, l2 = ccT[bp][:, 0, kt * 128:(kt + 1) * 128], ccT[bp][:, 1, kt * 128:(kt + 1) * 128], ckpT[bp][:, kt * 128:(kt + 1) * 128]
                    ltoks = [ccT[bp], ckpT[bp]]
                    vtok, vt = cc_tok[bp], cc_tok[bp][:, kt, :]
                else:
                    l0, l1, l2 = cT_all[:, 0, SEQ:SEQ + 128], cT_all[:, 1, SEQ:SEQ + 128], kpT4_all[:, SEQ:SEQ + 128]
                    ltoks = [cT_all, kpT4_all]
                    vtok, vt = c_tok_all, c_tok_all[:, NTP, :]
                K.mm(pS[:, 0:256], l0, qaS[:, :, 0, qs], True, False, ltoks + [qaS], [pS])
                K.mm(pS[:, 0:256], l1, qaS[:, :, 1, qs], False, False, ltoks + [qaS], [pS])
                K.mm(pS[:, 0:256], l2, qpadS[:, :, qs], False, kt < 16, ltoks + [qpadS], [pS])
                if kt == 16:
                    K.mm(pS[:, 0:256], ident_bf[:], maskK[:, b, :], False, True, [ident_bf, maskK], [pS])
                P_ = PTs[pti % 3]
                pti += 1
                K.actv(P_[:], pS[:, 0:256], AF.Exp, [pS], [P_], scale=MLA_SCALE)
                K.mm(psOa[:, 0:256], vt[:, 0:128], P_[:], kt == 0, kt == 16, [vtok, P_], [psOa])
                K.mm(psOb[:, 0:256], vt[:, 128:256], P_[:], kt == 0, kt == 16, [vtok, P_], [psOb])
                K.mm(psDn[:, 0:256], ones_bf[:], P_[:], kt == 0, kt == 16, [ones_bf, P_], [psDn])
            K.op("dve", lambda e: e.reciprocal(rdenS[:], psDn[:, 0:256]), [psDn], [rdenS])
            K.tt(olS[:, 0, :], psOa[:, 0:256], rdenS[:], ALU.mult, [psOa, rdenS], [olS])
            K.tt(olS[:, 1, :], psOb[:, 0:256], rdenS[:], ALU.mult, [psOb, rdenS], [olS])
            for hp in range(4):
                pm = gen.next()
                for hh in range(2):
                    h = 2 * hp + hh
                    for cc in range(2):
                        K.mm(pm[hh * 64:hh * 64 + 64, 0:32], w_uv[:, cc, h * 64:(h + 1) * 64], olS[:, cc, h * 32:(h + 1) * 32],
                             cc == 0, cc == 1, [w_uv, olS], [pm])
                K.cp(B["mixT"][:, 4 + hp, qs], pm[:, 0:32], [pm], [B["mixT"]], eng="act")
        out_proj(B, 128, SEQ)


def odd_consts(SEQ):
    pos = token_positions(SEQ)
    NTOK = SEQ + 128
    C, S = rope_tables(pos, 32, 32)
    ropeQC, ropeQS = np.tile(C, (4, 1)), np.tile(S, (4, 1))
    inv = ROPE_THETA ** (-np.arange(0, 32, 2, dtype=np.float32) / 32)
    ang = pos.astype(np.float32)[:, None] * inv.astype(np.float32)[None, :]
    cos, sin = np.cos(ang).astype(np.float32), np.sin(ang).astype(np.float32)
    ropeK = np.concatenate([cos, cos, -sin, sin], axis=1).astype(np.float32)
    rcnt = np.stack([1.0 / np.minimum(pos + 1, w).astype(np.float32) for w in (2, 4, 8, 16)]).astype(np.float32)
    maskcol = (np.arange(128)[:, None] // 32 == np.arange(4)[None, :]).astype(np.float32)
    k = np.arange(128)[:, None]
    q = np.arange(512)[None, :]
    mask4 = np.zeros((128, 4, 512), np.float32)
    for d in range(4):
        kchunk = 2 * d + (k >= 64)
        qchunk = 2 * (q // 128) + ((q % 128) >= 64)
        mask4[:, d, :] = np.where(kchunk <= qchunk, 0.0, MASKV)
    maskK = np.zeros((128, NSB, 256), np.float32)
    for b in range(NSB):
        maskK[:, b, :] = np.where((np.arange(128) // 32 == b)[:, None], 0.0, MASKV)
    return {"ident": np.eye(128, dtype=np.float32), "ropeQC": ropeQC, "ropeQS": ropeQS, "ropeK": ropeK, "rcnt": rcnt,
            "maskcol": maskcol, "mask4": mask4, "maskK": maskK}


def odd_weights(w_in, w_out, pool_w, pool_scale, q_norm, kv_norm, w_uq, w_uk, w_uv, gmix):
    uq = w_uq.reshape(512, 8, 96)
    w_uqN = np.ascontiguousarray(uq[:, :, :64].reshape(512, 512))
    w_uqR = np.ascontiguousarray(uq[:, :, 64:].reshape(512, 256))
    return {"w_in": w_in, "gmix": gmix.reshape(1, 1024), "w_out": w_out, "pool_w": pool_w, "pscale": pool_scale.reshape(4, 128),
            "qn": q_norm.reshape(1, 512), "kvn": kv_norm.reshape(1, 256), "w_uqN": w_uqN, "w_uqR": w_uqR,
            "w_uqRP": perm_rope_cols(w_uqR, 32, 32), "w_uk": np.ascontiguousarray(w_uk.reshape(256, 512)),
            "w_uv": np.ascontiguousarray(w_uv.reshape(256, 512))}


SEQ_FULL = 4096
N_CORES = 8


def build(SEQ, upto=5):
    cfg = {"SEQ": SEQ}
    NTOK = SEQ + 128
    K = KB()
    consts = {}
    for pre, d in (("s_", swa_consts(SEQ)), ("f_", s5_consts()), ("o_", odd_consts(SEQ))):
        for k, v in d.items():
            consts[pre + k] = v
    consts["iota16"] = np.tile(np.arange(16, dtype=np.float32), (128, 1))
    C = {k: K.const(k, v) for k, v in consts.items()}
    I = lambda n, shp: K.inp(n, shp, F32)
    O = lambda n, shp: K.outp(n, shp, F32)
    x_all = I("x_all", [NTOK, 1024])
    w_in0, w_in0P, gmix0, sink = I("w_in0", [1024, 1280]), I("w_in0P", [1024, 640]), I("gmix0", [1, 1024]), I("sink", [1, 8])
    ck, cv = I("ck", [NSB, 128, 128]), I("cv", [NSB, 128, 128])
    attT = K.outp("attT_scr", [512, NTOK], BF16)
    uT = K.outp("uT_scr", [512, NTOK], BF16)
    y = O("y", [NTOK, 1024])
    x1 = x2 = x3 = x4 = y
    outs0 = {"k_p": O("k_p", [128, 128]), "v_p": O("v_p", [128, 128]), "k_s": O("k_s", [NSB, 128, 128]), "v_s": O("v_s", [NSB, 128, 128])}
    phase_swa(K, cfg, x_all, w_in0, w_in0P, gmix0, sink, ck, cv, C["s_ropeC"], C["s_ropeS"], C["s_maskA"], C["s_maskB"], C["s_maskS"],
              C["s_ident"], attT, uT, outs0)
    P = {"lam_re": I("lam_re", [16, 128]), "lam_im": I("lam_im", [16, 128]), "log_dt": I("log_dt", [16, 2]),
         "b_re": I("b_re", [2048, 16]), "b_im": I("b_im", [2048, 16]), "c_re": I("c_re", [512, 64]), "c_im": I("c_im", [512, 64]),
         "dsk": I("dsk", [4, 128]), "w_glu": I("w_glu", [512, 512]), "b_glu": I("b_glu", [4, 128]), "w_out": I("w_out0", [1024, 1024]),
         "h0_re": I("h0_re", [NSB, 16, 128]), "h0_im": I("h0_im", [NSB, 16, 128])}
    outs1 = {"hp_re": O("hp_re", [16, 128]), "hp_im": O("hp_im", [16, 128]), "hs_re": O("hs_re", [NSB, 16, 128]), "hs_im": O("hs_im", [NSB, 16, 128])}
    if upto >= 2:
        phase_s5(K, cfg, x_all, uT, attT, x1, P, C["f_ident"], C["f_iotaL"], outs1)
    peer_in = []
    for l in range(2):
        peer_in.append({"wq": I(f"pwq{l}", [1024, 1024]), "keys": I(f"pkeys{l}", [8, 2, 128, 64]), "u": I(f"pu{l}", [N_EXPERTS, 1024]),
                        "v": I(f"pv{l}", [N_EXPERTS, 1024]), "g": I(f"gffn{l}", [1, 1024])})
    gfin = I("gfin", [1, 1024])
    pi = peer_in[0]
    tabs = [K.scratch(f"peer_tab{l}", [N_EXPERTS, 2048], BF16) for l in range(2)]
    if upto >= 3:
      phase_convert(K, [(peer_in[l]["u"], peer_in[l]["v"], tabs[l]) for l in range(2)])
      phase_peer(K, x1, x2, NTOK // 128, pi["wq"], pi["keys"], tabs[0], pi["g"], C["s_ident"], C["s_ident"], C["iota16"])
    wshapes = {"w_in": [1024, 1312], "gmix": [1, 1024], "w_out": [1024, 1024], "pool_w": [4, 128, 128], "pscale": [4, 128], "qn": [1, 512],
               "kvn": [1, 256], "w_uqN": [512, 512], "w_uqR": [512, 256], "w_uqRP": [512, 256], "w_uk": [256, 512], "w_uv": [256, 512]}
    Wd = {k: I("W1_" + k, shp) for k, shp in wshapes.items()}
    Cd = {k[2:]: v for k, v in C.items() if k.startswith("o_")}
    caches = {"pool": I("c_pool", [NSB, 15, 512]), "ckv": I("c_ckv", [NSB, PAST_LEN, 256]), "kpe": I("c_kpe", [NSB, PAST_LEN, 32])}
    outs3 = {"pool_p": O("pool_p", [15, 512]), "ckv_p": O("ckv_p", [SEQ, 256]), "kpe_p": O("kpe_p", [SEQ, 32]),
             "pool_s": O("pool_s", [NSB, 15, 512]), "ckv_s": O("ckv_s", [NSB, 32, 256]), "kpe_s": O("kpe_s", [NSB, 32, 32])}
    if upto >= 4:
        phase_odd(K, cfg, x2, x3, Wd, Cd, caches, outs3)
    pi = peer_in[1]
    if upto >= 5:
      phase_peer(K, x3, x4, NTOK // 128, pi["wq"], pi["keys"], tabs[1], pi["g"], C["s_ident"], C["s_ident"], C["iota16"],
               final=(gfin, [(0, NTOK, y)]))
    nc = K.emit()
    return nc, K


def make_in_maps(inp, SEQ, n_cores, K):
    f = lambda a: np.ascontiguousarray(np.asarray(a, dtype=np.float32))
    g0 = lambda k: f(inp[k])[0]
    w_in0 = g0("w_in_even")
    shared = {
        "w_in0": w_in0, "w_in0P": perm_rope_cols(w_in0[:, :640], 64, 16), "gmix0": f(inp["norm_mix"])[0:1], "sink": g0("swa_sink").reshape(1, 8),
        "lam_re": g0("s5_lam_re").reshape(16, 128), "lam_im": g0("s5_lam_im").reshape(16, 128), "log_dt": g0("s5_log_dt").reshape(16, 2),
        "b_re": g0("s5_b_re").reshape(2048, 16), "b_im": g0("s5_b_im").reshape(2048, 16),
        "c_re": g0("s5_c_re").reshape(512, 64), "c_im": g0("s5_c_im").reshape(512, 64), "dsk": g0("s5_d").reshape(4, 128),
        "w_glu": g0("s5_w_glu"), "b_glu": g0("s5_b_glu").reshape(4, 128), "w_out0": g0("w_out_even"), "gfin": f(inp["norm_final"]).reshape(1, 1024),
    }
    for l in range(2):
        shared[f"pwq{l}"] = f(inp["peer_w_q"])[l]
        shared[f"pkeys{l}"] = f(inp["peer_keys"])[l]
        shared[f"pu{l}"] = f(inp["peer_u"])[l]
        shared[f"pv{l}"] = f(inp["peer_v"])[l]
        shared[f"gffn{l}"] = f(inp["norm_ffn"])[l:l + 1]
    W1 = odd_weights(g0("w_in_odd"), g0("w_out_odd"), g0("pool_w"), g0("pool_scale"), g0("mla_q_norm"), g0("mla_kv_norm"), g0("mla_w_uq"),
                     g0("mla_w_uk"), g0("mla_w_uv"), f(inp["norm_mix"])[1])
    for k, v in W1.items():
        shared["W1_" + k] = np.ascontiguousarray(v)
    shared.update(K.consts)
    xp, xs = f(inp["x_prompt"]), f(inp["x_sample"])
    maps = []
    for c in range(n_cores):
        sb = slice(NSB * c, NSB * (c + 1))
        m = dict(shared)
        m["x_all"] = np.ascontiguousarray(np.concatenate([xp[c, :SEQ], xs[sb].reshape(NSB * DEC_SEQ, 1024)], 0))
        m["ck"] = np.ascontiguousarray(g0("cache_swa_k")[sb].reshape(NSB, 128, 128))
        m["cv"] = np.ascontiguousarray(g0("cache_swa_v")[sb].reshape(NSB, 128, 128))
        m["h0_re"] = np.ascontiguousarray(g0("state_ssm_re")[sb].reshape(NSB, 16, 128))
        m["h0_im"] = np.ascontiguousarray(g0("state_ssm_im")[sb].reshape(NSB, 16, 128))
        m["c_pool"] = np.ascontiguousarray(g0("state_pool")[sb])
        m["c_ckv"] = np.ascontiguousarray(g0("cache_mla_ckv")[sb])
        m["c_kpe"] = np.ascontiguousarray(g0("cache_mla_kpe")[sb])
        maps.append(m)
    return maps


def assemble(results, SEQ, n_cores):
    R = [{k: np.asarray(v) for k, v in r.items()} for r in results]
    st = lambda fn: np.stack([fn(r) for r in R])
    cat = lambda fn: np.concatenate([fn(r) for r in R], 0)
    y_p = st(lambda r: r["y"][:SEQ])
    y_s = cat(lambda r: r["y"][SEQ:].reshape(NSB, DEC_SEQ, 1024))
    out = (y_p, y_s,
           st(lambda r: r["k_p"].reshape(128, 2, 64))[None], st(lambda r: r["v_p"].reshape(128, 2, 64))[None],
           st(lambda r: r["hp_re"].reshape(32, 64))[None], st(lambda r: r["hp_im"].reshape(32, 64))[None],
           st(lambda r: r["pool_p"])[None], st(lambda r: r["ckv_p"])[None], st(lambda r: r["kpe_p"])[None],
           cat(lambda r: r["k_s"].reshape(NSB, 128, 2, 64))[None], cat(lambda r: r["v_s"].reshape(NSB, 128, 2, 64))[None],
           cat(lambda r: r["hs_re"].reshape(NSB, 32, 64))[None], cat(lambda r: r["hs_im"].reshape(NSB, 32, 64))[None],
           cat(lambda r: r["pool_s"])[None], cat(lambda r: r["ckv_s"])[None], cat(lambda r: r["kpe_s"])[None])
    return tuple(np.ascontiguousarray(o.astype(np.float32)) for o in out)


def kernel(_upto=5, **inputs):
    SEQ = int(np.asarray(inputs["x_prompt"]).shape[1])
    n_cores = int(np.asarray(inputs["x_prompt"]).shape[0])
    nc, K = build(SEQ, _upto)
    print("n sems", len(K.dsem) + 4, {e: len(K.q[e]) for e in ENGS}, flush=True)
    maps = make_in_maps(inputs, SEQ, n_cores, K)
    res = run_bass_kernel_spmd(nc, maps, core_ids=list(range(n_cores)))
    return assemble(res.results, SEQ, n_cores)
```

```python
import numpy as np
from contextlib import ExitStack, contextmanager
import concourse.bass as bass
import concourse.mybir as mybir
from concourse.bass_utils import run_bass_kernel_spmd

F32 = mybir.dt.float32
BF16 = mybir.dt.bfloat16
I32 = mybir.dt.int32
U32 = mybir.dt.uint32
AF = mybir.ActivationFunctionType
ALU = mybir.AluOpType
AX = mybir.AxisListType

DEBUG_ISA = False
DEBUG_SEM = None
ENGS = ("pe", "act", "dve", "pool", "sp")
CENG = ("pe", "act", "dve", "pool")

D_MODEL = 1024
RMS_EPS = 1e-6
N_EXPERTS = 16384


class KB:
    def __init__(self):
        self.nc = bass.Bass("TRN2", target_bir_lowering=False)
        self.es = ExitStack()
        self.q = {e: [] for e in ENGS}
        self.csem = {e: self.es.enter_context(self.nc.semaphore("c_" + e)) for e in CENG}
        self.ccnt = {e: 0 for e in CENG}
        self.dsem = {}
        self.dcnt = {}
        self.waited = {e: {} for e in ENGS}
        self.lastw = {}
        self.readers = {}
        self.consts = {}
        self.dram_in = {}
        self.dram_out = {}
        self.stack = [self.es]
        self.uid = 0
        self.bufsem = {}
        self.dfree = {"sw": [], "hw": []}
        self.psum_ids = set()

    @staticmethod
    def _where():
        import sys
        f = sys._getframe(2)
        out = []
        while f is not None and len(out) < 4:
            if f.f_code.co_name not in ("op", "dma", "mm", "tr", "actv", "tt", "ts", "stt", "cp", "memset", "load"):
                out.append(f.f_lineno)
            f = f.f_back
        return out

    def sb(self, name, shape, dtype):
        self.uid += 1
        return self.stack[-1].enter_context(self.nc.sbuf_tensor(f"{name}_{self.uid}", list(shape), dtype))

    def ps(self, name, shape, dtype):
        self.uid += 1
        t = self.stack[-1].enter_context(self.nc.psum_tensor(f"{name}_{self.uid}", list(shape), dtype))
        self.psum_ids.add(id(t))
        return t

    def inp(self, name, shape, dtype):
        t = self.nc.dram_tensor(name, list(shape), dtype, kind="ExternalInput")
        self.dram_in[name] = t
        return t.ap()

    def outp(self, name, shape, dtype):
        t = self.nc.dram_tensor(name, list(shape), dtype, kind="ExternalOutput")
        self.dram_out[name] = t
        return t.ap()

    def scratch(self, name, shape, dtype):
        t = self.nc.dram_tensor(name, list(shape), dtype, kind="Internal")
        return t.ap()

    def const(self, name, arr):
        arr = np.ascontiguousarray(arr)
        dt = {np.dtype(np.float32): F32, np.dtype(np.int32): I32}[arr.dtype]
        self.consts[name] = arr
        return self.inp(name, arr.shape, dt)

    @contextmanager
    def phase(self):
        st = ExitStack()
        self.stack.append(st)
        yield
        self.stack.pop()
        self.barrier()
        st.close()

    def _sem(self, key):
        return self.csem[key] if key in self.csem else self.dsem[key]

    @staticmethod
    def _k(b):
        return b if isinstance(b, (str, tuple)) else id(b)

    def _toks(self, bs):
        return [self._k(b) for b in bs if not (isinstance(b, tuple) and b and b[0] == "dram")]

    def _dsem_for(self, buf, queue):
        qt = "sw" if queue == "pool" else "hw"
        k = (buf, qt)
        if k not in self.bufsem:
            if self.dfree[qt]:
                sk = self.dfree[qt].pop()
            else:
                sk = "d%s%d" % (qt, len(self.dsem))
                self.dsem[sk] = self.es.enter_context(self.nc.semaphore(sk))
                self.dcnt[sk] = 0
            self.bufsem[k] = sk
        return self.bufsem[k]

    def _deps(self, eng, reads, writes, own=None):
        waits = {}

        def need(k, v, waw=False):
            if (waw and k == own) or (eng == "pe" and k == "pe"):
                return
            if k in self.dcnt:
                v = self.dcnt[k]
            if self.waited[eng].get(k, 0) >= v:
                return
            if waits.get(k, 0) < v:
                waits[k] = v

        for b in reads:
            for k, v in self.lastw.get(b, {}).items():
                need(k, v)
            if b in self.psum_ids:
                for k, v in self.readers.get(b, {}).items():
                    if k != eng:
                        need(k, v)
        for b in writes:
            for k, v in self.lastw.get(b, {}).items():
                need(k, v, True)
            for k, v in self.readers.get(b, {}).items():
                need(k, v)
        for k, v in waits.items():
            self.waited[eng][k] = v
        return list(waits.items())

    def _commit(self, tok, reads, writes):
        for b in reads:
            d = self.readers.setdefault(b, {})
            if d.get(tok[0], 0) < tok[1]:
                d[tok[0]] = tok[1]
        for b in writes:
            self.lastw[b] = {tok[0]: tok[1]}
            self.readers[b] = {}

    def op(self, eng, fn, reads=(), writes=()):
        reads, writes = self._toks(reads), self._toks(writes)
        waits = self._deps(eng, reads, writes)
        self.ccnt[eng] += 1
        tok = (eng, self.ccnt[eng])
        self._commit(tok, reads, writes)
        self.q[eng].append((waits, fn, (eng, 1), self._where()))

    def dma(self, queue, fn, sem, reads=(), writes=()):
        reads, writes = self._toks(reads), self._toks(writes)
        buf = writes[0] if writes else reads[0]
        sk = self._dsem_for(buf, queue)
        waits = self._deps(queue, reads, writes, own=sk)
        self.dcnt[sk] += 16
        tok = (sk, self.dcnt[sk])
        self._commit(tok, reads, writes)
        self.q[queue].append((waits, fn, (sk, 16), self._where()))

    def barrier(self):
        allv = dict(self.ccnt)
        allv.update(self.dcnt)
        for e in ENGS:
            waits = []
            for k, v in allv.items():
                if v > 0 and self.waited[e].get(k, 0) < v:
                    waits.append((k, v))
                    self.waited[e][k] = v
            if waits:
                self.q[e].append((waits, None, None, 0))
        self.lastw = {}
        self.readers = {}
        for (bk, qt), sk in self.bufsem.items():
            self.dfree[qt].append(sk)
        self.bufsem = {}

    def emit(self):
        self.barrier()
        nc = self.nc
        with nc.Block() as block:
            def mk(name):
                def body(eng):
                    for waits, fn, inc, _w in self.q[name]:
                        for k, v in waits:
                            if DEBUG_SEM and k == DEBUG_SEM:
                                print("SEMDBG-WAIT:", name, k, v)
                            eng.wait_ge(self._sem(k), v)
                        if fn is not None:
                            ins = fn(eng)
                            if DEBUG_ISA and isinstance(ins.ins, mybir.InstISA):
                                print("InstISA:", name, type(ins.ins).__name__, str(ins)[:300])
                            if DEBUG_SEM and inc[0] == DEBUG_SEM:
                                print("SEMDBG:", name, [w for w in waits], str(ins)[:260])
                            ins.then_inc(self._sem(inc[0]), inc[1])
                return body
            block.tensor(mk("pe"))
            block.scalar(mk("act"))
            block.vector(mk("dve"))
            block.gpsimd(mk("pool"))
            block.sync(mk("sp"))
        self.es.close()
        return nc

    def mm(self, out, lhsT, rhs, start, stop, reads, writes):
        self.op("pe", lambda e: e.matmul(out, lhsT, rhs, start=start, stop=stop), reads, writes)

    def tr(self, out, in_, ident, reads, writes):
        self.op("pe", lambda e: e.transpose(out, in_, ident), reads, writes)

    def actv(self, out, in_, func, reads, writes, bias=None, scale=None, accum_out=None, eng="act"):
        kw = {}
        if bias is not None:
            kw["bias"] = bias
        if scale is not None:
            kw["scale"] = scale
        if accum_out is not None:
            kw["accum_out"] = accum_out
        self.op(eng, lambda e: e.activation(out, in_, func, **kw), reads, writes)

    def tt(self, out, in0, in1, op, reads, writes, eng="dve"):
        self.op(eng, lambda e: e.tensor_tensor(out, in0, in1, op), reads, writes)

    def ts(self, out, in0, s1, s2, op0, op1, reads, writes, eng="dve"):
        if op1 is None:
            self.op(eng, lambda e: e.tensor_scalar(out, in0, s1, None, op0), reads, writes)
        else:
            self.op(eng, lambda e: e.tensor_scalar(out, in0, s1, s2, op0, op1), reads, writes)

    def stt(self, out, in0, scalar, in1, op0, op1, reads, writes):
        self.op("dve", lambda e: e.scalar_tensor_tensor(out, in0, scalar, in1, op0, op1), reads, writes)

    def cp(self, out, in_, reads, writes, eng="dve"):
        if eng == "act":
            self.op("act", lambda e: e.activation(out, in_, AF.Copy), reads, writes)
        else:
            self.op(eng, lambda e: e.tensor_copy(out, in_), reads, writes)

    def memset(self, ap, val, writes, eng="dve"):
        self.op(eng, lambda e: e.memset(ap, val), (), writes)

    def load(self, out, in_, sem, reads, writes, queue="sp", **kw):
        self.dma(queue, lambda e: e.dma_start(out=out, in_=in_, **kw), sem, reads, writes)


def bcast_rows(ap_row, p=128):
    return bass.AP(ap_row.tensor, ap_row.offset, [[0, p]] + [list(x) for x in ap_row.ap[-1:]])


def phase_convert(K, pairs):
    with K.phase():
        NSL = 4
        ub = [K.sb(f"cvu{i}", [128, 4, 1024], BF16) for i in range(NSL)]
        vb = [K.sb(f"cvv{i}", [128, 4, 1024], BF16) for i in range(NSL)]
        i = 0
        for (u_d, v_d, tab) in pairs:
            for ch in range(N_EXPERTS // 512):
                s_ = i % NSL
                i += 1
                rows = slice(ch * 512, (ch + 1) * 512)
                K.dma("pool", lambda e, s_=s_, rows=rows, u_d=u_d: e.dma_start(
                    out=ub[s_][:], in_=u_d[rows, :].rearrange("(p j) d -> p j d", j=4), max_dma_last_dim=4096), None, (), [ub[s_]])
                K.dma("pool", lambda e, s_=s_, rows=rows, v_d=v_d: e.dma_start(
                    out=vb[s_][:], in_=v_d[rows, :].rearrange("(p j) d -> p j d", j=4), max_dma_last_dim=4096), None, (), [vb[s_]])
                K.load(tab[rows, 0:1024].rearrange("(p j) d -> p j d", j=4), ub[s_][:], None, [ub[s_]], ())
                K.load(tab[rows, 1024:2048].rearrange("(p j) d -> p j d", j=4), vb[s_][:], None, [vb[s_]], ())


def phase_peer(K, x_in, x_out, ntiles, wq_d, keys_d, tab_d, gffn_d, ident_bf_d, ident_f_d, iota16_d,
               final=None):
    nc = K.nc
    with K.phase():
        ident_bf = K.sb("identbf", [128, 128], BF16)
        ident_f = K.sb("identf", [128, 128], F32)
        iota16 = K.sb("iota16", [128, 16], F32)
        gffn = K.sb("gffn", [128, 1024], F32)
        wq = K.sb("wq", [128, 8, 1024], BF16)
        keysBD = K.sb("keysBD", [128, 8, 256], BF16)
        K.load(ident_f[:], ident_f_d, "c0", (), [ident_f])
        K.load(ident_bf[:], ident_f_d, "c0", (), [ident_bf], queue="pool")
        K.load(iota16[:], iota16_d, "c0", (), [iota16])
        K.load(gffn[:], bcast_rows(gffn_d), "c0", (), [gffn])
        for c in range(8):
            K.load(wq[:, c, :], wq_d[c * 128:(c + 1) * 128, :], "c1", (), [wq], queue="pool")
        if final is not None:
            gfin = K.sb("gfin", [128, 1024], F32)
            K.load(gfin[:], bcast_rows(final[0]), "c0", (), [gfin])

        psB = K.ps("psB", [128, 2048], F32)
        psO = K.ps("psO", [128, 1024], F32)
        psA = K.ps("psA", [128, 512], F32)
        psA_bf = psA[:].bitcast(BF16)

        def two(name, shape, dt):
            return [K.sb(name + str(i), shape, dt) for i in range(2)]
        xt = two("xt", [128, 1024], F32)
        hn = two("hn", [128, 1024], BF16)
        hnT = two("hnT", [128, 8, 128], BF16)
        qT = two("qT", [128, 8, 128], BF16)
        eid = two("eid", [128, 128], I32)
        gate = two("gate", [128, 128], F32)
        act = two("actv", [128, 128], F32)
        wgt = two("wgt", [128, 128], F32)
        wg2 = two("wg2", [128, 128], F32)
        ss = two("ss", [128, 4], F32)
        junk = K.sb("junk", [128, 1024], BF16)
        junk2 = K.sb("junk2", [128, 1024], BF16)
        s2 = K.sb("s2", [128, 128], F32)
        sv = K.sb("sv", [128, 16, 16], F32)
        si = K.sb("si", [128, 16, 16], U32)
        sif = K.sb("sif", [128, 16, 16], BF16)
        cand = K.sb("cand", [128, 8, 256], F32)
        cand2 = K.sb("cand2", [128, 256], F32)
        cv = K.sb("cv", [128, 8, 16], F32)
        ci = K.sb("ci", [128, 8, 16], U32)
        ca = K.sb("ca", [128, 8, 16], U32)
        cb = K.sb("cb", [128, 8, 16], U32)
        oh = K.sb("oh", [128, 8, 16, 16], BF16)
        oh2 = K.sb("oh2", [128, 8, 16, 16], BF16)
        i1f = K.sb("i1f", [128, 8, 16], F32)
        i2f = K.sb("i2f", [128, 8, 16], F32)
        ce = K.sb("ce", [128, 8, 16], F32)
        csum = K.sb("csum", [128, 8], F32)
        NS = 16
        UV = [K.sb(f"UV{i}", [128, 2048], BF16) for i in range(NS)]
        NDG = 4
        dg = [K.sb(f"dg{i}", [128, 128], BF16) for i in range(NDG)]
        xo = two("xo", [128, 1024], F32)
        if final is not None:
            yo = [K.sb("yo", [128, 1024], F32)] * 2

        def prep(i):
            p = i % 2
            X, HN, HT, QT = xt[p], hn[p], hnT[p], qT[p]
            K.load(X[:], x_in[i * 128:(i + 1) * 128, :], f"xl{p}", (), [X])
            K.actv(junk2[:], X[:], AF.Square, [X], [ss[p], junk2], accum_out=ss[p][:, 0:1])
            K.actv(ss[p][:, 1:2], ss[p][:, 0:1], AF.Sqrt, [ss[p], eps_t], [ss[p]], scale=1.0 / D_MODEL, bias=eps_t[:, 0:1])
            K.op("dve", lambda e: e.reciprocal(ss[p][:, 2:3], ss[p][:, 1:2]), [ss[p]], [ss[p]])
            K.stt(HN[:], X[:], ss[p][:, 2:3], gffn[:], ALU.mult, ALU.mult, [X, ss[p], gffn], [HN])
            yield
            for c in range(8):
                K.tr(psA_bf[:, c * 128:(c + 1) * 128], HN[:, c * 128:(c + 1) * 128], ident_bf[:],
                     [HN, ident_bf], [psA])
            K.cp(HT[:].rearrange("p c t -> p (c t)"), psA_bf[:, :], [psA], [HT], eng="act")
            for co in range(8):
                for kc in range(8):
                    K.mm(psB[:, co * 128:(co + 1) * 128], wq[:, kc, co * 128:(co + 1) * 128], HT[:, kc, :],
                         kc == 0, kc == 7, [wq, HT], [psB])
            K.cp(QT[:].rearrange("p c t -> p (c t)"), psB[:, 0:1024], [psB], [QT], eng="act")
            for c in range(8):
                K.mm(psB[:, c * 256:(c + 1) * 256], QT[:, c, :], keysBD[:, c, :], True, True, [QT, keysBD], [psB])
            yield
            for gq in range(4):
                for g_ in range(4):
                    g = gq * 4 + g_
                    S = psB[:, g * 128:(g + 1) * 128]
                    K.op("dve", lambda e, S=S, g=g: e.max(sv[:, g, 0:8], S), [psB], [sv])
                    K.op("dve", lambda e, S=S, g=g: e.match_replace(s2[:], sv[:, g, 0:8], S, -1e30), [psB, sv], [s2])
                    K.op("dve", lambda e, g=g: e.max(sv[:, g, 8:16], s2[:]), [s2], [sv])
                    K.op("dve", lambda e, S=S, g=g: e.max_index(si[:, g, 0:8], sv[:, g, 0:8], S), [psB, sv], [si])
                    K.op("dve", lambda e, S=S, g=g: e.max_index(si[:, g, 8:16], sv[:, g, 8:16], S), [psB, sv], [si])
                yield
            sv4 = sv[:].rearrange("n (h p) k -> n h p k", p=2)
            K.tt(cand[:].rearrange("n h (a b) -> n h a b", b=16),
                 sv4[:, :, 0, :].unsqueeze(3).to_broadcast([128, 8, 16, 16]),
                 sv4[:, :, 1, :].unsqueeze(2).to_broadcast([128, 8, 16, 16]), ALU.add, [sv], [cand])
            K.cp(sif[:], si[:], [si], [sif])
            for h in range(8):
                C = cand[:, h, :]
                K.op("dve", lambda e, C=C, h=h: e.max(cv[:, h, 0:8], C), [cand], [cv])
                K.op("dve", lambda e, C=C, h=h: e.match_replace(cand2[:], cv[:, h, 0:8], C, -1e30), [cand, cv], [cand2])
                K.op("dve", lambda e, h=h: e.max(cv[:, h, 8:16], cand2[:]), [cand2], [cv])
                K.op("dve", lambda e, C=C, h=h: e.max_index(ci[:, h, 0:8], cv[:, h, 0:8], C), [cand, cv], [ci])
                K.op("dve", lambda e, C=C, h=h: e.max_index(ci[:, h, 8:16], cv[:, h, 8:16], C), [cand, cv], [ci])
                if h % 4 == 3:
                    yield
            K.ts(ca[:], ci[:], bitc[:, 0:1], None, ALU.logical_shift_right, None, [ci, bitc], [ca])
            K.ts(cb[:], ci[:], bitc[:, 1:2], None, ALU.bitwise_and, None, [ci, bitc], [cb])
            sif4 = sif[:].rearrange("n (h p) k -> n h p k", p=2)
            io = iota16[:].unsqueeze(1).unsqueeze(1).to_broadcast([128, 8, 16, 16])
            for (cx, pp, dst) in ((ca, 0, i1f), (cb, 1, i2f)):
                K.tt(oh[:], cx[:].unsqueeze(3).to_broadcast([128, 8, 16, 16]), io, ALU.is_equal, [cx, iota16], [oh])
                K.tt(oh2[:], oh[:], sif4[:, :, pp, :].unsqueeze(2).to_broadcast([128, 8, 16, 16]), ALU.mult,
                     [oh, sif], [oh2])
                K.op("dve", lambda e, dst=dst: e.tensor_reduce(dst[:], oh2[:], AX.X, ALU.add), [oh2], [dst])
            K.stt(eid[p][:].rearrange("n (h k) -> n h k", k=16), i1f[:], 128.0, i2f[:], ALU.mult, ALU.add,
                  [i1f, i2f], [eid[p]])
            yield
            K.tt(ce[:], cv[:], cv[:, :, 0:1].to_broadcast([128, 8, 16]), ALU.subtract, [cv], [ce])
            K.actv(ce[:], ce[:], AF.Exp, [ce], [ce])
            K.op("dve", lambda e: e.tensor_reduce(csum[:], ce[:], AX.X, ALU.add), [ce], [csum])
            K.op("dve", lambda e: e.reciprocal(csum[:], csum[:]), [csum], [csum])
            K.tt(gate[p][:].rearrange("n (h k) -> n h k", k=16), ce[:],
                 csum[:].unsqueeze(2).to_broadcast([128, 8, 16]), ALU.mult, [ce, csum], [gate[p]])
            yield

        eps_t = K.sb("eps_t", [128, 1], F32)
        K.memset(eps_t[:], RMS_EPS, [eps_t])
        bitc = K.sb("bitc", [128, 2], U32)
        K.memset(bitc[:, 0:1], 4, [bitc])
        K.memset(bitc[:, 1:2], 15, [bitc])

        slot_u = [0]
        slot_of = {}
        dgc = [0]

        def gatherU(p, r):
            s = slot_u[0] % NS
            slot_u[0] += 1
            slot_of[(p, r)] = s
            K.dma("pool", lambda e, s=s, r=r: e.indirect_dma_start(
                out=UV[s][:, :], out_offset=None, in_=tab_d,
                in_offset=bass.IndirectOffsetOnAxis(ap=eid[p][:, r:r + 1], axis=0)),
                None, [eid[p]], [UV[s]])
            K.op("dve", lambda e, s=s, r=r: e.scalar_tensor_tensor(
                junk[:], UV[s][:, 0:1024], 1.0, hn[p][:], ALU.mult, ALU.mult, accum_out=act[p][:, r:r + 1]),
                [UV[s], hn[p]], [("act", p, r), junk])

        def gatherV(p, r):
            s = slot_of[(p, r)]
            d = dgc[0] % NDG
            dgc[0] += 1
            K.actv(wg2[p][:, r:r + 1], wgt[p][:, r:r + 1], AF.Copy, [("wgt", p, r), gate[p]], [("wg2", p, r)],
                   scale=gate[p][:, r:r + 1])
            K.actv(dg[d][:], ident_bf[:], AF.Copy, [ident_bf, ("wg2", p, r)], [dg[d]], scale=wg2[p][:, r:r + 1])
            for hf in range(2):
                K.mm(psO[:, hf * 512:(hf + 1) * 512], dg[d][:], UV[s][:, 1024 + hf * 512:1024 + (hf + 1) * 512],
                     r == 0, r == 127, [dg[d], UV[s]], [psO])

        def run(gen):
            for _ in gen:
                pass

        ksc = K.phase()
        ksc.__enter__()
        knat = K.sb("knat", [128, 16, 64], F32)
        K.load(knat[:], keys_d.rearrange("h p k d -> k (h p) d"), "c0", (), [knat])
        K.memset(keysBD[:], 0.0, [keysBD])
        for c in range(8):
            K.tr(psA[:, 0:128], knat[:, 2 * c:2 * c + 2, :].rearrange("k a d -> k (a d)"), ident_f[:],
                 [knat, ident_f], [psA])
            K.cp(keysBD[0:64, c, 0:128], psA[0:64, 0:128], [psA], [keysBD], eng="act")
            K.cp(keysBD[64:128, c, 128:256], psA[64:128, 0:128], [psA], [keysBD], eng="act")

        ksc.__exit__(None, None, None)
        run(prep(0))
        for i in range(ntiles):
            p = i % 2
            nxt = prep(i + 1) if i + 1 < ntiles else iter(())
            for r in range(128):
                gatherU(p, r)
                K.actv(wgt[p][:, r:r + 1], act[p][:, r:r + 1], AF.Gelu, [("act", p, r)], [("wgt", p, r)])
                if r >= 1:
                    gatherV(p, r - 1)
                if r % 16 == 15:
                    next(nxt, None)
            gatherV(p, 127)
            run(nxt)
            K.tt(xo[p][:], psO[:], xt[p][:], ALU.add, [psO, xt[p]], [xo[p]])
            if final is None:
                K.load(x_out[i * 128:(i + 1) * 128, :], xo[p][:], f"xs{p}", [xo[p]], [("dram", "x_out")])
            if final is not None:
                K.actv(junk2[:], xo[p][:], AF.Square, [xo[p]], [ss[p], junk2], accum_out=ss[p][:, 3:4])
                K.actv(ss[p][:, 3:4], ss[p][:, 3:4], AF.Sqrt, [ss[p], eps_t], [ss[p]], scale=1.0 / D_MODEL, bias=eps_t[:, 0:1])
                K.op("dve", lambda e, p=p: e.reciprocal(ss[p][:, 3:4], ss[p][:, 3:4]), [ss[p]], [ss[p]])
                K.stt(yo[p][:], xo[p][:], ss[p][:, 3:4], gfin[:], ALU.mult, ALU.mult, [xo[p], ss[p], gfin], [yo[p]])
                for (row0, nrows, oap) in final[1]:
                    lo, hi = max(row0, i * 128), min(row0 + nrows, (i + 1) * 128)
                    if lo < hi:
                        K.load(oap[lo - row0:hi - row0, :], yo[p][lo - i * 128:hi - i * 128, :], f"ys{p}",
                               [yo[p]], [("dram", "y")])


class Rot:
    def __init__(self, items):
        self.items = list(items)
        self.i = 0

    def next(self):
        t = self.items[self.i % len(self.items)]
        self.i += 1
        return t


CHUNK = 64
SWA_SCALE = 64 ** -0.5
MLA_SCALE = 96 ** -0.5
ROPE_THETA = 500000.0
PAST_LEN = 2048
DEC_SEQ = 32
NSB = 4
MASKV = -30000.0


def rmsnorm_tile(K, X, HN, gb, ssb, eps_t, junk, xtok, hntok, gtok, jtok, width=D_MODEL):
    K.actv(junk, X, AF.Square, [xtok], [ssb, jtok], accum_out=ssb[:, 0:1])
    K.actv(ssb[:, 1:2], ssb[:, 0:1], AF.Sqrt, [ssb, eps_t], [ssb], scale=1.0 / width, bias=eps_t[:, 0:1])
    K.op("dve", lambda e: e.reciprocal(ssb[:, 2:3], ssb[:, 1:2]), [ssb], [ssb])
    K.stt(HN, X, ssb[:, 2:3], gb, ALU.mult, ALU.mult, [xtok, ssb, gtok], [hntok])


def phase_swa(K, cfg, x_all, w_in_d, w_inP_d, gmix_d, sink_d, ck_d, cv_d, ropeC_d, ropeS_d, maskA_d, maskB_d, maskS_d,
              ident_f_d, attT_d, uT_d, outs):
    SEQ = cfg["SEQ"]
    NTOK = SEQ + 128
    NTP = SEQ // 128
    with K.phase():
        ident_f = K.sb("identf", [128, 128], F32)
        ident_bf = K.sb("identbf", [128, 128], BF16)
        K.load(ident_f[:], ident_f_d, "c0", (), [ident_f])
        K.load(ident_bf[:], ident_f_d, "c1", (), [ident_bf], queue="pool")
        gmix = K.sb("gmix", [128, 1024], F32)
        K.load(gmix[:], bcast_rows(gmix_d), "c0", (), [gmix])
        w_in = K.sb("w_in", [128, 8, 1280], BF16)
        w_inP = K.sb("w_inP", [128, 8, 640], BF16)
        for c in range(8):
            K.load(w_in[:, c, 0:640], w_in_d[c * 128:(c + 1) * 128, 0:640], "c1", (), [w_in], queue="pool")
            K.load(w_in[:, c, 640:1280], w_in_d[c * 128:(c + 1) * 128, 640:1280], "c1", (), [w_in], queue="pool")
            K.load(w_inP[:, c, :], w_inP_d[c * 128:(c + 1) * 128, :], "c1", (), [w_inP], queue="pool")
        maskA = K.sb("maskA", [128, 512], BF16)
        maskB = K.sb("maskB", [128, 512], BF16)
        K.load(maskA[:], maskA_d, "c1", (), [maskA], queue="pool")
        K.load(maskB[:], maskB_d, "c1", (), [maskB], queue="pool")
        sinkexp = K.sb("sinkexp", [128, 8], F32)
        K.load(sinkexp[:], bcast_rows(sink_d), "c0", (), [sinkexp])
        K.actv(sinkexp[:], sinkexp[:], AF.Exp, [sinkexp], [sinkexp])
        eps_t = K.sb("eps_t", [128, 1], F32)
        K.memset(eps_t[:], RMS_EPS, [eps_t])

        kT_all = K.sb("kT_all", [64, 2, NTOK], BF16)
        v_all = K.sb("v_all", [128, NTP + 1, 2, 65], BF16)
        K.memset(v_all[:, :, :, 64:65], 1.0, [v_all])
        junk = K.sb("junk", [128, 1024], BF16)
        xt = [K.sb(f"xt{i}", [128, 1024], F32) for i in range(2)]
        ssb = [K.sb(f"ssb{i}", [128, 4], F32) for i in range(2)]
        hn = [K.sb(f"hn{i}", [128, 1024], BF16) for i in range(2)]
        hnT = K.sb("hnT", [128, 8, 512], BF16)
        ropeC = [K.sb(f"ropeC{i}", [64, 512], F32) for i in range(2)]
        ropeS = [K.sb(f"ropeS{i}", [64, 512], F32) for i in range(2)]
        qT = K.sb("qT", [64, 8, 512], BF16)
        t1 = [K.sb(f"t1_{i}", [64, 512], F32) for i in range(2)]
        t2 = [K.sb(f"t2_{i}", [64, 512], F32) for i in range(2)]
        kf32 = K.sb("kf32", [64, 2, 128], F32)
        ktok_p = K.sb("ktok", [128, 128], F32)
        vtok_p = K.sb("vtok", [128, 128], F32)
        ktok_s = K.sb("ktok_s", [128, 128], F32)
        vtok_s = K.sb("vtok_s", [128, 128], F32)
        uT = [K.sb(f"uT{i}", [128, 4, 512], BF16) for i in range(2)]
        attT = [K.sb(f"attT{i}", [128, 4, 512], BF16) for i in range(2)]
        PT = [K.sb(f"PT{i}", [128, 2, 2, 512], BF16) for i in range(2)]
        att_tok = [K.sb(f"att_tok{i}", [128, 512], BF16) for i in range(2)]
        den = K.sb("den", [128, 8], F32)
        kc32 = [K.sb(f"kc32_{i}", [128, 128], F32) for i in range(2)]
        vc32 = [K.sb(f"vc32_{i}", [128, 128], F32) for i in range(2)]
        kcb = [K.sb(f"kcb{i}", [128, 128], BF16) for i in range(2)]
        kcT = K.sb("kcT", [64, NSB, 2, 128], BF16)
        vcb = K.sb("vcb", [128, NSB, 2, 65], BF16)
        K.memset(vcb[:, :, :, 64:65], 1.0, [vcb])
        PTc = K.sb("PTc", [128, NSB, 2, 512], BF16)
        PTn = K.sb("PTn", [128, 2, 512], BF16)
        maskS = K.sb("maskS", [128, NSB + 1, 512], BF16)
        K.load(maskS[:], maskS_d, "c1", (), [maskS], queue="pool")

        psT = K.ps("psT", [128, 512], F32)
        psT_bf = psT[:].bitcast(BF16)
        gen = Rot([K.ps(f"gen{i}", [128, 512], F32) for i in range(4)])
        psOX = K.ps("psOX", [128, 512], F32)
        psOY = K.ps("psOY", [128, 512], F32)
        psAT = K.ps("psAT", [128, 512], F32)
        psAT_bf = psAT[:].bitcast(BF16)

        STOP = cfg.get("stop", 99)
        SSTOP = cfg.get("sstop", 99)
        if STOP <= 0:
            return
        nsup = SEQ // 512
        sups = [(s * 512, 512, False) for s in range(nsup)] + [(SEQ, 128, True)]
        xi = 0
        for si_, (tok0, W, is_s) in enumerate(sups):
            sp = si_ % 2
            ntl = W // 128
            if is_s and STOP <= 5:
                return
            K.load(ropeC[sp][:, 0:W], ropeC_d[:, tok0:tok0 + W], f"rope{sp}", (), [ropeC[sp]])
            K.load(ropeS[sp][:, 0:W], ropeS_d[:, tok0:tok0 + W], f"rope{sp}", (), [ropeS[sp]])
            for tl in range(ntl):
                p = xi % 2
                xi += 1
                K.load(xt[p][:], x_all[tok0 + tl * 128: tok0 + (tl + 1) * 128, :], f"xl{p}", (), [xt[p]])
                rmsnorm_tile(K, xt[p][:], hn[p][:], gmix[:], ssb[p], eps_t, junk[:], xt[p], hn[p], gmix, junk)
                for c in range(8):
                    K.tr(psT_bf[:, c * 128:(c + 1) * 128], hn[p][:, c * 128:(c + 1) * 128], ident_bf[:],
                         [hn[p], ident_bf], [psT])
                K.cp(hnT[:, :, tl * 128:(tl + 1) * 128], psT_bf[:, :].rearrange("p (c t) -> p c t", c=8), [psT], [hnT],
                     eng="act")
            if STOP <= 1 or (is_s and SSTOP <= 1):
                return
            for hh in range(10):
                col0 = hh * 64
                pq, pp = gen.next(), gen.next()
                for kc in range(8):
                    K.mm(pq[0:64, 0:W], w_in[:, kc, col0:col0 + 64], hnT[:, kc, 0:W], kc == 0, kc == 7, [w_in, hnT], [pq])
                for kc in range(8):
                    K.mm(pp[0:64, 0:W], w_inP[:, kc, col0:col0 + 64], hnT[:, kc, 0:W], kc == 0, kc == 7, [w_inP, hnT], [pp])
                a, b = t1[hh % 2], t2[hh % 2]
                K.tt(a[:, 0:W], pq[0:64, 0:W], ropeC[sp][:, 0:W], ALU.mult, [pq, ropeC[sp]], [a])
                K.tt(b[:, 0:W], pp[0:64, 0:W], ropeS[sp][:, 0:W], ALU.mult, [pp, ropeS[sp]], [b])
                if hh < 8:
                    K.tt(qT[:, hh, 0:W], a[:, 0:W], b[:, 0:W], ALU.add, [a, b], [qT])
                else:
                    g = hh - 8
                    K.tt(kT_all[:, g, tok0:tok0 + W], a[:, 0:W], b[:, 0:W], ALU.add, [a, b], [kT_all])
                    if is_s or tok0 + W == SEQ:
                        K.tt(kf32[:, g, :], a[:, W - 128:W], b[:, W - 128:W], ALU.add, [a, b], [kf32])
            if STOP <= 2 or (is_s and SSTOP <= 2):
                return
            for c in range(4):
                pu = gen.next()
                for kc in range(8):
                    K.mm(pu[:, 0:W], w_in[:, kc, 768 + c * 128:768 + (c + 1) * 128], hnT[:, kc, 0:W], kc == 0, kc == 7,
                         [w_in, hnT], [pu])
                K.cp(uT[sp][:, c, 0:W], pu[:, 0:W], [pu], [uT[sp]], eng="act")
            for c in range(4):
                K.load(uT_d[c * 128:(c + 1) * 128, tok0:tok0 + W], uT[sp][:, c, 0:W], f"ust{sp}", [uT[sp]], [("dram", "uT")])
            if STOP <= 3 or (is_s and SSTOP <= 3):
                return
            last_tile = is_s or (tok0 + W == SEQ)
            ktok, vtok = (ktok_s, vtok_s) if is_s else (ktok_p, vtok_p)
            for tl in range(ntl):
                j = tok0 // 128 + tl
                pv = gen.next()
                for kc in range(8):
                    K.mm(pv[:, 0:128], hnT[:, kc, tl * 128:(tl + 1) * 128], w_in[:, kc, 640:768], kc == 0, kc == 7,
                         [w_in, hnT], [pv])
                K.cp(v_all[:, j, :, 0:64], pv[:, 0:128].rearrange("p (g d) -> p g d", g=2), [pv], [v_all], eng="act")
                if last_tile and tl == ntl - 1:
                    K.cp(vtok[:], pv[:, 0:128], [pv], [vtok])
            if last_tile and not (is_s and cfg.get("nok")):
                pk = gen.next()
                for g in range(2):
                    K.tr(pk[:, g * 64:(g + 1) * 64], kf32[:, g, :], ident_f[0:64, 0:64], [kf32, ident_f], [pk])
                K.cp(ktok[:], pk[:, 0:128], [pk], [ktok])
                if not is_s:
                    K.load(outs["k_p"], ktok[:], "ost", [ktok], [("dram", "o")])
                    K.load(outs["v_p"], vtok[:], "ost", [vtok], [("dram", "o")])
                elif not cfg.get("nostore"):
                    for b in range(NSB):
                        K.load(outs["k_s"][b, 96:128, :], ktok[b * 32:(b + 1) * 32, :], "ost", [ktok], [("dram", "o")])
                        K.load(outs["v_s"][b, 96:128, :], vtok[b * 32:(b + 1) * 32, :], "ost", [vtok], [("dram", "o")])
            if STOP <= 4 or (is_s and SSTOP <= 4):
                return
            if not is_s:
                for tl in range(ntl):
                    j = tok0 // 128 + tl
                    ap_ = j % 2
                    jjs = [jj for jj in (j - 1, j) if jj >= 0]
                    for g in range(2):
                        for jj in jjs:
                            pS = gen.next()
                            K.mm(pS[:, :], kT_all[:, g, jj * 128:(jj + 1) * 128], qT[:, 4 * g:4 * g + 4, tl * 128:(tl + 1) * 128],
                                 True, False, [kT_all, qT], [pS])
                            K.mm(pS[:, :], ident_bf[:], (maskA if jj == j - 1 else maskB)[:], False, True,
                                 [ident_bf, maskA, maskB], [pS])
                            K.actv(PT[ap_][:, jj - j + 1, g, :], pS[:, :], AF.Exp, [pS], [PT[ap_]], scale=SWA_SCALE)
                    for h in range(8):
                        g = h // 4
                        po = psOX if h < 4 else psOY
                        for n_, jj in enumerate(jjs):
                            K.mm(po[:, (h % 4) * 65:(h % 4) * 65 + 65],
                                 PT[ap_][:, jj - j + 1, g, (h % 4) * 128:(h % 4) * 128 + 128], v_all[:, jj, g, :],
                                 n_ == 0, n_ == len(jjs) - 1, [PT[ap_], v_all], [po])
                    swa_finish(K, psOX, psOY, den, sinkexp, att_tok[ap_], psAT, psAT_bf, ident_bf, attT[sp], tl, 128)
            else:
                if STOP <= 6:
                    return
                for b in range(NSB):
                    bp = b % 2
                    K.load(kc32[bp][:], ck_d[b], f"kcl{bp}", (), [kc32[bp]])
                    K.load(vc32[bp][:], cv_d[b], f"kcl{bp}", (), [vc32[bp]])
                    K.load(outs["k_s"][b, 0:96, :], kc32[bp][32:128, :], "ost", [kc32[bp]], [("dram", "o")])
                    K.load(outs["v_s"][b, 0:96, :], vc32[bp][32:128, :], "ost", [vc32[bp]], [("dram", "o")])
                    K.cp(kcb[bp][:], kc32[bp][:], [kc32[bp]], [kcb[bp]])
                    K.cp(vcb[:, b, :, 0:64], vc32[bp][:].rearrange("p (g d) -> p g d", g=2), [vc32[bp]], [vcb])
                    for g in range(2):
                        K.tr(psT_bf[0:64, g * 128:(g + 1) * 128], kcb[bp][:, g * 64:(g + 1) * 64], ident_bf[:],
                             [kcb[bp], ident_bf], [psT])
                    K.cp(kcT[:, b, :, :].rearrange("p g t -> p (g t)"), psT_bf[0:64, 0:256], [psT], [kcT], eng="act")
                    for g in range(2):
                        pS = gen.next()
                        K.mm(pS[:, :], kcT[:, b, g, :], qT[:, 4 * g:4 * g + 4, 0:128], True, False, [kcT, qT], [pS])
                        K.mm(pS[:, :], ident_bf[:], maskS[:, b, :], False, True, [ident_bf, maskS], [pS])
                        K.actv(PTc[:, b, g, :], pS[:, :], AF.Exp, [pS], [PTc], scale=SWA_SCALE)
                if STOP <= 7:
                    return
                for g in range(2):
                    pS = gen.next()
                    K.mm(pS[:, :], kT_all[:, g, SEQ:SEQ + 128], qT[:, 4 * g:4 * g + 4, 0:128], True, False, [kT_all, qT], [pS])
                    K.mm(pS[:, :], ident_bf[:], maskS[:, 4, :], False, True, [ident_bf, maskS], [pS])
                    K.actv(PTn[:, g, :], pS[:, :], AF.Exp, [pS], [PTn], scale=SWA_SCALE)
                if STOP <= 8:
                    return
                for h in range(8):
                    g = h // 4
                    po = psOX if h < 4 else psOY
                    oo = po[:, (h % 4) * 65:(h % 4) * 65 + 65]
                    hs = slice((h % 4) * 128, (h % 4) * 128 + 128)
                    for b in range(NSB):
                        K.mm(oo, PTc[:, b, g, hs], vcb[:, b, g, :], b == 0, False, [PTc, vcb], [po])
                    K.mm(oo, PTn[:, g, hs], v_all[:, NTP, g, :], False, True, [PTn, v_all], [po])
                swa_finish(K, psOX, psOY, den, sinkexp, att_tok[0], psAT, psAT_bf, ident_bf, attT[sp], 0, 128)
            for c in range(4):
                K.load(attT_d[c * 128:(c + 1) * 128, tok0:tok0 + W], attT[sp][:, c, 0:W], f"ast{sp}", [attT[sp]],
                       [("dram", "attT")])


def swa_finish(K, psOX, psOY, den, sinkexp, att_tok, psAT, psAT_bf, ident_bf, attT, tl, W):
    for half, po in enumerate((psOX, psOY)):
        o3 = po[:, 0:260].rearrange("p (h e) -> p h e", e=65)
        K.tt(den[:, half * 4:half * 4 + 4], o3[:, :, 64], sinkexp[:, half * 4:half * 4 + 4], ALU.add, [po, sinkexp], [den])
    K.op("dve", lambda e: e.reciprocal(den[:], den[:]), [den], [den])
    for half, po in enumerate((psOX, psOY)):
        o3 = po[:, 0:260].rearrange("p (h e) -> p h e", e=65)
        K.tt(att_tok[:, half * 256:(half + 1) * 256].rearrange("p (h d) -> p h d", d=64), o3[:, :, 0:64],
             den[:, half * 4:half * 4 + 4].unsqueeze(2).to_broadcast([128, 4, 64]), ALU.mult, [po, den], [att_tok])
    for c in range(4):
        K.tr(psAT_bf[:, c * 128:(c + 1) * 128], att_tok[:, c * 128:(c + 1) * 128], ident_bf[:], [att_tok, ident_bf], [psAT])
    K.cp(attT[:, :, tl * 128:(tl + 1) * 128], psAT_bf[:, 0:512].rearrange("p (c t) -> p c t", c=4), [psAT], [attT], eng="act")


def rope_tables(pos, rot, head_dim, nrep=1):
    half = rot // 2
    inv = ROPE_THETA ** (-np.arange(0, rot, 2, dtype=np.float32) / rot)
    ang = pos.astype(np.float32)[None, :] * inv.astype(np.float32)[:, None]
    cos, sin = np.cos(ang).astype(np.float32), np.sin(ang).astype(np.float32)
    C = np.ones((head_dim, len(pos)), np.float32)
    S = np.zeros((head_dim, len(pos)), np.float32)
    C[0:half] = cos
    C[half:rot] = cos
    S[0:half] = -sin
    S[half:rot] = sin
    return C, S


def perm_rope_cols(w, head_dim, rot):
    half = rot // 2
    n = w.shape[1]
    idx = np.arange(n)
    d = idx % head_dim
    src = np.where(d < half, idx + half, np.where(d < rot, idx - half, idx))
    return np.ascontiguousarray(w[:, src])


def token_positions(SEQ):
    return np.concatenate([np.arange(SEQ), np.tile(PAST_LEN + np.arange(DEC_SEQ), NSB)])


def swa_consts(SEQ):
    C, S = rope_tables(token_positions(SEQ), 16, 64)
    k = np.arange(128)[:, None]
    q = (np.arange(512) % 128)[None, :]
    maskA = np.where((k < 64) & (q >= 64), MASKV, 0.0).astype(np.float32)
    maskB = np.where((k >= 64) & (q < 64), MASKV, 0.0).astype(np.float32)
    qb = ((np.arange(512) % 128) // 32)[None, :]
    maskS = np.zeros((128, NSB + 1, 512), np.float32)
    for b in range(NSB):
        maskS[:, b, :] = np.where(qb == b, 0.0, MASKV)
    maskS[:, NSB, :] = np.where(qb == (np.arange(128) // 32)[:, None], 0.0, MASKV)
    return {"ropeC": C, "ropeS": S, "maskA": maskA, "maskB": maskB, "maskS": maskS, "ident": np.eye(128, dtype=np.float32)}


TWO_PI = float(2.0 * np.pi)
PI = float(np.pi)


def sincos(K, ang, sin_o, cos_o, tmp_i, tmp_a, tmp_b, toks_in, tok_sin, tok_cos, tok_tmp):
    ki, ka, kb = tok_tmp
    a, b = tmp_a, tmp_b
    K.ts(b, ang, 1.0 / TWO_PI, None, ALU.mult, None, toks_in, [kb])
    K.cp(tmp_i, b, [kb], [ki])
    K.cp(b, tmp_i, [ki], [kb])
    K.stt(a, b, -TWO_PI, ang, ALU.mult, ALU.add, [kb] + list(toks_in), [ka])
    for _ in range(2):
        K.ts(b, a, PI, -TWO_PI, ALU.is_gt, ALU.mult, [ka], [kb])
        K.tt(a, a, b, ALU.add, [ka, kb], [ka])
        K.ts(b, a, -PI, TWO_PI, ALU.is_lt, ALU.mult, [ka], [kb])
        K.tt(a, a, b, ALU.add, [ka, kb], [ka])
    K.actv(sin_o, a, AF.Sin, [ka], [tok_sin])
    K.ts(a, a, PI / 2, None, ALU.add, None, [ka], [ka])
    K.ts(b, a, PI, -TWO_PI, ALU.is_gt, ALU.mult, [ka], [kb])
    K.tt(a, a, b, ALU.add, [ka, kb], [ka])
    K.actv(cos_o, a, AF.Sin, [ka], [tok_cos])


def phase_s5(K, cfg, x_all, uT_d, attT_d, x1_d, P, ident_f_d, iotaL_d, outs):
    SEQ = cfg["SEQ"]
    LC = 256
    with K.phase():
        ident_f = K.sb("identf", [128, 128], F32)
        K.load(ident_f[:], ident_f_d, None, (), [ident_f])
        iotaL = K.sb("iotaL", [128, LC], F32)
        K.load(iotaL[:], iotaL_d, None, (), [iotaL])
        w_glu = K.sb("w_glu", [128, 4, 512], BF16)
        w_out = K.sb("w_out", [128, 8, 1024], BF16)
        for c in range(4):
            K.load(w_glu[:, c, :], P["w_glu"][c * 128:(c + 1) * 128, :], None, (), [w_glu], queue="pool")
        for c in range(8):
            K.load(w_out[:, c, :], P["w_out"][c * 128:(c + 1) * 128, :], None, (), [w_out], queue="pool")
        psS = K.ps("psS", [128, 512], F32)
        gp = Rot([K.ps(f"pb{i}", [128, 512], F32) for i in range(2)])
        pyr = Rot([K.ps(f"py{i}", [128, 512], F32) for i in range(2)])
        psG = K.ps("psG", [128, 512], F32)
        psO = K.ps("psO", [128, 1024], F32)

        def load_T(name, src_16x128):
            raw = K.sb(name + "_raw", [16, 128], F32)
            dst = K.sb(name, [128, 16], F32)
            K.load(raw[:], src_16x128, None, (), [raw])
            K.tr(psS[:, 0:16], raw[:], ident_f[0:16, 0:16], [raw, ident_f], [psS])
            K.cp(dst[:], psS[:, 0:16], [psS], [dst])
            return dst
        lre = load_T("lre", P["lam_re"])
        lim = load_T("lim", P["lam_im"])
        ldr = K.sb("ldr", [16, 2], F32)
        K.load(ldr[:], P["log_dt"], None, (), [ldr])
        ldx = K.sb("ldx", [16, 2, 64], F32)
        K.cp(ldx[:], ldr[:].unsqueeze(2).to_broadcast([16, 2, 64]), [ldr], [ldx])
        dtt = K.sb("dtt", [128, 16], F32)
        K.tr(psS[:, 0:16], ldx[:].rearrange("t a n -> t (a n)"), ident_f[0:16, 0:16], [ldx, ident_f], [psS])
        K.actv(dtt[:], psS[:, 0:16], AF.Exp, [psS], [dtt])

        def sm(name):
            return K.sb(name, [128, 16], F32)
        lr, th, mag, sn, cs, abr, abi = sm("lr"), sm("th"), sm("mag"), sm("sn"), sm("cs"), sm("abr"), sm("abi")
        ta, tb, nr, dn, fre, fim = sm("ta"), sm("tb"), sm("nr"), sm("dn"), sm("fre"), sm("fim")
        ti = K.sb("ti", [128, 16], I32)
        K.ts(lr[:], lre[:], -1e-4, None, ALU.min, None, [lre], [lr])
        K.tt(th[:], lim[:], dtt[:], ALU.mult, [lim, dtt], [th])
        K.tt(mag[:], lr[:], dtt[:], ALU.mult, [lr, dtt], [mag])
        K.actv(mag[:], mag[:], AF.Exp, [mag], [mag])
        sincos(K, th[:], sn[:], cs[:], ti[:], ta[:], tb[:], [th], sn, cs, (ti, ta, tb))
        K.tt(abr[:], mag[:], cs[:], ALU.mult, [mag, cs], [abr])
        K.tt(abi[:], mag[:], sn[:], ALU.mult, [mag, sn], [abi])
        K.ts(nr[:], abr[:], -1.0, None, ALU.add, None, [abr], [nr])
        K.tt(dn[:], lr[:], lr[:], ALU.mult, [lr], [dn])
        K.tt(ta[:], lim[:], lim[:], ALU.mult, [lim], [ta])
        K.tt(dn[:], dn[:], ta[:], ALU.add, [dn, ta], [dn])
        K.op("dve", lambda e: e.reciprocal(dn[:], dn[:]), [dn], [dn])
        K.tt(ta[:], nr[:], lr[:], ALU.mult, [nr, lr], [ta])
        K.tt(tb[:], abi[:], lim[:], ALU.mult, [abi, lim], [tb])
        K.tt(fre[:], ta[:], tb[:], ALU.add, [ta, tb], [fre])
        K.tt(fre[:], fre[:], dn[:], ALU.mult, [fre, dn], [fre])
        K.tt(ta[:], abi[:], lr[:], ALU.mult, [abi, lr], [ta])
        K.tt(tb[:], nr[:], lim[:], ALU.mult, [nr, lim], [tb])
        K.tt(fim[:], ta[:], tb[:], ALU.subtract, [ta, tb], [fim])
        K.tt(fim[:], fim[:], dn[:], ALU.mult, [fim, dn], [fim])

        cosT = K.sb("cosT", [128, 16, LC], F32)
        sinT = K.sb("sinT", [128, 16, LC], F32)
        BTp = [K.sb(f"BTp{j}", [128, 16, 128], BF16) for j in range(2)]
        CTp = [K.sb(f"CTp{j}", [128, 16, 128], F32) for j in range(2)]
        dcol = K.sb("dcol", [128, 4], F32)
        bgcol = K.sb("bgcol", [128, 4], F32)
        setup_scope = K.phase()
        setup_scope.__enter__()
        angT = K.sb("angT", [128, 16, LC], F32)
        tmpT = K.sb("tmpT", [128, 16, LC], F32)
        tmpI = K.sb("tmpI", [128, 16, LC], I32)
        K.tt(angT[:], th[:].unsqueeze(2).to_broadcast([128, 16, LC]), iotaL[:].unsqueeze(1).to_broadcast([128, 16, LC]),
             ALU.mult, [th, iotaL], [angT])
        sincos_big(K, angT, sinT, cosT, tmpI, tmpT)

        br = K.sb("br", [128, 16, 16], F32)
        bi = K.sb("bi", [128, 16, 16], F32)
        K.load(br[:], P["b_re"].rearrange("(t p) c -> p t c", p=128), None, (), [br])
        K.load(bi[:], P["b_im"].rearrange("(t p) c -> p t c", p=128), None, (), [bi])
        bbr = K.sb("bbr", [128, 16, 16], F32)
        bbi = K.sb("bbi", [128, 16, 16], F32)
        tq = K.sb("tq", [128, 16, 16], F32)
        fre_b = fre[:].unsqueeze(2).to_broadcast([128, 16, 16])
        fim_b = fim[:].unsqueeze(2).to_broadcast([128, 16, 16])
        K.tt(bbr[:], br[:], fre_b, ALU.mult, [br, fre], [bbr])
        K.tt(tq[:], bi[:], fim_b, ALU.mult, [bi, fim], [tq])
        K.tt(bbr[:], bbr[:], tq[:], ALU.subtract, [bbr, tq], [bbr])
        K.tt(bbi[:], bi[:], fre_b, ALU.mult, [bi, fre], [bbi])
        K.tt(tq[:], br[:], fim_b, ALU.mult, [br, fim], [tq])
        K.tt(bbi[:], bbi[:], tq[:], ALU.add, [bbi, tq], [bbi])
        bpad = K.sb("bpad", [128, 16, 128], F32)
        for j, bb in enumerate((bbr, bbi)):
            K.memset(bpad[:], 0.0, [bpad])
            bp4 = bpad[:].rearrange("p (r i) f -> p r i f", i=4)
            bb4 = bb[:].rearrange("p (r i) c -> p r i c", i=4)
            for i in range(4):
                K.cp(bp4[0:64, :, i, 32 * i:32 * i + 16], bb4[0:64, :, i, :], [bb], [bpad])
                K.cp(bp4[64:128, :, i, 32 * i + 16:32 * i + 32], bb4[64:128, :, i, :], [bb], [bpad])
            for t in range(16):
                K.tr(psS[:, 0:128], bpad[:, t, :], ident_f[:], [bpad, ident_f], [psS])
                K.cp(BTp[j][:, t, :], psS[:, 0:128], [psS], [BTp[j]], eng="act")
        cin = K.sb("cin", [128, 128], F32)
        for j, src in enumerate((P["c_re"], P["c_im"])):
            K.memset(CTp[j][:], 0.0, [CTp[j]])
            for rt in range(4):
                K.memset(cin[:], 0.0, [cin])
                for gl in range(8):
                    K.load(cin[gl * 16:(gl + 1) * 16, (gl % 2) * 64:(gl % 2) * 64 + 64],
                           src[rt * 128 + gl * 16: rt * 128 + (gl + 1) * 16, :], None, (), [cin])
                K.tr(psS[:, 0:128], cin[:], ident_f[:], [cin, ident_f], [psS])
                for i in range(4):
                    if j == 0:
                        K.cp(CTp[j][:, 4 * rt + i, 32 * i:32 * i + 32], psS[:, 32 * i:32 * i + 32], [psS], [CTp[j]], eng="act")
                    else:
                        K.ts(CTp[j][:, 4 * rt + i, 32 * i:32 * i + 32], psS[:, 32 * i:32 * i + 32], -1.0, None, ALU.mult, None,
                             [psS], [CTp[j]])
        dsk = K.sb("dsk_raw", [4, 128], F32)
        K.load(dsk[:], P["dsk"], None, (), [dsk])
        K.tr(psS[:, 0:4], dsk[:], ident_f[0:4, 0:4], [dsk, ident_f], [psS])
        K.cp(dcol[:], psS[:, 0:4], [psS], [dcol])
        bgr = K.sb("bg_raw", [4, 128], F32)
        K.load(bgr[:], P["b_glu"], None, (), [bgr])
        K.tr(psS[:, 0:4], bgr[:], ident_f[0:4, 0:4], [bgr, ident_f], [psS])
        K.cp(bgcol[:], psS[:, 0:4], [psS], [bgcol])

        setup_scope.__exit__(None, None, None)
        hpr = K.sb("hpr", [128, 16], F32)
        hpi = K.sb("hpi", [128, 16], F32)
        uTb = [K.sb(f"uTb{i}", [128, 4, LC], BF16) for i in range(2)]
        mixT = [K.sb(f"mixT{i}", [128, 8, LC], BF16) for i in range(2)]
        mixS = K.sb("mixS", [128, 8, 128], BF16)
        W = {n: [K.sb(f"{n}{i}", [128, LC], F32) for i in range(2)] for n in
             ("t1", "t2", "t3", "t4", "p1", "p2", "p3", "p4", "wri", "wii", "wr", "wi", "hr", "hi")}
        yT = K.sb("yT", [128, LC], F32)
        sq = K.sb("sq", [128, LC], F32)
        z2 = [K.sb(f"z2_{i}", [128, 4, LC], BF16) for i in range(2)]
        gt = K.sb("gt", [128, LC], F32)
        xt = [K.sb(f"xt{i}", [128, 1024], F32) for i in range(2)]
        xo = [K.sb(f"xo{i}", [128, 1024], F32) for i in range(2)]
        hout = K.sb("hout", [16, 128], F32)
        h0raw = K.sb("h0raw", [16, 128], F32)

        def set_state(src_re, src_im):
            if src_re is None:
                K.memset(hpr[:], 0.0, [hpr])
                K.memset(hpi[:], 0.0, [hpi])
                return
            for src, dst in ((src_re, hpr), (src_im, hpi)):
                K.load(h0raw[:], src, None, (), [h0raw])
                K.tr(psS[:, 0:16], h0raw[:], ident_f[0:16, 0:16], [h0raw, ident_f], [psS])
                K.cp(dst[:], psS[:, 0:16], [psS], [dst])

        def put_state(dst_re, dst_im):
            for dst, src in ((dst_re, hpr), (dst_im, hpi)):
                K.tr(psS[0:16, 0:128], src[:], ident_f[:], [src, ident_f], [psS])
                K.cp(hout[:], psS[0:16, 0:128], [psS], [hout])
                K.load(dst, hout[:], None, [hout], ())

        cnt = [0]

        def s5_chunk(tok0, L, mix, mcol0):
            ci = cnt[0]
            cnt[0] += 1
            ub, zz = uTb[ci % 2], z2[ci % 2]
            for c in range(4):
                K.load(ub[:, c, 0:L], uT_d[c * 128:(c + 1) * 128, tok0:tok0 + L], None, (), [ub])
                K.load(mix[:, c, mcol0:mcol0 + L], attT_d[c * 128:(c + 1) * 128, tok0:tok0 + L], None, (), [mix])
            for rt in range(4):
                py = pyr.next()
                for i in range(4):
                    t = 4 * rt + i
                    w = {n: W[n][t % 2] for n in W}
                    pb = gp.next()
                    K.mm(pb[:, 0:L], BTp[0][:, t, :], ub[:, rt, 0:L], True, True, [BTp[0], ub], [pb])
                    K.mm(pb[:, 256:256 + L], BTp[1][:, t, :], ub[:, rt, 0:L], True, True, [BTp[1], ub], [pb])
                    cT, sT = cosT[:, t, 0:L], sinT[:, t, 0:L]
                    bre, bim = pb[:, 0:L], pb[:, 256:256 + L]
                    K.tt(w["t1"][:, 0:L], bre, cT, ALU.mult, [pb, cosT], [w["t1"]])
                    K.tt(w["t2"][:, 0:L], bim, sT, ALU.mult, [pb, sinT], [w["t2"]])
                    K.tt(w["t3"][:, 0:L], bim, cT, ALU.mult, [pb, cosT], [w["t3"]])
                    K.tt(w["t4"][:, 0:L], bre, sT, ALU.mult, [pb, sinT], [w["t4"]])
                    K.tt(w["wri"][:, 0:L], w["t1"][:, 0:L], w["t2"][:, 0:L], ALU.add, [w["t1"], w["t2"]], [w["wri"]])
                    K.tt(w["wii"][:, 0:L], w["t3"][:, 0:L], w["t4"][:, 0:L], ALU.subtract, [w["t3"], w["t4"]], [w["wii"]])
                    rb = mag[:, t:t + 1].to_broadcast([128, L])
                    K.op("dve", lambda e, w=w, rb=rb, t=t: e.tensor_tensor_scan(
                        w["wr"][:, 0:L], rb, w["wri"][:, 0:L], hpr[:, t:t + 1], ALU.mult, ALU.add),
                        [mag, w["wri"], hpr], [w["wr"]])
                    K.op("dve", lambda e, w=w, rb=rb, t=t: e.tensor_tensor_scan(
                        w["wi"][:, 0:L], rb, w["wii"][:, 0:L], hpi[:, t:t + 1], ALU.mult, ALU.add),
                        [mag, w["wii"], hpi], [w["wi"]])
                    PE_ = cfg.get("s5_post_eng", "dve")
                    K.tt(w["p1"][:, 0:L], w["wr"][:, 0:L], cT, ALU.mult, [w["wr"], cosT], [w["p1"]], eng=PE_)
                    K.tt(w["p2"][:, 0:L], w["wi"][:, 0:L], sT, ALU.mult, [w["wi"], sinT], [w["p2"]], eng=PE_)
                    K.tt(w["p3"][:, 0:L], w["wi"][:, 0:L], cT, ALU.mult, [w["wi"], cosT], [w["p3"]], eng=PE_)
                    K.tt(w["p4"][:, 0:L], w["wr"][:, 0:L], sT, ALU.mult, [w["wr"], sinT], [w["p4"]], eng=PE_)
                    K.tt(w["hr"][:, 0:L], w["p1"][:, 0:L], w["p2"][:, 0:L], ALU.subtract, [w["p1"], w["p2"]], [w["hr"]], eng=PE_)
                    K.tt(w["hi"][:, 0:L], w["p3"][:, 0:L], w["p4"][:, 0:L], ALU.add, [w["p3"], w["p4"]], [w["hi"]], eng=PE_)
                    K.cp(hpr[:, t:t + 1], w["hr"][:, L - 1:L], [w["hr"]], [hpr], eng="act")
                    K.cp(hpi[:, t:t + 1], w["hi"][:, L - 1:L], [w["hi"]], [hpi], eng="act")
                    K.mm(py[:, 0:L], CTp[0][:, t, :], w["hr"][:, 0:L], i == 0, False, [CTp[0], w["hr"]], [py])
                    K.mm(py[:, 0:L], CTp[1][:, t, :], w["hi"][:, 0:L], False, i == 3, [CTp[1], w["hi"]], [py])
                K.stt(yT[:, 0:L], ub[:, rt, 0:L], dcol[:, rt:rt + 1], py[:, 0:L], ALU.mult, ALU.add, [ub, dcol, py], [yT])
                K.actv(sq[:, 0:L], yT[:, 0:L], AF.Square, [yT], [sq])
                K.ts(sq[:, 0:L], sq[:, 0:L], 0.044715, 1.0, ALU.mult, ALU.add, [sq], [sq])
                K.tt(sq[:, 0:L], sq[:, 0:L], yT[:, 0:L], ALU.mult, [sq, yT], [sq])
                K.actv(sq[:, 0:L], sq[:, 0:L], AF.Tanh, [sq], [sq], scale=float(np.sqrt(2.0 / np.pi)))
                K.stt(zz[:, rt, 0:L], sq[:, 0:L], 1.0, yT[:, 0:L], ALU.add, ALU.mult, [sq, yT], [zz])
            for fo in range(4):
                for kc in range(4):
                    K.mm(psG[:, 0:L], w_glu[:, kc, fo * 128:(fo + 1) * 128], zz[:, kc, 0:L], kc == 0, kc == 3, [w_glu, zz], [psG])
                K.actv(gt[:, 0:L], psG[:, 0:L], AF.Sigmoid, [psG, bgcol], [gt], scale=0.5, bias=bgcol[:, fo:fo + 1])
                K.stt(mix[:, 4 + fo, mcol0:mcol0 + L], zz[:, fo, 0:L], 0.5, gt[:, 0:L], ALU.mult, ALU.mult, [zz, gt], [mix])

        xc = [0]

        def out_proj(mix, col0, tok0):
            p = xc[0] % 2
            xc[0] += 1
            K.load(xt[p][:], x_all[tok0:tok0 + 128, :], None, (), [xt[p]])
            for hf in range(2):
                for kc in range(8):
                    K.mm(psO[:, hf * 512:(hf + 1) * 512], mix[:, kc, col0:col0 + 128], w_out[:, kc, hf * 512:(hf + 1) * 512],
                         kc == 0, kc == 7, [mix, w_out], [psO])
            K.tt(xo[p][:], psO[:], xt[p][:], ALU.add, [psO, xt[p]], [xo[p]])
            K.load(x1_d[tok0:tok0 + 128, :], xo[p][:], None, [xo[p]], ())

        set_state(None, None)
        for ci in range(SEQ // LC):
            mx = mixT[ci % 2]
            s5_chunk(ci * LC, LC, mx, 0)
            for tl in range(LC // 128):
                out_proj(mx, tl * 128, ci * LC + tl * 128)
        put_state(outs["hp_re"], outs["hp_im"])
        for b in range(NSB):
            set_state(P["h0_re"][b], P["h0_im"][b])
            s5_chunk(SEQ + b * 32, 32, mixS, b * 32)
            put_state(outs["hs_re"][b], outs["hs_im"][b])
        out_proj(mixS, 0, SEQ)


def sincos_big(K, angT, sinT, cosT, tmpI, tmpT):
    a = angT[:]
    b = tmpT[:]
    K.ts(b, a, 1.0 / TWO_PI, None, ALU.mult, None, [angT], [tmpT])
    K.cp(tmpI[:], b, [tmpT], [tmpI])
    K.cp(b, tmpI[:], [tmpI], [tmpT])
    K.stt(a, b, -TWO_PI, a, ALU.mult, ALU.add, [tmpT, angT], [angT])
    for _ in range(2):
        K.ts(b, a, PI, -TWO_PI, ALU.is_gt, ALU.mult, [angT], [tmpT])
        K.tt(a, a, b, ALU.add, [angT, tmpT], [angT])
        K.ts(b, a, -PI, TWO_PI, ALU.is_lt, ALU.mult, [angT], [tmpT])
        K.tt(a, a, b, ALU.add, [angT, tmpT], [angT])
    K.actv(sinT[:], a, AF.Sin, [angT], [sinT])
    K.ts(a, a, PI / 2, None, ALU.add, None, [angT], [angT])
    K.ts(b, a, PI, -TWO_PI, ALU.is_gt, ALU.mult, [angT], [tmpT])
    K.tt(a, a, b, ALU.add, [angT, tmpT], [angT])
    K.actv(cosT[:], a, AF.Sin, [angT], [cosT])


def s5_consts():
    return {"ident": np.eye(128, dtype=np.float32), "iotaL": np.tile(np.arange(1, 257, dtype=np.float32), (128, 1))}


def phase_odd(K, cfg, x_in, x_out, Wd, Cd, caches, outs):
    SEQ = cfg["SEQ"]
    NTOK = SEQ + 128
    NTP = SEQ // 128
    with K.phase():
        ident_f = K.sb("identf", [128, 128], F32)
        ident_bf = K.sb("identbf", [128, 128], BF16)
        ones_bf = K.sb("onesbf", [128, 128], BF16)
        K.load(ident_f[:], Cd["ident"], None, (), [ident_f])
        K.load(ident_bf[:], Cd["ident"], None, (), [ident_bf], queue="pool")
        K.memset(ones_bf[:], 1.0, [ones_bf])
        eps_t = K.sb("eps_t", [128, 1], F32)
        K.memset(eps_t[:], RMS_EPS, [eps_t])
        gmix = K.sb("gmix", [128, 1024], F32)
        K.load(gmix[:], bcast_rows(Wd["gmix"]), None, (), [gmix])
        qn = K.sb("qn", [128, 512], F32)
        K.load(qn[:], bcast_rows(Wd["qn"]), None, (), [qn])
        kvn = K.sb("kvn", [128, 256], F32)
        K.load(kvn[:], bcast_rows(Wd["kvn"]), None, (), [kvn])
        maskcol = K.sb("maskcol", [128, 4], F32)
        K.load(maskcol[:], Cd["maskcol"], None, (), [maskcol])
        mask4 = K.sb("mask4", [128, 4, 512], BF16)
        K.load(mask4[:], Cd["mask4"], None, (), [mask4], queue="pool")
        maskK = K.sb("maskK", [128, 4, 256], BF16)
        K.load(maskK[:], Cd["maskK"], None, (), [maskK], queue="pool")

        def wload(name, src, kch, ncol):
            t = K.sb(name, [128, kch, ncol], BF16)
            for c in range(kch):
                for c0 in range(0, ncol, 1024):
                    c1 = min(ncol, c0 + 1024)
                    K.load(t[:, c, c0:c1], src[c * 128:(c + 1) * 128, c0:c1], None, (), [t], queue="pool")
            return t
        w_in = wload("w_in", Wd["w_in"], 8, 1312)
        w_out = wload("w_out", Wd["w_out"], 8, 1024)
        w_uqN = wload("w_uqN", Wd["w_uqN"], 4, 512)
        w_uqR = wload("w_uqR", Wd["w_uqR"], 4, 256)
        w_uqRP = wload("w_uqRP", Wd["w_uqRP"], 4, 256)
        w_uv = wload("w_uv", Wd["w_uv"], 2, 512)
        pool_w = K.sb("pool_w", [128, 4, 128], BF16)
        for g in range(4):
            K.load(pool_w[:, g, :], Wd["pool_w"][g], None, (), [pool_w], queue="pool")
        psr = K.sb("psr", [4, 128], F32)
        K.load(psr[:], Wd["pscale"], None, (), [psr])
        pscol = K.sb("pscol", [128, 4], F32)

        gen = Rot([K.ps(f"gen{i}", [128, 512], F32) for i in range(3)])
        psSc = Rot([K.ps(f"psSc{i}", [128, 512], F32) for i in range(2)])
        psOa = K.ps("psOa", [128, 512], F32)
        psOb = K.ps("psOb", [128, 512], F32)
        psDn = K.ps("psDn", [128, 512], F32)

        g0 = gen.next()
        K.tr(g0[:, 0:4], psr[:], ident_f[0:4, 0:4], [psr, ident_f], [g0])
        K.cp(pscol[:], g0[:, 0:4], [g0], [pscol])
        wuk_raw = K.sb("wuk_raw", [128, 2, 512], F32)
        K.load(wuk_raw[:], Wd["w_uk"].rearrange("(ct p) f -> p ct f", p=128), None, (), [wuk_raw])
        w_ukT = K.sb("w_ukT", [128, 4, 256], BF16)
        for hp in range(4):
            for ct in range(2):
                g1 = gen.next()
                K.tr(g1[:, 0:128], wuk_raw[:, ct, hp * 128:(hp + 1) * 128], ident_f[:], [wuk_raw, ident_f], [g1])
                K.cp(w_ukT[:, hp, ct * 128:(ct + 1) * 128], g1[:, 0:128], [g1], [w_ukT], eng="act")

        cT_all = K.sb("cT_all", [128, 2, NTOK], BF16)
        c_tok_all = K.sb("c_tok_all", [128, NTP + 1, 256], BF16)
        kpT4_all = K.sb("kpT4_all", [128, NTOK], BF16)

        junk = K.sb("junk", [128, 1024], BF16)
        xt = [K.sb(f"xt{i}", [128, 1024], F32) for i in range(2)]
        xo = [K.sb(f"xo{i}", [128, 1024], F32) for i in range(2)]
        ssb = [K.sb(f"ssb{i}", [128, 4], F32) for i in range(2)]
        sscq = [K.sb(f"sscq{i}", [128, 4], F32) for i in range(2)]
        sscc = [K.sb(f"sscc{i}", [128, 4], F32) for i in range(2)]
        hn = [K.sb(f"hn{i}", [128, 1024], BF16) for i in range(2)]
        cqn = [K.sb(f"cqn{i}", [128, 512], BF16) for i in range(2)]
        cf = [K.sb(f"cf{i}", [128, 256], F32) for i in range(2)]
        kt1 = [K.sb(f"kt1_{i}", [128, 32], F32) for i in range(2)]
        kt2 = [K.sb(f"kt2_{i}", [128, 32], F32) for i in range(2)]
        kpf = [K.sb(f"kpf{i}", [128, 32], F32) for i in range(2)]
        kp4 = [K.sb(f"kp4{i}", [128, 4, 32], BF16) for i in range(2)]
        ropeK = [K.sb(f"ropeK{i}", [128, 64], F32) for i in range(2)]

        def make_bufs(W):
            B = {}
            B["hnT"] = K.sb("hnT", [128, 8, W], BF16)
            B["cqnT"] = K.sb("cqnT", [128, 4, W], BF16)
            B["qnT"] = K.sb("qnT", [128, 4, W], BF16)
            B["qpeT"] = K.sb("qpeT", [128, 2, W], BF16)
            B["rC"] = K.sb("rC", [128, W], F32)
            B["rS"] = K.sb("rS", [128, W], F32)
            B["r1"] = K.sb("r1", [128, W], F32)
            B["r2"] = K.sb("r2", [128, W], F32)
            B["extA"] = K.sb("extA", [128, 15 + W], F32)
            B["extB"] = K.sb("extB", [128, 15 + W], F32)
            B["extC"] = K.sb("extC", [128, 15 + W], F32)
            for nm in ("extA", "extB", "extC"):
                K.memset(B[nm][:], 0.0, [B[nm]])
            B["rc"] = K.sb("rc", [128, W], F32)
            B["mT"] = K.sb("mT", [128, W], BF16)
            B["mixT"] = K.sb("mixT", [128, 8, W], BF16)
            B["hist"] = K.sb("hist", [128, 4, 15], F32)
            B["utail"] = K.sb("utail", [15, 512], F32)
            B["hraw"] = K.sb("hraw", [15, 512], F32)
            B["uS"] = K.sb("uS", [128, W], F32)
            return B

        def norm_small(X, OUT, gb, width, sc, xtok, otok, gtok):
            K.actv(junk[:, 0:width], X, AF.Square, [xtok], [sc, junk], accum_out=sc[:, 0:1])
            K.actv(sc[:, 1:2], sc[:, 0:1], AF.Sqrt, [sc, eps_t], [sc], scale=1.0 / width, bias=eps_t[:, 0:1])
            K.op("dve", lambda e: e.reciprocal(sc[:, 2:3], sc[:, 1:2]), [sc], [sc])
            K.stt(OUT, X, sc[:, 2:3], gb, ALU.mult, ALU.mult, [xtok, sc, gtok], [otok])

        cnt = {"x": 0, "t": 0}

        def front(B, tok0, W, hist_src):
            ntl = W // 128
            hnT, cqnT = B["hnT"], B["cqnT"]
            for tl in range(ntl):
                p = cnt["t"] % 2
                cnt["t"] += 1
                X = xt[cnt["x"] % 2]
                cnt["x"] += 1
                j = (tok0 + tl * 128) // 128
                K.load(X[:], x_in[tok0 + tl * 128: tok0 + (tl + 1) * 128, :], None, (), [X])
                K.load(ropeK[p][:], Cd["ropeK"][tok0 + tl * 128: tok0 + (tl + 1) * 128, :], None, (), [ropeK[p]])
                rmsnorm_tile(K, X[:], hn[p][:], gmix[:], ssb[p], eps_t, junk[:], X, hn[p], gmix, junk)
                gT = gen.next()
                gT_bf = gT[:].bitcast(BF16)
                for c in range(8):
                    K.tr(gT_bf[:, c * 128:(c + 1) * 128], hn[p][:, c * 128:(c + 1) * 128], ident_bf[:], [hn[p], ident_bf], [gT])
                K.cp(hnT[:, :, tl * 128:(tl + 1) * 128], gT_bf[:, :].rearrange("p (c t) -> p c t", c=8), [gT], [hnT], eng="act")
                pq, pc = gen.next(), gen.next()
                for kc in range(8):
                    K.mm(pq[:, 0:512], hnT[:, kc, tl * 128:(tl + 1) * 128], w_in[:, kc, 0:512], kc == 0, kc == 7, [hnT, w_in], [pq])
                for kc in range(8):
                    K.mm(pc[:, 0:288], hnT[:, kc, tl * 128:(tl + 1) * 128], w_in[:, kc, 512:800], kc == 0, kc == 7, [hnT, w_in], [pc])
                norm_small(pq[:, 0:512], cqn[p][:], qn[:], 512, sscq[p], pq, cqn[p], qn)
                norm_small(pc[:, 0:256], cf[p][:], kvn[:], 256, sscc[p], pc, cf[p], kvn)
                K.tt(kt1[p][:], pc[:, 256:288], ropeK[p][:, 0:32], ALU.mult, [pc, ropeK[p]], [kt1[p]])
                K.tt(kt2[p][:, 0:16], pc[:, 272:288], ropeK[p][:, 32:48], ALU.mult, [pc, ropeK[p]], [kt2[p]])
                K.tt(kt2[p][:, 16:32], pc[:, 256:272], ropeK[p][:, 48:64], ALU.mult, [pc, ropeK[p]], [kt2[p]])
                K.tt(kpf[p][:], kt1[p][:], kt2[p][:], ALU.add, [kt1[p], kt2[p]], [kpf[p]])
                K.cp(kp4[p][:], kpf[p][:].unsqueeze(1).to_broadcast([128, 4, 32]), [kpf[p]], [kp4[p]])
                K.cp(c_tok_all[:, j, :], cf[p][:], [cf[p]], [c_tok_all])
                if tok0 < SEQ:
                    K.load(outs["ckv_p"][tok0 + tl * 128: tok0 + (tl + 1) * 128, :], cf[p][:], None, [cf[p]], ())
                    K.load(outs["kpe_p"][tok0 + tl * 128: tok0 + (tl + 1) * 128, :], kpf[p][:], None, [kpf[p]], ())
                else:
                    for b in range(NSB):
                        K.load(outs["ckv_s"][b], cf[p][b * 32:(b + 1) * 32, :], None, [cf[p]], ())
                        K.load(outs["kpe_s"][b], kpf[p][b * 32:(b + 1) * 32, :], None, [kpf[p]], ())
                gT2 = gen.next()
                gT2_bf = gT2[:].bitcast(BF16)
                for c in range(4):
                    K.tr(gT2_bf[:, c * 128:(c + 1) * 128], cqn[p][:, c * 128:(c + 1) * 128], ident_bf[:], [cqn[p], ident_bf], [gT2])
                K.cp(cqnT[:, :, tl * 128:(tl + 1) * 128], gT2_bf[:, 0:512].rearrange("p (c t) -> p c t", c=4), [gT2], [cqnT], eng="act")
                gT3 = gen.next()
                gT3_bf = gT3[:].bitcast(BF16)
                for c in range(2):
                    K.tr(gT3_bf[:, c * 128:(c + 1) * 128], c_tok_all[:, j, c * 128:(c + 1) * 128], ident_bf[:],
                         [c_tok_all, ident_bf], [gT3])
                K.tr(gT3_bf[:, 256:384], kp4[p][:].rearrange("p a r -> p (a r)"), ident_bf[:], [kp4[p], ident_bf], [gT3])
                K.cp(cT_all[:, :, j * 128:(j + 1) * 128], gT3_bf[:, 0:256].rearrange("p (c t) -> p c t", c=2), [gT3], [cT_all], eng="act")
                K.cp(kpT4_all[:, j * 128:(j + 1) * 128], gT3_bf[:, 256:384], [gT3], [kpT4_all], eng="act")
            K.load(B["rC"][:, 0:W], Cd["ropeQC"][:, tok0:tok0 + W], None, (), [B["rC"]])
            K.load(B["rS"][:, 0:W], Cd["ropeQS"][:, tok0:tok0 + W], None, (), [B["rS"]])
            for c in range(4):
                pn = gen.next()
                for kc in range(4):
                    K.mm(pn[:, 0:W], w_uqN[:, kc, c * 128:(c + 1) * 128], cqnT[:, kc, 0:W], kc == 0, kc == 3, [w_uqN, cqnT], [pn])
                K.cp(B["qnT"][:, c, 0:W], pn[:, 0:W], [pn], [B["qnT"]], eng="act")
            for c in range(2):
                pr, pp = gen.next(), gen.next()
                for kc in range(4):
                    K.mm(pr[:, 0:W], w_uqR[:, kc, c * 128:(c + 1) * 128], cqnT[:, kc, 0:W], kc == 0, kc == 3, [w_uqR, cqnT], [pr])
                for kc in range(4):
                    K.mm(pp[:, 0:W], w_uqRP[:, kc, c * 128:(c + 1) * 128], cqnT[:, kc, 0:W], kc == 0, kc == 3, [w_uqRP, cqnT], [pp])
                K.tt(B["r1"][:, 0:W], pr[:, 0:W], B["rC"][:, 0:W], ALU.mult, [pr, B["rC"]], [B["r1"]])
                K.tt(B["r2"][:, 0:W], pp[:, 0:W], B["rS"][:, 0:W], ALU.mult, [pp, B["rS"]], [B["r2"]])
                K.tt(B["qpeT"][:, c, 0:W], B["r1"][:, 0:W], B["r2"][:, 0:W], ALU.add, [B["r1"], B["r2"]], [B["qpeT"]])
            segs = [(0, W)] if not isinstance(hist_src, list) else [(b * 32, 32) for b in range(len(hist_src))]
            if isinstance(hist_src, list):
                hraw = B["hraw"]
            for g in range(4):
                pu = gen.next()
                for kc in range(8):
                    K.mm(pu[:, 0:W], w_in[:, kc, 800 + g * 128:800 + (g + 1) * 128], hnT[:, kc, 0:W], kc == 0, kc == 7, [w_in, hnT], [pu])
                K.cp(B["uS"][:, 0:W], pu[:, 0:W], [pu], [B["uS"]], eng="act")
                for si_, (s0, L) in enumerate(segs):
                    A, Bb, Cc = B["extA"], B["extB"], B["extC"]
                    if hist_src == "zero":
                        K.memset(A[:, 0:15], 0.0, [A])
                    elif isinstance(hist_src, list):
                        K.load(hraw[:, g * 128:(g + 1) * 128], hist_src[si_][:, g * 128:(g + 1) * 128], None, (), [hraw])
                        ph = gen.next()
                        K.tr(ph[:, 0:15], hraw[:, g * 128:(g + 1) * 128], ident_f[0:15, 0:15], [hraw, ident_f], [ph])
                        K.cp(A[:, 0:15], ph[:, 0:15], [ph], [A])
                    else:
                        K.cp(A[:, 0:15], B["hist"][:, g, :], [B["hist"]], [A])
                    K.cp(A[:, 15:15 + L], B["uS"][:, s0:s0 + L], [B["uS"]], [A])
                    if hist_src is None or hist_src == "zero":
                        K.cp(B["hist"][:, g, :], A[:, L:L + 15], [A], [B["hist"]])
                    src, dst = A, Bb
                    n = 15 + L
                    for st in range(g + 1):
                        sh = 1 << st
                        K.tt(dst[:, sh:n], src[:, sh:n], src[:, 0:n - sh], ALU.add, [src], [dst])
                        src, dst = dst, (Cc if dst is Bb else Bb)
                    K.load(B["rc"][:, 0:L], bcast_rows(Cd["rcnt"][g:g + 1, tok0 + s0: tok0 + s0 + L]), None, (), [B["rc"]])
                    K.tt(Cc[:, 15:n] if src is not Cc else Bb[:, 15:n], src[:, 15:n], B["rc"][:, 0:L], ALU.mult, [src, B["rc"]],
                         [Cc if src is not Cc else Bb])
                    tot = Cc if src is not Cc else Bb
                    K.tt(B["mT"][:, s0:s0 + L], tot[:, 15:n], A[:, 15:n], ALU.subtract, [tot, A], [B["mT"]])
                    last = (tok0 + W == SEQ) or isinstance(hist_src, list)
                    if last:
                        pt = gen.next()
                        K.tr(pt[0:15, 0:128], A[:, L:L + 15], ident_f[:], [A, ident_f], [pt])
                        K.cp(B["utail"][:, g * 128:(g + 1) * 128], pt[0:15, 0:128], [pt], [B["utail"]])
                        if g == 3 or isinstance(hist_src, list):
                            dsto = outs["pool_p"] if not isinstance(hist_src, list) else outs["pool_s"][si_]
                            K.load(dsto[:, g * 128:(g + 1) * 128], B["utail"][:, g * 128:(g + 1) * 128], None, [B["utail"]], ())
                            if not isinstance(hist_src, list):
                                for g2 in range(3):
                                    K.load(dsto[:, g2 * 128:(g2 + 1) * 128], B["utail"][:, g2 * 128:(g2 + 1) * 128], None,
                                           [B["utail"]], ())
                po = gen.next()
                K.mm(po[:, 0:W], pool_w[:, g, :], B["mT"][:, 0:W], True, True, [pool_w, B["mT"]], [po])
                K.actv(B["mixT"][:, g, 0:W], po[:, 0:W], AF.Copy, [po, pscol], [B["mixT"]], scale=pscol[:, g:g + 1])

        def qhead(B, h, W, qa, qpad):
            r0 = (h % 2) * 64
            for cc in range(2):
                pa = gen.next()
                K.mm(pa[:, 0:W], w_ukT[r0:r0 + 64, h // 2, cc * 128:(cc + 1) * 128], B["qnT"][r0:r0 + 64, h // 2, 0:W], True, True,
                     [w_ukT, B["qnT"]], [pa])
                K.cp(qa[:, cc, 0:W], pa[:, 0:W], [pa], [qa], eng="act")
            K.ts(qpad[:, 0:W], B["qpeT"][:, h // 4, 0:W], maskcol[:, h % 4:h % 4 + 1], None, ALU.mult, None,
                 [B["qpeT"], maskcol], [qpad])

        def out_proj(B, W, tok0):
            for tl in range(W // 128):
                p = cnt["t"] % 2
                cnt["t"] += 1
                X = xt[cnt["x"] % 2]
                cnt["x"] += 1
                K.load(X[:], x_in[tok0 + tl * 128: tok0 + (tl + 1) * 128, :], None, (), [X])
                pso = [gen.next(), gen.next()]
                for hf in range(2):
                    for kc in range(8):
                        K.mm(pso[hf][:, :], B["mixT"][:, kc, tl * 128:(tl + 1) * 128], w_out[:, kc, hf * 512:(hf + 1) * 512],
                             kc == 0, kc == 7, [B["mixT"], w_out], [pso[hf]])
                    K.tt(xo[p][:, hf * 512:(hf + 1) * 512], pso[hf][:, :], X[:, hf * 512:(hf + 1) * 512], ALU.add, [pso[hf], X], [xo[p]])
                K.load(x_out[tok0 + tl * 128: tok0 + (tl + 1) * 128, :], xo[p][:], None, [xo[p]], ())

        pscope = K.phase()
        pscope.__enter__()
        B = make_bufs(512)
        qa = [K.sb(f"qa{i}", [128, 2, 512], BF16) for i in range(2)]
        qpad = [K.sb(f"qpad{i}", [128, 512], BF16) for i in range(2)]
        PT = [K.sb(f"PT{i}", [128, 512], BF16) for i in range(3)]
        rden = K.sb("rden", [128, 512], F32)
        olat = [K.sb(f"olat{i}", [128, 2, 512], BF16) for i in range(2)]
        pti = 0
        for s in range(SEQ // 512):
            tok0 = s * 512
            front(B, tok0, 512, "zero" if s == 0 else None)
            nkt = (tok0 + 512) // 128
            for h in range(8):
                qh, qp = qa[h % 2], qpad[h % 2]
                qhead(B, h, 512, qh, qp)
                for kt in range(nkt):
                    pS = psSc.next()
                    diag = kt - tok0 // 128
                    K.mm(pS[:, :], cT_all[:, 0, kt * 128:(kt + 1) * 128], qh[:, 0, :], True, False, [cT_all, qh], [pS])
                    K.mm(pS[:, :], cT_all[:, 1, kt * 128:(kt + 1) * 128], qh[:, 1, :], False, False, [cT_all, qh], [pS])
                    K.mm(pS[:, :], kpT4_all[:, kt * 128:(kt + 1) * 128], qp[:, :], False, diag < 0, [kpT4_all, qp], [pS])
                    if diag >= 0:
                        K.mm(pS[:, :], ident_bf[:], mask4[:, diag, :], False, True, [ident_bf, mask4], [pS])
                    P_ = PT[pti % 3]
                    pti += 1
                    K.actv(P_[:], pS[:, :], AF.Exp, [pS], [P_], scale=MLA_SCALE)
                    K.mm(psOa[:, :], c_tok_all[:, kt, 0:128], P_[:], kt == 0, kt == nkt - 1, [c_tok_all, P_], [psOa])
                    K.mm(psOb[:, :], c_tok_all[:, kt, 128:256], P_[:], kt == 0, kt == nkt - 1, [c_tok_all, P_], [psOb])
                    K.mm(psDn[:, :], ones_bf[:], P_[:], kt == 0, kt == nkt - 1, [ones_bf, P_], [psDn])
                ol = olat[h % 2]
                K.op("dve", lambda e: e.reciprocal(rden[:], psDn[:, :]), [psDn], [rden])
                K.tt(ol[:, 0, :], psOa[:, :], rden[:], ALU.mult, [psOa, rden], [ol])
                K.tt(ol[:, 1, :], psOb[:, :], rden[:], ALU.mult, [psOb, rden], [ol])
                if h % 2 == 0:
                    pm = gen.next()
                for cc in range(2):
                    K.mm(pm[(h % 2) * 64:(h % 2) * 64 + 64, :], w_uv[:, cc, h * 64:(h + 1) * 64], ol[:, cc, :], cc == 0, cc == 1,
                         [w_uv, ol], [pm])
                if h % 2 == 1:
                    K.cp(B["mixT"][:, 4 + h // 2, :], pm[:, :], [pm], [B["mixT"]], eng="act")
            out_proj(B, 512, tok0)
        pscope.__exit__(None, None, None)

        B = make_bufs(128)
        qaS = K.sb("qaS", [128, 8, 2, 128], BF16)
        qpadS = K.sb("qpadS", [128, 8, 128], BF16)
        front(B, SEQ, 128, [caches["pool"][b] for b in range(NSB)])
        for h in range(8):
            r0 = (h % 2) * 64
            for cc in range(2):
                pa = gen.next()
                K.mm(pa[:, 0:128], w_ukT[r0:r0 + 64, h // 2, cc * 128:(cc + 1) * 128], B["qnT"][r0:r0 + 64, h // 2, 0:128], True, True,
                     [w_ukT, B["qnT"]], [pa])
                K.cp(qaS[:, h, cc, :], pa[:, 0:128], [pa], [qaS], eng="act")
            K.ts(qpadS[:, h, :], B["qpeT"][:, h // 4, 0:128], maskcol[:, h % 4:h % 4 + 1], None, ALU.mult, None,
                 [B["qpeT"], maskcol], [qpadS])
        cc_tok = [K.sb(f"cc_tok{i}", [128, 16, 256], BF16) for i in range(2)]
        ccT = [K.sb(f"ccT{i}", [128, 2, 2048], BF16) for i in range(2)]
        ckp = [K.sb(f"ckp{i}", [128, 16, 32], BF16) for i in range(2)]
        ckp4 = [K.sb(f"ckp4{i}", [128, 16, 4, 32], BF16) for i in range(1)] * 2
        ckpT = [K.sb(f"ckpT{i}", [128, 2048], BF16) for i in range(1)] * 2
        PTs = [K.sb(f"PTs{i}", [128, 256], BF16) for i in range(3)]
        rdenS = K.sb("rdenS", [128, 256], F32)
        olS = K.sb("olS", [128, 2, 256], BF16)
        pti = 0
        for b in range(NSB):
            bp = b % 2
            for half in range(2):
                K.load(cc_tok[bp][:, half * 8:(half + 1) * 8, :],
                       caches["ckv"][b, half * 1024:(half + 1) * 1024, :].rearrange("(kt p) c -> p kt c", p=128), None, (),
                       [cc_tok[bp]], queue="pool")
            K.load(ckp[bp][:], caches["kpe"][b].rearrange("(kt p) r -> p kt r", p=128), None, (), [ckp[bp]], queue="pool")
            K.cp(ckp4[bp][:], ckp[bp][:].unsqueeze(2).to_broadcast([128, 16, 4, 32]), [ckp[bp]], [ckp4[bp]])
            for kt in range(16):
                gt_ = gen.next()
                gt_bf = gt_[:].bitcast(BF16)
                for c in range(2):
                    K.tr(gt_bf[:, c * 128:(c + 1) * 128], cc_tok[bp][:, kt, c * 128:(c + 1) * 128], ident_bf[:], [cc_tok[bp], ident_bf], [gt_])
                K.tr(gt_bf[:, 256:384], ckp4[bp][:, kt, :, :].rearrange("p a r -> p (a r)"), ident_bf[:], [ckp4[bp], ident_bf], [gt_])
                K.cp(ccT[bp][:, :, kt * 128:(kt + 1) * 128], gt_bf[:, 0:256].rearrange("p (c t) -> p c t", c=2), [gt_], [ccT[bp]], eng="act")
                K.cp(ckpT[bp][:, kt * 128:(kt + 1) * 128], gt_bf[:, 256:384], [gt_], [ckpT[bp]], eng="act")
            qs = slice(b * 32, (b + 1) * 32)
            for kt in range(17):
                pS = psSc.next()
                if kt < 16:
                    l0, l1, l2 = ccT[bp][:, 0, kt * 128:(kt + 1) * 128], ccT[bp][:, 1, kt * 128:(kt + 1) * 128], ckpT[bp][:, kt * 128:(kt + 1) * 128]
                    ltoks = [ccT[bp], ckpT[bp]]
                    vtok, vt = cc_tok[bp], cc_tok[bp][:, kt, :]
                else:
                    l0, l1, l2 = cT_all[:, 0, SEQ:SEQ + 128], cT_all[:, 1, SEQ:SEQ + 128], kpT4_all[:, SEQ:SEQ + 128]
                    ltoks = [cT_all, kpT4_all]
                    vtok, vt = c_tok_all, c_tok_all[:, NTP, :]
                K.mm(pS[:, 0:256], l0, qaS[:, :, 0, qs], True, False, ltoks + [qaS], [pS])
                K.mm(pS[:, 0:256], l1, qaS[:, :, 1, qs], False, False, ltoks + [qaS], [pS])
                K.mm(pS[:, 0:256], l2, qpadS[:, :, qs], False, kt < 16, ltoks + [qpadS], [pS])
                if kt == 16:
                    K.mm(pS[:, 0:256], ident_bf[:], maskK[:, b, :], False, True, [ident_bf, maskK], [pS])
                P_ = PTs[pti % 3]
                pti += 1
                K.actv(P_[:], pS[:, 0:256], AF.Exp, [pS], [P_], scale=MLA_SCALE)
                K.mm(psOa[:, 0:256], vt[:, 0:128], P_[:], kt == 0, kt == 16, [vtok, P_], [psOa])
                K.mm(psOb[:, 0:256], vt[:, 128:256], P_[:], kt == 0, kt == 16, [vtok, P_], [psOb])
                K.mm(psDn[:, 0:256], ones_bf[:], P_[:], kt == 0, kt == 16, [ones_bf, P_], [psDn])
            K.op("dve", lambda e: e.reciprocal(rdenS[:], psDn[:, 0:256]), [psDn], [rdenS])
            K.tt(olS[:, 0, :], psOa[:, 0:256], rdenS[:], ALU.mult, [psOa, rdenS], [olS])
            K.tt(olS[:, 1, :], psOb[:, 0:256], rdenS[:], ALU.mult, [psOb, rdenS], [olS])
            for hp in range(4):
                pm = gen.next()
                for hh in range(2):
                    h = 2 * hp + hh
                    for cc in range(2):
                        K.mm(pm[hh * 64:hh * 64 + 64, 0:32], w_uv[:, cc, h * 64:(h + 1) * 64], olS[:, cc, h * 32:(h + 1) * 32],
                             cc == 0, cc == 1, [w_uv, olS], [pm])
                K.cp(B["mixT"][:, 4 + hp, qs], pm[:, 0:32], [pm], [B["mixT"]], eng="act")
        out_proj(B, 128, SEQ)


def odd_consts(SEQ):
    pos = token_positions(SEQ)
    NTOK = SEQ + 128
    C, S = rope_tables(pos, 32, 32)
    ropeQC, ropeQS = np.tile(C, (4, 1)), np.tile(S, (4, 1))
    inv = ROPE_THETA ** (-np.arange(0, 32, 2, dtype=np.float32) / 32)
    ang = pos.astype(np.float32)[:, None] * inv.astype(np.float32)[None, :]
    cos, sin = np.cos(ang).astype(np.float32), np.sin(ang).astype(np.float32)
    ropeK = np.concatenate([cos, cos, -sin, sin], axis=1).astype(np.float32)
    rcnt = np.stack([1.0 / np.minimum(pos + 1, w).astype(np.float32) for w in (2, 4, 8, 16)]).astype(np.float32)
    maskcol = (np.arange(128)[:, None] // 32 == np.arange(4)[None, :]).astype(np.float32)
    k = np.arange(128)[:, None]
    q = np.arange(512)[None, :]
    mask4 = np.zeros((128, 4, 512), np.float32)
    for d in range(4):
        kchunk = 2 * d + (k >= 64)
        qchunk = 2 * (q // 128) + ((q % 128) >= 64)
        mask4[:, d, :] = np.where(kchunk <= qchunk, 0.0, MASKV)
    maskK = np.zeros((128, NSB, 256), np.float32)
    for b in range(NSB):
        maskK[:, b, :] = np.where((np.arange(128) // 32 == b)[:, None], 0.0, MASKV)
    return {"ident": np.eye(128, dtype=np.float32), "ropeQC": ropeQC, "ropeQS": ropeQS, "ropeK": ropeK, "rcnt": rcnt,
            "maskcol": maskcol, "mask4": mask4, "maskK": maskK}


def odd_weights(w_in, w_out, pool_w, pool_scale, q_norm, kv_norm, w_uq, w_uk, w_uv, gmix):
    uq = w_uq.reshape(512, 8, 96)
    w_uqN = np.ascontiguousarray(uq[:, :, :64].reshape(512, 512))
    w_uqR = np.ascontiguousarray(uq[:, :, 64:].reshape(512, 256))
    return {"w_in": w_in, "gmix": gmix.reshape(1, 1024), "w_out": w_out, "pool_w": pool_w, "pscale": pool_scale.reshape(4, 128),
            "qn": q_norm.reshape(1, 512), "kvn": kv_norm.reshape(1, 256), "w_uqN": w_uqN, "w_uqR": w_uqR,
            "w_uqRP": perm_rope_cols(w_uqR, 32, 32), "w_uk": np.ascontiguousarray(w_uk.reshape(256, 512)),
            "w_uv": np.ascontiguousarray(w_uv.reshape(256, 512))}


SEQ_FULL = 4096
N_CORES = 8


def build(SEQ, upto=5):
    cfg = {"SEQ": SEQ}
    NTOK = SEQ + 128
    K = KB()
    consts = {}
    for pre, d in (("s_", swa_consts(SEQ)), ("f_", s5_consts()), ("o_", odd_consts(SEQ))):
        for k, v in d.items():
            consts[pre + k] = v
    consts["iota16"] = np.tile(np.arange(16, dtype=np.float32), (128, 1))
    C = {k: K.const(k, v) for k, v in consts.items()}
    I = lambda n, shp: K.inp(n, shp, F32)
    O = lambda n, shp: K.outp(n, shp, F32)
    x_all = I("x_all", [NTOK, 1024])
    w_in0, w_in0P, gmix0, sink = I("w_in0", [1024, 1280]), I("w_in0P", [1024, 640]), I("gmix0", [1, 1024]), I("sink", [1, 8])
    ck, cv = I("ck", [NSB, 128, 128]), I("cv", [NSB, 128, 128])
    attT = K.outp("attT_scr", [512, NTOK], BF16)
    uT = K.outp("uT_scr", [512, NTOK], BF16)
    y = O("y", [NTOK, 1024])
    x1 = x2 = x3 = x4 = y
    outs0 = {"k_p": O("k_p", [128, 128]), "v_p": O("v_p", [128, 128]), "k_s": O("k_s", [NSB, 128, 128]), "v_s": O("v_s", [NSB, 128, 128])}
    phase_swa(K, cfg, x_all, w_in0, w_in0P, gmix0, sink, ck, cv, C["s_ropeC"], C["s_ropeS"], C["s_maskA"], C["s_maskB"], C["s_maskS"],
              C["s_ident"], attT, uT, outs0)
    P = {"lam_re": I("lam_re", [16, 128]), "lam_im": I("lam_im", [16, 128]), "log_dt": I("log_dt", [16, 2]),
         "b_re": I("b_re", [2048, 16]), "b_im": I("b_im", [2048, 16]), "c_re": I("c_re", [512, 64]), "c_im": I("c_im", [512, 64]),
         "dsk": I("dsk", [4, 128]), "w_glu": I("w_glu", [512, 512]), "b_glu": I("b_glu", [4, 128]), "w_out": I("w_out0", [1024, 1024]),
         "h0_re": I("h0_re", [NSB, 16, 128]), "h0_im": I("h0_im", [NSB, 16, 128])}
    outs1 = {"hp_re": O("hp_re", [16, 128]), "hp_im": O("hp_im", [16, 128]), "hs_re": O("hs_re", [NSB, 16, 128]), "hs_im": O("hs_im", [NSB, 16, 128])}
    if upto >= 2:
        phase_s5(K, cfg, x_all, uT, attT, x1, P, C["f_ident"], C["f_iotaL"], outs1)
    peer_in = []
    for l in range(2):
        peer_in.append({"wq": I(f"pwq{l}", [1024, 1024]), "keys": I(f"pkeys{l}", [8, 2, 128, 64]), "u": I(f"pu{l}", [N_EXPERTS, 1024]),
                        "v": I(f"pv{l}", [N_EXPERTS, 1024]), "g": I(f"gffn{l}", [1, 1024])})
    gfin = I("gfin", [1, 1024])
    pi = peer_in[0]
    tabs = [K.scratch(f"peer_tab{l}", [N_EXPERTS, 2048], BF16) for l in range(2)]
    if upto >= 3:
      phase_convert(K, [(peer_in[l]["u"], peer_in[l]["v"], tabs[l]) for l in range(2)])
      phase_peer(K, x1, x2, NTOK // 128, pi["wq"], pi["keys"], tabs[0], pi["g"], C["s_ident"], C["s_ident"], C["iota16"])
    wshapes = {"w_in": [1024, 1312], "gmix": [1, 1024], "w_out": [1024, 1024], "pool_w": [4, 128, 128], "pscale": [4, 128], "qn": [1, 512],
               "kvn": [1, 256], "w_uqN": [512, 512], "w_uqR": [512, 256], "w_uqRP": [512, 256], "w_uk": [256, 512], "w_uv": [256, 512]}
    Wd = {k: I("W1_" + k, shp) for k, shp in wshapes.items()}
    Cd = {k[2:]: v for k, v in C.items() if k.startswith("o_")}
    caches = {"pool": I("c_pool", [NSB, 15, 512]), "ckv": I("c_ckv", [NSB, PAST_LEN, 256]), "kpe": I("c_kpe", [NSB, PAST_LEN, 32])}
    outs3 = {"pool_p": O("pool_p", [15, 512]), "ckv_p": O("ckv_p", [SEQ, 256]), "kpe_p": O("kpe_p", [SEQ, 32]),
             "pool_s": O("pool_s", [NSB, 15, 512]), "ckv_s": O("ckv_s", [NSB, 32, 256]), "kpe_s": O("kpe_s", [NSB, 32, 32])}
    if upto >= 4:
        phase_odd(K, cfg, x2, x3, Wd, Cd, caches, outs3)
    pi = peer_in[1]
    if upto >= 5:
      phase_peer(K, x3, x4, NTOK // 128, pi["wq"], pi["keys"], tabs[1], pi["g"], C["s_ident"], C["s_ident"], C["iota16"],
               final=(gfin, [(0, NTOK, y)]))
    nc = K.emit()
    return nc, K


def make_in_maps(inp, SEQ, n_cores, K):
    f = lambda a: np.ascontiguousarray(np.asarray(a, dtype=np.float32))
    g0 = lambda k: f(inp[k])[0]
    w_in0 = g0("w_in_even")
    shared = {
        "w_in0": w_in0, "w_in0P": perm_rope_cols(w_in0[:, :640], 64, 16), "gmix0": f(inp["norm_mix"])[0:1], "sink": g0("swa_sink").reshape(1, 8),
        "lam_re": g0("s5_lam_re").reshape(16, 128), "lam_im": g0("s5_lam_im").reshape(16, 128), "log_dt": g0("s5_log_dt").reshape(16, 2),
        "b_re": g0("s5_b_re").reshape(2048, 16), "b_im": g0("s5_b_im").reshape(2048, 16),
        "c_re": g0("s5_c_re").reshape(512, 64), "c_im": g0("s5_c_im").reshape(512, 64), "dsk": g0("s5_d").reshape(4, 128),
        "w_glu": g0("s5_w_glu"), "b_glu": g0("s5_b_glu").reshape(4, 128), "w_out0": g0("w_out_even"), "gfin": f(inp["norm_final"]).reshape(1, 1024),
    }
    for l in range(2):
        shared[f"pwq{l}"] = f(inp["peer_w_q"])[l]
        shared[f"pkeys{l}"] = f(inp["peer_keys"])[l]
        shared[f"pu{l}"] = f(inp["peer_u"])[l]
        shared[f"pv{l}"] = f(inp["peer_v"])[l]
        shared[f"gffn{l}"] = f(inp["norm_ffn"])[l:l + 1]
    W1 = odd_weights(g0("w_in_odd"), g0("w_out_odd"), g0("pool_w"), g0("pool_scale"), g0("mla_q_norm"), g0("mla_kv_norm"), g0("mla_w_uq"),
                     g0("mla_w_uk"), g0("mla_w_uv"), f(inp["norm_mix"])[1])
    for k, v in W1.items():
        shared["W1_" + k] = np.ascontiguousarray(v)
    shared.update(K.consts)
    xp, xs = f(inp["x_prompt"]), f(inp["x_sample"])
    maps = []
    for c in range(n_cores):
        sb = slice(NSB * c, NSB * (c + 1))
        m = dict(shared)
        m["x_all"] = np.ascontiguousarray(np.concatenate([xp[c, :SEQ], xs[sb].reshape(NSB * DEC_SEQ, 1024)], 0))
        m["ck"] = np.ascontiguousarray(g0("cache_swa_k")[sb].reshape(NSB, 128, 128))
        m["cv"] = np.ascontiguousarray(g0("cache_swa_v")[sb].reshape(NSB, 128, 128))
        m["h0_re"] = np.ascontiguousarray(g0("state_ssm_re")[sb].reshape(NSB, 16, 128))
        m["h0_im"] = np.ascontiguousarray(g0("state_ssm_im")[sb].reshape(NSB, 16, 128))
        m["c_pool"] = np.ascontiguousarray(g0("state_pool")[sb])
        m["c_ckv"] = np.ascontiguousarray(g0("cache_mla_ckv")[sb])
        m["c_kpe"] = np.ascontiguousarray(g0("cache_mla_kpe")[sb])
        maps.append(m)
    return maps


def assemble(results, SEQ, n_cores):
    R = [{k: np.asarray(v) for k, v in r.items()} for r in results]
    st = lambda fn: np.stack([fn(r) for r in R])
    cat = lambda fn: np.concatenate([fn(r) for r in R], 0)
    y_p = st(lambda r: r["y"][:SEQ])
    y_s = cat(lambda r: r["y"][SEQ:].reshape(NSB, DEC_SEQ, 1024))
    out = (y_p, y_s,
           st(lambda r: r["k_p"].reshape(128, 2, 64))[None], st(lambda r: r["v_p"].reshape(128, 2, 64))[None],
           st(lambda r: r["hp_re"].reshape(32, 64))[None], st(lambda r: r["hp_im"].reshape(32, 64))[None],
           st(lambda r: r["pool_p"])[None], st(lambda r: r["ckv_p"])[None], st(lambda r: r["kpe_p"])[None],
           cat(lambda r: r["k_s"].reshape(NSB, 128, 2, 64))[None], cat(lambda r: r["v_s"].reshape(NSB, 128, 2, 64))[None],
           cat(lambda r: r["hs_re"].reshape(NSB, 32, 64))[None], cat(lambda r: r["hs_im"].reshape(NSB, 32, 64))[None],
           cat(lambda r: r["pool_s"])[None], cat(lambda r: r["ckv_s"])[None], cat(lambda r: r["kpe_s"])[None])
    return tuple(np.ascontiguousarray(o.astype(np.float32)) for o in out)


def kernel(_upto=5, **inputs):
    SEQ = int(np.asarray(inputs["x_prompt"]).shape[1])
    n_cores = int(np.asarray(inputs["x_prompt"]).shape[0])
    nc, K = build(SEQ, _upto)
    print("n sems", len(K.dsem) + 4, {e: len(K.q[e]) for e in ENGS}, flush=True)
    maps = make_in_maps(inputs, SEQ, n_cores, K)
    res = run_bass_kernel_spmd(nc, maps, core_ids=list(range(n_cores)))
    return assemble(res.results, SEQ, n_cores)
```

```python
import numpy as np
from contextlib import ExitStack, contextmanager
import concourse.bass as bass
import concourse.mybir as mybir
from concourse.bass_utils import run_bass_kernel_spmd

F32 = mybir.dt.float32
BF16 = mybir.dt.bfloat16
I32 = mybir.dt.int32
U32 = mybir.dt.uint32
AF = mybir.ActivationFunctionType
ALU = mybir.AluOpType
AX = mybir.AxisListType

DEBUG_ISA = False
DEBUG_SEM = None
ENGS = ("pe", "act", "dve", "pool", "sp")
CENG = ("pe", "act", "dve", "pool")

D_MODEL = 1024
RMS_EPS = 1e-6
N_EXPERTS = 16384


class KB:
    def __init__(self):
        self.nc = bass.Bass("TRN2", target_bir_lowering=False)
        self.es = ExitStack()
        self.q = {e: [] for e in ENGS}
        self.csem = {e: self.es.enter_context(self.nc.semaphore("c_" + e)) for e in CENG}
        self.ccnt = {e: 0 for e in CENG}
        self.dsem = {}
        self.dcnt = {}
        self.waited = {e: {} for e in ENGS}
        self.lastw = {}
        self.readers = {}
        self.consts = {}
        self.dram_in = {}
        self.dram_out = {}
        self.stack = [self.es]
        self.uid = 0
        self.bufsem = {}
        self.dfree = {"sw": [], "hw": []}
        self.psum_ids = set()

    @staticmethod
    def _where():
        import sys
        f = sys._getframe(2)
        out = []
        while f is not None and len(out) < 4:
            if f.f_code.co_name not in ("op", "dma", "mm", "tr", "actv", "tt", "ts", "stt", "cp", "memset", "load"):
                out.append(f.f_lineno)
            f = f.f_back
        return out

    def sb(self, name, shape, dtype):
        self.uid += 1
        return self.stack[-1].enter_context(self.nc.sbuf_tensor(f"{name}_{self.uid}", list(shape), dtype))

    def ps(self, name, shape, dtype):
        self.uid += 1
        t = self.stack[-1].enter_context(self.nc.psum_tensor(f"{name}_{self.uid}", list(shape), dtype))
        self.psum_ids.add(id(t))
        return t

    def inp(self, name, shape, dtype):
        t = self.nc.dram_tensor(name, list(shape), dtype, kind="ExternalInput")
        self.dram_in[name] = t
        return t.ap()

    def outp(self, name, shape, dtype):
        t = self.nc.dram_tensor(name, list(shape), dtype, kind="ExternalOutput")
        self.dram_out[name] = t
        return t.ap()

    def scratch(self, name, shape, dtype):
        t = self.nc.dram_tensor(name, list(shape), dtype, kind="Internal")
        return t.ap()

    def const(self, name, arr):
        arr = np.ascontiguousarray(arr)
        dt = {np.dtype(np.float32): F32, np.dtype(np.int32): I32}[arr.dtype]
        self.consts[name] = arr
        return self.inp(name, arr.shape, dt)

    @contextmanager
    def phase(self):
        st = ExitStack()
        self.stack.append(st)
        yield
        self.stack.pop()
        self.barrier()
        st.close()

    def _sem(self, key):
        return self.csem[key] if key in self.csem else self.dsem[key]

    @staticmethod
    def _k(b):
        return b if isinstance(b, (str, tuple)) else id(b)

    def _toks(self, bs):
        return [self._k(b) for b in bs if not (isinstance(b, tuple) and b and b[0] == "dram")]

    def _dsem_for(self, buf, queue):
        qt = "sw" if queue == "pool" else "hw"
        k = (buf, qt)
        if k not in self.bufsem:
            if self.dfree[qt]:
                sk = self.dfree[qt].pop()
            else:
                sk = "d%s%d" % (qt, len(self.dsem))
                self.dsem[sk] = self.es.enter_context(self.nc.semaphore(sk))
                self.dcnt[sk] = 0
            self.bufsem[k] = sk
        return self.bufsem[k]

    def _deps(self, eng, reads, writes, own=None):
        waits = {}

        def need(k, v, waw=False):
            if (waw and k == own) or (eng == "pe" and k == "pe"):
                return
            if k in self.dcnt:
                v = self.dcnt[k]
            if self.waited[eng].get(k, 0) >= v:
                return
            if waits.get(k, 0) < v:
                waits[k] = v

        for b in reads:
            for k, v in self.lastw.get(b, {}).items():
                need(k, v)
            if b in self.psum_ids:
                for k, v in self.readers.get(b, {}).items():
                    if k != eng:
                        need(k, v)
        for b in writes:
            for k, v in self.lastw.get(b, {}).items():
                need(k, v, True)
            for k, v in self.readers.get(b, {}).items():
                need(k, v)
        for k, v in waits.items():
            self.waited[eng][k] = v
        return list(waits.items())

    def _commit(self, tok, reads, writes):
        for b in reads:
            d = self.readers.setdefault(b, {})
            if d.get(tok[0], 0) < tok[1]:
                d[tok[0]] = tok[1]
        for b in writes:
            self.lastw[b] = {tok[0]: tok[1]}
            self.readers[b] = {}

    def op(self, eng, fn, reads=(), writes=()):
        reads, writes = self._toks(reads), self._toks(writes)
        waits = self._deps(eng, reads, writes)
        self.ccnt[eng] += 1
        tok = (eng, self.ccnt[eng])
        self._commit(tok, reads, writes)
        self.q[eng].append((waits, fn, (eng, 1), self._where()))

    def dma(self, queue, fn, sem, reads=(), writes=()):
        reads, writes = self._toks(reads), self._toks(writes)
        buf = writes[0] if writes else reads[0]
        sk = self._dsem_for(buf, queue)
        waits = self._deps(queue, reads, writes, own=sk)
        self.dcnt[sk] += 16
        tok = (sk, self.dcnt[sk])
        self._commit(tok, reads, writes)
        self.q[queue].append((waits, fn, (sk, 16), self._where()))

    def barrier(self):
        allv = dict(self.ccnt)
        allv.update(self.dcnt)
        for e in ENGS:
            waits = []
            for k, v in allv.items():
                if v > 0 and self.waited[e].get(k, 0) < v:
                    waits.append((k, v))
                    self.waited[e][k] = v
            if waits:
                self.q[e].append((waits, None, None, 0))
        self.lastw = {}
        self.readers = {}
        for (bk, qt), sk in self.bufsem.items():
            self.dfree[qt].append(sk)
        self.bufsem = {}

    def emit(self):
        self.barrier()
        nc = self.nc
        with nc.Block() as block:
            def mk(name):
                def body(eng):
                    for waits, fn, inc, _w in self.q[name]:
                        for k, v in waits:
                            if DEBUG_SEM and k == DEBUG_SEM:
                                print("SEMDBG-WAIT:", name, k, v)
                            eng.wait_ge(self._sem(k), v)
                        if fn is not None:
                            ins = fn(eng)
                            if DEBUG_ISA and isinstance(ins.ins, mybir.InstISA):
                                print("InstISA:", name, type(ins.ins).__name__, str(ins)[:300])
                            if DEBUG_SEM and inc[0] == DEBUG_SEM:
                                print("SEMDBG:", name, [w for w in waits], str(ins)[:260])
                            ins.then_inc(self._sem(inc[0]), inc[1])
                return body
            block.tensor(mk("pe"))
            block.scalar(mk("act"))
            block.vector(mk("dve"))
            block.gpsimd(mk("pool"))
            block.sync(mk("sp"))
        self.es.close()
        return nc

    def mm(self, out, lhsT, rhs, start, stop, reads, writes):
        self.op("pe", lambda e: e.matmul(out, lhsT, rhs, start=start, stop=stop), reads, writes)

    def tr(self, out, in_, ident, reads, writes):
        self.op("pe", lambda e: e.transpose(out, in_, ident), reads, writes)

    def actv(self, out, in_, func, reads, writes, bias=None, scale=None, accum_out=None, eng="act"):
        kw = {}
        if bias is not None:
            kw["bias"] = bias
        if scale is not None:
            kw["scale"] = scale
        if accum_out is not None:
            kw["accum_out"] = accum_out
        self.op(eng, lambda e: e.activation(out, in_, func, **kw), reads, writes)

    def tt(self, out, in0, in1, op, reads, writes, eng="dve"):
        self.op(eng, lambda e: e.tensor_tensor(out, in0, in1, op), reads, writes)

    def ts(self, out, in0, s1, s2, op0, op1, reads, writes, eng="dve"):
        if op1 is None:
            self.op(eng, lambda e: e.tensor_scalar(out, in0, s1, None, op0), reads, writes)
        else:
            self.op(eng, lambda e: e.tensor_scalar(out, in0, s1, s2, op0, op1), reads, writes)

    def stt(self, out, in0, scalar, in1, op0, op1, reads, writes):
        self.op("dve", lambda e: e.scalar_tensor_tensor(out, in0, scalar, in1, op0, op1), reads, writes)

    def cp(self, out, in_, reads, writes, eng="dve"):
        if eng == "act":
            self.op("act", lambda e: e.activation(out, in_, AF.Copy), reads, writes)
        else:
            self.op(eng, lambda e: e.tensor_copy(out, in_), reads, writes)

    def memset(self, ap, val, writes, eng="dve"):
        self.op(eng, lambda e: e.memset(ap, val), (), writes)

    def load(self, out, in_, sem, reads, writes, queue="sp", **kw):
        self.dma(queue, lambda e: e.dma_start(out=out, in_=in_, **kw), sem, reads, writes)


def bcast_rows(ap_row, p=128):
    return bass.AP(ap_row.tensor, ap_row.offset, [[0, p]] + [list(x) for x in ap_row.ap[-1:]])


def phase_convert(K, pairs):
    with K.phase():
        NSL = 4
        ub = [K.sb(f"cvu{i}", [128, 4, 1024], BF16) for i in range(NSL)]
        vb = [K.sb(f"cvv{i}", [128, 4, 1024], BF16) for i in range(NSL)]
        i = 0
        for (u_d, v_d, tab) in pairs:
            for ch in range(N_EXPERTS // 512):
                s_ = i % NSL
                i += 1
                rows = slice(ch * 512, (ch + 1) * 512)
                K.dma("pool", lambda e, s_=s_, rows=rows, u_d=u_d: e.dma_start(
                    out=ub[s_][:], in_=u_d[rows, :].rearrange("(p j) d -> p j d", j=4), max_dma_last_dim=4096), None, (), [ub[s_]])
                K.dma("pool", lambda e, s_=s_, rows=rows, v_d=v_d: e.dma_start(
                    out=vb[s_][:], in_=v_d[rows, :].rearrange("(p j) d -> p j d", j=4), max_dma_last_dim=4096), None, (), [vb[s_]])
                K.load(tab[rows, 0:1024].rearrange("(p j) d -> p j d", j=4), ub[s_][:], None, [ub[s_]], ())
                K.load(tab[rows, 1024:2048].rearrange("(p j) d -> p j d", j=4), vb[s_][:], None, [vb[s_]], ())


def phase_peer(K, x_in, x_out, ntiles, wq_d, keys_d, tab_d, gffn_d, ident_bf_d, ident_f_d, iota16_d,
               final=None):
    nc = K.nc
    with K.phase():
        ident_bf = K.sb("identbf", [128, 128], BF16)
        ident_f = K.sb("identf", [128, 128], F32)
        iota16 = K.sb("iota16", [128, 16], F32)
        gffn = K.sb("gffn", [128, 1024], F32)
        wq = K.sb("wq", [128, 8, 1024], BF16)
        keysBD = K.sb("keysBD", [128, 8, 256], BF16)
        K.load(ident_f[:], ident_f_d, "c0", (), [ident_f])
        K.load(ident_bf[:], ident_f_d, "c0", (), [ident_bf], queue="pool")
        K.load(iota16[:], iota16_d, "c0", (), [iota16])
        K.load(gffn[:], bcast_rows(gffn_d), "c0", (), [gffn])
        for c in range(8):
            K.load(wq[:, c, :], wq_d[c * 128:(c + 1) * 128, :], "c1", (), [wq], queue="pool")
        if final is not None:
            gfin = K.sb("gfin", [128, 1024], F32)
            K.load(gfin[:], bcast_rows(final[0]), "c0", (), [gfin])

        psB = K.ps("psB", [128, 2048], F32)
        psO = K.ps("psO", [128, 1024], F32)
        psA = K.ps("psA", [128, 512], F32)
        psA_bf = psA[:].bitcast(BF16)

        def two(name, shape, dt):
            return [K.sb(name + str(i), shape, dt) for i in range(2)]
        xt = two("xt", [128, 1024], F32)
        hn = two("hn", [128, 1024], BF16)
        hnT = two("hnT", [128, 8, 128], BF16)
        qT = two("qT", [128, 8, 128], BF16)
        eid = two("eid", [128, 128], I32)
        gate = two("gate", [128, 128], F32)
        act = two("actv", [128, 128], F32)
        wgt = two("wgt", [128, 128], F32)
        wg2 = two("wg2", [128, 128], F32)
        ss = two("ss", [128, 4], F32)
        junk = K.sb("junk", [128, 1024], BF16)
        junk2 = K.sb("junk2", [128, 1024], BF16)
        s2 = K.sb("s2", [128, 128], F32)
        sv = K.sb("sv", [128, 16, 16], F32)
        si = K.sb("si", [128, 16, 16], U32)
        sif = K.sb("sif", [128, 16, 16], BF16)
        cand = K.sb("cand", [128, 8, 256], F32)
        cand2 = K.sb("cand2", [128, 256], F32)
        cv = K.sb("cv", [128, 8, 16], F32)
        ci = K.sb("ci", [128, 8, 16], U32)
        ca = K.sb("ca", [128, 8, 16], U32)
        cb = K.sb("cb", [128, 8, 16], U32)
        oh = K.sb("oh", [128, 8, 16, 16], BF16)
        oh2 = K.sb("oh2", [128, 8, 16, 16], BF16)
        i1f = K.sb("i1f", [128, 8, 16], F32)
        i2f = K.sb("i2f", [128, 8, 16], F32)
        ce = K.sb("ce", [128, 8, 16], F32)
        csum = K.sb("csum", [128, 8], F32)
        NS = 24
        UV = [K.sb(f"UV{i}", [128, 2048], BF16) for i in range(NS)]
        NDG = 8
        dg = [K.sb(f"dg{i}", [128, 128], BF16) for i in range(NDG)]
        xo = two("xo", [128, 1024], F32)
        if final is not None:
            yo = [K.sb("yo", [128, 1024], F32)] * 2

        def prep(i):
            p = i % 2
            X, HN, HT, QT = xt[p], hn[p], hnT[p], qT[p]
            K.load(X[:], x_in[i * 128:(i + 1) * 128, :], f"xl{p}", (), [X])
            K.actv(junk2[:], X[:], AF.Square, [X], [ss[p], junk2], accum_out=ss[p][:, 0:1])
            K.actv(ss[p][:, 1:2], ss[p][:, 0:1], AF.Sqrt, [ss[p], eps_t], [ss[p]], scale=1.0 / D_MODEL, bias=eps_t[:, 0:1])
            K.op("dve", lambda e: e.reciprocal(ss[p][:, 2:3], ss[p][:, 1:2]), [ss[p]], [ss[p]])
            K.stt(HN[:], X[:], ss[p][:, 2:3], gffn[:], ALU.mult, ALU.mult, [X, ss[p], gffn], [HN])
            yield
            for c in range(8):
                K.tr(psA_bf[:, c * 128:(c + 1) * 128], HN[:, c * 128:(c + 1) * 128], ident_bf[:],
                     [HN, ident_bf], [psA])
            K.cp(HT[:].rearrange("p c t -> p (c t)"), psA_bf[:, :], [psA], [HT], eng="act")
            for co in range(8):
                for kc in range(8):
                    K.mm(psB[:, co * 128:(co + 1) * 128], wq[:, kc, co * 128:(co + 1) * 128], HT[:, kc, :],
                         kc == 0, kc == 7, [wq, HT], [psB])
            K.cp(QT[:].rearrange("p c t -> p (c t)"), psB[:, 0:1024], [psB], [QT], eng="act")
            for c in range(8):
                K.mm(psB[:, c * 256:(c + 1) * 256], QT[:, c, :], keysBD[:, c, :], True, True, [QT, keysBD], [psB])
            yield
            for gq in range(4):
                for g_ in range(4):
                    g = gq * 4 + g_
                    S = psB[:, g * 128:(g + 1) * 128]
                    K.op("dve", lambda e, S=S, g=g: e.max(sv[:, g, 0:8], S), [psB], [sv])
                    K.op("dve", lambda e, S=S, g=g: e.match_replace(s2[:], sv[:, g, 0:8], S, -1e30), [psB, sv], [s2])
                    K.op("dve", lambda e, g=g: e.max(sv[:, g, 8:16], s2[:]), [s2], [sv])
                    K.op("dve", lambda e, S=S, g=g: e.max_index(si[:, g, 0:8], sv[:, g, 0:8], S), [psB, sv], [si])
                    K.op("dve", lambda e, S=S, g=g: e.max_index(si[:, g, 8:16], sv[:, g, 8:16], S), [psB, sv], [si])
                yield
            sv4 = sv[:].rearrange("n (h p) k -> n h p k", p=2)
            K.tt(cand[:].rearrange("n h (a b) -> n h a b", b=16),
                 sv4[:, :, 0, :].unsqueeze(3).to_broadcast([128, 8, 16, 16]),
                 sv4[:, :, 1, :].unsqueeze(2).to_broadcast([128, 8, 16, 16]), ALU.add, [sv], [cand])
            K.cp(sif[:], si[:], [si], [sif])
            for h in range(8):
                C = cand[:, h, :]
                K.op("dve", lambda e, C=C, h=h: e.max(cv[:, h, 0:8], C), [cand], [cv])
                K.op("dve", lambda e, C=C, h=h: e.match_replace(cand2[:], cv[:, h, 0:8], C, -1e30), [cand, cv], [cand2])
                K.op("dve", lambda e, h=h: e.max(cv[:, h, 8:16], cand2[:]), [cand2], [cv])
                K.op("dve", lambda e, C=C, h=h: e.max_index(ci[:, h, 0:8], cv[:, h, 0:8], C), [cand, cv], [ci])
                K.op("dve", lambda e, C=C, h=h: e.max_index(ci[:, h, 8:16], cv[:, h, 8:16], C), [cand, cv], [ci])
                if h % 4 == 3:
                    yield
            K.ts(ca[:], ci[:], bitc[:, 0:1], None, ALU.logical_shift_right, None, [ci, bitc], [ca])
            K.ts(cb[:], ci[:], bitc[:, 1:2], None, ALU.bitwise_and, None, [ci, bitc], [cb])
            sif4 = sif[:].rearrange("n (h p) k -> n h p k", p=2)
            io = iota16[:].unsqueeze(1).unsqueeze(1).to_broadcast([128, 8, 16, 16])
            for (cx, pp, dst) in ((ca, 0, i1f), (cb, 1, i2f)):
                K.tt(oh[:], cx[:].unsqueeze(3).to_broadcast([128, 8, 16, 16]), io, ALU.is_equal, [cx, iota16], [oh])
                K.tt(oh2[:], oh[:], sif4[:, :, pp, :].unsqueeze(2).to_broadcast([128, 8, 16, 16]), ALU.mult,
                     [oh, sif], [oh2])
                K.op("dve", lambda e, dst=dst: e.tensor_reduce(dst[:], oh2[:], AX.X, ALU.add), [oh2], [dst])
            K.stt(eid[p][:].rearrange("n (h k) -> n h k", k=16), i1f[:], 128.0, i2f[:], ALU.mult, ALU.add,
                  [i1f, i2f], [eid[p]])
            yield
            K.tt(ce[:], cv[:], cv[:, :, 0:1].to_broadcast([128, 8, 16]), ALU.subtract, [cv], [ce])
            K.actv(ce[:], ce[:], AF.Exp, [ce], [ce])
            K.op("dve", lambda e: e.tensor_reduce(csum[:], ce[:], AX.X, ALU.add), [ce], [csum])
            K.op("dve", lambda e: e.reciprocal(csum[:], csum[:]), [csum], [csum])
            K.tt(gate[p][:].rearrange("n (h k) -> n h k", k=16), ce[:],
                 csum[:].unsqueeze(2).to_broadcast([128, 8, 16]), ALU.mult, [ce, csum], [gate[p]])
            yield

        eps_t = K.sb("eps_t", [128, 1], F32)
        K.memset(eps_t[:], RMS_EPS, [eps_t])
        bitc = K.sb("bitc", [128, 2], U32)
        K.memset(bitc[:, 0:1], 4, [bitc])
        K.memset(bitc[:, 1:2], 15, [bitc])

        slot_u = [0]
        slot_of = {}
        dgc = [0]

        def gatherU(p, r):
            s = slot_u[0] % NS
            slot_u[0] += 1
            slot_of[(p, r)] = s
            K.dma("pool", lambda e, s=s, r=r: e.indirect_dma_start(
                out=UV[s][:, :], out_offset=None, in_=tab_d,
                in_offset=bass.IndirectOffsetOnAxis(ap=eid[p][:, r:r + 1], axis=0)),
                None, [eid[p]], [UV[s]])
            K.op("dve", lambda e, s=s, r=r: e.scalar_tensor_tensor(
                junk[:], UV[s][:, 0:1024], 1.0, hn[p][:], ALU.mult, ALU.mult, accum_out=act[p][:, r:r + 1]),
                [UV[s], hn[p]], [("act", p, r), junk])

        def gatherV(p, r):
            s = slot_of[(p, r)]
            d = dgc[0] % NDG
            dgc[0] += 1
            K.actv(wg2[p][:, r:r + 1], wgt[p][:, r:r + 1], AF.Copy, [("wgt", p, r), gate[p]], [("wg2", p, r)],
                   scale=gate[p][:, r:r + 1])
            K.actv(dg[d][:], ident_bf[:], AF.Copy, [ident_bf, ("wg2", p, r)], [dg[d]], scale=wg2[p][:, r:r + 1])
            for hf in range(2):
                K.mm(psO[:, hf * 512:(hf + 1) * 512], dg[d][:], UV[s][:, 1024 + hf * 512:1024 + (hf + 1) * 512],
                     r == 0, r == 127, [dg[d], UV[s]], [psO])

        def run(gen):
            for _ in gen:
                pass

        ksc = K.phase()
        ksc.__enter__()
        knat = K.sb("knat", [128, 16, 64], F32)
        K.load(knat[:], keys_d.rearrange("h p k d -> k (h p) d"), "c0", (), [knat])
        K.memset(keysBD[:], 0.0, [keysBD])
        for c in range(8):
            K.tr(psA[:, 0:128], knat[:, 2 * c:2 * c + 2, :].rearrange("k a d -> k (a d)"), ident_f[:],
                 [knat, ident_f], [psA])
            K.cp(keysBD[0:64, c, 0:128], psA[0:64, 0:128], [psA], [keysBD], eng="act")
            K.cp(keysBD[64:128, c, 128:256], psA[64:128, 0:128], [psA], [keysBD], eng="act")

        ksc.__exit__(None, None, None)
        run(prep(0))
        for i in range(ntiles):
            p = i % 2
            nxt = prep(i + 1) if i + 1 < ntiles else iter(())
            for r in range(128):
                gatherU(p, r)
                K.actv(wgt[p][:, r:r + 1], act[p][:, r:r + 1], AF.Gelu, [("act", p, r)], [("wgt", p, r)])
                if r >= 1:
                    gatherV(p, r - 1)
                if r % 16 == 15:
                    next(nxt, None)
            gatherV(p, 127)
            run(nxt)
            K.tt(xo[p][:], psO[:], xt[p][:], ALU.add, [psO, xt[p]], [xo[p]])
            if final is None:
                K.load(x_out[i * 128:(i + 1) * 128, :], xo[p][:], f"xs{p}", [xo[p]], [("dram", "x_out")])
            if final is not None:
                K.actv(junk2[:], xo[p][:], AF.Square, [xo[p]], [ss[p], junk2], accum_out=ss[p][:, 3:4])
                K.actv(ss[p][:, 3:4], ss[p][:, 3:4], AF.Sqrt, [ss[p], eps_t], [ss[p]], scale=1.0 / D_MODEL, bias=eps_t[:, 0:1])
                K.op("dve", lambda e, p=p: e.reciprocal(ss[p][:, 3:4], ss[p][:, 3:4]), [ss[p]], [ss[p]])
                K.stt(yo[p][:], xo[p][:], ss[p][:, 3:4], gfin[:], ALU.mult, ALU.mult, [xo[p], ss[p], gfin], [yo[p]])
                for (row0, nrows, oap) in final[1]:
                    lo, hi = max(row0, i * 128), min(row0 + nrows, (i + 1) * 128)
                    if lo < hi:
                        K.load(oap[lo - row0:hi - row0, :], yo[p][lo - i * 128:hi - i * 128, :], f"ys{p}",
                               [yo[p]], [("dram", "y")])


class Rot:
    def __init__(self, items):
        self.items = list(items)
        self.i = 0

    def next(self):
        t = self.items[self.i % len(self.items)]
        self.i += 1
        return t


CHUNK = 64
SWA_SCALE = 64 ** -0.5
MLA_SCALE = 96 ** -0.5
ROPE_THETA = 500000.0
PAST_LEN = 2048
DEC_SEQ = 32
NSB = 4
MASKV = -30000.0


def rmsnorm_tile(K, X, HN, gb, ssb, eps_t, junk, xtok, hntok, gtok, jtok, width=D_MODEL):
    K.actv(junk, X, AF.Square, [xtok], [ssb, jtok], accum_out=ssb[:, 0:1])
    K.actv(ssb[:, 1:2], ssb[:, 0:1], AF.Sqrt, [ssb, eps_t], [ssb], scale=1.0 / width, bias=eps_t[:, 0:1])
    K.op("dve", lambda e: e.reciprocal(ssb[:, 2:3], ssb[:, 1:2]), [ssb], [ssb])
    K.stt(HN, X, ssb[:, 2:3], gb, ALU.mult, ALU.mult, [xtok, ssb, gtok], [hntok])


def phase_swa(K, cfg, x_all, w_in_d, w_inP_d, gmix_d, sink_d, ck_d, cv_d, ropeC_d, ropeS_d, maskA_d, maskB_d, maskS_d,
              ident_f_d, attT_d, uT_d, outs):
    SEQ = cfg["SEQ"]
    NTOK = SEQ + 128
    NTP = SEQ // 128
    with K.phase():
        ident_f = K.sb("identf", [128, 128], F32)
        ident_bf = K.sb("identbf", [128, 128], BF16)
        K.load(ident_f[:], ident_f_d, "c0", (), [ident_f])
        K.load(ident_bf[:], ident_f_d, "c1", (), [ident_bf], queue="pool")
        gmix = K.sb("gmix", [128, 1024], F32)
        K.load(gmix[:], bcast_rows(gmix_d), "c0", (), [gmix])
        w_in = K.sb("w_in", [128, 8, 1280], BF16)
        w_inP = K.sb("w_inP", [128, 8, 640], BF16)
        for c in range(8):
            K.load(w_in[:, c, 0:640], w_in_d[c * 128:(c + 1) * 128, 0:640], "c1", (), [w_in], queue="pool")
            K.load(w_in[:, c, 640:1280], w_in_d[c * 128:(c + 1) * 128, 640:1280], "c1", (), [w_in], queue="pool")
            K.load(w_inP[:, c, :], w_inP_d[c * 128:(c + 1) * 128, :], "c1", (), [w_inP], queue="pool")
        maskA = K.sb("maskA", [128, 512], BF16)
        maskB = K.sb("maskB", [128, 512], BF16)
        K.load(maskA[:], maskA_d, "c1", (), [maskA], queue="pool")
        K.load(maskB[:], maskB_d, "c1", (), [maskB], queue="pool")
        sinkexp = K.sb("sinkexp", [128, 8], F32)
        K.load(sinkexp[:], bcast_rows(sink_d), "c0", (), [sinkexp])
        K.actv(sinkexp[:], sinkexp[:], AF.Exp, [sinkexp], [sinkexp])
        eps_t = K.sb("eps_t", [128, 1], F32)
        K.memset(eps_t[:], RMS_EPS, [eps_t])

        kT_all = K.sb("kT_all", [64, 2, NTOK], BF16)
        v_all = K.sb("v_all", [128, NTP + 1, 2, 65], BF16)
        K.memset(v_all[:, :, :, 64:65], 1.0, [v_all])
        junk = K.sb("junk", [128, 1024], BF16)
        xt = [K.sb(f"xt{i}", [128, 1024], F32) for i in range(2)]
        ssb = [K.sb(f"ssb{i}", [128, 4], F32) for i in range(2)]
        hn = [K.sb(f"hn{i}", [128, 1024], BF16) for i in range(2)]
        hnT = K.sb("hnT", [128, 8, 512], BF16)
        ropeC = [K.sb(f"ropeC{i}", [64, 512], F32) for i in range(2)]
        ropeS = [K.sb(f"ropeS{i}", [64, 512], F32) for i in range(2)]
        qT = K.sb("qT", [64, 8, 512], BF16)
        t1 = [K.sb(f"t1_{i}", [64, 512], F32) for i in range(2)]
        t2 = [K.sb(f"t2_{i}", [64, 512], F32) for i in range(2)]
        kf32 = K.sb("kf32", [64, 2, 128], F32)
        ktok_p = K.sb("ktok", [128, 128], F32)
        vtok_p = K.sb("vtok", [128, 128], F32)
        ktok_s = K.sb("ktok_s", [128, 128], F32)
        vtok_s = K.sb("vtok_s", [128, 128], F32)
        uT = [K.sb(f"uT{i}", [128, 4, 512], BF16) for i in range(2)]
        attT = [K.sb(f"attT{i}", [128, 4, 512], BF16) for i in range(2)]
        PT = [K.sb(f"PT{i}", [128, 2, 2, 512], BF16) for i in range(2)]
        att_tok = [K.sb(f"att_tok{i}", [128, 512], BF16) for i in range(2)]
        den = K.sb("den", [128, 8], F32)
        kc32 = [K.sb(f"kc32_{i}", [128, 128], F32) for i in range(2)]
        vc32 = [K.sb(f"vc32_{i}", [128, 128], F32) for i in range(2)]
        kcb = [K.sb(f"kcb{i}", [128, 128], BF16) for i in range(2)]
        kcT = K.sb("kcT", [64, NSB, 2, 128], BF16)
        vcb = K.sb("vcb", [128, NSB, 2, 65], BF16)
        K.memset(vcb[:, :, :, 64:65], 1.0, [vcb])
        PTc = K.sb("PTc", [128, NSB, 2, 512], BF16)
        PTn = K.sb("PTn", [128, 2, 512], BF16)
        maskS = K.sb("maskS", [128, NSB + 1, 512], BF16)
        K.load(maskS[:], maskS_d, "c1", (), [maskS], queue="pool")

        psT = K.ps("psT", [128, 512], F32)
        psT_bf = psT[:].bitcast(BF16)
        gen = Rot([K.ps(f"gen{i}", [128, 512], F32) for i in range(4)])
        psOX = K.ps("psOX", [128, 512], F32)
        psOY = K.ps("psOY", [128, 512], F32)
        psAT = K.ps("psAT", [128, 512], F32)
        psAT_bf = psAT[:].bitcast(BF16)

        STOP = cfg.get("stop", 99)
        SSTOP = cfg.get("sstop", 99)
        if STOP <= 0:
            return
        nsup = SEQ // 512
        sups = [(s * 512, 512, False) for s in range(nsup)] + [(SEQ, 128, True)]
        xi = 0
        for si_, (tok0, W, is_s) in enumerate(sups):
            sp = si_ % 2
            ntl = W // 128
            if is_s and STOP <= 5:
                return
            K.load(ropeC[sp][:, 0:W], ropeC_d[:, tok0:tok0 + W], f"rope{sp}", (), [ropeC[sp]])
            K.load(ropeS[sp][:, 0:W], ropeS_d[:, tok0:tok0 + W], f"rope{sp}", (), [ropeS[sp]])
            for tl in range(ntl):
                p = xi % 2
                xi += 1
                K.load(xt[p][:], x_all[tok0 + tl * 128: tok0 + (tl + 1) * 128, :], f"xl{p}", (), [xt[p]])
                rmsnorm_tile(K, xt[p][:], hn[p][:], gmix[:], ssb[p], eps_t, junk[:], xt[p], hn[p], gmix, junk)
                for c in range(8):
                    K.tr(psT_bf[:, c * 128:(c + 1) * 128], hn[p][:, c * 128:(c + 1) * 128], ident_bf[:],
                         [hn[p], ident_bf], [psT])
                K.cp(hnT[:, :, tl * 128:(tl + 1) * 128], psT_bf[:, :].rearrange("p (c t) -> p c t", c=8), [psT], [hnT],
                     eng="act")
            if STOP <= 1 or (is_s and SSTOP <= 1):
                return
            for hh in range(10):
                col0 = hh * 64
                pq, pp = gen.next(), gen.next()
                for kc in range(8):
                    K.mm(pq[0:64, 0:W], w_in[:, kc, col0:col0 + 64], hnT[:, kc, 0:W], kc == 0, kc == 7, [w_in, hnT], [pq])
                for kc in range(8):
                    K.mm(pp[0:64, 0:W], w_inP[:, kc, col0:col0 + 64], hnT[:, kc, 0:W], kc == 0, kc == 7, [w_inP, hnT], [pp])
                a, b = t1[hh % 2], t2[hh % 2]
                K.tt(a[:, 0:W], pq[0:64, 0:W], ropeC[sp][:, 0:W], ALU.mult, [pq, ropeC[sp]], [a])
                K.tt(b[:, 0:W], pp[0:64, 0:W], ropeS[sp][:, 0:W], ALU.mult, [pp, ropeS[sp]], [b])
                if hh < 8:
                    K.tt(qT[:, hh, 0:W], a[:, 0:W], b[:, 0:W], ALU.add, [a, b], [qT])
                else:
                    g = hh - 8
                    K.tt(kT_all[:, g, tok0:tok0 + W], a[:, 0:W], b[:, 0:W], ALU.add, [a, b], [kT_all])
                    if is_s or tok0 + W == SEQ:
                        K.tt(kf32[:, g, :], a[:, W - 128:W], b[:, W - 128:W], ALU.add, [a, b], [kf32])
            if STOP <= 2 or (is_s and SSTOP <= 2):
                return
            for c in range(4):
                pu = gen.next()
                for kc in range(8):
                    K.mm(pu[:, 0:W], w_in[:, kc, 768 + c * 128:768 + (c + 1) * 128], hnT[:, kc, 0:W], kc == 0, kc == 7,
                         [w_in, hnT], [pu])
                K.cp(uT[sp][:, c, 0:W], pu[:, 0:W], [pu], [uT[sp]], eng="act")
            for c in range(4):
                K.load(uT_d[c * 128:(c + 1) * 128, tok0:tok0 + W], uT[sp][:, c, 0:W], f"ust{sp}", [uT[sp]], [("dram", "uT")])
            if STOP <= 3 or (is_s and SSTOP <= 3):
                return
            last_tile = is_s or (tok0 + W == SEQ)
            ktok, vtok = (ktok_s, vtok_s) if is_s else (ktok_p, vtok_p)
            for tl in range(ntl):
                j = tok0 // 128 + tl
                pv = gen.next()
                for kc in range(8):
                    K.mm(pv[:, 0:128], hnT[:, kc, tl * 128:(tl + 1) * 128], w_in[:, kc, 640:768], kc == 0, kc == 7,
                         [w_in, hnT], [pv])
                K.cp(v_all[:, j, :, 0:64], pv[:, 0:128].rearrange("p (g d) -> p g d", g=2), [pv], [v_all], eng="act")
                if last_tile and tl == ntl - 1:
                    K.cp(vtok[:], pv[:, 0:128], [pv], [vtok])
            if last_tile and not (is_s and cfg.get("nok")):
                pk = gen.next()
                for g in range(2):
                    K.tr(pk[:, g * 64:(g + 1) * 64], kf32[:, g, :], ident_f[0:64, 0:64], [kf32, ident_f], [pk])
                K.cp(ktok[:], pk[:, 0:128], [pk], [ktok])
                if not is_s:
                    K.load(outs["k_p"], ktok[:], "ost", [ktok], [("dram", "o")])
                    K.load(outs["v_p"], vtok[:], "ost", [vtok], [("dram", "o")])
                elif not cfg.get("nostore"):
                    for b in range(NSB):
                        K.load(outs["k_s"][b, 96:128, :], ktok[b * 32:(b + 1) * 32, :], "ost", [ktok], [("dram", "o")])
                        K.load(outs["v_s"][b, 96:128, :], vtok[b * 32:(b + 1) * 32, :], "ost", [vtok], [("dram", "o")])
            if STOP <= 4 or (is_s and SSTOP <= 4):
                return
            if not is_s:
                for tl in range(ntl):
                    j = tok0 // 128 + tl
                    ap_ = j % 2
                    jjs = [jj for jj in (j - 1, j) if jj >= 0]
                    for g in range(2):
                        for jj in jjs:
                            pS = gen.next()
                            K.mm(pS[:, :], kT_all[:, g, jj * 128:(jj + 1) * 128], qT[:, 4 * g:4 * g + 4, tl * 128:(tl + 1) * 128],
                                 True, False, [kT_all, qT], [pS])
                            K.mm(pS[:, :], ident_bf[:], (maskA if jj == j - 1 else maskB)[:], False, True,
                                 [ident_bf, maskA, maskB], [pS])
                            K.actv(PT[ap_][:, jj - j + 1, g, :], pS[:, :], AF.Exp, [pS], [PT[ap_]], scale=SWA_SCALE)
                    for h in range(8):
                        g = h // 4
                        po = psOX if h < 4 else psOY
                        for n_, jj in enumerate(jjs):
                            K.mm(po[:, (h % 4) * 65:(h % 4) * 65 + 65],
                                 PT[ap_][:, jj - j + 1, g, (h % 4) * 128:(h % 4) * 128 + 128], v_all[:, jj, g, :],
                                 n_ == 0, n_ == len(jjs) - 1, [PT[ap_], v_all], [po])
                    swa_finish(K, psOX, psOY, den, sinkexp, att_tok[ap_], psAT, psAT_bf, ident_bf, attT[sp], tl, 128)
            else:
                if STOP <= 6:
                    return
                for b in range(NSB):
                    bp = b % 2
                    K.load(kc32[bp][:], ck_d[b], f"kcl{bp}", (), [kc32[bp]])
                    K.load(vc32[bp][:], cv_d[b], f"kcl{bp}", (), [vc32[bp]])
                    K.load(outs["k_s"][b, 0:96, :], kc32[bp][32:128, :], "ost", [kc32[bp]], [("dram", "o")])
                    K.load(outs["v_s"][b, 0:96, :], vc32[bp][32:128, :], "ost", [vc32[bp]], [("dram", "o")])
                    K.cp(kcb[bp][:], kc32[bp][:], [kc32[bp]], [kcb[bp]])
                    K.cp(vcb[:, b, :, 0:64], vc32[bp][:].rearrange("p (g d) -> p g d", g=2), [vc32[bp]], [vcb])
                    for g in range(2):
                        K.tr(psT_bf[0:64, g * 128:(g + 1) * 128], kcb[bp][:, g * 64:(g + 1) * 64], ident_bf[:],
                             [kcb[bp], ident_bf], [psT])
                    K.cp(kcT[:, b, :, :].rearrange("p g t -> p (g t)"), psT_bf[0:64, 0:256], [psT], [kcT], eng="act")
                    for g in range(2):
                        pS = gen.next()
                        K.mm(pS[:, :], kcT[:, b, g, :], qT[:, 4 * g:4 * g + 4, 0:128], True, False, [kcT, qT], [pS])
                        K.mm(pS[:, :], ident_bf[:], maskS[:, b, :], False, True, [ident_bf, maskS], [pS])
                        K.actv(PTc[:, b, g, :], pS[:, :], AF.Exp, [pS], [PTc], scale=SWA_SCALE)
                if STOP <= 7:
                    return
                for g in range(2):
                    pS = gen.next()
                    K.mm(pS[:, :], kT_all[:, g, SEQ:SEQ + 128], qT[:, 4 * g:4 * g + 4, 0:128], True, False, [kT_all, qT], [pS])
                    K.mm(pS[:, :], ident_bf[:], maskS[:, 4, :], False, True, [ident_bf, maskS], [pS])
                    K.actv(PTn[:, g, :], pS[:, :], AF.Exp, [pS], [PTn], scale=SWA_SCALE)
                if STOP <= 8:
                    return
                for h in range(8):
                    g = h // 4
                    po = psOX if h < 4 else psOY
                    oo = po[:, (h % 4) * 65:(h % 4) * 65 + 65]
                    hs = slice((h % 4) * 128, (h % 4) * 128 + 128)
                    for b in range(NSB):
                        K.mm(oo, PTc[:, b, g, hs], vcb[:, b, g, :], b == 0, False, [PTc, vcb], [po])
                    K.mm(oo, PTn[:, g, hs], v_all[:, NTP, g, :], False, True, [PTn, v_all], [po])
                swa_finish(K, psOX, psOY, den, sinkexp, att_tok[0], psAT, psAT_bf, ident_bf, attT[sp], 0, 128)
            for c in range(4):
                K.load(attT_d[c * 128:(c + 1) * 128, tok0:tok0 + W], attT[sp][:, c, 0:W], f"ast{sp}", [attT[sp]],
                       [("dram", "attT")])


def swa_finish(K, psOX, psOY, den, sinkexp, att_tok, psAT, psAT_bf, ident_bf, attT, tl, W):
    for half, po in enumerate((psOX, psOY)):
        o3 = po[:, 0:260].rearrange("p (h e) -> p h e", e=65)
        K.tt(den[:, half * 4:half * 4 + 4], o3[:, :, 64], sinkexp[:, half * 4:half * 4 + 4], ALU.add, [po, sinkexp], [den])
    K.op("dve", lambda e: e.reciprocal(den[:], den[:]), [den], [den])
    for half, po in enumerate((psOX, psOY)):
        o3 = po[:, 0:260].rearrange("p (h e) -> p h e", e=65)
        K.tt(att_tok[:, half * 256:(half + 1) * 256].rearrange("p (h d) -> p h d", d=64), o3[:, :, 0:64],
             den[:, half * 4:half * 4 + 4].unsqueeze(2).to_broadcast([128, 4, 64]), ALU.mult, [po, den], [att_tok])
    for c in range(4):
        K.tr(psAT_bf[:, c * 128:(c + 1) * 128], att_tok[:, c * 128:(c + 1) * 128], ident_bf[:], [att_tok, ident_bf], [psAT])
    K.cp(attT[:, :, tl * 128:(tl + 1) * 128], psAT_bf[:, 0:512].rearrange("p (c t) -> p c t", c=4), [psAT], [attT], eng="act")


def rope_tables(pos, rot, head_dim, nrep=1):
    half = rot // 2
    inv = ROPE_THETA ** (-np.arange(0, rot, 2, dtype=np.float32) / rot)
    ang = pos.astype(np.float32)[None, :] * inv.astype(np.float32)[:, None]
    cos, sin = np.cos(ang).astype(np.float32), np.sin(ang).astype(np.float32)
    C = np.ones((head_dim, len(pos)), np.float32)
    S = np.zeros((head_dim, len(pos)), np.float32)
    C[0:half] = cos
    C[half:rot] = cos
    S[0:half] = -sin
    S[half:rot] = sin
    return C, S


def perm_rope_cols(w, head_dim, rot):
    half = rot // 2
    n = w.shape[1]
    idx = np.arange(n)
    d = idx % head_dim
    src = np.where(d < half, idx + half, np.where(d < rot, idx - half, idx))
    return np.ascontiguousarray(w[:, src])


def token_positions(SEQ):
    return np.concatenate([np.arange(SEQ), np.tile(PAST_LEN + np.arange(DEC_SEQ), NSB)])


def swa_consts(SEQ):
    C, S = rope_tables(token_positions(SEQ), 16, 64)
    k = np.arange(128)[:, None]
    q = (np.arange(512) % 128)[None, :]
    maskA = np.where((k < 64) & (q >= 64), MASKV, 0.0).astype(np.float32)
    maskB = np.where((k >= 64) & (q < 64), MASKV, 0.0).astype(np.float32)
    qb = ((np.arange(512) % 128) // 32)[None, :]
    maskS = np.zeros((128, NSB + 1, 512), np.float32)
    for b in range(NSB):
        maskS[:, b, :] = np.where(qb == b, 0.0, MASKV)
    maskS[:, NSB, :] = np.where(qb == (np.arange(128) // 32)[:, None], 0.0, MASKV)
    return {"ropeC": C, "ropeS": S, "maskA": maskA, "maskB": maskB, "maskS": maskS, "ident": np.eye(128, dtype=np.float32)}


TWO_PI = float(2.0 * np.pi)
PI = float(np.pi)


def sincos(K, ang, sin_o, cos_o, tmp_i, tmp_a, tmp_b, toks_in, tok_sin, tok_cos, tok_tmp):
    ki, ka, kb = tok_tmp
    a, b = tmp_a, tmp_b
    K.ts(b, ang, 1.0 / TWO_PI, None, ALU.mult, None, toks_in, [kb])
    K.cp(tmp_i, b, [kb], [ki])
    K.cp(b, tmp_i, [ki], [kb])
    K.stt(a, b, -TWO_PI, ang, ALU.mult, ALU.add, [kb] + list(toks_in), [ka])
    for _ in range(2):
        K.ts(b, a, PI, -TWO_PI, ALU.is_gt, ALU.mult, [ka], [kb])
        K.tt(a, a, b, ALU.add, [ka, kb], [ka])
        K.ts(b, a, -PI, TWO_PI, ALU.is_lt, ALU.mult, [ka], [kb])
        K.tt(a, a, b, ALU.add, [ka, kb], [ka])
    K.actv(sin_o, a, AF.Sin, [ka], [tok_sin])
    K.ts(a, a, PI / 2, None, ALU.add, None, [ka], [ka])
    K.ts(b, a, PI, -TWO_PI, ALU.is_gt, ALU.mult, [ka], [kb])
    K.tt(a, a, b, ALU.add, [ka, kb], [ka])
    K.actv(cos_o, a, AF.Sin, [ka], [tok_cos])


def phase_s5(K, cfg, x_all, uT_d, attT_d, x1_d, P, ident_f_d, iotaL_d, outs):
    SEQ = cfg["SEQ"]
    LC = 256
    with K.phase():
        ident_f = K.sb("identf", [128, 128], F32)
        K.load(ident_f[:], ident_f_d, None, (), [ident_f])
        iotaL = K.sb("iotaL", [128, LC], F32)
        K.load(iotaL[:], iotaL_d, None, (), [iotaL])
        w_glu = K.sb("w_glu", [128, 4, 512], BF16)
        w_out = K.sb("w_out", [128, 8, 1024], BF16)
        for c in range(4):
            K.load(w_glu[:, c, :], P["w_glu"][c * 128:(c + 1) * 128, :], None, (), [w_glu], queue="pool")
        for c in range(8):
            K.load(w_out[:, c, :], P["w_out"][c * 128:(c + 1) * 128, :], None, (), [w_out], queue="pool")
        psS = K.ps("psS", [128, 512], F32)
        gp = Rot([K.ps(f"pb{i}", [128, 512], F32) for i in range(2)])
        pyr = Rot([K.ps(f"py{i}", [128, 512], F32) for i in range(2)])
        psG = K.ps("psG", [128, 512], F32)
        psO = K.ps("psO", [128, 1024], F32)

        def load_T(name, src_16x128):
            raw = K.sb(name + "_raw", [16, 128], F32)
            dst = K.sb(name, [128, 16], F32)
            K.load(raw[:], src_16x128, None, (), [raw])
            K.tr(psS[:, 0:16], raw[:], ident_f[0:16, 0:16], [raw, ident_f], [psS])
            K.cp(dst[:], psS[:, 0:16], [psS], [dst])
            return dst
        lre = load_T("lre", P["lam_re"])
        lim = load_T("lim", P["lam_im"])
        ldr = K.sb("ldr", [16, 2], F32)
        K.load(ldr[:], P["log_dt"], None, (), [ldr])
        ldx = K.sb("ldx", [16, 2, 64], F32)
        K.cp(ldx[:], ldr[:].unsqueeze(2).to_broadcast([16, 2, 64]), [ldr], [ldx])
        dtt = K.sb("dtt", [128, 16], F32)
        K.tr(psS[:, 0:16], ldx[:].rearrange("t a n -> t (a n)"), ident_f[0:16, 0:16], [ldx, ident_f], [psS])
        K.actv(dtt[:], psS[:, 0:16], AF.Exp, [psS], [dtt])

        def sm(name):
            return K.sb(name, [128, 16], F32)
        lr, th, mag, sn, cs, abr, abi = sm("lr"), sm("th"), sm("mag"), sm("sn"), sm("cs"), sm("abr"), sm("abi")
        ta, tb, nr, dn, fre, fim = sm("ta"), sm("tb"), sm("nr"), sm("dn"), sm("fre"), sm("fim")
        ti = K.sb("ti", [128, 16], I32)
        K.ts(lr[:], lre[:], -1e-4, None, ALU.min, None, [lre], [lr])
        K.tt(th[:], lim[:], dtt[:], ALU.mult, [lim, dtt], [th])
        K.tt(mag[:], lr[:], dtt[:], ALU.mult, [lr, dtt], [mag])
        K.actv(mag[:], mag[:], AF.Exp, [mag], [mag])
        sincos(K, th[:], sn[:], cs[:], ti[:], ta[:], tb[:], [th], sn, cs, (ti, ta, tb))
        K.tt(abr[:], mag[:], cs[:], ALU.mult, [mag, cs], [abr])
        K.tt(abi[:], mag[:], sn[:], ALU.mult, [mag, sn], [abi])
        K.ts(nr[:], abr[:], -1.0, None, ALU.add, None, [abr], [nr])
        K.tt(dn[:], lr[:], lr[:], ALU.mult, [lr], [dn])
        K.tt(ta[:], lim[:], lim[:], ALU.mult, [lim], [ta])
        K.tt(dn[:], dn[:], ta[:], ALU.add, [dn, ta], [dn])
        K.op("dve", lambda e: e.reciprocal(dn[:], dn[:]), [dn], [dn])
        K.tt(ta[:], nr[:], lr[:], ALU.mult, [nr, lr], [ta])
        K.tt(tb[:], abi[:], lim[:], ALU.mult, [abi, lim], [tb])
        K.tt(fre[:], ta[:], tb[:], ALU.add, [ta, tb], [fre])
        K.tt(fre[:], fre[:], dn[:], ALU.mult, [fre, dn], [fre])
        K.tt(ta[:], abi[:], lr[:], ALU.mult, [abi, lr], [ta])
        K.tt(tb[:], nr[:], lim[:], ALU.mult, [nr, lim], [tb])
        K.tt(fim[:], ta[:], tb[:], ALU.subtract, [ta, tb], [fim])
        K.tt(fim[:], fim[:], dn[:], ALU.mult, [fim, dn], [fim])

        cosT = K.sb("cosT", [128, 16, LC], F32)
        sinT = K.sb("sinT", [128, 16, LC], F32)
        BTp = [K.sb(f"BTp{j}", [128, 16, 128], BF16) for j in range(2)]
        CTp = [K.sb(f"CTp{j}", [128, 16, 128], F32) for j in range(2)]
        dcol = K.sb("dcol", [128, 4], F32)
        bgcol = K.sb("bgcol", [128, 4], F32)
        setup_scope = K.phase()
        setup_scope.__enter__()
        angT = K.sb("angT", [128, 16, LC], F32)
        tmpT = K.sb("tmpT", [128, 16, LC], F32)
        tmpI = K.sb("tmpI", [128, 16, LC], I32)
        K.tt(angT[:], th[:].unsqueeze(2).to_broadcast([128, 16, LC]), iotaL[:].unsqueeze(1).to_broadcast([128, 16, LC]),
             ALU.mult, [th, iotaL], [angT])
        sincos_big(K, angT, sinT, cosT, tmpI, tmpT)

        br = K.sb("br", [128, 16, 16], F32)
        bi = K.sb("bi", [128, 16, 16], F32)
        K.load(br[:], P["b_re"].rearrange("(t p) c -> p t c", p=128), None, (), [br])
        K.load(bi[:], P["b_im"].rearrange("(t p) c -> p t c", p=128), None, (), [bi])
        bbr = K.sb("bbr", [128, 16, 16], F32)
        bbi = K.sb("bbi", [128, 16, 16], F32)
        tq = K.sb("tq", [128, 16, 16], F32)
        fre_b = fre[:].unsqueeze(2).to_broadcast([128, 16, 16])
        fim_b = fim[:].unsqueeze(2).to_broadcast([128, 16, 16])
        K.tt(bbr[:], br[:], fre_b, ALU.mult, [br, fre], [bbr])
        K.tt(tq[:], bi[:], fim_b, ALU.mult, [bi, fim], [tq])
        K.tt(bbr[:], bbr[:], tq[:], ALU.subtract, [bbr, tq], [bbr])
        K.tt(bbi[:], bi[:], fre_b, ALU.mult, [bi, fre], [bbi])
        K.tt(tq[:], br[:], fim_b, ALU.mult, [br, fim], [tq])
        K.tt(bbi[:], bbi[:], tq[:], ALU.add, [bbi, tq], [bbi])
        bpad = K.sb("bpad", [128, 16, 128], F32)
        for j, bb in enumerate((bbr, bbi)):
            K.memset(bpad[:], 0.0, [bpad])
            bp4 = bpad[:].rearrange("p (r i) f -> p r i f", i=4)
            bb4 = bb[:].rearrange("p (r i) c -> p r i c", i=4)
            for i in range(4):
                K.cp(bp4[0:64, :, i, 32 * i:32 * i + 16], bb4[0:64, :, i, :], [bb], [bpad])
                K.cp(bp4[64:128, :, i, 32 * i + 16:32 * i + 32], bb4[64:128, :, i, :], [bb], [bpad])
            for t in range(16):
                K.tr(psS[:, 0:128], bpad[:, t, :], ident_f[:], [bpad, ident_f], [psS])
                K.cp(BTp[j][:, t, :], psS[:, 0:128], [psS], [BTp[j]], eng="act")
        cin = K.sb("cin", [128, 128], F32)
        for j, src in enumerate((P["c_re"], P["c_im"])):
            K.memset(CTp[j][:], 0.0, [CTp[j]])
            for rt in range(4):
                K.memset(cin[:], 0.0, [cin])
                for gl in range(8):
                    K.load(cin[gl * 16:(gl + 1) * 16, (gl % 2) * 64:(gl % 2) * 64 + 64],
                           src[rt * 128 + gl * 16: rt * 128 + (gl + 1) * 16, :], None, (), [cin])
                K.tr(psS[:, 0:128], cin[:], ident_f[:], [cin, ident_f], [psS])
                for i in range(4):
                    if j == 0:
                        K.cp(CTp[j][:, 4 * rt + i, 32 * i:32 * i + 32], psS[:, 32 * i:32 * i + 32], [psS], [CTp[j]], eng="act")
                    else:
                        K.ts(CTp[j][:, 4 * rt + i, 32 * i:32 * i + 32], psS[:, 32 * i:32 * i + 32], -1.0, None, ALU.mult, None,
                             [psS], [CTp[j]])
        dsk = K.sb("dsk_raw", [4, 128], F32)
        K.load(dsk[:], P["dsk"], None, (), [dsk])
        K.tr(psS[:, 0:4], dsk[:], ident_f[0:4, 0:4], [dsk, ident_f], [psS])
        K.cp(dcol[:], psS[:, 0:4], [psS], [dcol])
        bgr = K.sb("bg_raw", [4, 128], F32)
        K.load(bgr[:], P["b_glu"], None, (), [bgr])
        K.tr(psS[:, 0:4], bgr[:], ident_f[0:4, 0:4], [bgr, ident_f], [psS])
        K.cp(bgcol[:], psS[:, 0:4], [psS], [bgcol])

        setup_scope.__exit__(None, None, None)
        hpr = K.sb("hpr", [128, 16], F32)
        hpi = K.sb("hpi", [128, 16], F32)
        uTb = [K.sb(f"uTb{i}", [128, 4, LC], BF16) for i in range(2)]
        mixT = [K.sb(f"mixT{i}", [128, 8, LC], BF16) for i in range(2)]
        mixS = K.sb("mixS", [128, 8, 128], BF16)
        W = {n: [K.sb(f"{n}{i}", [128, LC], F32) for i in range(2)] for n in
             ("t1", "t2", "t3", "t4", "p1", "p2", "p3", "p4", "wri", "wii", "wr", "wi", "hr", "hi")}
        yT = K.sb("yT", [128, LC], F32)
        sq = K.sb("sq", [128, LC], F32)
        z2 = [K.sb(f"z2_{i}", [128, 4, LC], BF16) for i in range(2)]
        gt = K.sb("gt", [128, LC], F32)
        xt = [K.sb(f"xt{i}", [128, 1024], F32) for i in range(2)]
        xo = [K.sb(f"xo{i}", [128, 1024], F32) for i in range(2)]
        hout = K.sb("hout", [16, 128], F32)
        h0raw = K.sb("h0raw", [16, 128], F32)

        def set_state(src_re, src_im):
            if src_re is None:
                K.memset(hpr[:], 0.0, [hpr])
                K.memset(hpi[:], 0.0, [hpi])
                return
            for src, dst in ((src_re, hpr), (src_im, hpi)):
                K.load(h0raw[:], src, None, (), [h0raw])
                K.tr(psS[:, 0:16], h0raw[:], ident_f[0:16, 0:16], [h0raw, ident_f], [psS])
                K.cp(dst[:], psS[:, 0:16], [psS], [dst])

        def put_state(dst_re, dst_im):
            for dst, src in ((dst_re, hpr), (dst_im, hpi)):
                K.tr(psS[0:16, 0:128], src[:], ident_f[:], [src, ident_f], [psS])
                K.cp(hout[:], psS[0:16, 0:128], [psS], [hout])
                K.load(dst, hout[:], None, [hout], ())

        cnt = [0]

        def s5_chunk(tok0, L, mix, mcol0):
            ci = cnt[0]
            cnt[0] += 1
            ub, zz = uTb[ci % 2], z2[ci % 2]
            for c in range(4):
                K.load(ub[:, c, 0:L], uT_d[c * 128:(c + 1) * 128, tok0:tok0 + L], None, (), [ub])
                K.load(mix[:, c, mcol0:mcol0 + L], attT_d[c * 128:(c + 1) * 128, tok0:tok0 + L], None, (), [mix])
            for rt in range(4):
                py = pyr.next()
                for i in range(4):
                    t = 4 * rt + i
                    w = {n: W[n][t % 2] for n in W}
                    pb = gp.next()
                    K.mm(pb[:, 0:L], BTp[0][:, t, :], ub[:, rt, 0:L], True, True, [BTp[0], ub], [pb])
                    K.mm(pb[:, 256:256 + L], BTp[1][:, t, :], ub[:, rt, 0:L], True, True, [BTp[1], ub], [pb])
                    cT, sT = cosT[:, t, 0:L], sinT[:, t, 0:L]
                    bre, bim = pb[:, 0:L], pb[:, 256:256 + L]
                    K.tt(w["t1"][:, 0:L], bre, cT, ALU.mult, [pb, cosT], [w["t1"]])
                    K.tt(w["t2"][:, 0:L], bim, sT, ALU.mult, [pb, sinT], [w["t2"]])
                    K.tt(w["t3"][:, 0:L], bim, cT, ALU.mult, [pb, cosT], [w["t3"]])
                    K.tt(w["t4"][:, 0:L], bre, sT, ALU.mult, [pb, sinT], [w["t4"]])
                    K.tt(w["wri"][:, 0:L], w["t1"][:, 0:L], w["t2"][:, 0:L], ALU.add, [w["t1"], w["t2"]], [w["wri"]])
                    K.tt(w["wii"][:, 0:L], w["t3"][:, 0:L], w["t4"][:, 0:L], ALU.subtract, [w["t3"], w["t4"]], [w["wii"]])
                    rb = mag[:, t:t + 1].to_broadcast([128, L])
                    K.op("dve", lambda e, w=w, rb=rb, t=t: e.tensor_tensor_scan(
                        w["wr"][:, 0:L], rb, w["wri"][:, 0:L], hpr[:, t:t + 1], ALU.mult, ALU.add),
                        [mag, w["wri"], hpr], [w["wr"]])
                    K.op("dve", lambda e, w=w, rb=rb, t=t: e.tensor_tensor_scan(
                        w["wi"][:, 0:L], rb, w["wii"][:, 0:L], hpi[:, t:t + 1], ALU.mult, ALU.add),
                        [mag, w["wii"], hpi], [w["wi"]])
                    PE_ = cfg.get("s5_post_eng", "dve")
                    K.tt(w["p1"][:, 0:L], w["wr"][:, 0:L], cT, ALU.mult, [w["wr"], cosT], [w["p1"]], eng=PE_)
                    K.tt(w["p2"][:, 0:L], w["wi"][:, 0:L], sT, ALU.mult, [w["wi"], sinT], [w["p2"]], eng=PE_)
                    K.tt(w["p3"][:, 0:L], w["wi"][:, 0:L], cT, ALU.mult, [w["wi"], cosT], [w["p3"]], eng=PE_)
                    K.tt(w["p4"][:, 0:L], w["wr"][:, 0:L], sT, ALU.mult, [w["wr"], sinT], [w["p4"]], eng=PE_)
                    K.tt(w["hr"][:, 0:L], w["p1"][:, 0:L], w["p2"][:, 0:L], ALU.subtract, [w["p1"], w["p2"]], [w["hr"]], eng=PE_)
                    K.tt(w["hi"][:, 0:L], w["p3"][:, 0:L], w["p4"][:, 0:L], ALU.add, [w["p3"], w["p4"]], [w["hi"]], eng=PE_)
                    K.cp(hpr[:, t:t + 1], w["hr"][:, L - 1:L], [w["hr"]], [hpr], eng="act")
                    K.cp(hpi[:, t:t + 1], w["hi"][:, L - 1:L], [w["hi"]], [hpi], eng="act")
                    K.mm(py[:, 0:L], CTp[0][:, t, :], w["hr"][:, 0:L], i == 0, False, [CTp[0], w["hr"]], [py])
                    K.mm(py[:, 0:L], CTp[1][:, t, :], w["hi"][:, 0:L], False, i == 3, [CTp[1], w["hi"]], [py])
                K.stt(yT[:, 0:L], ub[:, rt, 0:L], dcol[:, rt:rt + 1], py[:, 0:L], ALU.mult, ALU.add, [ub, dcol, py], [yT])
                K.actv(sq[:, 0:L], yT[:, 0:L], AF.Square, [yT], [sq])
                K.ts(sq[:, 0:L], sq[:, 0:L], 0.044715, 1.0, ALU.mult, ALU.add, [sq], [sq])
                K.tt(sq[:, 0:L], sq[:, 0:L], yT[:, 0:L], ALU.mult, [sq, yT], [sq])
                K.actv(sq[:, 0:L], sq[:, 0:L], AF.Tanh, [sq], [sq], scale=float(np.sqrt(2.0 / np.pi)))
                K.stt(zz[:, rt, 0:L], sq[:, 0:L], 1.0, yT[:, 0:L], ALU.add, ALU.mult, [sq, yT], [zz])
            for fo in range(4):
                for kc in range(4):
                    K.mm(psG[:, 0:L], w_glu[:, kc, fo * 128:(fo + 1) * 128], zz[:, kc, 0:L], kc == 0, kc == 3, [w_glu, zz], [psG])
                K.actv(gt[:, 0:L], psG[:, 0:L], AF.Sigmoid, [psG, bgcol], [gt], scale=0.5, bias=bgcol[:, fo:fo + 1])
                K.stt(mix[:, 4 + fo, mcol0:mcol0 + L], zz[:, fo, 0:L], 0.5, gt[:, 0:L], ALU.mult, ALU.mult, [zz, gt], [mix])

        xc = [0]

        def out_proj(mix, col0, tok0):
            p = xc[0] % 2
            xc[0] += 1
            K.load(xt[p][:], x_all[tok0:tok0 + 128, :], None, (), [xt[p]])
            for hf in range(2):
                for kc in range(8):
                    K.mm(psO[:, hf * 512:(hf + 1) * 512], mix[:, kc, col0:col0 + 128], w_out[:, kc, hf * 512:(hf + 1) * 512],
                         kc == 0, kc == 7, [mix, w_out], [psO])
            K.tt(xo[p][:], psO[:], xt[p][:], ALU.add, [psO, xt[p]], [xo[p]])
            K.load(x1_d[tok0:tok0 + 128, :], xo[p][:], None, [xo[p]], ())

        set_state(None, None)
        for ci in range(SEQ // LC):
            mx = mixT[ci % 2]
            s5_chunk(ci * LC, LC, mx, 0)
            for tl in range(LC // 128):
                out_proj(mx, tl * 128, ci * LC + tl * 128)
        put_state(outs["hp_re"], outs["hp_im"])
        for b in range(NSB):
            set_state(P["h0_re"][b], P["h0_im"][b])
            s5_chunk(SEQ + b * 32, 32, mixS, b * 32)
            put_state(outs["hs_re"][b], outs["hs_im"][b])
        out_proj(mixS, 0, SEQ)


def sincos_big(K, angT, sinT, cosT, tmpI, tmpT):
    a = angT[:]
    b = tmpT[:]
    K.ts(b, a, 1.0 / TWO_PI, None, ALU.mult, None, [angT], [tmpT])
    K.cp(tmpI[:], b, [tmpT], [tmpI])
    K.cp(b, tmpI[:], [tmpI], [tmpT])
    K.stt(a, b, -TWO_PI, a, ALU.mult, ALU.add, [tmpT, angT], [angT])
    for _ in range(2):
        K.ts(b, a, PI, -TWO_PI, ALU.is_gt, ALU.mult, [angT], [tmpT])
        K.tt(a, a, b, ALU.add, [angT, tmpT], [angT])
        K.ts(b, a, -PI, TWO_PI, ALU.is_lt, ALU.mult, [angT], [tmpT])
        K.tt(a, a, b, ALU.add, [angT, tmpT], [angT])
    K.actv(sinT[:], a, AF.Sin, [angT], [sinT])
    K.ts(a, a, PI / 2, None, ALU.add, None, [angT], [angT])
    K.ts(b, a, PI, -TWO_PI, ALU.is_gt, ALU.mult, [angT], [tmpT])
    K.tt(a, a, b, ALU.add, [angT, tmpT], [angT])
    K.actv(cosT[:], a, AF.Sin, [angT], [cosT])


def s5_consts():
    return {"ident": np.eye(128, dtype=np.float32), "iotaL": np.tile(np.arange(1, 257, dtype=np.float32), (128, 1))}


def phase_odd(K, cfg, x_in, x_out, Wd, Cd, caches, outs):
    SEQ = cfg["SEQ"]
    NTOK = SEQ + 128
    NTP = SEQ // 128
    with K.phase():
        ident_f = K.sb("identf", [128, 128], F32)
        ident_bf = K.sb("identbf", [128, 128], BF16)
        ones_bf = K.sb("onesbf", [128, 128], BF16)
        K.load(ident_f[:], Cd["ident"], None, (), [ident_f])
        K.load(ident_bf[:], Cd["ident"], None, (), [ident_bf], queue="pool")
        K.memset(ones_bf[:], 1.0, [ones_bf])
        eps_t = K.sb("eps_t", [128, 1], F32)
        K.memset(eps_t[:], RMS_EPS, [eps_t])
        gmix = K.sb("gmix", [128, 1024], F32)
        K.load(gmix[:], bcast_rows(Wd["gmix"]), None, (), [gmix])
        qn = K.sb("qn", [128, 512], F32)
        K.load(qn[:], bcast_rows(Wd["qn"]), None, (), [qn])
        kvn = K.sb("kvn", [128, 256], F32)
        K.load(kvn[:], bcast_rows(Wd["kvn"]), None, (), [kvn])
        maskcol = K.sb("maskcol", [128, 4], F32)
        K.load(maskcol[:], Cd["maskcol"], None, (), [maskcol])
        mask4 = K.sb("mask4", [128, 4, 512], BF16)
        K.load(mask4[:], Cd["mask4"], None, (), [mask4], queue="pool")
        maskK = K.sb("maskK", [128, 4, 256], BF16)
        K.load(maskK[:], Cd["maskK"], None, (), [maskK], queue="pool")

        def wload(name, src, kch, ncol):
            t = K.sb(name, [128, kch, ncol], BF16)
            for c in range(kch):
                for c0 in range(0, ncol, 1024):
                    c1 = min(ncol, c0 + 1024)
                    K.load(t[:, c, c0:c1], src[c * 128:(c + 1) * 128, c0:c1], None, (), [t], queue="pool")
            return t
        w_in = wload("w_in", Wd["w_in"], 8, 1312)
        w_out = wload("w_out", Wd["w_out"], 8, 1024)
        w_uqN = wload("w_uqN", Wd["w_uqN"], 4, 512)
        w_uqR = wload("w_uqR", Wd["w_uqR"], 4, 256)
        w_uqRP = wload("w_uqRP", Wd["w_uqRP"], 4, 256)
        w_uv = wload("w_uv", Wd["w_uv"], 2, 512)
        pool_w = K.sb("pool_w", [128, 4, 128], BF16)
        for g in range(4):
            K.load(pool_w[:, g, :], Wd["pool_w"][g], None, (), [pool_w], queue="pool")
        psr = K.sb("psr", [4, 128], F32)
        K.load(psr[:], Wd["pscale"], None, (), [psr])
        pscol = K.sb("pscol", [128, 4], F32)

        gen = Rot([K.ps(f"gen{i}", [128, 512], F32) for i in range(3)])
        psSc = Rot([K.ps(f"psSc{i}", [128, 512], F32) for i in range(2)])
        psOa = K.ps("psOa", [128, 512], F32)
        psOb = K.ps("psOb", [128, 512], F32)
        psDn = K.ps("psDn", [128, 512], F32)

        g0 = gen.next()
        K.tr(g0[:, 0:4], psr[:], ident_f[0:4, 0:4], [psr, ident_f], [g0])
        K.cp(pscol[:], g0[:, 0:4], [g0], [pscol])
        wuk_raw = K.sb("wuk_raw", [128, 2, 512], F32)
        K.load(wuk_raw[:], Wd["w_uk"].rearrange("(ct p) f -> p ct f", p=128), None, (), [wuk_raw])
        w_ukT = K.sb("w_ukT", [128, 4, 256], BF16)
        for hp in range(4):
            for ct in range(2):
                g1 = gen.next()
                K.tr(g1[:, 0:128], wuk_raw[:, ct, hp * 128:(hp + 1) * 128], ident_f[:], [wuk_raw, ident_f], [g1])
                K.cp(w_ukT[:, hp, ct * 128:(ct + 1) * 128], g1[:, 0:128], [g1], [w_ukT], eng="act")

        cT_all = K.sb("cT_all", [128, 2, NTOK], BF16)
        c_tok_all = K.sb("c_tok_all", [128, NTP + 1, 256], BF16)
        kpT4_all = K.sb("kpT4_all", [128, NTOK], BF16)

        junk = K.sb("junk", [128, 1024], BF16)
        xt = [K.sb(f"xt{i}", [128, 1024], F32) for i in range(2)]
        xo = [K.sb(f"xo{i}", [128, 1024], F32) for i in range(2)]
        ssb = [K.sb(f"ssb{i}", [128, 4], F32) for i in range(2)]
        sscq = [K.sb(f"sscq{i}", [128, 4], F32) for i in range(2)]
        sscc = [K.sb(f"sscc{i}", [128, 4], F32) for i in range(2)]
        hn = [K.sb(f"hn{i}", [128, 1024], BF16) for i in range(2)]
        cqn = [K.sb(f"cqn{i}", [128, 512], BF16) for i in range(2)]
        cf = [K.sb(f"cf{i}", [128, 256], F32) for i in range(2)]
        kt1 = [K.sb(f"kt1_{i}", [128, 32], F32) for i in range(2)]
        kt2 = [K.sb(f"kt2_{i}", [128, 32], F32) for i in range(2)]
        kpf = [K.sb(f"kpf{i}", [128, 32], F32) for i in range(2)]
        kp4 = [K.sb(f"kp4{i}", [128, 4, 32], BF16) for i in range(2)]
        ropeK = [K.sb(f"ropeK{i}", [128, 64], F32) for i in range(2)]

        def make_bufs(W):
            B = {}
            B["hnT"] = K.sb("hnT", [128, 8, W], BF16)
            B["cqnT"] = K.sb("cqnT", [128, 4, W], BF16)
            B["qnT"] = K.sb("qnT", [128, 4, W], BF16)
            B["qpeT"] = K.sb("qpeT", [128, 2, W], BF16)
            B["rC"] = K.sb("rC", [128, W], F32)
            B["rS"] = K.sb("rS", [128, W], F32)
            B["r1"] = K.sb("r1", [128, W], F32)
            B["r2"] = K.sb("r2", [128, W], F32)
            B["extA"] = K.sb("extA", [128, 15 + W], F32)
            B["extB"] = K.sb("extB", [128, 15 + W], F32)
            B["extC"] = K.sb("extC", [128, 15 + W], F32)
            for nm in ("extA", "extB", "extC"):
                K.memset(B[nm][:], 0.0, [B[nm]])
            B["rc"] = K.sb("rc", [128, W], F32)
            B["mT"] = K.sb("mT", [128, W], BF16)
            B["mixT"] = K.sb("mixT", [128, 8, W], BF16)
            B["hist"] = K.sb("hist", [128, 4, 15], F32)
            B["utail"] = K.sb("utail", [15, 512], F32)
            B["hraw"] = K.sb("hraw", [15, 512], F32)
            B["uS"] = K.sb("uS", [128, W], F32)
            return B

        def norm_small(X, OUT, gb, width, sc, xtok, otok, gtok):
            K.actv(junk[:, 0:width], X, AF.Square, [xtok], [sc, junk], accum_out=sc[:, 0:1])
            K.actv(sc[:, 1:2], sc[:, 0:1], AF.Sqrt, [sc, eps_t], [sc], scale=1.0 / width, bias=eps_t[:, 0:1])
            K.op("dve", lambda e: e.reciprocal(sc[:, 2:3], sc[:, 1:2]), [sc], [sc])
            K.stt(OUT, X, sc[:, 2:3], gb, ALU.mult, ALU.mult, [xtok, sc, gtok], [otok])

        cnt = {"x": 0, "t": 0}

        def front(B, tok0, W, hist_src):
            ntl = W // 128
            hnT, cqnT = B["hnT"], B["cqnT"]
            for tl in range(ntl):
                p = cnt["t"] % 2
                cnt["t"] += 1
                X = xt[cnt["x"] % 2]
                cnt["x"] += 1
                j = (tok0 + tl * 128) // 128
                K.load(X[:], x_in[tok0 + tl * 128: tok0 + (tl + 1) * 128, :], None, (), [X])
                K.load(ropeK[p][:], Cd["ropeK"][tok0 + tl * 128: tok0 + (tl + 1) * 128, :], None, (), [ropeK[p]])
                rmsnorm_tile(K, X[:], hn[p][:], gmix[:], ssb[p], eps_t, junk[:], X, hn[p], gmix, junk)
                gT = gen.next()
                gT_bf = gT[:].bitcast(BF16)
                for c in range(8):
                    K.tr(gT_bf[:, c * 128:(c + 1) * 128], hn[p][:, c * 128:(c + 1) * 128], ident_bf[:], [hn[p], ident_bf], [gT])
                K.cp(hnT[:, :, tl * 128:(tl + 1) * 128], gT_bf[:, :].rearrange("p (c t) -> p c t", c=8), [gT], [hnT], eng="act")
                pq, pc = gen.next(), gen.next()
                for kc in range(8):
                    K.mm(pq[:, 0:512], hnT[:, kc, tl * 128:(tl + 1) * 128], w_in[:, kc, 0:512], kc == 0, kc == 7, [hnT, w_in], [pq])
                for kc in range(8):
                    K.mm(pc[:, 0:288], hnT[:, kc, tl * 128:(tl + 1) * 128], w_in[:, kc, 512:800], kc == 0, kc == 7, [hnT, w_in], [pc])
                norm_small(pq[:, 0:512], cqn[p][:], qn[:], 512, sscq[p], pq, cqn[p], qn)
                norm_small(pc[:, 0:256], cf[p][:], kvn[:], 256, sscc[p], pc, cf[p], kvn)
                K.tt(kt1[p][:], pc[:, 256:288], ropeK[p][:, 0:32], ALU.mult, [pc, ropeK[p]], [kt1[p]])
                K.tt(kt2[p][:, 0:16], pc[:, 272:288], ropeK[p][:, 32:48], ALU.mult, [pc, ropeK[p]], [kt2[p]])
                K.tt(kt2[p][:, 16:32], pc[:, 256:272], ropeK[p][:, 48:64], ALU.mult, [pc, ropeK[p]], [kt2[p]])
                K.tt(kpf[p][:], kt1[p][:], kt2[p][:], ALU.add, [kt1[p], kt2[p]], [kpf[p]])
                K.cp(kp4[p][:], kpf[p][:].unsqueeze(1).to_broadcast([128, 4, 32]), [kpf[p]], [kp4[p]])
                K.cp(c_tok_all[:, j, :], cf[p][:], [cf[p]], [c_tok_all])
                if tok0 < SEQ:
                    K.load(outs["ckv_p"][tok0 + tl * 128: tok0 + (tl + 1) * 128, :], cf[p][:], None, [cf[p]], ())
                    K.load(outs["kpe_p"][tok0 + tl * 128: tok0 + (tl + 1) * 128, :], kpf[p][:], None, [kpf[p]], ())
                else:
                    for b in range(NSB):
                        K.load(outs["ckv_s"][b], cf[p][b * 32:(b + 1) * 32, :], None, [cf[p]], ())
                        K.load(outs["kpe_s"][b], kpf[p][b * 32:(b + 1) * 32, :], None, [kpf[p]], ())
                gT2 = gen.next()
                gT2_bf = gT2[:].bitcast(BF16)
                for c in range(4):
                    K.tr(gT2_bf[:, c * 128:(c + 1) * 128], cqn[p][:, c * 128:(c + 1) * 128], ident_bf[:], [cqn[p], ident_bf], [gT2])
                K.cp(cqnT[:, :, tl * 128:(tl + 1) * 128], gT2_bf[:, 0:512].rearrange("p (c t) -> p c t", c=4), [gT2], [cqnT], eng="act")
                gT3 = gen.next()
                gT3_bf = gT3[:].bitcast(BF16)
                for c in range(2):
                    K.tr(gT3_bf[:, c * 128:(c + 1) * 128], c_tok_all[:, j, c * 128:(c + 1) * 128], ident_bf[:],
                         [c_tok_all, ident_bf], [gT3])
                K.tr(gT3_bf[:, 256:384], kp4[p][:].rearrange("p a r -> p (a r)"), ident_bf[:], [kp4[p], ident_bf], [gT3])
                K.cp(cT_all[:, :, j * 128:(j + 1) * 128], gT3_bf[:, 0:256].rearrange("p (c t) -> p c t", c=2), [gT3], [cT_all], eng="act")
                K.cp(kpT4_all[:, j * 128:(j + 1) * 128], gT3_bf[:, 256:384], [gT3], [kpT4_all], eng="act")
            K.load(B["rC"][:, 0:W], Cd["ropeQC"][:, tok0:tok0 + W], None, (), [B["rC"]])
            K.load(B["rS"][:, 0:W], Cd["ropeQS"][:, tok0:tok0 + W], None, (), [B["rS"]])
            for c in range(4):
                pn = gen.next()
                for kc in range(4):
                    K.mm(pn[:, 0:W], w_uqN[:, kc, c * 128:(c + 1) * 128], cqnT[:, kc, 0:W], kc == 0, kc == 3, [w_uqN, cqnT], [pn])
                K.cp(B["qnT"][:, c, 0:W], pn[:, 0:W], [pn], [B["qnT"]], eng="act")
            for c in range(2):
                pr, pp = gen.next(), gen.next()
                for kc in range(4):
                    K.mm(pr[:, 0:W], w_uqR[:, kc, c * 128:(c + 1) * 128], cqnT[:, kc, 0:W], kc == 0, kc == 3, [w_uqR, cqnT], [pr])
                for kc in range(4):
                    K.mm(pp[:, 0:W], w_uqRP[:, kc, c * 128:(c + 1) * 128], cqnT[:, kc, 0:W], kc == 0, kc == 3, [w_uqRP, cqnT], [pp])
                K.tt(B["r1"][:, 0:W], pr[:, 0:W], B["rC"][:, 0:W], ALU.mult, [pr, B["rC"]], [B["r1"]])
                K.tt(B["r2"][:, 0:W], pp[:, 0:W], B["rS"][:, 0:W], ALU.mult, [pp, B["rS"]], [B["r2"]])
                K.tt(B["qpeT"][:, c, 0:W], B["r1"][:, 0:W], B["r2"][:, 0:W], ALU.add, [B["r1"], B["r2"]], [B["qpeT"]])
            segs = [(0, W)] if not isinstance(hist_src, list) else [(b * 32, 32) for b in range(len(hist_src))]
            if isinstance(hist_src, list):
                hraw = B["hraw"]
            for g in range(4):
                pu = gen.next()
                for kc in range(8):
                    K.mm(pu[:, 0:W], w_in[:, kc, 800 + g * 128:800 + (g + 1) * 128], hnT[:, kc, 0:W], kc == 0, kc == 7, [w_in, hnT], [pu])
                K.cp(B["uS"][:, 0:W], pu[:, 0:W], [pu], [B["uS"]], eng="act")
                for si_, (s0, L) in enumerate(segs):
                    A, Bb, Cc = B["extA"], B["extB"], B["extC"]
                    if hist_src == "zero":
                        K.memset(A[:, 0:15], 0.0, [A])
                    elif isinstance(hist_src, list):
                        K.load(hraw[:, g * 128:(g + 1) * 128], hist_src[si_][:, g * 128:(g + 1) * 128], None, (), [hraw])
                        ph = gen.next()
                        K.tr(ph[:, 0:15], hraw[:, g * 128:(g + 1) * 128], ident_f[0:15, 0:15], [hraw, ident_f], [ph])
                        K.cp(A[:, 0:15], ph[:, 0:15], [ph], [A])
                    else:
                        K.cp(A[:, 0:15], B["hist"][:, g, :], [B["hist"]], [A])
                    K.cp(A[:, 15:15 + L], B["uS"][:, s0:s0 + L], [B["uS"]], [A])
                    if hist_src is None or hist_src == "zero":
                        K.cp(B["hist"][:, g, :], A[:, L:L + 15], [A], [B["hist"]])
                    src, dst = A, Bb
                    n = 15 + L
                    for st in range(g + 1):
                        sh = 1 << st
                        K.tt(dst[:, sh:n], src[:, sh:n], src[:, 0:n - sh], ALU.add, [src], [dst])
                        src, dst = dst, (Cc if dst is Bb else Bb)
                    K.load(B["rc"][:, 0:L], bcast_rows(Cd["rcnt"][g:g + 1, tok0 + s0: tok0 + s0 + L]), None, (), [B["rc"]])
                    K.tt(Cc[:, 15:n] if src is not Cc else Bb[:, 15:n], src[:, 15:n], B["rc"][:, 0:L], ALU.mult, [src, B["rc"]],
                         [Cc if src is not Cc else Bb])
                    tot = Cc if src is not Cc else Bb
                    K.tt(B["mT"][:, s0:s0 + L], tot[:, 15:n], A[:, 15:n], ALU.subtract, [tot, A], [B["mT"]])
                    last = (tok0 + W == SEQ) or isinstance(hist_src, list)
                    if last:
                        pt = gen.next()
                        K.tr(pt[0:15, 0:128], A[:, L:L + 15], ident_f[:], [A, ident_f], [pt])
                        K.cp(B["utail"][:, g * 128:(g + 1) * 128], pt[0:15, 0:128], [pt], [B["utail"]])
                        if g == 3 or isinstance(hist_src, list):
                            dsto = outs["pool_p"] if not isinstance(hist_src, list) else outs["pool_s"][si_]
                            K.load(dsto[:, g * 128:(g + 1) * 128], B["utail"][:, g * 128:(g + 1) * 128], None, [B["utail"]], ())
                            if not isinstance(hist_src, list):
                                for g2 in range(3):
                                    K.load(dsto[:, g2 * 128:(g2 + 1) * 128], B["utail"][:, g2 * 128:(g2 + 1) * 128], None,
                                           [B["utail"]], ())
                po = gen.next()
                K.mm(po[:, 0:W], pool_w[:, g, :], B["mT"][:, 0:W], True, True, [pool_w, B["mT"]], [po])
                K.actv(B["mixT"][:, g, 0:W], po[:, 0:W], AF.Copy, [po, pscol], [B["mixT"]], scale=pscol[:, g:g + 1])

        def qhead(B, h, W, qa, qpad):
            r0 = (h % 2) * 64
            for cc in range(2):
                pa = gen.next()
                K.mm(pa[:, 0:W], w_ukT[r0:r0 + 64, h // 2, cc * 128:(cc + 1) * 128], B["qnT"][r0:r0 + 64, h // 2, 0:W], True, True,
                     [w_ukT, B["qnT"]], [pa])
                K.cp(qa[:, cc, 0:W], pa[:, 0:W], [pa], [qa], eng="act")
            K.ts(qpad[:, 0:W], B["qpeT"][:, h // 4, 0:W], maskcol[:, h % 4:h % 4 + 1], None, ALU.mult, None,
                 [B["qpeT"], maskcol], [qpad])

        def out_proj(B, W, tok0):
            for tl in range(W // 128):
                p = cnt["t"] % 2
                cnt["t"] += 1
                X = xt[cnt["x"] % 2]
                cnt["x"] += 1
                K.load(X[:], x_in[tok0 + tl * 128: tok0 + (tl + 1) * 128, :], None, (), [X])
                pso = [gen.next(), gen.next()]
                for hf in range(2):
                    for kc in range(8):
                        K.mm(pso[hf][:, :], B["mixT"][:, kc, tl * 128:(tl + 1) * 128], w_out[:, kc, hf * 512:(hf + 1) * 512],
                             kc == 0, kc == 7, [B["mixT"], w_out], [pso[hf]])
                    K.tt(xo[p][:, hf * 512:(hf + 1) * 512], pso[hf][:, :], X[:, hf * 512:(hf + 1) * 512], ALU.add, [pso[hf], X], [xo[p]])
                K.load(x_out[tok0 + tl * 128: tok0 + (tl + 1) * 128, :], xo[p][:], None, [xo[p]], ())

        pscope = K.phase()
        pscope.__enter__()
        B = make_bufs(512)
        qa = [K.sb(f"qa{i}", [128, 2, 512], BF16) for i in range(2)]
        qpad = [K.sb(f"qpad{i}", [128, 512], BF16) for i in range(2)]
        PT = [K.sb(f"PT{i}", [128, 512], BF16) for i in range(3)]
        rden = K.sb("rden", [128, 512], F32)
        olat = [K.sb(f"olat{i}", [128, 2, 512], BF16) for i in range(2)]
        pti = 0
        for s in range(SEQ // 512):
            tok0 = s * 512
            front(B, tok0, 512, "zero" if s == 0 else None)
            nkt = (tok0 + 512) // 128
            for h in range(8):
                qh, qp = qa[h % 2], qpad[h % 2]
                qhead(B, h, 512, qh, qp)
                for kt in range(nkt):
                    pS = psSc.next()
                    diag = kt - tok0 // 128
                    K.mm(pS[:, :], cT_all[:, 0, kt * 128:(kt + 1) * 128], qh[:, 0, :], True, False, [cT_all, qh], [pS])
                    K.mm(pS[:, :], cT_all[:, 1, kt * 128:(kt + 1) * 128], qh[:, 1, :], False, False, [cT_all, qh], [pS])
                    K.mm(pS[:, :], kpT4_all[:, kt * 128:(kt + 1) * 128], qp[:, :], False, diag < 0, [kpT4_all, qp], [pS])
                    if diag >= 0:
                        K.mm(pS[:, :], ident_bf[:], mask4[:, diag, :], False, True, [ident_bf, mask4], [pS])
                    P_ = PT[pti % 3]
                    pti += 1
                    K.actv(P_[:], pS[:, :], AF.Exp, [pS], [P_], scale=MLA_SCALE)
                    K.mm(psOa[:, :], c_tok_all[:, kt, 0:128], P_[:], kt == 0, kt == nkt - 1, [c_tok_all, P_], [psOa])
                    K.mm(psOb[:, :], c_tok_all[:, kt, 128:256], P_[:], kt == 0, kt == nkt - 1, [c_tok_all, P_], [psOb])
                    K.mm(psDn[:, :], ones_bf[:], P_[:], kt == 0, kt == nkt - 1, [ones_bf, P_], [psDn])
                ol = olat[h % 2]
                K.op("dve", lambda e: e.reciprocal(rden[:], psDn[:, :]), [psDn], [rden])
                K.tt(ol[:, 0, :], psOa[:, :], rden[:], ALU.mult, [psOa, rden], [ol])
                K.tt(ol[:, 1, :], psOb[:, :], rden[:], ALU.mult, [psOb, rden], [ol])
                if h % 2 == 0:
                    pm = gen.next()
                for cc in range(2):
                    K.mm(pm[(h % 2) * 64:(h % 2) * 64 + 64, :], w_uv[:, cc, h * 64:(h + 1) * 64], ol[:, cc, :], cc == 0, cc == 1,
                         [w_uv, ol], [pm])
                if h % 2 == 1:
                    K.cp(B["mixT"][:, 4 + h // 2, :], pm[:, :], [pm], [B["mixT"]], eng="act")
            out_proj(B, 512, tok0)
        pscope.__exit__(None, None, None)

        B = make_bufs(128)
        qaS = K.sb("qaS", [128, 8, 2, 128], BF16)
        qpadS = K.sb("qpadS", [128, 8, 128], BF16)
        front(B, SEQ, 128, [caches["pool"][b] for b in range(NSB)])
        for h in range(8):
            r0 = (h % 2) * 64
            for cc in range(2):
                pa = gen.next()
                K.mm(pa[:, 0:128], w_ukT[r0:r0 + 64, h // 2, cc * 128:(cc + 1) * 128], B["qnT"][r0:r0 + 64, h // 2, 0:128], True, True,
                     [w_ukT, B["qnT"]], [pa])
                K.cp(qaS[:, h, cc, :], pa[:, 0:128], [pa], [qaS], eng="act")
            K.ts(qpadS[:, h, :], B["qpeT"][:, h // 4, 0:128], maskcol[:, h % 4:h % 4 + 1], None, ALU.mult, None,
                 [B["qpeT"], maskcol], [qpadS])
        cc_tok = [K.sb(f"cc_tok{i}", [128, 16, 256], BF16) for i in range(2)]
        ccT = [K.sb(f"ccT{i}", [128, 2, 2048], BF16) for i in range(2)]
        ckp = [K.sb(f"ckp{i}", [128, 16, 32], BF16) for i in range(2)]
        ckp4 = [K.sb(f"ckp4{i}", [128, 16, 4, 32], BF16) for i in range(1)] * 2
        ckpT = [K.sb(f"ckpT{i}", [128, 2048], BF16) for i in range(1)] * 2
        PTs = [K.sb(f"PTs{i}", [128, 256], BF16) for i in range(3)]
        rdenS = K.sb("rdenS", [128, 256], F32)
        olS = K.sb("olS", [128, 2, 256], BF16)
        pti = 0
        for b in range(NSB):
            bp = b % 2
            for half in range(2):
                K.load(cc_tok[bp][:, half * 8:(half + 1) * 8, :],
                       caches["ckv"][b, half * 1024:(half + 1) * 1024, :].rearrange("(kt p) c -> p kt c", p=128), None, (),
                       [cc_tok[bp]], queue="pool")
            K.load(ckp[bp][:], caches["kpe"][b].rearrange("(kt p) r -> p kt r", p=128), None, (), [ckp[bp]], queue="pool")
            K.cp(ckp4[bp][:], ckp[bp][:].unsqueeze(2).to_broadcast([128, 16, 4, 32]), [ckp[bp]], [ckp4[bp]])
            for kt in range(16):
                gt_ = gen.next()
                gt_bf = gt_[:].bitcast(BF16)
                for c in range(2):
                    K.tr(gt_bf[:, c * 128:(c + 1) * 128], cc_tok[bp][:, kt, c * 128:(c + 1) * 128], ident_bf[:], [cc_tok[bp], ident_bf], [gt_])
                K.tr(gt_bf[:, 256:384], ckp4[bp][:, kt, :, :].rearrange("p a r -> p (a r)"), ident_bf[:], [ckp4[bp], ident_bf], [gt_])
                K.cp(ccT[bp][:, :, kt * 128:(kt + 1) * 128], gt_bf[:, 0:256].rearrange("p (c t) -> p c t", c=2), [gt_], [ccT[bp]], eng="act")
                K.cp(ckpT[bp][:, kt * 128:(kt + 1) * 128], gt_bf[:, 256:384], [gt_], [ckpT[bp]], eng="act")
            qs = slice(b * 32, (b + 1) * 32)
            for kt in range(17):
                pS = psSc.next()
                if kt < 16:
                    l0, l1, l2 = ccT[bp][:, 0, kt * 128:(kt + 1) * 128], ccT[bp][:, 1, kt * 128:(kt + 1) * 128], ckpT[bp][:, kt * 128:(kt + 1) * 128]
                    ltoks = [ccT[bp], ckpT[bp]]
                    vtok, vt = cc_tok[bp], cc_tok[bp][:, kt, :]
                else:
                    l0, l1, l2 = cT_all[:, 0, SEQ:SEQ + 128], cT_all[:, 1, SEQ:SEQ + 128], kpT4_all[:, SEQ:SEQ + 128]
                    ltoks = [cT_all, kpT4_all]
                    vtok, vt = c_tok_all, c_tok_all[:, NTP, :]
                K.mm(pS[:, 0:256], l0, qaS[:, :, 0, qs], True, False, ltoks + [qaS], [pS])
                K.mm(pS[:, 0:256], l1, qaS[:, :, 1, qs], False, False, ltoks + [qaS], [pS])
                K.mm(pS[:, 0:256], l2, qpadS[:, :, qs], False, kt < 16, ltoks + [qpadS], [pS])
                if kt == 16:
                    K.mm(pS[:, 0:256], ident_bf[:], maskK[:, b, :], False, True, [ident_bf, maskK], [pS])
                P_ = PTs[pti % 3]
                pti += 1
                K.actv(P_[:], pS[:, 0:256], AF.Exp, [pS], [P_], scale=MLA_SCALE)
                K.mm(psOa[:, 0:256], vt[:, 0:128], P_[:], kt == 0, kt == 16, [vtok, P_], [psOa])
                K.mm(psOb[:, 0:256], vt[:, 128:256], P_[:], kt == 0, kt == 16, [vtok, P_], [psOb])
                K.mm(psDn[:, 0:256], ones_bf[:], P_[:], kt == 0, kt == 16, [ones_bf, P_], [psDn])
            K.op("dve", lambda e: e.reciprocal(rdenS[:], psDn[:, 0:256]), [psDn], [rdenS])
            K.tt(olS[:, 0, :], psOa[:, 0:256], rdenS[:], ALU.mult, [psOa, rdenS], [olS])
            K.tt(olS[:, 1, :], psOb[:, 0:256], rdenS[:], ALU.mult, [psOb, rdenS], [olS])
            for hp in range(4):
                pm = gen.next()
                for hh in range(2):
                    h = 2 * hp + hh
                    for cc in range(2):
                        K.mm(pm[hh * 64:hh * 64 + 64, 0:32], w_uv[:, cc, h * 64:(h + 1) * 64], olS[:, cc, h * 32:(h + 1) * 32],
                             cc == 0, cc == 1, [w_uv, olS], [pm])
                K.cp(B["mixT"][:, 4 + hp, qs], pm[:, 0:32], [pm], [B["mixT"]], eng="act")
        out_proj(B, 128, SEQ)


def odd_consts(SEQ):
    pos = token_positions(SEQ)
    NTOK = SEQ + 128
    C, S = rope_tables(pos, 32, 32)
    ropeQC, ropeQS = np.tile(C, (4, 1)), np.tile(S, (4, 1))
    inv = ROPE_THETA ** (-np.arange(0, 32, 2, dtype=np.float32) / 32)
    ang = pos.astype(np.float32)[:, None] * inv.astype(np.float32)[None, :]
    cos, sin = np.cos(ang).astype(np.float32), np.sin(ang).astype(np.float32)
    ropeK = np.concatenate([cos, cos, -sin, sin], axis=1).astype(np.float32)
    rcnt = np.stack([1.0 / np.minimum(pos + 1, w).astype(np.float32) for w in (2, 4, 8, 16)]).astype(np.float32)
    maskcol = (np.arange(128)[:, None] // 32 == np.arange(4)[None, :]).astype(np.float32)
    k = np.arange(128)[:, None]
    q = np.arange(512)[None, :]
    mask4 = np.zeros((128, 4, 512), np.float32)
    for d in range(4):
        kchunk = 2 * d + (k >= 64)
        qchunk = 2 * (q // 128) + ((q % 128) >= 64)
        mask4[:, d, :] = np.where(kchunk <= qchunk, 0.0, MASKV)
    maskK = np.zeros((128, NSB, 256), np.float32)
    for b in range(NSB):
        maskK[:, b, :] = np.where((np.arange(128) // 32 == b)[:, None], 0.0, MASKV)
    return {"ident": np.eye(128, dtype=np.float32), "ropeQC": ropeQC, "ropeQS": ropeQS, "ropeK": ropeK, "rcnt": rcnt,
            "maskcol": maskcol, "mask4": mask4, "maskK": maskK}


def odd_weights(w_in, w_out, pool_w, pool_scale, q_norm, kv_norm, w_uq, w_uk, w_uv, gmix):
    uq = w_uq.reshape(512, 8, 96)
    w_uqN = np.ascontiguousarray(uq[:, :, :64].reshape(512, 512))
    w_uqR = np.ascontiguousarray(uq[:, :, 64:].reshape(512, 256))
    return {"w_in": w_in, "gmix": gmix.reshape(1, 1024), "w_out": w_out, "pool_w": pool_w, "pscale": pool_scale.reshape(4, 128),
            "qn": q_norm.reshape(1, 512), "kvn": kv_norm.reshape(1, 256), "w_uqN": w_uqN, "w_uqR": w_uqR,
            "w_uqRP": perm_rope_cols(w_uqR, 32, 32), "w_uk": np.ascontiguousarray(w_uk.reshape(256, 512)),
            "w_uv": np.ascontiguousarray(w_uv.reshape(256, 512))}


SEQ_FULL = 4096
N_CORES = 8


def build(SEQ, upto=5):
    cfg = {"SEQ": SEQ}
    NTOK = SEQ + 128
    K = KB()
    consts = {}
    for pre, d in (("s_", swa_consts(SEQ)), ("f_", s5_consts()), ("o_", odd_consts(SEQ))):
        for k, v in d.items():
            consts[pre + k] = v
    consts["iota16"] = np.tile(np.arange(16, dtype=np.float32), (128, 1))
    C = {k: K.const(k, v) for k, v in consts.items()}
    I = lambda n, shp: K.inp(n, shp, F32)
    O = lambda n, shp: K.outp(n, shp, F32)
    x_all = I("x_all", [NTOK, 1024])
    w_in0, w_in0P, gmix0, sink = I("w_in0", [1024, 1280]), I("w_in0P", [1024, 640]), I("gmix0", [1, 1024]), I("sink", [1, 8])
    ck, cv = I("ck", [NSB, 128, 128]), I("cv", [NSB, 128, 128])
    attT = K.outp("attT_scr", [512, NTOK], BF16)
    uT = K.outp("uT_scr", [512, NTOK], BF16)
    y = O("y", [NTOK, 1024])
    x1 = x2 = x3 = x4 = y
    outs0 = {"k_p": O("k_p", [128, 128]), "v_p": O("v_p", [128, 128]), "k_s": O("k_s", [NSB, 128, 128]), "v_s": O("v_s", [NSB, 128, 128])}
    phase_swa(K, cfg, x_all, w_in0, w_in0P, gmix0, sink, ck, cv, C["s_ropeC"], C["s_ropeS"], C["s_maskA"], C["s_maskB"], C["s_maskS"],
              C["s_ident"], attT, uT, outs0)
    P = {"lam_re": I("lam_re", [16, 128]), "lam_im": I("lam_im", [16, 128]), "log_dt": I("log_dt", [16, 2]),
         "b_re": I("b_re", [2048, 16]), "b_im": I("b_im", [2048, 16]), "c_re": I("c_re", [512, 64]), "c_im": I("c_im", [512, 64]),
         "dsk": I("dsk", [4, 128]), "w_glu": I("w_glu", [512, 512]), "b_glu": I("b_glu", [4, 128]), "w_out": I("w_out0", [1024, 1024]),
         "h0_re": I("h0_re", [NSB, 16, 128]), "h0_im": I("h0_im", [NSB, 16, 128])}
    outs1 = {"hp_re": O("hp_re", [16, 128]), "hp_im": O("hp_im", [16, 128]), "hs_re": O("hs_re", [NSB, 16, 128]), "hs_im": O("hs_im", [NSB, 16, 128])}
    if upto >= 2:
        phase_s5(K, cfg, x_all, uT, attT, x1, P, C["f_ident"], C["f_iotaL"], outs1)
    peer_in = []
    for l in range(2):
        peer_in.append({"wq": I(f"pwq{l}", [1024, 1024]), "keys": I(f"pkeys{l}", [8, 2, 128, 64]), "u": I(f"pu{l}", [N_EXPERTS, 1024]),
                        "v": I(f"pv{l}", [N_EXPERTS, 1024]), "g": I(f"gffn{l}", [1, 1024])})
    gfin = I("gfin", [1, 1024])
    pi = peer_in[0]
    tabs = [K.scratch(f"peer_tab{l}", [N_EXPERTS, 2048], BF16) for l in range(2)]
    if upto >= 3:
      phase_convert(K, [(peer_in[l]["u"], peer_in[l]["v"], tabs[l]) for l in range(2)])
      phase_peer(K, x1, x2, NTOK // 128, pi["wq"], pi["keys"], tabs[0], pi["g"], C["s_ident"], C["s_ident"], C["iota16"])
    wshapes = {"w_in": [1024, 1312], "gmix": [1, 1024], "w_out": [1024, 1024], "pool_w": [4, 128, 128], "pscale": [4, 128], "qn": [1, 512],
               "kvn": [1, 256], "w_uqN": [512, 512], "w_uqR": [512, 256], "w_uqRP": [512, 256], "w_uk": [256, 512], "w_uv": [256, 512]}
    Wd = {k: I("W1_" + k, shp) for k, shp in wshapes.items()}
    Cd = {k[2:]: v for k, v in C.items() if k.startswith("o_")}
    caches = {"pool": I("c_pool", [NSB, 15, 512]), "ckv": I("c_ckv", [NSB, PAST_LEN, 256]), "kpe": I("c_kpe", [NSB, PAST_LEN, 32])}
    outs3 = {"pool_p": O("pool_p", [15, 512]), "ckv_p": O("ckv_p", [SEQ, 256]), "kpe_p": O("kpe_p", [SEQ, 32]),
             "pool_s": O("pool_s", [NSB, 15, 512]), "ckv_s": O("ckv_s", [NSB, 32, 256]), "kpe_s": O("kpe_s", [NSB, 32, 32])}
    if upto >= 4:
        phase_odd(K, cfg, x2, x3, Wd, Cd, caches, outs3)
    pi = peer_in[1]
    if upto >= 5:
      phase_peer(K, x3, x4, NTOK // 128, pi["wq"], pi["keys"], tabs[1], pi["g"], C["s_ident"], C["s_ident"], C["iota16"],
               final=(gfin, [(0, NTOK, y)]))
    nc = K.emit()
    return nc, K


def make_in_maps(inp, SEQ, n_cores, K):
    f = lambda a: np.ascontiguousarray(np.asarray(a, dtype=np.float32))
    g0 = lambda k: f(inp[k])[0]
    w_in0 = g0("w_in_even")
    shared = {
        "w_in0": w_in0, "w_in0P": perm_rope_cols(w_in0[:, :640], 64, 16), "gmix0": f(inp["norm_mix"])[0:1], "sink": g0("swa_sink").reshape(1, 8),
        "lam_re": g0("s5_lam_re").reshape(16, 128), "lam_im": g0("s5_lam_im").reshape(16, 128), "log_dt": g0("s5_log_dt").reshape(16, 2),
        "b_re": g0("s5_b_re").reshape(2048, 16), "b_im": g0("s5_b_im").reshape(2048, 16),
        "c_re": g0("s5_c_re").reshape(512, 64), "c_im": g0("s5_c_im").reshape(512, 64), "dsk": g0("s5_d").reshape(4, 128),
        "w_glu": g0("s5_w_glu"), "b_glu": g0("s5_b_glu").reshape(4, 128), "w_out0": g0("w_out_even"), "gfin": f(inp["norm_final"]).reshape(1, 1024),
    }
    for l in range(2):
        shared[f"pwq{l}"] = f(inp["peer_w_q"])[l]
        shared[f"pkeys{l}"] = f(inp["peer_keys"])[l]
        shared[f"pu{l}"] = f(inp["peer_u"])[l]
        shared[f"pv{l}"] = f(inp["peer_v"])[l]
        shared[f"gffn{l}"] = f(inp["norm_ffn"])[l:l + 1]
    W1 = odd_weights(g0("w_in_odd"), g0("w_out_odd"), g0("pool_w"), g0("pool_scale"), g0("mla_q_norm"), g0("mla_kv_norm"), g0("mla_w_uq"),
                     g0("mla_w_uk"), g0("mla_w_uv"), f(inp["norm_mix"])[1])
    for k, v in W1.items():
        shared["W1_" + k] = np.ascontiguousarray(v)
    shared.update(K.consts)
    xp, xs = f(inp["x_prompt"]), f(inp["x_sample"])
    maps = []
    for c in range(n_cores):
        sb = slice(NSB * c, NSB * (c + 1))
        m = dict(shared)
        m["x_all"] = np.ascontiguousarray(np.concatenate([xp[c, :SEQ], xs[sb].reshape(NSB * DEC_SEQ, 1024)], 0))
        m["ck"] = np.ascontiguousarray(g0("cache_swa_k")[sb].reshape(NSB, 128, 128))
        m["cv"] = np.ascontiguousarray(g0("cache_swa_v")[sb].reshape(NSB, 128, 128))
        m["h0_re"] = np.ascontiguousarray(g0("state_ssm_re")[sb].reshape(NSB, 16, 128))
        m["h0_im"] = np.ascontiguousarray(g0("state_ssm_im")[sb].reshape(NSB, 16, 128))
        m["c_pool"] = np.ascontiguousarray(g0("state_pool")[sb])
        m["c_ckv"] = np.ascontiguousarray(g0("cache_mla_ckv")[sb])
        m["c_kpe"] = np.ascontiguousarray(g0("cache_mla_kpe")[sb])
        maps.append(m)
    return maps


def assemble(results, SEQ, n_cores):
    R = [{k: np.asarray(v) for k, v in r.items()} for r in results]
    st = lambda fn: np.stack([fn(r) for r in R])
    cat = lambda fn: np.concatenate([fn(r) for r in R], 0)
    y_p = st(lambda r: r["y"][:SEQ])
    y_s = cat(lambda r: r["y"][SEQ:].reshape(NSB, DEC_SEQ, 1024))
    out = (y_p, y_s,
           st(lambda r: r["k_p"].reshape(128, 2, 64))[None], st(lambda r: r["v_p"].reshape(128, 2, 64))[None],
           st(lambda r: r["hp_re"].reshape(32, 64))[None], st(lambda r: r["hp_im"].reshape(32, 64))[None],
           st(lambda r: r["pool_p"])[None], st(lambda r: r["ckv_p"])[None], st(lambda r: r["kpe_p"])[None],
           cat(lambda r: r["k_s"].reshape(NSB, 128, 2, 64))[None], cat(lambda r: r["v_s"].reshape(NSB, 128, 2, 64))[None],
           cat(lambda r: r["hs_re"].reshape(NSB, 32, 64))[None], cat(lambda r: r["hs_im"].reshape(NSB, 32, 64))[None],
           cat(lambda r: r["pool_s"])[None], cat(lambda r: r["ckv_s"])[None], cat(lambda r: r["kpe_s"])[None])
    return tuple(np.ascontiguousarray(o.astype(np.float32)) for o in out)


def kernel(_upto=5, **inputs):
    SEQ = int(np.asarray(inputs["x_prompt"]).shape[1])
    n_cores = int(np.asarray(inputs["x_prompt"]).shape[0])
    nc, K = build(SEQ, _upto)
    print("n sems", len(K.dsem) + 4, {e: len(K.q[e]) for e in ENGS}, flush=True)
    maps = make_in_maps(inputs, SEQ, n_cores, K)
    res = run_bass_kernel_spmd(nc, maps, core_ids=list(range(n_cores)))
    return assemble(res.results, SEQ, n_cores)
```
